# Optimizing a Trainium2 kernel written in Bass

```python
import math
import jax, jax.numpy as jnp
from jax import lax
import numpy as np

D_MODEL = 2048
BATCH = 2
SEQ = 8192
DEPTH = 2

N_HEADS = 16
HEAD_DIM = D_MODEL // N_HEADS
D_FF = 4 * D_MODEL
CONV_WIDTH = 31
DILATED_BRANCHES = ((128, 1), (512, 4), (2048, 16))
BAND = 128
REL_BUCKETS = 32
REL_MAX_DIST = 2048
N_A = DEPTH // 2
N_B = DEPTH - N_A
ALPHA = (2 * DEPTH) ** 0.25
BETA = (8 * DEPTH) ** -0.25
LN_EPS = 1e-5

kernel_name = "yoco_conformer_dilated_hybrid"


def layer_norm(x, g, b):
    xf = x.astype(jnp.float32)
    mu = jnp.mean(xf, axis=-1, keepdims=True)
    var = jnp.mean(jnp.square(xf - mu), axis=-1, keepdims=True)
    y = (xf - mu) * lax.rsqrt(var + LN_EPS) * g.astype(jnp.float32) + b.astype(jnp.float32)
    return y.astype(x.dtype)


def conv_module(x, pw1_w, pw1_b, dw_w, dw_b, ln_g, ln_b, pw2_w, pw2_b):
    h = x @ pw1_w + pw1_b
    a, gate = jnp.split(h, 2, axis=-1)
    h = a * jax.nn.sigmoid(gate)
    h = lax.conv_general_dilated(
        h, dw_w[:, None, :].astype(h.dtype), window_strides=(1,),
        padding=[(CONV_WIDTH - 1, 0)],
        dimension_numbers=('NWC', 'WIO', 'NWC'),
        feature_group_count=D_MODEL) + dw_b
    h = jax.nn.silu(layer_norm(h, ln_g, ln_b))
    return h @ pw2_w + pw2_b


def sq_relu_mlp(x, w1, w2):
    return jnp.square(jax.nn.relu(x @ w1)) @ w2


def t5_bucket(dist):
    max_exact = REL_BUCKETS // 2
    large = max_exact + (np.log(np.maximum(dist, 1) / max_exact)
                         / math.log(REL_MAX_DIST / max_exact)
                         * (REL_BUCKETS - max_exact)).astype(np.int32)
    large = np.minimum(large, REL_BUCKETS - 1)
    return np.where(dist < max_exact, dist, large).astype(np.int32)


def dilated_branch(q, k, v, rel_bias, window, dil):
    bsz, seq, nh, dh = q.shape
    n_keys = window // dil
    L = seq // dil
    nb = -(-L // BAND)
    Lp = nb * BAND

    def by_residue(t):
        return t.reshape(bsz, L, dil, nh, dh).transpose(0, 2, 3, 1, 4)

    qd = jnp.pad(by_residue(q), ((0, 0), (0, 0), (0, 0), (0, Lp - L), (0, 0)))
    kd = jnp.pad(by_residue(k), ((0, 0), (0, 0), (0, 0), (BAND, Lp - L), (0, 0)))
    vd = jnp.pad(by_residue(v), ((0, 0), (0, 0), (0, 0), (BAND, Lp - L), (0, 0)))
    qb = qd.reshape(bsz, dil, nh, nb, BAND, dh)
    kb = kd.reshape(bsz, dil, nh, nb + 1, BAND, dh)
    vb = vd.reshape(bsz, dil, nh, nb + 1, BAND, dh)
    kc = jnp.concatenate([kb[:, :, :, :-1], kb[:, :, :, 1:]], axis=4)
    vc = jnp.concatenate([vb[:, :, :, :-1], vb[:, :, :, 1:]], axis=4)

    i = np.arange(BAND)[:, None]
    j = np.arange(2 * BAND)[None, :]
    delta = i - j + BAND
    band_ok = (delta >= 0) & (delta <= n_keys)
    blk = np.arange(nb)[:, None, None]
    valid = band_ok[None] & ~((blk == 0) & (j[None] < BAND))
    bucket = t5_bucket(np.clip(delta, 0, None) * dil)
    bias = jnp.transpose(rel_bias[bucket], (2, 0, 1)).astype(jnp.float32)

    s = jnp.einsum('bdhnqe,bdhnke->bdhnqk', qb, kc) * (dh ** -0.5)
    s = s + bias[None, None, :, None]
    s = jnp.where(jnp.asarray(valid)[None, None, None], s, -jnp.inf)
    m = jnp.max(s, axis=-1, keepdims=True)
    p = jnp.exp(s - m)
    den = jnp.sum(p, axis=-1, keepdims=True)
    o = jnp.einsum('bdhnqk,bdhnke->bdhnqe', p, vc) / den
    lse = (m + jnp.log(den))[..., 0]

    o = o.reshape(bsz, dil, nh, Lp, dh)[:, :, :, :L].transpose(0, 3, 1, 2, 4).reshape(bsz, seq, nh, dh)
    lse = lse.reshape(bsz, dil, nh, Lp)[:, :, :, :L].transpose(0, 3, 1, 2).reshape(bsz, seq, nh)
    return o, lse


def dilated_attention(x, k_sh, v_sh, wq, wo, rel_bias):
    bsz, seq, _ = x.shape
    q = (x @ wq).reshape(bsz, seq, N_HEADS, HEAD_DIM).astype(jnp.float32)
    outs, lses = [], []
    for window, dil in DILATED_BRANCHES:
        o, lse = dilated_branch(q, k_sh, v_sh, rel_bias, window, dil)
        outs.append(o)
        lses.append(lse)
    w = jax.nn.softmax(jnp.stack(lses, axis=0), axis=0)
    o = jnp.sum(w[..., None] * jnp.stack(outs, axis=0), axis=0)
    return o.reshape(bsz, seq, N_HEADS * HEAD_DIM).astype(x.dtype) @ wo


def setup_inputs(seed: int = 0) -> dict:
    key = jax.random.key(seed)
    ks = jax.random.split(key, 20)
    D = D_MODEL
    HD = N_HEADS * HEAD_DIM

    def nrm(k, shape, scale):
        return jax.random.normal(k, shape, jnp.float32) * scale

    w_k = nrm(ks[9], (D, HD), D ** -0.5)
    w_v = nrm(ks[10], (D, HD), D ** -0.5 * BETA)
    return {
        "x": nrm(ks[0], (BATCH, SEQ, D), 1.0),
        "conv_pw1_w": nrm(ks[1], (N_A, D, 2 * D), D ** -0.5),
        "conv_pw1_b": nrm(ks[2], (N_A, 2 * D), 0.02),
        "conv_dw_w": nrm(ks[3], (N_A, CONV_WIDTH, D), CONV_WIDTH ** -0.5),
        "conv_dw_b": nrm(ks[4], (N_A, D), 0.02),
        "conv_ln_g": 1.0 + nrm(ks[5], (N_A, D), 0.02),
        "conv_ln_b": nrm(ks[6], (N_A, D), 0.02),
        "conv_pw2_w": nrm(ks[7], (N_A, D, D), D ** -0.5 * BETA),
        "conv_pw2_b": nrm(ks[8], (N_A, D), 0.02),
        "w_kv": jnp.concatenate([w_k, w_v], axis=1),
        "attn_wq": nrm(ks[11], (N_B, D, HD), D ** -0.5),
        "attn_wo": nrm(ks[12], (N_B, HD, D), HD ** -0.5 * BETA),
        "rel_bias": nrm(ks[13], (REL_BUCKETS, N_HEADS), 0.2),
        "mlp_w1": nrm(ks[14], (DEPTH, D, D_FF), D ** -0.5 * BETA),
        "mlp_w2": nrm(ks[15], (DEPTH, D_FF, D), D_FF ** -0.5 * BETA),
        "ln_mix_g": 1.0 + nrm(ks[16], (DEPTH, D), 0.02),
        "ln_mix_b": nrm(ks[17], (DEPTH, D), 0.02),
        "ln_mlp_g": 1.0 + nrm(ks[18], (DEPTH, D), 0.02),
        "ln_mlp_b": nrm(ks[19], (DEPTH, D), 0.02),
    }


def reference(x, conv_pw1_w, conv_pw1_b, conv_dw_w, conv_dw_b, conv_ln_g, conv_ln_b,
              conv_pw2_w, conv_pw2_b, w_kv, attn_wq, attn_wo, rel_bias,
              mlp_w1, mlp_w2, ln_mix_g, ln_mix_b, ln_mlp_g, ln_mlp_b):
    bsz, seq, _ = x.shape
    k_sh = None
    v_sh = None
    for layer in range(DEPTH):
        if layer < N_A:
            i = layer
            mix = conv_module(x, conv_pw1_w[i], conv_pw1_b[i], conv_dw_w[i], conv_dw_b[i],
                              conv_ln_g[i], conv_ln_b[i], conv_pw2_w[i], conv_pw2_b[i])
        else:
            if layer == N_A:
                kv = (x @ w_kv).astype(jnp.float32)
                k_sh, v_sh = jnp.split(kv, 2, axis=-1)
                k_sh = k_sh.reshape(bsz, seq, N_HEADS, HEAD_DIM)
                v_sh = v_sh.reshape(bsz, seq, N_HEADS, HEAD_DIM)
            j = layer - N_A
            mix = dilated_attention(x, k_sh, v_sh, attn_wq[j], attn_wo[j], rel_bias)
        x = layer_norm(ALPHA * x + mix, ln_mix_g[layer], ln_mix_b[layer])
        x = layer_norm(ALPHA * x + sq_relu_mlp(x, mlp_w1[layer], mlp_w2[layer]),
                       ln_mlp_g[layer], ln_mlp_b[layer])
    return x
```

```python
import contextlib
import math
import numpy as np
import ml_dtypes
import concourse.bass as bass
import concourse.mybir as mybir
from concourse.bass_utils import run_bass_kernel_spmd

F32 = mybir.dt.float32
BF16 = mybir.dt.bfloat16
ALU = mybir.AluOpType
AF = mybir.ActivationFunctionType

D = 2048
NCH = 16
DFF = 8192
T = 512
NT = 4
TOK = 2048
HALO = 32
XH = 128
NHEAD = 16
ALPHA = float(4 ** 0.25)
EPS = 1e-5
QSCALE = float(128 ** -0.5)
NEG = -30000.0
CONVW = 31

V_PW1B = 0
V_DWW = V_PW1B + 32
V_DWB = V_DWW + CONVW * 16
V_CLNG = V_DWB + 16
V_CLNB = V_CLNG + 16
V_PW2B = V_CLNB + 16
V_MIXG = V_PW2B + 16
V_MIXB = V_MIXG + 32
V_MLPG = V_MIXB + 32
V_MLPB = V_MLPG + 32
NV = V_MLPB + 32


class Buf:
    __slots__ = ("name", "writers", "readers")

    def __init__(self, name):
        self.name = name
        self.writers = []
        self.readers = []


class Op:
    __slots__ = ("eng", "fn", "deps", "dsem", "sig", "val", "waits", "inc")

    def __init__(self, eng, fn, dsem, inc=None):
        self.eng = eng
        self.fn = fn
        self.deps = []
        self.dsem = dsem
        self.inc = inc if inc is not None else (16 if dsem is not None else 1)
        self.sig = False
        self.val = 0
        self.waits = None


class Prog:
    ENGS = ("pe", "act", "dve", "pool", "sp")

    def __init__(self, nc):
        self.nc = nc
        self.ops = {e: [] for e in self.ENGS}

    def op(self, eng, fn, reads=(), writes=(), accum=False, dsem=None, inc=None):
        o = Op(eng, fn, dsem, inc)
        deps = o.deps
        for b in reads:
            deps.extend(b.writers)
        for b in writes:
            if not accum:
                deps.extend(b.writers)
            deps.extend(b.readers)
        for b in writes:
            if accum:
                b.writers.append(o)
            else:
                b.writers = [o]
            b.readers = []
        for b in reads:
            b.readers.append(o)
        self.ops[eng].append(o)
        return o

    def emit(self, final_wait_eng="sp"):
        nc = self.nc
        for e in self.ENGS:
            for o in self.ops[e]:
                if o.dsem is not None:
                    o.sig = True
                nd = []
                for d in o.deps:
                    if d.eng == "pe" and e == "pe" and d.dsem is None:
                        continue
                    d.sig = True
                    nd.append(d)
                o.deps = nd
        counts = {}
        sem_names = []
        for e in self.ENGS:
            for o in self.ops[e]:
                if not o.sig:
                    continue
                key = o.dsem if o.dsem is not None else "E_" + e
                counts[key] = counts.get(key, 0) + o.inc
                o.val = counts[key]
                if key not in sem_names:
                    sem_names.append(key)
        with contextlib.ExitStack() as st:
            sems = {k: st.enter_context(nc.semaphore(k)) for k in sem_names}
            for e in self.ENGS:
                seen = {}
                for o in self.ops[e]:
                    need = {}
                    for d in o.deps:
                        key = d.dsem if d.dsem is not None else "E_" + d.eng
                        if d.val > need.get(key, 0):
                            need[key] = d.val
                    w = []
                    for k, v in need.items():
                        if seen.get(k, 0) < v:
                            seen[k] = v
                            w.append((k, v))
                    o.waits = w
            block = st.enter_context(nc.Block())
            ops = self.ops
            final = [(k, v) for k, v in counts.items() if not k.startswith("E_")]

            def run(eh, ename):
                for o in ops[ename]:
                    for k, v in o.waits:
                        eh.wait_ge(sems[k], v)
                    ins = o.fn(eh)
                    if o.sig:
                        key = o.dsem if o.dsem is not None else "E_" + ename
                        ins.then_inc(sems[key], o.inc)
                if ename == final_wait_eng:
                    for k, v in final:
                        eh.wait_ge(sems[k], v)

            @block.tensor
            def _(eh):
                run(eh, "pe")

            @block.scalar
            def _(eh):
                run(eh, "act")

            @block.vector
            def _(eh):
                run(eh, "dve")

            @block.gpsimd
            def _(eh):
                run(eh, "pool")

            @block.sync
            def _(eh):
                run(eh, "sp")


class Ctx:
    def __init__(self, nc, st, nslots, keep_prev=False):
        self.keep_prev = keep_prev
        self.nc = nc
        self.st = st
        self.P = Prog(nc)
        self.ps = [st.enter_context(nc.psum_tensor(f"ps{i}", [128, 512], F32)) for i in range(8)]
        self.pb = [Buf(f"ps{i}") for i in range(8)]
        self.rot = 0
        self.nmain = 6
        self.nslots = nslots
        self.wslots = [st.enter_context(nc.sbuf_tensor(f"wsl{i}", [128, 8192], BF16)) for i in range(nslots)]
        self.wbufs = [Buf(f"wsl{i}") for i in range(nslots)]
        self.witems = []
        self.wnext_load = 0
        self.wnext_get = 0
        self.uid = 0
        self.cc_pending = []
        self.cc_since = 0
        self.cc_gap = 14
        self.ccB = Buf("cc_order")
        self.cc_n = 0

    def sb(self, name, shape, dt):
        return self.st.enter_context(self.nc.sbuf_tensor("sb_" + name, shape, dt))

    def bank(self):
        i = self.rot
        self.rot = (self.rot + 1) % self.nmain
        return self.ps[i], self.pb[i]

    def wadd(self, key, src, c):
        self.witems.append((key, src, c))

    def _wload(self, i):
        key, src, c = self.witems[i]
        s = i % self.nslots
        dst = self.wslots[s][:, :].rearrange("p (c n) -> p c n", c=c)
        if not isinstance(src, list):
            self.P.op("pool", lambda e: e.dma_start(out=dst, in_=src), writes=[self.wbufs[s]], dsem=f"W{s}")
            return
        for pi, (lo, hi, sv) in enumerate(src):
            self.P.op("pool", lambda e, lo=lo, hi=hi, sv=sv: e.dma_start(out=dst[:, :, lo:hi], in_=sv),
                      writes=[self.wbufs[s]], accum=(pi > 0), dsem=f"W{s}")

    def cc_issue(self):
        fn, reads, writes = self.cc_pending.pop(0)
        self.cc_n += 1
        self.P.op("pool", fn, reads=list(reads) + [self.ccB], writes=list(writes) + [self.ccB], accum=True, dsem=f"CC{self.cc_n % 4}", inc=1)
        self.cc_since = 0

    def cc_flush(self):
        while self.cc_pending:
            self.cc_issue()

    def wget(self, key):
        self.cc_since += 1
        if self.cc_pending and self.cc_since >= self.cc_gap:
            self.cc_issue()
        i = self.wnext_get
        self.wnext_get += 1
        assert self.witems[i][0] == key, (self.witems[i][0], key)
        lim = min(i + self.nslots - (1 if self.keep_prev else 0), len(self.witems))
        while self.wnext_load < lim:
            self._wload(self.wnext_load)
            self.wnext_load += 1
        s = i % self.nslots
        c = self.witems[i][2]
        return self.wslots[s][:, :].rearrange("p (c n) -> p c n", c=c), self.wbufs[s]


def wsrc_k2048(w, col0):
    return w.rearrange("(c p) n -> p c n", p=128)[:, :, col0:col0 + 512]


def wsrc_k8192(w, col0):
    return w.rearrange("(c p) n -> p c n", p=128)[:, :, col0:col0 + 128]


class LN:
    def __init__(self, cx, alloc=None, banks=(6, 7)):
        self.cx = cx
        self.b1, self.b2 = banks
        if alloc is None:
            alloc = lambda name, n, dt: cx.sb(name, [128, n], dt)[:, :]
        self.zb = [alloc(f"ln_zb{i}", T, BF16) for i in range(2)]
        self.z2b = [alloc(f"ln_z2b{i}", T, BF16) for i in range(2)]
        self.zbB = [Buf(f"ln_zb{i}") for i in range(2)]
        self.z2bB = [Buf(f"ln_z2b{i}") for i in range(2)]
        self.mean = alloc("ln_mean", T, F32)
        self.var = alloc("ln_var", T, F32)
        self.rstd = alloc("ln_rstd", T, F32)
        self.meanB, self.varB, self.rstdB = Buf("ln_mean"), Buf("ln_var"), Buf("ln_rstd")
        self.r = 0

    def stats_chunk(self, c, src, srcB):
        cx, P = self.cx, self.cx.P
        i = self.r
        self.r ^= 1
        zb, z2b = self.zb[i], self.z2b[i]
        P.op("act", lambda e: e.activation(out=zb, in_=src, func=AF.Identity), reads=[srcB], writes=[self.zbB[i]])
        P.op("act", lambda e: e.activation(out=z2b, in_=src, func=AF.Square), reads=[srcB], writes=[self.z2bB[i]])
        ones = cx.ones
        P.op("pe", lambda e: e.matmul(cx.ps[self.b1][:], ones[:], zb, start=(c == 0), stop=(c == NCH - 1)),
             reads=[self.zbB[i], cx.onesB], writes=[cx.pb[self.b1]], accum=(c > 0))
        P.op("pe", lambda e: e.matmul(cx.ps[self.b2][:], ones[:], z2b, start=(c == 0), stop=(c == NCH - 1)),
             reads=[self.z2bB[i], cx.onesB], writes=[cx.pb[self.b2]], accum=(c > 0))

    def finalize(self):
        cx, P = self.cx, self.cx.P
        mean, var, rstd = self.mean, self.var, self.rstd
        P.op("dve", lambda e: e.tensor_scalar(out=mean, in0=cx.ps[self.b1][:], scalar1=1.0 / D, scalar2=None, op0=ALU.mult),
             reads=[cx.pb[self.b1]], writes=[self.meanB])
        P.op("dve", lambda e: e.tensor_tensor(out=var, in0=mean, in1=mean, op=ALU.mult),
             reads=[self.meanB], writes=[self.varB])
        P.op("dve", lambda e: e.scalar_tensor_tensor(out=var, in0=cx.ps[self.b2][:], scalar=1.0 / D, in1=var,
                                                     op0=ALU.mult, op1=ALU.subtract),
             reads=[cx.pb[self.b2], self.varB], writes=[self.varB])
        P.op("act", lambda e: e.activation(out=var, in_=var, func=AF.Sqrt, bias=cx.epsc[:, 0:1]),
             reads=[self.varB], writes=[self.varB])
        P.op("dve", lambda e: e.reciprocal(out=rstd, in_=var), reads=[self.varB], writes=[self.rstdB])
        P.op("dve", lambda e: e.scalar_tensor_tensor(out=mean, in0=mean, scalar=-1.0, in1=rstd,
                                                     op0=ALU.mult, op1=ALU.mult),
             reads=[self.meanB, self.rstdB], writes=[self.meanB])

    def norm_chunk(self, src, srcB):
        P = self.cx.P
        rstd, mean = self.rstd, self.mean
        P.op("dve", lambda e: e.tensor_tensor(out=src, in0=src, in1=rstd, op=ALU.mult),
             reads=[srcB, self.rstdB], writes=[srcB])
        P.op("dve", lambda e: e.tensor_tensor(out=src, in0=src, in1=mean, op=ALU.add),
             reads=[srcB, self.meanB], writes=[srcB])


def post_ln(cx, ln, xf, xfB, xb, xbB, gcol, bcol):
    P = cx.P
    vecs = cx.vecs
    ln.finalize()
    for c in range(NCH):
        src = xf[:, c, :]
        ln.norm_chunk(src, xfB[c])
        P.op("act", lambda e, c=c, src=src: e.activation(out=src, in_=src, func=AF.Identity,
                                                         bias=vecs[:, bcol + c:bcol + c + 1],
                                                         scale=vecs[:, gcol + c:gcol + c + 1]),
             reads=[xfB[c]], writes=[xfB[c]])
        P.op("act", lambda e, c=c, src=src: e.activation(out=xb[:, c, :], in_=src, func=AF.Identity),
             reads=[xfB[c]], writes=[xbB[c]])


def mlp_sublayer(cx, ln, layer, xf, xfB, xb, xbB, h, hB, rbuf, rB):
    P = cx.P
    rr = 0
    for g in range(16):
        wv, wB = cx.wget(("w1", layer, g))
        for i in range(4):
            hc = 4 * g + i
            pt, pB = cx.bank()
            for k in range(NCH):
                P.op("pe", lambda e, k=k, i=i, wv=wv, pt=pt: e.matmul(pt[:], wv[:, k, i * 128:(i + 1) * 128], xb[:, k, :],
                                                                      start=(k == 0), stop=(k == NCH - 1)),
                     reads=[wB, xbB[k]], writes=[pB], accum=(k > 0))
            r, rb = rbuf[rr], rB[rr]
            rr ^= 1
            P.op("act", lambda e, pt=pt, r=r: e.activation(out=r[:], in_=pt[:], func=AF.Relu), reads=[pB], writes=[rb])
            P.op("dve", lambda e, hc=hc, r=r: e.tensor_tensor(out=h[:, hc, :], in0=r[:], in1=r[:], op=ALU.mult),
                 reads=[rb], writes=[hB[hc]])
    for dc in range(NCH):
        wv, wB = cx.wget(("w2", layer, dc))
        pt, pB = cx.bank()
        for k in range(64):
            P.op("pe", lambda e, k=k, wv=wv, pt=pt: e.matmul(pt[:], wv[:, k, :], h[:, k, :], start=(k == 0), stop=(k == 63)),
                 reads=[wB, hB[k]], writes=[pB], accum=(k > 0))
        P.op("dve", lambda e, dc=dc, pt=pt: e.scalar_tensor_tensor(out=xf[:, dc, :], in0=xf[:, dc, :], scalar=ALPHA, in1=pt[:],
                                                                   op0=ALU.mult, op1=ALU.add),
             reads=[pB, xfB[dc]], writes=[xfB[dc]])
        ln.stats_chunk(dc, xf[:, dc, :], xfB[dc])
    post_ln(cx, ln, xf, xfB, xb, xbB, V_MLPG + 16 * layer, V_MLPB + 16 * layer)


def proj_residual_sublayer(cx, ln, wkey, inb, inbB, xf, xfB, xb, xbB, gcol, bcol, bias_col, tb=None, tbB=None):
    P = cx.P
    vecs = cx.vecs
    tr = 0
    for g in range(4):
        wv, wB = cx.wget((wkey, g))
        for i in range(4):
            dc = 4 * g + i
            pt, pB = cx.bank()
            for k in range(NCH):
                P.op("pe", lambda e, k=k, i=i, wv=wv, pt=pt: e.matmul(pt[:], wv[:, k, i * 128:(i + 1) * 128], inb[:, k, :],
                                                                      start=(k == 0), stop=(k == NCH - 1)),
                     reads=[wB, inbB[k]], writes=[pB], accum=(k > 0))
            if bias_col is not None:
                t_, tB_ = tb[tr], tbB[tr]
                tr ^= 1
                P.op("act", lambda e, dc=dc, pt=pt, t_=t_: e.activation(out=t_[:], in_=pt[:], func=AF.Identity,
                                                                        bias=vecs[:, bias_col + dc:bias_col + dc + 1]),
                     reads=[pB], writes=[tB_])
                P.op("dve", lambda e, dc=dc, t_=t_: e.scalar_tensor_tensor(out=xf[:, dc, :], in0=xf[:, dc, :], scalar=ALPHA,
                                                                          in1=t_[:], op0=ALU.mult, op1=ALU.add),
                     reads=[tB_, xfB[dc]], writes=[xfB[dc]])
            else:
                P.op("dve", lambda e, dc=dc, pt=pt: e.scalar_tensor_tensor(out=xf[:, dc, :], in0=xf[:, dc, :], scalar=ALPHA,
                                                                          in1=pt[:], op0=ALU.mult, op1=ALU.add),
                     reads=[pB, xfB[dc]], writes=[xfB[dc]])
            ln.stats_chunk(dc, xf[:, dc, :], xfB[dc])
    post_ln(cx, ln, xf, xfB, xb, xbB, gcol, bcol)


def load_consts(cx, vecs_d, ident_d):
    P = cx.P
    cx.vecs = cx.sb("vecs", [128, NV], F32)
    cx.ident = cx.sb("ident", [128, 128], F32)
    cx.ones = cx.sb("ones", [128, 128], BF16)
    cx.epsc = cx.sb("epsc", [128, 1], F32)
    cx.vecsB, cx.identB, cx.onesB = Buf("vecs"), Buf("ident"), Buf("ones")
    P.op("sp", lambda e: e.dma_start(out=cx.vecs[:], in_=vecs_d), writes=[cx.vecsB], dsem="CV")
    P.op("sp", lambda e: e.dma_start(out=cx.ident[:], in_=ident_d), writes=[cx.identB], dsem="CI")
    P.op("dve", lambda e: e.memset(cx.ones[:], 1.0), writes=[cx.onesB])
    P.op("dve", lambda e: e.memset(cx.epsc[:], EPS), writes=[cx.onesB], accum=True)
    cx.dmy = cx.sb("dmy", [128, 4], F32)
    consts = [cx.vecsB, cx.identB, cx.onesB]
    P.op("act", lambda e: e.activation(out=cx.dmy[:, 0:1], in_=cx.vecs[:, 0:1], func=AF.Identity), reads=consts, writes=[Buf("dmy_a")])
    P.op("dve", lambda e: e.tensor_copy(out=cx.dmy[:, 1:2], in_=cx.vecs[:, 0:1]), reads=consts, writes=[Buf("dmy_d")])
    P.op("pool", lambda e: e.tensor_copy(out=cx.dmy[:, 2:3], in_=cx.vecs[:, 0:1]), reads=consts, writes=[Buf("dmy_p")])


def layer0_and_qkv(cx, alloc, xc, hmask_d, pw1, pw2, w1, w2, wkv, wq, x2T, qT, kT, vtm, x2TB, qTB, kTB, vtB,
                   kpieces, vpieces, after_tile):
    P = cx.P
    vecs = cx.vecs
    bufs = []

    def NB(name):
        b_ = Buf(name)
        bufs.append(b_)
        return b_

    hmask = alloc("hmask", 2, F32)
    hmB = NB("hmask")
    P.op("sp", lambda e: e.dma_start(out=hmask[:, 0:1], in_=hmask_d), writes=[hmB], dsem="CH")
    xf = alloc("xf", NCH * T, F32).rearrange("p (c t) -> p c t", c=NCH)
    xb = alloc("xb", NCH * T, BF16).rearrange("p (c t) -> p c t", c=NCH)
    xfB = [NB(f"xf{c}") for c in range(NCH)]
    xbB = [NB(f"xb{c}") for c in range(NCH)]
    scratch = alloc("scratch", 64 * T, BF16)
    hB = [NB(f"h{c}") for c in range(64)]
    h = scratch.rearrange("p (c t) -> p c t", c=64)
    cv = scratch[:, 0:32 * T].bitcast(F32).rearrange("p (c t) -> p c t", c=NCH)
    u = scratch[:, 32 * T:48 * T].rearrange("p (c t) -> p c t", c=NCH)
    kst = scratch[:, 0:16 * T].rearrange("p (c t) -> p c t", c=NCH)
    qst = scratch[:, 16 * T:32 * T].rearrange("p (c t) -> p c t", c=NCH)
    vst = scratch[:, 32 * T:48 * T].rearrange("p (b f) -> p b f", b=4)

    def cvB(c):
        return [hB[2 * c], hB[2 * c + 1]]

    xh = alloc("xh", NCH * HALO, BF16).rearrange("p (c t) -> p c t", c=NCH)
    xhB = NB("xh")
    xhf = alloc("xhf", NCH * HALO, F32).rearrange("p (c t) -> p c t", c=NCH)
    xhfB = NB("xhf")
    ghalo = alloc("ghalo", NCH * HALO, F32).rearrange("p (c t) -> p c t", c=NCH)
    ghB = [NB(f"gh{c}") for c in range(NCH)]
    gbuf = [alloc(f"gbuf{i}", T + 32, BF16) for i in range(2)]
    gbB = [NB(f"gbuf{i}") for i in range(2)]
    NDG = 16
    dg = [alloc(f"dg{i}", 128, BF16) for i in range(NDG)]
    dgB = [NB(f"dg{i}") for i in range(NDG)]
    dgr = 0
    sg = [alloc(f"sg{i}", T, F32) for i in range(2)]
    sgB = [NB(f"sg{i}") for i in range(2)]
    sgh = alloc("sgh", HALO, F32)
    sghB = NB("sgh")
    ln = LN(cx, alloc=alloc)
    bufs += ln.zbB + ln.z2bB + [ln.meanB, ln.varB, ln.rstdB]

    pw1v = pw1.rearrange("(c p) n -> p c n", p=128)
    for j in range(NT):
        for g in range(8):
            cx.wadd(("pw1", j, g), [(0, 256, pw1v[:, :, 256 * g:256 * g + 256]),
                                    (256, 512, pw1v[:, :, D + 256 * g:D + 256 * g + 256])], 16)
        for g in range(4):
            cx.wadd(("pw2", g), wsrc_k2048(pw2, 512 * g), 16)
        for g in range(16):
            cx.wadd(("w1", 0, g), wsrc_k2048(w1, 512 * g), 16)
        for dc in range(16):
            cx.wadd(("w2", 0, dc), wsrc_k8192(w2, 128 * dc), 64)
        for g in range(4):
            cx.wadd(("wk", g), wsrc_k2048(wkv, 512 * g), 16)
        for g in range(4):
            cx.wadd(("wq", g), wsrc_k2048(wq, 512 * g), 16)
        for g in range(4):
            cx.wadd(("wv", g), wsrc_k2048(wkv, D + 512 * g), 16)

    xcv = xc.rearrange("(c p) t -> p c t", p=128)
    P.op("sp", lambda e: e.dma_start(out=xhf, in_=xcv[:, :, XH - HALO:XH]), writes=[xhfB], dsem="X0")
    P.op("act", lambda e: e.activation(out=xh, in_=xhf, func=AF.Identity), reads=[xhfB], writes=[xhB])

    for j in range(NT):
        P.op("sp", lambda e, j=j: e.dma_start(out=xf, in_=xcv[:, :, XH + T * j:XH + T * (j + 1)]), writes=xfB, dsem="X1")
        for c in range(NCH):
            if c % 2 == 0:
                P.op("act", lambda e, c=c: e.activation(out=xb[:, c, :], in_=xf[:, c, :], func=AF.Identity), reads=[xfB[c]], writes=[xbB[c]])
            else:
                P.op("dve", lambda e, c=c: e.tensor_copy(out=xb[:, c, :], in_=xf[:, c, :]), reads=[xfB[c]], writes=[xbB[c]])
        gr = 0
        for g in range(8):
            wv, wB = cx.wget(("pw1", j, g))
            for i in range(2):
                c = 2 * g + i
                acol = slice(i * 128, (i + 1) * 128)
                gcol = slice(256 + i * 128, 256 + (i + 1) * 128)
                pa, paB = cx.bank()
                for k in range(NCH):
                    P.op("pe", lambda e, k=k, acol=acol, wv=wv, pa=pa: e.matmul(pa[:], wv[:, k, acol], xb[:, k, :],
                                                                              start=(k == 0), stop=(k == NCH - 1)),
                         reads=[wB, xbB[k]], writes=[paB], accum=(k > 0))
                pg, pgB = cx.bank()
                for k in range(NCH):
                    P.op("pe", lambda e, k=k, gcol=gcol, wv=wv, pg=pg: e.matmul(pg[:], wv[:, k, gcol], xb[:, k, :],
                                                                              start=(k == 0), stop=(k == NCH - 1)),
                         reads=[wB, xbB[k]], writes=[pgB], accum=(k > 0))
                gi = gr
                gr ^= 1
                gb_, gbB_ = gbuf[gi], gbB[gi]
                sg_, sgB_ = sg[gi], sgB[gi]
                ba = vecs[:, V_PW1B + c:V_PW1B + c + 1]
                bg = vecs[:, V_PW1B + 16 + c:V_PW1B + 16 + c + 1]
                if j == 0:
                    ph, phB = cx.bank()
                    for k in range(NCH):
                        P.op("pe", lambda e, k=k, acol=acol, wv=wv, ph=ph: e.matmul(ph[:, 0:HALO], wv[:, k, acol], xh[:, k, :],
                                                                                  start=(k == 0), stop=(k == NCH - 1)),
                             reads=[wB, xhB], writes=[phB], accum=(k > 0))
                    for k in range(NCH):
                        P.op("pe", lambda e, k=k, gcol=gcol, wv=wv, ph=ph: e.matmul(ph[:, HALO:2 * HALO], wv[:, k, gcol], xh[:, k, :],
                                                                                  start=(k == 0), stop=(k == NCH - 1)),
                             reads=[wB, xhB], writes=[phB], accum=True)
                    P.op("act", lambda e, ph=ph, bg=bg: e.activation(out=sgh, in_=ph[:, HALO:2 * HALO], func=AF.Sigmoid, bias=bg),
                         reads=[phB], writes=[sghB])
                    P.op("dve", lambda e, c=c, ph=ph, ba=ba: e.scalar_tensor_tensor(out=ghalo[:, c, :], in0=ph[:, 0:HALO], scalar=ba, in1=sgh,
                                                                                   op0=ALU.add, op1=ALU.mult),
                         reads=[phB, sghB], writes=[ghB[c]])
                    P.op("dve", lambda e, c=c, gb_=gb_: e.tensor_scalar(out=gb_[:, 0:30], in0=ghalo[:, c, 2:32], scalar1=hmask[:, 0:1], scalar2=None,
                                                                       op0=ALU.mult),
                         reads=[ghB[c], hmB], writes=[gbB_])
                else:
                    P.op("dve", lambda e, c=c, gb_=gb_: e.tensor_copy(out=gb_[:, 0:30], in_=ghalo[:, c, 2:32]),
                         reads=[ghB[c]], writes=[gbB_])
                P.op("act", lambda e, pg=pg, sg_=sg_, bg=bg: e.activation(out=sg_, in_=pg[:], func=AF.Sigmoid, bias=bg),
                     reads=[pgB], writes=[sgB_])
                P.op("dve", lambda e, pa=pa, sg_=sg_, gb_=gb_, ba=ba: e.scalar_tensor_tensor(out=gb_[:, 30:30 + T], in0=pa[:], scalar=ba, in1=sg_,
                                                                                           op0=ALU.add, op1=ALU.mult),
                     reads=[paB, sgB_], writes=[gbB_], accum=True)
                if j < NT - 1:
                    P.op("dve", lambda e, c=c, gb_=gb_: e.tensor_copy(out=ghalo[:, c, 2:32], in_=gb_[:, T:T + 30]),
                         reads=[gbB_], writes=[ghB[c]])
                cvc = cv[:, c, :]
                bdw = vecs[:, V_DWB + c:V_DWB + c + 1]
                pc, pcB = cx.bank()
                for tap in range(CONVW):
                    wj = vecs[:, V_DWW + 16 * tap + c:V_DWW + 16 * tap + c + 1]
                    dgi, dgiB = dg[dgr], dgB[dgr]
                    dgr = (dgr + 1) % NDG
                    if tap % 2 == 0:
                        P.op("dve", lambda e, dgi=dgi, wj=wj: e.tensor_scalar(out=dgi, in0=cx.identb[:], scalar1=wj, scalar2=None, op0=ALU.mult),
                             writes=[dgiB])
                    else:
                        P.op("act", lambda e, dgi=dgi, wj=wj: e.activation(out=dgi, in_=cx.identb[:], func=AF.Identity, scale=wj),
                             writes=[dgiB])
                    P.op("pe", lambda e, pc=pc, dgi=dgi, gb_=gb_, tap=tap: e.matmul(pc[:], dgi, gb_[:, tap:tap + T],
                                                                                  start=(tap == 0), stop=(tap == CONVW - 1)),
                         reads=[dgiB, gbB_], writes=[pcB], accum=(tap > 0))
                P.op("act", lambda e, cvc=cvc, pc=pc, bdw=bdw: e.activation(out=cvc, in_=pc[:], func=AF.Identity, bias=bdw),
                     reads=[pcB], writes=cvB(c))
                ln.stats_chunk(c, cvc, hB[2 * c])
        ln.finalize()
        for c in range(NCH):
            cvc = cv[:, c, :]
            P.op("dve", lambda e, cvc=cvc: e.tensor_tensor(out=cvc, in0=cvc, in1=ln.rstd, op=ALU.mult),
                 reads=cvB(c) + [ln.rstdB], writes=cvB(c))
            P.op("dve", lambda e, cvc=cvc: e.tensor_tensor(out=cvc, in0=cvc, in1=ln.mean, op=ALU.add),
                 reads=cvB(c) + [ln.meanB], writes=cvB(c))
            P.op("act", lambda e, c=c, cvc=cvc: e.activation(out=u[:, c, :], in_=cvc, func=AF.Silu,
                                                             bias=vecs[:, V_CLNB + c:V_CLNB + c + 1], scale=vecs[:, V_CLNG + c:V_CLNG + c + 1]),
                 reads=cvB(c), writes=[hB[32 + c]])
        uB = [hB[32 + c] for c in range(NCH)]
        proj_residual_sublayer(cx, ln, "pw2", u, uB, xf, xfB, xb, xbB, V_MIXG, V_MIXB, V_PW2B, tb=sg, tbB=sgB)
        mlp_sublayer(cx, ln, 0, xf, xfB, xb, xbB, h, hB, sg, sgB)
        P.op("sp", lambda e, j=j: e.dma_start(out=x2T.rearrange("(c p) t -> p c t", p=128)[:, :, T * j:T * (j + 1)], in_=xf),
             reads=xfB, writes=[x2TB], accum=True, dsem="OX")
        for (nm, stg, cell0, scale) in (("wk", kst, 0, 1.0), ("wq", qst, 16, QSCALE)):
            for g in range(4):
                wv_, wB_ = cx.wget((nm, g))
                for i in range(4):
                    hd = 4 * g + i
                    pt, pB = cx.bank()
                    for k in range(NCH):
                        P.op("pe", lambda e, k=k, i=i, wv_=wv_, pt=pt: e.matmul(pt[:], wv_[:, k, i * 128:(i + 1) * 128], xb[:, k, :],
                                                                                start=(k == 0), stop=(k == NCH - 1)),
                             reads=[wB_, xbB[k]], writes=[pB], accum=(k > 0))
                    P.op("act", lambda e, hd=hd, pt=pt, stg=stg, scale=scale: e.activation(out=stg[:, hd, :], in_=pt[:], func=AF.Identity, scale=scale),
                         reads=[pB], writes=[hB[cell0 + hd]])
            dst, dB = (kT, kTB) if nm == "wk" else (qT, qTB)
            P.op("sp", lambda e, j=j, dst=dst, stg=stg: e.dma_start(out=dst.rearrange("(c p) t -> p c t", p=128)[:, :, T * j:T * (j + 1)], in_=stg),
                 reads=[hB[cell0 + c] for c in range(NCH)], writes=[dB], accum=True, dsem="OK" if nm == "wk" else "OQ")
            if nm == "wk":
                for p_ in range(2):
                    pap, pB_ = kpieces[j][p_]
                    P.op("sp", lambda e, pap=pap, p_=p_, stg=stg: e.dma_start(out=pap.rearrange("(c p) t -> p c t", p=128), in_=stg[:, 8 * p_:8 * p_ + 8, :]),
                         reads=[hB[cell0 + c] for c in range(8 * p_, 8 * p_ + 8)], writes=[pB_], dsem=f"PK{p_}")
        for g in range(4):
            wv_, wB_ = cx.wget(("wv", g))
            for tb in range(4):
                pt, pB = cx.bank()
                for k in range(NCH):
                    P.op("pe", lambda e, k=k, tb=tb, wv_=wv_, pt=pt: e.matmul(pt[:], xb[:, k, tb * 128:(tb + 1) * 128], wv_[:, k, :],
                                                                              start=(k == 0), stop=(k == NCH - 1)),
                         reads=[wB_, xbB[k]], writes=[pB], accum=(k > 0))
                cells = [hB[32 + 4 * tb + q] for q in range(4)]
                if g % 2 == 0:
                    P.op("act", lambda e, g=g, tb=tb, pt=pt: e.activation(out=vst[:, tb, 512 * g:512 * (g + 1)], in_=pt[:], func=AF.Identity),
                         reads=[pB], writes=cells, accum=(g > 0))
                else:
                    P.op("dve", lambda e, g=g, tb=tb, pt=pt: e.tensor_copy(out=vst[:, tb, 512 * g:512 * (g + 1)], in_=pt[:]),
                         reads=[pB], writes=cells, accum=True)
        P.op("sp", lambda e, j=j: e.dma_start(out=vtm[T * j:T * (j + 1), :].rearrange("(b p) f -> p b f", p=128), in_=vst),
             reads=[hB[32 + q] for q in range(16)], writes=[vtB], accum=True, dsem="OV")
        for p_ in range(2):
            pap, pB_ = vpieces[j][p_]
            P.op("sp", lambda e, pap=pap, p_=p_: e.dma_start(out=pap.rearrange("(b p) f -> p b f", p=128), in_=vst[:, 2 * p_:2 * p_ + 2, :]),
                 reads=[hB[32 + q] for q in range(8 * p_, 8 * p_ + 8)], writes=[pB_], dsem=f"PV{p_}")
        after_tile(j)
    return bufs


BR = (1, 4, 16)


BIGN = 76 * 1024


def attention_and_layer1(cx, carve, off, a_bufs, x2T, qT, kTs, kTp, vts, vtp, bias_d, wo, w1, w2, oT, out, identb, identbB,
                         x2TB, qTB, kTsB, kTpB, vtsB, vtpB):
    P = cx.P
    off[0] = 0
    kwin = [carve(4096) for _ in range(2)]
    qn = [carve(2048) for _ in range(2)]
    kd = [[carve(4096) for _ in range(2)] for _ in range(2)]
    qd = [[carve(2048) for _ in range(2)] for _ in range(2)]
    vsl = [carve(32 * 256) for _ in range(2)]
    bia = [carve(9 * 128) for _ in range(2)]
    ptb = [carve(128) for _ in range(4)]
    oacc = [carve(4096).bitcast(F32) for _ in range(2)]
    dacc = [carve(4096).bitcast(F32) for _ in range(2)]
    ost = [carve(2048) for _ in range(2)]
    kwinB = [Buf("kwin0"), Buf("kwin1")]
    qnB = [Buf("qn0"), Buf("qn1")]
    kdB = [[Buf(f"kd{r}{h}") for h in range(2)] for r in range(2)]
    qdB = [[Buf(f"qd{r}{h}") for h in range(2)] for r in range(2)]
    vslB = [Buf(f"vsl{i}") for i in range(2)]
    biaB = [Buf("bia0"), Buf("bia1")]
    ptbB = [Buf(f"ptb{i}") for i in range(4)]
    oaccB = [Buf("oacc0"), Buf("oacc1")]
    daccB = [Buf("dacc0"), Buf("dacc1")]
    ostB = [Buf("ost0"), Buf("ost1")]
    oTdB = Buf("oT_dram")
    b1_bufs = kwinB + qnB + kdB[0] + kdB[1] + qdB[0] + qdB[1] + vslB + biaB + ptbB + oaccB + daccB + ostB
    P.op("dve", lambda e: e.memset(cx.dmy[:, 3:4], 0.0), reads=a_bufs, writes=a_bufs + b1_bufs)
    cx.nmain = 4
    cx.rot = 0
    accrot = 0
    vrot = 0
    drot = 0
    prot = 0
    LA = 2
    qTv = qT.rearrange("(h p) t -> h p t", p=128)
    kTsv = kTs.rearrange("(h p) t -> h p t", p=128)
    kTpv = kTp.rearrange("(h p) t -> h p t", p=128)
    for hp in range(8):
        for hh in range(2):
            hd = 2 * hp + hh
            P.op("sp", lambda e, hh=hh, hd=hd: e.dma_start(out=kwin[hh][:, 0:TOK], in_=kTpv[hd]), reads=[kTpB], writes=[kwinB[hh]], dsem=f"BK{hh}")
            P.op("sp", lambda e, hh=hh, hd=hd: e.dma_start(out=kwin[hh][:, TOK:2 * TOK], in_=kTsv[hd]), reads=[kTsB], writes=[kwinB[hh]],
                 accum=True, dsem=f"BK{hh}")
            P.op("sp", lambda e, hh=hh, hd=hd: e.dma_start(out=qn[hh], in_=qTv[hd]), reads=[qTB], writes=[qnB[hh]], dsem=f"BQ{hh}")
            P.op("pool", lambda e, hh=hh, hd=hd: e.dma_start(out=bia[hh], in_=bias_d[hd]), writes=[biaB[hh]], dsem=f"BB{hh}")
        for bi, d in enumerate(BR):
            Lw = 4096 // d
            Lq = 2048 // d
            nbr = 32 // d
            vs = vsl[vrot]
            vsB = vslB[vrot]
            vsem = f"BV{vrot}"
            vrot ^= 1
            vs3 = vs.rearrange("p (b f) -> p b f", f=256)
            cols = slice(256 * hp, 256 * hp + 256)
            lo = nbr // 2 - 1
            hb = nbr // 2
            for r in range(d):
                srcp = vtp.rearrange("(b i r) f -> r i b f", i=128, r=d)[r, :, hb - 1:hb, cols]
                P.op("sp", lambda e, vs3=vs3, srcp=srcp, r=r, lo=lo, nbr=nbr: e.dma_start(out=vs3[:, r * nbr + lo:r * nbr + lo + 1, :], in_=srcp),
                     reads=[vtpB], writes=[vsB], accum=(r > 0), dsem=vsem)
                srco = vts.rearrange("(b i r) f -> r i b f", i=128, r=d)[r, :, 0:hb, cols]
                P.op("sp", lambda e, vs3=vs3, srco=srco, r=r, hb=hb, nbr=nbr: e.dma_start(out=vs3[:, r * nbr + hb:r * nbr + nbr, :], in_=srco),
                     reads=[vtsB], writes=[vsB], accum=True, dsem=vsem)
            for hh in range(2):
                if d == 1:
                    kdv, kdvB = kwin[hh], kwinB[hh]
                    qdv, qdvB = qn[hh], qnB[hh]
                else:
                    kdv, kdvB = kd[drot][hh], kdB[drot][hh]
                    qdv, qdvB = qd[drot][hh], qdB[drot][hh]
                    P.op("pool", lambda e, hh=hh, kdv=kdv, d=d: e.tensor_copy(out=kdv.rearrange("p (r m) -> p r m", r=d),
                                                                            in_=kwin[hh].rearrange("p (m r) -> p r m", r=d)),
                         reads=[kwinB[hh]], writes=[kdvB])
                    P.op("dve", lambda e, hh=hh, qdv=qdv, d=d: e.tensor_copy(out=qdv.rearrange("p (r m) -> p r m", r=d),
                                                                           in_=qn[hh].rearrange("p (m r) -> p r m", r=d)),
                         reads=[qnB[hh]], writes=[qdvB])
                ov = oacc[hh] if d == 1 else oacc[hh].rearrange("p (m r) -> p r m", r=d)
                dv = dacc[hh] if d == 1 else dacc[hh].rearrange("p (m r) -> p r m", r=d)
                steps = [(g, nn, part) for g in range(4) for nn in range(4) for part in range(2)]
                pts = {}
                accs = {}
                for s_ in range(len(steps) + LA):
                    if s_ < len(steps):
                        g, nn, part = steps[s_]
                        qpos = 512 * g + 128 * nn
                        r = qpos // Lq
                        n = (qpos % Lq) // 128
                        kpos = r * Lw + Lw // 2 + 128 * (n - 1 + part)
                        vblk = r * nbr + nbr // 2 + n - 1 + part
                        bcol = (bi * 3 + (part if not (part == 0 and n == 0) else 2)) * 128
                        pst, pstB = cx.bank()
                        sq = pst[:, 0:128]
                        P.op("pe", lambda e, sq=sq, kdv=kdv, qdv=qdv, kpos=kpos, qpos=qpos: e.matmul(sq, kdv[:, kpos:kpos + 128], qdv[:, qpos:qpos + 128],
                                                                                                   start=True, stop=False),
                             reads=[kdvB, qdvB], writes=[pstB])
                        P.op("pe", lambda e, sq=sq, hh=hh, bcol=bcol: e.matmul(sq, identb[:], bia[hh][:, bcol:bcol + 128], start=False, stop=True),
                             reads=[identbB, biaB[hh]], writes=[pstB], accum=True)
                        pt_, ptB_ = ptb[prot], ptbB[prot]
                        prot = (prot + 1) % 4
                        P.op("act", lambda e, sq=sq, pt_=pt_: e.activation(out=pt_, in_=sq, func=AF.Exp), reads=[pstB], writes=[ptB_])
                        pts[s_] = (pt_, ptB_, vblk)
                    t_ = s_ - LA
                    if t_ < 0:
                        continue
                    g, nn, part = steps[t_]
                    pt_, ptB_, vblk = pts.pop(t_)
                    if nn == 0 and part == 0:
                        accs[g] = (cx.ps[4 + 2 * accrot], cx.pb[4 + 2 * accrot], cx.ps[5 + 2 * accrot], cx.pb[5 + 2 * accrot])
                        accrot ^= 1
                    po, poB, pd, pdB = accs[g]
                    P.op("pe", lambda e, po=po, nn=nn, vs3=vs3, vblk=vblk, hh=hh, pt_=pt_, part=part: e.matmul(
                        po[:, 128 * nn:128 * (nn + 1)], vs3[:, vblk, 128 * hh:128 * (hh + 1)], pt_, start=(part == 0), stop=(part == 1)),
                         reads=[vsB, ptB_], writes=[poB], accum=not (nn == 0 and part == 0))
                    P.op("pe", lambda e, pd=pd, nn=nn, pt_=pt_, part=part: e.matmul(
                        pd[:, 128 * nn:128 * (nn + 1)], cx.ones[:], pt_, start=(part == 0), stop=(part == 1)),
                         reads=[cx.onesB, ptB_], writes=[pdB], accum=not (nn == 0 and part == 0))
                    if not (nn == 3 and part == 1):
                        continue
                    if d == 1:
                        oview = ov[:, 512 * g:512 * (g + 1)]
                        dview = dv[:, 512 * g:512 * (g + 1)]
                        pov, pdv = po[:, :], pd[:, :]
                    elif d == 4:
                        oview = ov[:, g, :]
                        dview = dv[:, g, :]
                        pov, pdv = po[:, :], pd[:, :]
                    else:
                        oview = ov[:, 4 * g:4 * g + 4, :]
                        dview = dv[:, 4 * g:4 * g + 4, :]
                        pov = po[:, :].rearrange("p (r m) -> p r m", r=4)
                        pdv = pd[:, :].rearrange("p (r m) -> p r m", r=4)
                    if bi == 0:
                        P.op("dve", lambda e, oview=oview, pov=pov: e.tensor_copy(out=oview, in_=pov), reads=[poB], writes=[oaccB[hh]], accum=(g > 0))
                        P.op("dve", lambda e, dview=dview, pdv=pdv: e.tensor_copy(out=dview, in_=pdv), reads=[pdB], writes=[daccB[hh]], accum=(g > 0))
                    else:
                        P.op("dve", lambda e, oview=oview, pov=pov: e.tensor_tensor(out=oview, in0=pov, in1=oview, op=ALU.add),
                             reads=[poB, oaccB[hh]], writes=[oaccB[hh]])
                        P.op("dve", lambda e, dview=dview, pdv=pdv: e.tensor_tensor(out=dview, in0=pdv, in1=dview, op=ALU.add),
                             reads=[pdB, daccB[hh]], writes=[daccB[hh]])
            if d != 1:
                drot ^= 1
        for hh in range(2):
            hd = 2 * hp + hh
            P.op("dve", lambda e, hh=hh: e.reciprocal(out=dacc[hh], in_=dacc[hh]), reads=[daccB[hh]], writes=[daccB[hh]])
            P.op("dve", lambda e, hh=hh: e.tensor_tensor(out=ost[hh], in0=oacc[hh], in1=dacc[hh], op=ALU.mult),
                 reads=[oaccB[hh], daccB[hh]], writes=[ostB[hh]])
            P.op("sp", lambda e, hh=hh, hd=hd: e.dma_start(out=oT[128 * hd:128 * (hd + 1), :], in_=ost[hh]),
                 reads=[ostB[hh]], writes=[oTdB], accum=True, dsem=f"BO{hh}")

    TB, NH = 1024, 2
    cx.nmain = 4
    cx.rot = 0
    off[0] = 0
    xf = carve(2 * NCH * TB).bitcast(F32).rearrange("p (c t) -> p c t", c=NCH)
    xb = carve(NCH * TB).rearrange("p (c t) -> p c t", c=NCH)
    hq = carve(16 * TB).rearrange("p (c t) -> p c t", c=16)
    sg = [carve(2 * T).bitcast(F32) for _ in range(2)]

    def ln_alloc(name, n, dt):
        return carve(n) if dt == BF16 else carve(2 * n).bitcast(F32)

    lns = [LN(cx, alloc=ln_alloc, banks=(4, 5)), LN(cx, alloc=ln_alloc, banks=(6, 7))]
    xfB = [[Buf(f"xf{c}_{hf}") for hf in range(NH)] for c in range(NCH)]
    xbB = [[Buf(f"xb{c}_{hf}") for hf in range(NH)] for c in range(NCH)]
    hqB = [[Buf(f"hq{c}_{hf}") for hf in range(NH)] for c in range(16)]
    sgB = [Buf("sg0"), Buf("sg1")]
    xfB_flat = [b_ for l_ in xfB for b_ in l_]
    xbB_flat = [b_ for l_ in xbB for b_ in l_]
    vecs = cx.vecs

    def hs(hf):
        return slice(T * hf, T * (hf + 1))

    def post_ln2(gcol, bcol):
        for hf in range(NH):
            ln = lns[hf]
            ln.finalize()
            for c in range(NCH):
                src = xf[:, c, hs(hf)]
                ln.norm_chunk(src, xfB[c][hf])
                P.op("act", lambda e, c=c, src=src: e.activation(out=src, in_=src, func=AF.Identity,
                                                                 bias=vecs[:, bcol + c:bcol + c + 1], scale=vecs[:, gcol + c:gcol + c + 1]),
                     reads=[xfB[c][hf]], writes=[xfB[c][hf]])
                P.op("act", lambda e, c=c, hf=hf, src=src: e.activation(out=xb[:, c, hs(hf)], in_=src, func=AF.Identity),
                     reads=[xfB[c][hf]], writes=[xbB[c][hf]])

    w2v = w2.rearrange("(q c p) n -> q p c n", p=128, c=16)
    for jj in range(TOK // TB):
        for g in range(4):
            cx.wadd(("wo", jj, g), wsrc_k2048(wo, 512 * g), 16)
        for q_ in range(4):
            for g in range(4):
                cx.wadd(("w1", jj, q_, g), wsrc_k2048(w1, 2048 * q_ + 512 * g), 16)
            for g in range(4):
                cx.wadd(("w2", jj, q_, g), w2v[q_][:, :, 512 * g:512 * (g + 1)], 16)

    for jj in range(TOK // TB):
        tsl = slice(TB * jj, TB * (jj + 1))
        P.op("sp", lambda e, tsl=tsl: e.dma_start(out=xf, in_=x2T.rearrange("(c p) t -> p c t", p=128)[:, :, tsl]),
             reads=[x2TB], writes=(xfB_flat + b1_bufs if jj == 0 else xfB_flat), dsem="B2X")
        P.op("sp", lambda e, tsl=tsl: e.dma_start(out=xb, in_=oT.rearrange("(c p) t -> p c t", p=128)[:, :, tsl]),
             reads=[oTdB], writes=xbB_flat, dsem="B2O")
        for g in range(4):
            wv, wB = cx.wget(("wo", jj, g))
            for i in range(4):
                dc = 4 * g + i
                for hf in range(NH):
                    pt, pB = cx.bank()
                    for k in range(NCH):
                        P.op("pe", lambda e, k=k, i=i, hf=hf, wv=wv, pt=pt: e.matmul(pt[:], wv[:, k, i * 128:(i + 1) * 128], xb[:, k, hs(hf)],
                                                                                     start=(k == 0), stop=(k == NCH - 1)),
                             reads=[wB, xbB[k][hf]], writes=[pB], accum=(k > 0))
                    P.op("dve", lambda e, dc=dc, hf=hf, pt=pt: e.scalar_tensor_tensor(out=xf[:, dc, hs(hf)], in0=xf[:, dc, hs(hf)], scalar=ALPHA,
                                                                                     in1=pt[:], op0=ALU.mult, op1=ALU.add),
                         reads=[pB, xfB[dc][hf]], writes=[xfB[dc][hf]])
                    lns[hf].stats_chunk(dc, xf[:, dc, hs(hf)], xfB[dc][hf])
        post_ln2(V_MIXG + 16, V_MIXB + 16)
        rr = 0
        for q_ in range(4):
            for g in range(4):
                wv, wB = cx.wget(("w1", jj, q_, g))
                for i in range(4):
                    hc = 4 * g + i
                    for hf in range(NH):
                        pt, pB = cx.bank()
                        for k in range(NCH):
                            P.op("pe", lambda e, k=k, i=i, hf=hf, wv=wv, pt=pt: e.matmul(pt[:], wv[:, k, i * 128:(i + 1) * 128], xb[:, k, hs(hf)],
                                                                                         start=(k == 0), stop=(k == NCH - 1)),
                                 reads=[wB, xbB[k][hf]], writes=[pB], accum=(k > 0))
                        r, rb = sg[rr], sgB[rr]
                        rr ^= 1
                        P.op("act", lambda e, pt=pt, r=r: e.activation(out=r, in_=pt[:], func=AF.Relu), reads=[pB], writes=[rb])
                        P.op("dve", lambda e, hc=hc, hf=hf, r=r: e.tensor_tensor(out=hq[:, hc, hs(hf)], in0=r, in1=r, op=ALU.mult),
                             reads=[rb], writes=[hqB[hc][hf]])
            for g in range(4):
                wv, wB = cx.wget(("w2", jj, q_, g))
                for i in range(4):
                    dc = 4 * g + i
                    for hf in range(NH):
                        pt, pB = cx.bank()
                        for k in range(16):
                            P.op("pe", lambda e, k=k, i=i, hf=hf, wv=wv, pt=pt: e.matmul(pt[:], wv[:, k, i * 128:(i + 1) * 128], hq[:, k, hs(hf)],
                                                                                         start=(k == 0), stop=(k == 15)),
                                 reads=[wB, hqB[k][hf]], writes=[pB], accum=(k > 0))
                        if q_ == 0:
                            P.op("dve", lambda e, dc=dc, hf=hf, pt=pt: e.scalar_tensor_tensor(out=xf[:, dc, hs(hf)], in0=xf[:, dc, hs(hf)], scalar=ALPHA,
                                                                                             in1=pt[:], op0=ALU.mult, op1=ALU.add),
                                 reads=[pB, xfB[dc][hf]], writes=[xfB[dc][hf]])
                        else:
                            P.op("dve", lambda e, dc=dc, hf=hf, pt=pt: e.tensor_tensor(out=xf[:, dc, hs(hf)], in0=pt[:], in1=xf[:, dc, hs(hf)], op=ALU.add),
                                 reads=[pB, xfB[dc][hf]], writes=[xfB[dc][hf]])
                        if q_ == 3:
                            lns[hf].stats_chunk(dc, xf[:, dc, hs(hf)], xfB[dc][hf])
        post_ln2(V_MLPG + 16, V_MLPB + 16)
        P.op("sp", lambda e, tsl=tsl: e.dma_start(out=out.rearrange("(c p) t -> p c t", p=128)[:, :, tsl], in_=xf),
             reads=xfB_flat, dsem="OUT")


def _fm(v):
    return np.ascontiguousarray(v.reshape(-1, 128).T)


def _t5_bucket(dist):
    max_exact = 16
    large = max_exact + (np.log(np.maximum(dist, 1) / max_exact) / math.log(2048 / max_exact) * (32 - max_exact)).astype(np.int32)
    large = np.minimum(large, 31)
    return np.where(dist < max_exact, dist, large).astype(np.int32)


def _bias_tiles(rel_bias, has_prev):
    i = np.arange(128)[:, None]
    j = np.arange(256)[None, :]
    delta = i - j + 128
    ok = (delta >= 0) & (delta <= 128)
    res = np.full((NHEAD, 128, 9, 128), NEG, np.float32)
    for bi, d in enumerate(BR):
        bucket = _t5_bucket(np.clip(delta, 0, None) * d)
        b = rel_bias[bucket]
        b = np.where(ok[:, :, None], b, np.float32(NEG))
        bt = np.transpose(b, (2, 1, 0))
        res[:, :, bi * 3 + 0, :] = bt[:, 0:128, :]
        res[:, :, bi * 3 + 1, :] = bt[:, 128:256, :]
        if has_prev:
            res[:, :, bi * 3 + 2, :] = bt[:, 0:128, :]
    return np.ascontiguousarray(res.reshape(NHEAD, 128, 9 * 128))


_NC_CACHE = {}


def build_fused():
    nc = bass.Bass("TRN2", target_bir_lowering=False)
    I32 = mybir.dt.int32
    xc = nc.dram_tensor("xc", [D, XH + TOK], F32, kind="ExternalInput").ap()
    hmask_d = nc.dram_tensor("hmask", [128, 1], F32, kind="ExternalInput").ap()
    prow_d = nc.dram_tensor("prevrow", [1, 1], I32, kind="ExternalInput").ap()
    vecs_d = nc.dram_tensor("vecs", [128, NV], F32, kind="ExternalInput").ap()
    ident_d = nc.dram_tensor("ident", [128, 128], F32, kind="ExternalInput").ap()
    bias_d = nc.dram_tensor("biasT", [NHEAD, 128, 9 * 128], F32, kind="ExternalInput").ap()
    pw1 = nc.dram_tensor("pw1", [D, 2 * D], F32, kind="ExternalInput").ap()
    pw2 = nc.dram_tensor("pw2", [D, D], F32, kind="ExternalInput").ap()
    w1a = nc.dram_tensor("w1a", [D, DFF], F32, kind="ExternalInput").ap()
    w2a = nc.dram_tensor("w2a", [DFF, D], F32, kind="ExternalInput").ap()
    w1b = nc.dram_tensor("w1b", [D, DFF], F32, kind="ExternalInput").ap()
    w2b = nc.dram_tensor("w2b", [DFF, D], F32, kind="ExternalInput").ap()
    wkv = nc.dram_tensor("wkv", [D, 2 * D], F32, kind="ExternalInput").ap()
    wq = nc.dram_tensor("wq", [D, D], F32, kind="ExternalInput").ap()
    wo = nc.dram_tensor("wo", [D, D], F32, kind="ExternalInput").ap()
    out = nc.dram_tensor("out", [D, TOK], F32, kind="ExternalOutput").ap()
    x2T = nc.dram_tensor("x2T", [D, TOK], F32).ap()
    qT = nc.dram_tensor("qT", [D, TOK], BF16).ap()
    kTs_t = nc.dram_tensor("kTs", [D, TOK], BF16)
    vts_t = nc.dram_tensor("vts", [TOK, D], BF16)
    kps = [[nc.dram_tensor(f"kps{j}{p}", [D // 2, T], BF16) for p in range(2)] for j in range(NT)]
    kpa = [[nc.dram_tensor(f"kpa{j}{p}", [4 * (D // 2), T], BF16) for p in range(2)] for j in range(NT)]
    vps = [[nc.dram_tensor(f"vps{j}{p}", [T // 2, D], BF16) for p in range(2)] for j in range(NT)]
    vpa = [[nc.dram_tensor(f"vpa{j}{p}", [4 * (T // 2), D], BF16) for p in range(2)] for j in range(NT)]
    kTp = nc.dram_tensor("kTp", [D, TOK], BF16).ap()
    vtp = nc.dram_tensor("vtp", [TOK, D], BF16).ap()
    oT = nc.dram_tensor("oT", [D, TOK], BF16).ap()
    kTs, vts = kTs_t.ap(), vts_t.ap()

    with contextlib.ExitStack() as st:
        cx = Ctx(nc, st, nslots=3)
        P = cx.P
        st.enter_context(nc.allow_low_precision("bf16 matmul operands, fp32 accumulation"))
        load_consts(cx, vecs_d, ident_d)
        identb = cx.sb("identb", [128, 128], BF16)
        identbB = Buf("identb")
        P.op("act", lambda e: e.activation(out=identb[:], in_=cx.ident[:], func=AF.Identity), reads=[cx.identB], writes=[identbB])
        cx.identb = identb
        P.op("dve", lambda e: e.tensor_copy(out=cx.dmy[:, 3:4], in_=identb[:, 0:1]), reads=[identbB], writes=[Buf("dmy_d2")])
        big = cx.sb("big", [128, BIGN], BF16)
        off = [0]

        def carve(n):
            n = (n + 15) // 16 * 16
            o = off[0]
            off[0] += n
            assert off[0] <= BIGN, off[0]
            return big[:, o:o + n]

        def alloc(name, n, dt):
            return carve(n) if dt == BF16 else carve(2 * n).bitcast(F32)[:, 0:n]

        x2TB, qTB, kTsB, vtsB, kTallB, vallB, kTpB, vtpB = (Buf(n_) for n_ in ("x2T", "qT", "kTs", "vts", "kTall", "vall", "kTp", "vtp"))
        groups = [[0, 1, 2, 3], [4, 5, 6, 7]]
        kpieces = [[(kps[j][p].ap(), Buf(f"kps{j}{p}")) for p in range(2)] for j in range(NT)]
        vpieces = [[(vps[j][p].ap(), Buf(f"vps{j}{p}")) for p in range(2)] for j in range(NT)]

        def after_tile(j):
            for p in range(2):
                cx.cc_pending.append((lambda e, j=j, p=p: e.collective_compute("AllGather", ALU.bypass, replica_groups=groups,
                                                                              ins=[kps[j][p].ap().opt()], outs=[kpa[j][p].ap().opt()]),
                                      [kpieces[j][p][1]], [kTallB]))
                cx.cc_pending.append((lambda e, j=j, p=p: e.collective_compute("AllGather", ALU.bypass, replica_groups=groups,
                                                                              ins=[vps[j][p].ap().opt()], outs=[vpa[j][p].ap().opt()]),
                                      [vpieces[j][p][1]], [vallB]))
            if j == NT - 1:
                cx.cc_flush()

        a_bufs = layer0_and_qkv(cx, alloc, xc, hmask_d, pw1, pw2, w1a, w2a, wkv, wq, x2T, qT, kTs, vts, x2TB, qTB, kTsB, vtsB,
                                kpieces, vpieces, after_tile)

        reg = st.enter_context(nc.sync.register("prevrow"))

        pval = []

        def copy_prev(dst, src_t, rows):
            def fn(e):
                if not pval:
                    e.reg_load(reg, prow_d[0:1, 0:1])
                    pval.append(e.snap(reg, min_val=0, max_val=3))
                return e.dma_start(out=dst, in_=src_t.ap()[bass.ts(pval[0], rows), :])
            return fn

        for j in range(NT):
            for p in range(2):
                P.op("sp", copy_prev(kTp[(D // 2) * p:(D // 2) * (p + 1), T * j:T * (j + 1)], kpa[j][p], D // 2),
                     reads=[kTallB], writes=[kTpB], accum=True, dsem="XK")
                P.op("sp", copy_prev(vtp[T * j + (T // 2) * p:T * j + (T // 2) * (p + 1), :], vpa[j][p], T // 2),
                     reads=[vallB], writes=[vtpB], accum=True, dsem="XV")

        attention_and_layer1(cx, carve, off, a_bufs, x2T, qT, kTs, kTp, vts, vtp, bias_d, wo, w1b, w2b, oT, out, identb, identbB,
                             x2TB, qTB, kTsB, kTpB, vtsB, vtpB)
        assert cx.wnext_get == len(cx.witems)
        import os
        if os.environ.get("KDEBUG"):
            for nm, ap_, shp, dt_, rb in (("dbg_kTp", kTp, [D, TOK], BF16, kTpB), ("dbg_kTs", kTs, [D, TOK], BF16, kTsB),
                                          ("dbg_vtp", vtp, [TOK, D], BF16, vtpB), ("dbg_vts", vts, [TOK, D], BF16, vtsB),
                                          ("dbg_oT", oT, [D, TOK], BF16, None), ("dbg_x2T", x2T, [D, TOK], F32, x2TB)):
                dd = nc.dram_tensor(nm, shp, dt_, kind="ExternalOutput").ap()
                P.op("sp", lambda e, dd=dd, ap_=ap_: e.dma_start(out=dd, in_=ap_), reads=([rb] if rb is not None else []), dsem="DBG")
        P.emit()
    return nc


def kernel(x, conv_pw1_w, conv_pw1_b, conv_dw_w, conv_dw_b, conv_ln_g, conv_ln_b, conv_pw2_w, conv_pw2_b,
           w_kv, attn_wq, attn_wo, rel_bias, mlp_w1, mlp_w2, ln_mix_g, ln_mix_b, ln_mlp_g, ln_mlp_b):
    f32 = np.float32
    x = np.asarray(x, f32)
    ncore = 8
    vecs = np.zeros((128, NV), f32)
    vecs[:, V_PW1B:V_PW1B + 32] = _fm(np.asarray(conv_pw1_b, f32)[0])
    dw = np.asarray(conv_dw_w, f32)[0]
    for tap in range(CONVW):
        vecs[:, V_DWW + 16 * tap:V_DWW + 16 * tap + 16] = _fm(dw[tap])
    vecs[:, V_DWB:V_DWB + 16] = _fm(np.asarray(conv_dw_b, f32)[0])
    vecs[:, V_CLNG:V_CLNG + 16] = _fm(np.asarray(conv_ln_g, f32)[0])
    vecs[:, V_CLNB:V_CLNB + 16] = _fm(np.asarray(conv_ln_b, f32)[0])
    vecs[:, V_PW2B:V_PW2B + 16] = _fm(np.asarray(conv_pw2_b, f32)[0])
    for l in range(2):
        vecs[:, V_MIXG + 16 * l:V_MIXG + 16 * l + 16] = _fm(np.asarray(ln_mix_g, f32)[l])
        vecs[:, V_MIXB + 16 * l:V_MIXB + 16 * l + 16] = _fm(np.asarray(ln_mix_b, f32)[l])
        vecs[:, V_MLPG + 16 * l:V_MLPG + 16 * l + 16] = _fm(np.asarray(ln_mlp_g, f32)[l])
        vecs[:, V_MLPB + 16 * l:V_MLPB + 16 * l + 16] = _fm(np.asarray(ln_mlp_b, f32)[l])
    ident = np.eye(128, dtype=f32)
    w1 = np.asarray(mlp_w1, f32)
    w2 = np.asarray(mlp_w2, f32)
    rel_bias = np.asarray(rel_bias, f32)
    shared = {
        "vecs": vecs, "ident": ident,
        "pw1": np.ascontiguousarray(np.asarray(conv_pw1_w, f32)[0]),
        "pw2": np.ascontiguousarray(np.asarray(conv_pw2_w, f32)[0]),
        "w1a": np.ascontiguousarray(w1[0]), "w2a": np.ascontiguousarray(w2[0]),
        "w1b": np.ascontiguousarray(w1[1]), "w2b": np.ascontiguousarray(w2[1]),
        "wkv": np.ascontiguousarray(np.asarray(w_kv, f32)),
        "wq": np.ascontiguousarray(np.asarray(attn_wq, f32)[0]),
        "wo": np.ascontiguousarray(np.asarray(attn_wo, f32)[0]),
    }
    bias_tiles = {True: _bias_tiles(rel_bias, True), False: _bias_tiles(rel_bias, False)}
    if "f" not in _NC_CACHE:
        _NC_CACHE["f"] = build_fused()
    ncf = _NC_CACHE["f"]
    in_maps = []
    for c in range(ncore):
        b, q = divmod(c, 4)
        xc = np.zeros((D, XH + TOK), f32)
        xc[:, XH:] = x[b, q * TOK:(q + 1) * TOK].T
        if q > 0:
            xc[:, :XH] = x[b, q * TOK - XH:q * TOK].T
        m = dict(shared)
        m.update({"xc": xc, "hmask": np.full((128, 1), 1.0 if q > 0 else 0.0, f32),
                  "prevrow": np.array([[max(q - 1, 0)]], np.int32), "biasT": bias_tiles[q > 0]})
        in_maps.append(m)
    res = run_bass_kernel_spmd(ncf, in_maps, core_ids=list(range(ncore))).results
    _NC_CACHE["last"] = res
    outp = np.zeros((2, 8192, D), f32)
    for c in range(ncore):
        b, q = divmod(c, 4)
        outp[b, q * TOK:(q + 1) * TOK] = np.asarray(res[c]["out"]).T
    return outp
```

```python
import contextlib
import math
import numpy as np
import ml_dtypes
import concourse.bass as bass
import concourse.mybir as mybir
from concourse.bass_utils import run_bass_kernel_spmd

F32 = mybir.dt.float32
BF16 = mybir.dt.bfloat16
ALU = mybir.AluOpType
AF = mybir.ActivationFunctionType

D = 2048
NCH = 16
DFF = 8192
T = 512
NT = 4
TOK = 2048
HALO = 32
XH = 128
NHEAD = 16
ALPHA = float(4 ** 0.25)
EPS = 1e-5
QSCALE = float(128 ** -0.5)
NEG = -30000.0
CONVW = 31

V_PW1B = 0
V_DWW = V_PW1B + 32
V_DWB = V_DWW + CONVW * 16
V_CLNG = V_DWB + 16
V_CLNB = V_CLNG + 16
V_PW2B = V_CLNB + 16
V_MIXG = V_PW2B + 16
V_MIXB = V_MIXG + 32
V_MLPG = V_MIXB + 32
V_MLPB = V_MLPG + 32
NV = V_MLPB + 32


class Buf:
    __slots__ = ("name", "writers", "readers")

    def __init__(self, name):
        self.name = name
        self.writers = []
        self.readers = []


class Op:
    __slots__ = ("eng", "fn", "deps", "dsem", "sig", "val", "waits", "inc")

    def __init__(self, eng, fn, dsem, inc=None):
        self.eng = eng
        self.fn = fn
        self.deps = []
        self.dsem = dsem
        self.inc = inc if inc is not None else (16 if dsem is not None else 1)
        self.sig = False
        self.val = 0
        self.waits = None


class Prog:
    ENGS = ("pe", "act", "dve", "pool", "sp")

    def __init__(self, nc):
        self.nc = nc
        self.ops = {e: [] for e in self.ENGS}

    def op(self, eng, fn, reads=(), writes=(), accum=False, dsem=None, inc=None):
        o = Op(eng, fn, dsem, inc)
        deps = o.deps
        for b in reads:
            deps.extend(b.writers)
        for b in writes:
            if not accum:
                deps.extend(b.writers)
            deps.extend(b.readers)
        for b in writes:
            if accum:
                b.writers.append(o)
            else:
                b.writers = [o]
            b.readers = []
        for b in reads:
            b.readers.append(o)
        self.ops[eng].append(o)
        return o

    def emit(self, final_wait_eng="sp"):
        nc = self.nc
        for e in self.ENGS:
            for o in self.ops[e]:
                if o.dsem is not None:
                    o.sig = True
                nd = []
                for d in o.deps:
                    if d.eng == "pe" and e == "pe" and d.dsem is None:
                        continue
                    d.sig = True
                    nd.append(d)
                o.deps = nd
        counts = {}
        sem_names = []
        for e in self.ENGS:
            for o in self.ops[e]:
                if not o.sig:
                    continue
                key = o.dsem if o.dsem is not None else "E_" + e
                counts[key] = counts.get(key, 0) + o.inc
                o.val = counts[key]
                if key not in sem_names:
                    sem_names.append(key)
        with contextlib.ExitStack() as st:
            sems = {k: st.enter_context(nc.semaphore(k)) for k in sem_names}
            for e in self.ENGS:
                seen = {}
                for o in self.ops[e]:
                    need = {}
                    for d in o.deps:
                        key = d.dsem if d.dsem is not None else "E_" + d.eng
                        if d.val > need.get(key, 0):
                            need[key] = d.val
                    w = []
                    for k, v in need.items():
                        if seen.get(k, 0) < v:
                            seen[k] = v
                            w.append((k, v))
                    o.waits = w
            block = st.enter_context(nc.Block())
            ops = self.ops
            final = [(k, v) for k, v in counts.items() if not k.startswith("E_")]

            def run(eh, ename):
                for o in ops[ename]:
                    for k, v in o.waits:
                        eh.wait_ge(sems[k], v)
                    ins = o.fn(eh)
                    if o.sig:
                        key = o.dsem if o.dsem is not None else "E_" + ename
                        ins.then_inc(sems[key], o.inc)
                if ename == final_wait_eng:
                    for k, v in final:
                        eh.wait_ge(sems[k], v)

            @block.tensor
            def _(eh):
                run(eh, "pe")

            @block.scalar
            def _(eh):
                run(eh, "act")

            @block.vector
            def _(eh):
                run(eh, "dve")

            @block.gpsimd
            def _(eh):
                run(eh, "pool")

            @block.sync
            def _(eh):
                run(eh, "sp")


class Ctx:
    def __init__(self, nc, st, nslots, keep_prev=False):
        self.keep_prev = keep_prev
        self.nc = nc
        self.st = st
        self.P = Prog(nc)
        self.ps = [st.enter_context(nc.psum_tensor(f"ps{i}", [128, 512], F32)) for i in range(8)]
        self.pb = [Buf(f"ps{i}") for i in range(8)]
        self.rot = 0
        self.nmain = 6
        self.nslots = nslots
        self.wslots = [st.enter_context(nc.sbuf_tensor(f"wsl{i}", [128, 8192], BF16)) for i in range(nslots)]
        self.wbufs = [Buf(f"wsl{i}") for i in range(nslots)]
        self.witems = []
        self.wnext_load = 0
        self.wnext_get = 0
        self.uid = 0
        self.cc_pending = []
        self.cc_since = 0
        self.cc_gap = 14
        self.ccB = Buf("cc_order")
        self.cc_n = 0

    def sb(self, name, shape, dt):
        return self.st.enter_context(self.nc.sbuf_tensor("sb_" + name, shape, dt))

    def bank(self):
        i = self.rot
        self.rot = (self.rot + 1) % self.nmain
        return self.ps[i], self.pb[i]

    def wadd(self, key, src, c):
        self.witems.append((key, src, c))

    def _wload(self, i):
        key, src, c = self.witems[i]
        s = i % self.nslots
        dst = self.wslots[s][:, :].rearrange("p (c n) -> p c n", c=c)
        if not isinstance(src, list):
            self.P.op("pool", lambda e: e.dma_start(out=dst, in_=src), writes=[self.wbufs[s]], dsem=f"W{s}")
            return
        for pi, (lo, hi, sv) in enumerate(src):
            self.P.op("pool", lambda e, lo=lo, hi=hi, sv=sv: e.dma_start(out=dst[:, :, lo:hi], in_=sv),
                      writes=[self.wbufs[s]], accum=(pi > 0), dsem=f"W{s}")

    def cc_issue(self):
        fn, reads, writes = self.cc_pending.pop(0)
        self.cc_n += 1
        self.P.op("pool", fn, reads=list(reads) + [self.ccB], writes=list(writes) + [self.ccB], accum=True, dsem=f"CC{self.cc_n % 4}", inc=1)
        self.cc_since = 0

    def cc_flush(self):
        while self.cc_pending:
            self.cc_issue()

    def wget(self, key):
        self.cc_since += 1
        if self.cc_pending and self.cc_since >= self.cc_gap:
            self.cc_issue()
        i = self.wnext_get
        self.wnext_get += 1
        assert self.witems[i][0] == key, (self.witems[i][0], key)
        lim = min(i + self.nslots - (1 if self.keep_prev else 0), len(self.witems))
        while self.wnext_load < lim:
            self._wload(self.wnext_load)
            self.wnext_load += 1
        s = i % self.nslots
        c = self.witems[i][2]
        return self.wslots[s][:, :].rearrange("p (c n) -> p c n", c=c), self.wbufs[s]


def wsrc_k2048(w, col0):
    return w.rearrange("(c p) n -> p c n", p=128)[:, :, col0:col0 + 512]


def wsrc_k8192(w, col0):
    return w.rearrange("(c p) n -> p c n", p=128)[:, :, col0:col0 + 128]


class LN:
    def __init__(self, cx, alloc=None, banks=(6, 7)):
        self.cx = cx
        self.b1, self.b2 = banks
        if alloc is None:
            alloc = lambda name, n, dt: cx.sb(name, [128, n], dt)[:, :]
        self.zb = [alloc(f"ln_zb{i}", T, BF16) for i in range(2)]
        self.z2b = [alloc(f"ln_z2b{i}", T, BF16) for i in range(2)]
        self.zbB = [Buf(f"ln_zb{i}") for i in range(2)]
        self.z2bB = [Buf(f"ln_z2b{i}") for i in range(2)]
        self.mean = alloc("ln_mean", T, F32)
        self.var = alloc("ln_var", T, F32)
        self.rstd = alloc("ln_rstd", T, F32)
        self.meanB, self.varB, self.rstdB = Buf("ln_mean"), Buf("ln_var"), Buf("ln_rstd")
        self.r = 0

    def stats_chunk(self, c, src, srcB):
        cx, P = self.cx, self.cx.P
        i = self.r
        self.r ^= 1
        zb, z2b = self.zb[i], self.z2b[i]
        P.op("act", lambda e: e.activation(out=zb, in_=src, func=AF.Identity), reads=[srcB], writes=[self.zbB[i]])
        P.op("act", lambda e: e.activation(out=z2b, in_=src, func=AF.Square), reads=[srcB], writes=[self.z2bB[i]])
        ones = cx.ones
        P.op("pe", lambda e: e.matmul(cx.ps[self.b1][:], ones[:], zb, start=(c == 0), stop=(c == NCH - 1)),
             reads=[self.zbB[i], cx.onesB], writes=[cx.pb[self.b1]], accum=(c > 0))
        P.op("pe", lambda e: e.matmul(cx.ps[self.b2][:], ones[:], z2b, start=(c == 0), stop=(c == NCH - 1)),
             reads=[self.z2bB[i], cx.onesB], writes=[cx.pb[self.b2]], accum=(c > 0))

    def finalize(self):
        cx, P = self.cx, self.cx.P
        mean, var, rstd = self.mean, self.var, self.rstd
        P.op("dve", lambda e: e.tensor_scalar(out=mean, in0=cx.ps[self.b1][:], scalar1=1.0 / D, scalar2=None, op0=ALU.mult),
             reads=[cx.pb[self.b1]], writes=[self.meanB])
        P.op("dve", lambda e: e.tensor_tensor(out=var, in0=mean, in1=mean, op=ALU.mult),
             reads=[self.meanB], writes=[self.varB])
        P.op("dve", lambda e: e.scalar_tensor_tensor(out=var, in0=cx.ps[self.b2][:], scalar=1.0 / D, in1=var,
                                                     op0=ALU.mult, op1=ALU.subtract),
             reads=[cx.pb[self.b2], self.varB], writes=[self.varB])
        P.op("act", lambda e: e.activation(out=var, in_=var, func=AF.Sqrt, bias=cx.epsc[:, 0:1]),
             reads=[self.varB], writes=[self.varB])
        P.op("dve", lambda e: e.reciprocal(out=rstd, in_=var), reads=[self.varB], writes=[self.rstdB])
        P.op("dve", lambda e: e.scalar_tensor_tensor(out=mean, in0=mean, scalar=-1.0, in1=rstd,
                                                     op0=ALU.mult, op1=ALU.mult),
             reads=[self.meanB, self.rstdB], writes=[self.meanB])

    def norm_chunk(self, src, srcB):
        P = self.cx.P
        rstd, mean = self.rstd, self.mean
        P.op("dve", lambda e: e.tensor_tensor(out=src, in0=src, in1=rstd, op=ALU.mult),
             reads=[srcB, self.rstdB], writes=[srcB])
        P.op("dve", lambda e: e.tensor_tensor(out=src, in0=src, in1=mean, op=ALU.add),
             reads=[srcB, self.meanB], writes=[srcB])


def post_ln(cx, ln, xf, xfB, xb, xbB, gcol, bcol):
    P = cx.P
    vecs = cx.vecs
    ln.finalize()
    for c in range(NCH):
        src = xf[:, c, :]
        ln.norm_chunk(src, xfB[c])
        P.op("act", lambda e, c=c, src=src: e.activation(out=src, in_=src, func=AF.Identity,
                                                         bias=vecs[:, bcol + c:bcol + c + 1],
                                                         scale=vecs[:, gcol + c:gcol + c + 1]),
             reads=[xfB[c]], writes=[xfB[c]])
        P.op("act", lambda e, c=c, src=src: e.activation(out=xb[:, c, :], in_=src, func=AF.Identity),
             reads=[xfB[c]], writes=[xbB[c]])


def mlp_sublayer(cx, ln, layer, xf, xfB, xb, xbB, h, hB, rbuf, rB):
    P = cx.P
    rr = 0
    for g in range(16):
        wv, wB = cx.wget(("w1", layer, g))
        for i in range(4):
            hc = 4 * g + i
            pt, pB = cx.bank()
            for k in range(NCH):
                P.op("pe", lambda e, k=k, i=i, wv=wv, pt=pt: e.matmul(pt[:], wv[:, k, i * 128:(i + 1) * 128], xb[:, k, :],
                                                                      start=(k == 0), stop=(k == NCH - 1)),
                     reads=[wB, xbB[k]], writes=[pB], accum=(k > 0))
            r, rb = rbuf[rr], rB[rr]
            rr ^= 1
            P.op("act", lambda e, pt=pt, r=r: e.activation(out=r[:], in_=pt[:], func=AF.Relu), reads=[pB], writes=[rb])
            P.op("dve", lambda e, hc=hc, r=r: e.tensor_tensor(out=h[:, hc, :], in0=r[:], in1=r[:], op=ALU.mult),
                 reads=[rb], writes=[hB[hc]])
    for dc in range(NCH):
        wv, wB = cx.wget(("w2", layer, dc))
        pt, pB = cx.bank()
        for k in range(64):
            P.op("pe", lambda e, k=k, wv=wv, pt=pt: e.matmul(pt[:], wv[:, k, :], h[:, k, :], start=(k == 0), stop=(k == 63)),
                 reads=[wB, hB[k]], writes=[pB], accum=(k > 0))
        P.op("dve", lambda e, dc=dc, pt=pt: e.scalar_tensor_tensor(out=xf[:, dc, :], in0=xf[:, dc, :], scalar=ALPHA, in1=pt[:],
                                                                   op0=ALU.mult, op1=ALU.add),
             reads=[pB, xfB[dc]], writes=[xfB[dc]])
        ln.stats_chunk(dc, xf[:, dc, :], xfB[dc])
    post_ln(cx, ln, xf, xfB, xb, xbB, V_MLPG + 16 * layer, V_MLPB + 16 * layer)


def mlp_sublayer_q(cx, ln, lkey, xf, xfB, xb, xbB, hq, hqB, rbuf, rB, gcol, bcol):
    P = cx.P
    rr = 0
    for q_ in range(4):
        for g in range(4):
            wv, wB = cx.wget(("w1", lkey, q_, g))
            for i in range(4):
                hc = 4 * g + i
                pt, pB = cx.bank()
                for k in range(NCH):
                    P.op("pe", lambda e, k=k, i=i, wv=wv, pt=pt: e.matmul(pt[:], wv[:, k, i * 128:(i + 1) * 128], xb[:, k, :],
                                                                          start=(k == 0), stop=(k == NCH - 1)),
                         reads=[wB, xbB[k]], writes=[pB], accum=(k > 0))
                r, rb = rbuf[rr], rB[rr]
                rr ^= 1
                P.op("act", lambda e, pt=pt, r=r: e.activation(out=r, in_=pt[:], func=AF.Relu), reads=[pB], writes=[rb])
                P.op("dve", lambda e, hc=hc, r=r: e.tensor_tensor(out=hq[:, hc, :], in0=r, in1=r, op=ALU.mult),
                     reads=[rb], writes=[hqB[hc]])
        for g in range(4):
            wv, wB = cx.wget(("w2", lkey, q_, g))
            for i in range(4):
                dc = 4 * g + i
                pt, pB = cx.bank()
                for k in range(16):
                    P.op("pe", lambda e, k=k, i=i, wv=wv, pt=pt: e.matmul(pt[:], wv[:, k, i * 128:(i + 1) * 128], hq[:, k, :],
                                                                          start=(k == 0), stop=(k == 15)),
                         reads=[wB, hqB[k]], writes=[pB], accum=(k > 0))
                if q_ == 0:
                    P.op("dve", lambda e, dc=dc, pt=pt: e.scalar_tensor_tensor(out=xf[:, dc, :], in0=xf[:, dc, :], scalar=ALPHA, in1=pt[:],
                                                                               op0=ALU.mult, op1=ALU.add),
                         reads=[pB, xfB[dc]], writes=[xfB[dc]])
                else:
                    P.op("dve", lambda e, dc=dc, pt=pt: e.tensor_tensor(out=xf[:, dc, :], in0=pt[:], in1=xf[:, dc, :], op=ALU.add),
                         reads=[pB, xfB[dc]], writes=[xfB[dc]])
                if q_ == 3:
                    ln.stats_chunk(dc, xf[:, dc, :], xfB[dc])
    post_ln(cx, ln, xf, xfB, xb, xbB, gcol, bcol)


def proj_residual_sublayer(cx, ln, wkey, inb, inbB, xf, xfB, xb, xbB, gcol, bcol, bias_col, tb=None, tbB=None):
    P = cx.P
    vecs = cx.vecs
    tr = 0
    for g in range(4):
        wv, wB = cx.wget((wkey, g))
        for i in range(4):
            dc = 4 * g + i
            pt, pB = cx.bank()
            for k in range(NCH):
                P.op("pe", lambda e, k=k, i=i, wv=wv, pt=pt: e.matmul(pt[:], wv[:, k, i * 128:(i + 1) * 128], inb[:, k, :],
                                                                      start=(k == 0), stop=(k == NCH - 1)),
                     reads=[wB, inbB[k]], writes=[pB], accum=(k > 0))
            if bias_col is not None:
                t_, tB_ = tb[tr], tbB[tr]
                tr ^= 1
                P.op("act", lambda e, dc=dc, pt=pt, t_=t_: e.activation(out=t_[:], in_=pt[:], func=AF.Identity,
                                                                        bias=vecs[:, bias_col + dc:bias_col + dc + 1]),
                     reads=[pB], writes=[tB_])
                P.op("dve", lambda e, dc=dc, t_=t_: e.scalar_tensor_tensor(out=xf[:, dc, :], in0=xf[:, dc, :], scalar=ALPHA,
                                                                          in1=t_[:], op0=ALU.mult, op1=ALU.add),
                     reads=[tB_, xfB[dc]], writes=[xfB[dc]])
            else:
                P.op("dve", lambda e, dc=dc, pt=pt: e.scalar_tensor_tensor(out=xf[:, dc, :], in0=xf[:, dc, :], scalar=ALPHA,
                                                                          in1=pt[:], op0=ALU.mult, op1=ALU.add),
                     reads=[pB, xfB[dc]], writes=[xfB[dc]])
            ln.stats_chunk(dc, xf[:, dc, :], xfB[dc])
    post_ln(cx, ln, xf, xfB, xb, xbB, gcol, bcol)


def load_consts(cx, vecs_d, ident_d):
    P = cx.P
    cx.vecs = cx.sb("vecs", [128, NV], F32)
    cx.ident = cx.sb("ident", [128, 128], F32)
    cx.ones = cx.sb("ones", [128, 128], BF16)
    cx.epsc = cx.sb("epsc", [128, 1], F32)
    cx.vecsB, cx.identB, cx.onesB = Buf("vecs"), Buf("ident"), Buf("ones")
    P.op("sp", lambda e: e.dma_start(out=cx.vecs[:], in_=vecs_d), writes=[cx.vecsB], dsem="CV")
    P.op("sp", lambda e: e.dma_start(out=cx.ident[:], in_=ident_d), writes=[cx.identB], dsem="CI")
    P.op("dve", lambda e: e.memset(cx.ones[:], 1.0), writes=[cx.onesB])
    P.op("dve", lambda e: e.memset(cx.epsc[:], EPS), writes=[cx.onesB], accum=True)
    cx.dmy = cx.sb("dmy", [128, 4], F32)
    consts = [cx.vecsB, cx.identB, cx.onesB]
    P.op("act", lambda e: e.activation(out=cx.dmy[:, 0:1], in_=cx.vecs[:, 0:1], func=AF.Identity), reads=consts, writes=[Buf("dmy_a")])
    P.op("dve", lambda e: e.tensor_copy(out=cx.dmy[:, 1:2], in_=cx.vecs[:, 0:1]), reads=consts, writes=[Buf("dmy_d")])
    P.op("pool", lambda e: e.tensor_copy(out=cx.dmy[:, 2:3], in_=cx.vecs[:, 0:1]), reads=consts, writes=[Buf("dmy_p")])


def layer0_and_qkv(cx, alloc, xc, hmask_d, pw1, pw2, w1, w2, wkv, wq, x2T, qT, kT, vtm, x2TB, qTB, kTB, vtB,
                   kpieces, vpieces, after_tile):
    P = cx.P
    vecs = cx.vecs
    bufs = []

    def NB(name):
        b_ = Buf(name)
        bufs.append(b_)
        return b_

    hmask = alloc("hmask", 2, F32)
    hmB = NB("hmask")
    P.op("sp", lambda e: e.dma_start(out=hmask[:, 0:1], in_=hmask_d), writes=[hmB], dsem="CH")
    xf = alloc("xf", NCH * T, F32).rearrange("p (c t) -> p c t", c=NCH)
    xb = alloc("xb", NCH * T, BF16).rearrange("p (c t) -> p c t", c=NCH)
    xfB = [NB(f"xf{c}") for c in range(NCH)]
    xbB = [NB(f"xb{c}") for c in range(NCH)]
    scratch = alloc("scratch", 64 * T, BF16)
    hB = [NB(f"h{c}") for c in range(64)]
    h = scratch.rearrange("p (c t) -> p c t", c=64)
    cv = scratch[:, 0:32 * T].bitcast(F32).rearrange("p (c t) -> p c t", c=NCH)
    u = scratch[:, 32 * T:48 * T].rearrange("p (c t) -> p c t", c=NCH)
    kst = scratch[:, 0:16 * T].rearrange("p (c t) -> p c t", c=NCH)
    qst = scratch[:, 16 * T:32 * T].rearrange("p (c t) -> p c t", c=NCH)
    vst = scratch[:, 32 * T:48 * T].rearrange("p (b f) -> p b f", b=4)

    def cvB(c):
        return [hB[2 * c], hB[2 * c + 1]]

    xh = alloc("xh", NCH * HALO, BF16).rearrange("p (c t) -> p c t", c=NCH)
    xhB = NB("xh")
    xhf = alloc("xhf", NCH * HALO, F32).rearrange("p (c t) -> p c t", c=NCH)
    xhfB = NB("xhf")
    ghalo = alloc("ghalo", NCH * HALO, F32).rearrange("p (c t) -> p c t", c=NCH)
    ghB = [NB(f"gh{c}") for c in range(NCH)]
    gbuf = [alloc(f"gbuf{i}", T + 32, BF16) for i in range(2)]
    gbB = [NB(f"gbuf{i}") for i in range(2)]
    NDG = 16
    dg = [alloc(f"dg{i}", 128, BF16) for i in range(NDG)]
    dgB = [NB(f"dg{i}") for i in range(NDG)]
    dgr = 0
    sg = [alloc(f"sg{i}", T, F32) for i in range(2)]
    sgB = [NB(f"sg{i}") for i in range(2)]
    sgh = alloc("sgh", HALO, F32)
    sghB = NB("sgh")
    ln = LN(cx, alloc=alloc)
    bufs += ln.zbB + ln.z2bB + [ln.meanB, ln.varB, ln.rstdB]

    pw1v = pw1.rearrange("(c p) n -> p c n", p=128)
    w2v = w2.rearrange("(q c p) n -> q p c n", p=128, c=16)
    for j in range(NT):
        for g in range(8):
            cx.wadd(("pw1", j, g), [(0, 256, pw1v[:, :, 256 * g:256 * g + 256]),
                                    (256, 512, pw1v[:, :, D + 256 * g:D + 256 * g + 256])], 16)
        for g in range(4):
            cx.wadd(("pw2", g), wsrc_k2048(pw2, 512 * g), 16)
        for q_ in range(4):
            for g in range(4):
                cx.wadd(("w1", ("A", j), q_, g), wsrc_k2048(w1, 2048 * q_ + 512 * g), 16)
            for g in range(4):
                cx.wadd(("w2", ("A", j), q_, g), w2v[q_][:, :, 512 * g:512 * (g + 1)], 16)
        for g in range(4):
            cx.wadd(("wk", g), wsrc_k2048(wkv, 512 * g), 16)
        for g in range(4):
            cx.wadd(("wq", g), wsrc_k2048(wq, 512 * g), 16)
        for g in range(4):
            cx.wadd(("wv", g), wsrc_k2048(wkv, D + 512 * g), 16)

    xcv = xc.rearrange("(c p) t -> p c t", p=128)
    P.op("sp", lambda e: e.dma_start(out=xhf, in_=xcv[:, :, XH - HALO:XH]), writes=[xhfB], dsem="X0")
    P.op("act", lambda e: e.activation(out=xh, in_=xhf, func=AF.Identity), reads=[xhfB], writes=[xhB])

    for j in range(NT):
        P.op("sp", lambda e, j=j: e.dma_start(out=xf, in_=xcv[:, :, XH + T * j:XH + T * (j + 1)]), writes=xfB, dsem="X1")
        for c in range(NCH):
            if c % 2 == 0:
                P.op("act", lambda e, c=c: e.activation(out=xb[:, c, :], in_=xf[:, c, :], func=AF.Identity), reads=[xfB[c]], writes=[xbB[c]])
            else:
                P.op("dve", lambda e, c=c: e.tensor_copy(out=xb[:, c, :], in_=xf[:, c, :]), reads=[xfB[c]], writes=[xbB[c]])
        gr = 0
        for g in range(8):
            wv, wB = cx.wget(("pw1", j, g))
            for i in range(2):
                c = 2 * g + i
                acol = slice(i * 128, (i + 1) * 128)
                gcol = slice(256 + i * 128, 256 + (i + 1) * 128)
                pa, paB = cx.bank()
                for k in range(NCH):
                    P.op("pe", lambda e, k=k, acol=acol, wv=wv, pa=pa: e.matmul(pa[:], wv[:, k, acol], xb[:, k, :],
                                                                              start=(k == 0), stop=(k == NCH - 1)),
                         reads=[wB, xbB[k]], writes=[paB], accum=(k > 0))
                pg, pgB = cx.bank()
                for k in range(NCH):
                    P.op("pe", lambda e, k=k, gcol=gcol, wv=wv, pg=pg: e.matmul(pg[:], wv[:, k, gcol], xb[:, k, :],
                                                                              start=(k == 0), stop=(k == NCH - 1)),
                         reads=[wB, xbB[k]], writes=[pgB], accum=(k > 0))
                gi = gr
                gr ^= 1
                gb_, gbB_ = gbuf[gi], gbB[gi]
                sg_, sgB_ = sg[gi], sgB[gi]
                ba = vecs[:, V_PW1B + c:V_PW1B + c + 1]
                bg = vecs[:, V_PW1B + 16 + c:V_PW1B + 16 + c + 1]
                if j == 0:
                    ph, phB = cx.bank()
                    for k in range(NCH):
                        P.op("pe", lambda e, k=k, acol=acol, wv=wv, ph=ph: e.matmul(ph[:, 0:HALO], wv[:, k, acol], xh[:, k, :],
                                                                                  start=(k == 0), stop=(k == NCH - 1)),
                             reads=[wB, xhB], writes=[phB], accum=(k > 0))
                    for k in range(NCH):
                        P.op("pe", lambda e, k=k, gcol=gcol, wv=wv, ph=ph: e.matmul(ph[:, HALO:2 * HALO], wv[:, k, gcol], xh[:, k, :],
                                                                                  start=(k == 0), stop=(k == NCH - 1)),
                             reads=[wB, xhB], writes=[phB], accum=True)
                    P.op("act", lambda e, ph=ph, bg=bg: e.activation(out=sgh, in_=ph[:, HALO:2 * HALO], func=AF.Sigmoid, bias=bg),
                         reads=[phB], writes=[sghB])
                    P.op("dve", lambda e, c=c, ph=ph, ba=ba: e.scalar_tensor_tensor(out=ghalo[:, c, :], in0=ph[:, 0:HALO], scalar=ba, in1=sgh,
                                                                                   op0=ALU.add, op1=ALU.mult),
                         reads=[phB, sghB], writes=[ghB[c]])
                    P.op("dve", lambda e, c=c, gb_=gb_: e.tensor_scalar(out=gb_[:, 0:30], in0=ghalo[:, c, 2:32], scalar1=hmask[:, 0:1], scalar2=None,
                                                                       op0=ALU.mult),
                         reads=[ghB[c], hmB], writes=[gbB_])
                else:
                    P.op("dve", lambda e, c=c, gb_=gb_: e.tensor_copy(out=gb_[:, 0:30], in_=ghalo[:, c, 2:32]),
                         reads=[ghB[c]], writes=[gbB_])
                P.op("act", lambda e, pg=pg, sg_=sg_, bg=bg: e.activation(out=sg_, in_=pg[:], func=AF.Sigmoid, bias=bg),
                     reads=[pgB], writes=[sgB_])
                P.op("dve", lambda e, pa=pa, sg_=sg_, gb_=gb_, ba=ba: e.scalar_tensor_tensor(out=gb_[:, 30:30 + T], in0=pa[:], scalar=ba, in1=sg_,
                                                                                           op0=ALU.add, op1=ALU.mult),
                     reads=[paB, sgB_], writes=[gbB_], accum=True)
                if j < NT - 1:
                    P.op("dve", lambda e, c=c, gb_=gb_: e.tensor_copy(out=ghalo[:, c, 2:32], in_=gb_[:, T:T + 30]),
                         reads=[gbB_], writes=[ghB[c]])
                cvc = cv[:, c, :]
                bdw = vecs[:, V_DWB + c:V_DWB + c + 1]
                pc, pcB = cx.bank()
                for tap in range(CONVW):
                    wj = vecs[:, V_DWW + 16 * tap + c:V_DWW + 16 * tap + c + 1]
                    dgi, dgiB = dg[dgr], dgB[dgr]
                    dgr = (dgr + 1) % NDG
                    if tap % 2 == 0:
                        P.op("dve", lambda e, dgi=dgi, wj=wj: e.tensor_scalar(out=dgi, in0=cx.identb[:], scalar1=wj, scalar2=None, op0=ALU.mult),
                             writes=[dgiB])
                    else:
                        P.op("act", lambda e, dgi=dgi, wj=wj: e.activation(out=dgi, in_=cx.identb[:], func=AF.Identity, scale=wj),
                             writes=[dgiB])
                    P.op("pe", lambda e, pc=pc, dgi=dgi, gb_=gb_, tap=tap: e.matmul(pc[:], dgi, gb_[:, tap:tap + T],
                                                                                  start=(tap == 0), stop=(tap == CONVW - 1)),
                         reads=[dgiB, gbB_], writes=[pcB], accum=(tap > 0))
                P.op("act", lambda e, cvc=cvc, pc=pc, bdw=bdw: e.activation(out=cvc, in_=pc[:], func=AF.Identity, bias=bdw),
                     reads=[pcB], writes=cvB(c))
                ln.stats_chunk(c, cvc, hB[2 * c])
        ln.finalize()
        for c in range(NCH):
            cvc = cv[:, c, :]
            P.op("dve", lambda e, cvc=cvc: e.tensor_tensor(out=cvc, in0=cvc, in1=ln.rstd, op=ALU.mult),
                 reads=cvB(c) + [ln.rstdB], writes=cvB(c))
            P.op("dve", lambda e, cvc=cvc: e.tensor_tensor(out=cvc, in0=cvc, in1=ln.mean, op=ALU.add),
                 reads=cvB(c) + [ln.meanB], writes=cvB(c))
            P.op("act", lambda e, c=c, cvc=cvc: e.activation(out=u[:, c, :], in_=cvc, func=AF.Silu,
                                                             bias=vecs[:, V_CLNB + c:V_CLNB + c + 1], scale=vecs[:, V_CLNG + c:V_CLNG + c + 1]),
                 reads=cvB(c), writes=[hB[32 + c]])
        uB = [hB[32 + c] for c in range(NCH)]
        proj_residual_sublayer(cx, ln, "pw2", u, uB, xf, xfB, xb, xbB, V_MIXG, V_MIXB, V_PW2B, tb=sg, tbB=sgB)
        mlp_sublayer_q(cx, ln, ("A", j), xf, xfB, xb, xbB, h[:, 0:16, :], hB[0:16], sg, sgB, V_MLPG, V_MLPB)
        P.op("sp", lambda e, j=j: e.dma_start(out=x2T.rearrange("(c p) t -> p c t", p=128)[:, :, T * j:T * (j + 1)], in_=xf),
             reads=xfB, writes=[x2TB], accum=True, dsem="OX")
        for (nm, stg, cell0, scale) in (("wk", kst, 0, 1.0), ("wq", qst, 16, QSCALE)):
            for g in range(4):
                wv_, wB_ = cx.wget((nm, g))
                for i in range(4):
                    hd = 4 * g + i
                    pt, pB = cx.bank()
                    for k in range(NCH):
                        P.op("pe", lambda e, k=k, i=i, wv_=wv_, pt=pt: e.matmul(pt[:], wv_[:, k, i * 128:(i + 1) * 128], xb[:, k, :],
                                                                                start=(k == 0), stop=(k == NCH - 1)),
                             reads=[wB_, xbB[k]], writes=[pB], accum=(k > 0))
                    P.op("act", lambda e, hd=hd, pt=pt, stg=stg, scale=scale: e.activation(out=stg[:, hd, :], in_=pt[:], func=AF.Identity, scale=scale),
                         reads=[pB], writes=[hB[cell0 + hd]])
            dst, dB = (kT, kTB) if nm == "wk" else (qT, qTB)
            P.op("sp", lambda e, j=j, dst=dst, stg=stg: e.dma_start(out=dst.rearrange("(c p) t -> p c t", p=128)[:, :, T * j:T * (j + 1)], in_=stg),
                 reads=[hB[cell0 + c] for c in range(NCH)], writes=[dB], accum=True, dsem="OK" if nm == "wk" else "OQ")
            if nm == "wk":
                for p_ in range(2):
                    pap, pB_ = kpieces[j][p_]
                    P.op("sp", lambda e, pap=pap, p_=p_, stg=stg: e.dma_start(out=pap.rearrange("(c p) t -> p c t", p=128), in_=stg[:, 8 * p_:8 * p_ + 8, :]),
                         reads=[hB[cell0 + c] for c in range(8 * p_, 8 * p_ + 8)], writes=[pB_], dsem=f"PK{p_}")
        for g in range(4):
            wv_, wB_ = cx.wget(("wv", g))
            for tb in range(4):
                pt, pB = cx.bank()
                for k in range(NCH):
                    P.op("pe", lambda e, k=k, tb=tb, wv_=wv_, pt=pt: e.matmul(pt[:], xb[:, k, tb * 128:(tb + 1) * 128], wv_[:, k, :],
                                                                              start=(k == 0), stop=(k == NCH - 1)),
                         reads=[wB_, xbB[k]], writes=[pB], accum=(k > 0))
                cells = [hB[32 + 4 * tb + q] for q in range(4)]
                if g % 2 == 0:
                    P.op("act", lambda e, g=g, tb=tb, pt=pt: e.activation(out=vst[:, tb, 512 * g:512 * (g + 1)], in_=pt[:], func=AF.Identity),
                         reads=[pB], writes=cells, accum=(g > 0))
                else:
                    P.op("dve", lambda e, g=g, tb=tb, pt=pt: e.tensor_copy(out=vst[:, tb, 512 * g:512 * (g + 1)], in_=pt[:]),
                         reads=[pB], writes=cells, accum=True)
        P.op("sp", lambda e, j=j: e.dma_start(out=vtm[T * j:T * (j + 1), :].rearrange("(b p) f -> p b f", p=128), in_=vst),
             reads=[hB[32 + q] for q in range(16)], writes=[vtB], accum=True, dsem="OV")
        for p_ in range(2):
            pap, pB_ = vpieces[j][p_]
            P.op("sp", lambda e, pap=pap, p_=p_: e.dma_start(out=pap.rearrange("(b p) f -> p b f", p=128), in_=vst[:, 2 * p_:2 * p_ + 2, :]),
                 reads=[hB[32 + q] for q in range(8 * p_, 8 * p_ + 8)], writes=[pB_], dsem=f"PV{p_}")
        after_tile(j)
    return bufs


BR = (1, 4, 16)


BIGN = 76 * 1024


def attention_and_layer1(cx, carve, off, a_bufs, x2T, qT, kTs, kTp, vts, vtp, bias_d, wo, w1, w2, oT, out, identb, identbB,
                         x2TB, qTB, kTsB, kTpB, vtsB, vtpB):
    P = cx.P
    off[0] = 0
    kwin = [carve(4096) for _ in range(2)]
    qn = [carve(2048) for _ in range(2)]
    kd = [[carve(4096) for _ in range(2)] for _ in range(2)]
    qd = [[carve(2048) for _ in range(2)] for _ in range(2)]
    vsl = [carve(32 * 256) for _ in range(2)]
    bia = [carve(9 * 128) for _ in range(2)]
    ptb = [carve(128) for _ in range(4)]
    oacc = [carve(4096).bitcast(F32) for _ in range(2)]
    dacc = [carve(4096).bitcast(F32) for _ in range(2)]
    ost = [carve(2048) for _ in range(2)]
    dtmp = carve(1024).bitcast(F32)
    dtmpB = Buf("dtmp")
    kwinB = [Buf("kwin0"), Buf("kwin1")]
    qnB = [Buf("qn0"), Buf("qn1")]
    kdB = [[Buf(f"kd{r}{h}") for h in range(2)] for r in range(2)]
    qdB = [[Buf(f"qd{r}{h}") for h in range(2)] for r in range(2)]
    vslB = [Buf(f"vsl{i}") for i in range(2)]
    biaB = [Buf("bia0"), Buf("bia1")]
    ptbB = [Buf(f"ptb{i}") for i in range(4)]
    oaccB = [Buf("oacc0"), Buf("oacc1")]
    daccB = [Buf("dacc0"), Buf("dacc1")]
    ostB = [Buf("ost0"), Buf("ost1")]
    oTdB = Buf("oT_dram")
    b1_bufs = kwinB + qnB + kdB[0] + kdB[1] + qdB[0] + qdB[1] + vslB + biaB + ptbB + oaccB + daccB + ostB + [dtmpB]
    P.op("dve", lambda e: e.memset(cx.dmy[:, 3:4], 0.0), reads=a_bufs, writes=a_bufs + b1_bufs)
    cx.nmain = 4
    cx.rot = 0
    accrot = 0
    vrot = 0
    drot = 0
    prot = 0
    LA = 2
    qTv = qT.rearrange("(h p) t -> h p t", p=128)
    kTsv = kTs.rearrange("(h p) t -> h p t", p=128)
    kTpv = kTp.rearrange("(h p) t -> h p t", p=128)
    for hp in range(8):
        for hh in range(2):
            hd = 2 * hp + hh
            P.op("sp", lambda e, hh=hh, hd=hd: e.dma_start(out=kwin[hh][:, 0:TOK], in_=kTpv[hd]), reads=[kTpB], writes=[kwinB[hh]], dsem=f"BK{hh}")
            P.op("sp", lambda e, hh=hh, hd=hd: e.dma_start(out=kwin[hh][:, TOK:2 * TOK], in_=kTsv[hd]), reads=[kTsB], writes=[kwinB[hh]],
                 accum=True, dsem=f"BK{hh}")
            P.op("sp", lambda e, hh=hh, hd=hd: e.dma_start(out=qn[hh], in_=qTv[hd]), reads=[qTB], writes=[qnB[hh]], dsem=f"BQ{hh}")
            P.op("pool", lambda e, hh=hh, hd=hd: e.dma_start(out=bia[hh], in_=bias_d[hd]), writes=[biaB[hh]], dsem=f"BB{hh}")
        for bi, d in enumerate(BR):
            Lw = 4096 // d
            Lq = 2048 // d
            nbr = 32 // d
            vs = vsl[vrot]
            vsB = vslB[vrot]
            vsem = f"BV{vrot}"
            vrot ^= 1
            vs3 = vs.rearrange("p (b f) -> p b f", f=256)
            cols = slice(256 * hp, 256 * hp + 256)
            lo = nbr // 2 - 1
            hb = nbr // 2
            for r in range(d):
                srcp = vtp.rearrange("(b i r) f -> r i b f", i=128, r=d)[r, :, hb - 1:hb, cols]
                P.op("sp", lambda e, vs3=vs3, srcp=srcp, r=r, lo=lo, nbr=nbr: e.dma_start(out=vs3[:, r * nbr + lo:r * nbr + lo + 1, :], in_=srcp),
                     reads=[vtpB], writes=[vsB], accum=(r > 0), dsem=vsem)
                srco = vts.rearrange("(b i r) f -> r i b f", i=128, r=d)[r, :, 0:hb, cols]
                P.op("sp", lambda e, vs3=vs3, srco=srco, r=r, hb=hb, nbr=nbr: e.dma_start(out=vs3[:, r * nbr + hb:r * nbr + nbr, :], in_=srco),
                     reads=[vtsB], writes=[vsB], accum=True, dsem=vsem)
            for hh in range(2):
                if d == 1:
                    kdv, kdvB = kwin[hh], kwinB[hh]
                    qdv, qdvB = qn[hh], qnB[hh]
                else:
                    kdv, kdvB = kd[drot][hh], kdB[drot][hh]
                    qdv, qdvB = qd[drot][hh], qdB[drot][hh]
                    P.op("pool", lambda e, hh=hh, kdv=kdv, d=d: e.tensor_copy(out=kdv.rearrange("p (r m) -> p r m", r=d),
                                                                            in_=kwin[hh].rearrange("p (m r) -> p r m", r=d)),
                         reads=[kwinB[hh]], writes=[kdvB])
                    P.op("act", lambda e, hh=hh, qdv=qdv, d=d: e.activation(out=qdv.rearrange("p (r m) -> p r m", r=d),
                                                                          in_=qn[hh].rearrange("p (m r) -> p r m", r=d), func=AF.Identity),
                         reads=[qnB[hh]], writes=[qdvB])
                ov = oacc[hh] if d == 1 else oacc[hh].rearrange("p (m r) -> p r m", r=d)
                dv = dacc[hh] if d == 1 else dacc[hh].rearrange("p (m r) -> p r m", r=d)
                steps = [(g, nn, part) for g in range(4) for nn in range(4) for part in range(2)]
                pts = {}
                accs = {}
                for s_ in range(len(steps) + LA):
                    if s_ < len(steps):
                        g, nn, part = steps[s_]
                        qpos = 512 * g + 128 * nn
                        r = qpos // Lq
                        n = (qpos % Lq) // 128
                        kpos = r * Lw + Lw // 2 + 128 * (n - 1 + part)
                        vblk = r * nbr + nbr // 2 + n - 1 + part
                        bcol = (bi * 3 + (part if not (part == 0 and n == 0) else 2)) * 128
                        pst, pstB = cx.bank()
                        sq = pst[:, 0:128]
                        P.op("pe", lambda e, sq=sq, kdv=kdv, qdv=qdv, kpos=kpos, qpos=qpos: e.matmul(sq, kdv[:, kpos:kpos + 128], qdv[:, qpos:qpos + 128],
                                                                                                   start=True, stop=False),
                             reads=[kdvB, qdvB], writes=[pstB])
                        P.op("pe", lambda e, sq=sq, hh=hh, bcol=bcol: e.matmul(sq, identb[:], bia[hh][:, bcol:bcol + 128], start=False, stop=True),
                             reads=[identbB, biaB[hh]], writes=[pstB], accum=True)
                        pt_, ptB_ = ptb[prot], ptbB[prot]
                        prot = (prot + 1) % 4
                        P.op("act", lambda e, sq=sq, pt_=pt_: e.activation(out=pt_, in_=sq, func=AF.Exp), reads=[pstB], writes=[ptB_])
                        pts[s_] = (pt_, ptB_, vblk)
                    t_ = s_ - LA
                    if t_ < 0:
                        continue
                    g, nn, part = steps[t_]
                    pt_, ptB_, vblk = pts.pop(t_)
                    if nn == 0 and part == 0:
                        accs[g] = (cx.ps[4 + 2 * accrot], cx.pb[4 + 2 * accrot], cx.ps[5 + 2 * accrot], cx.pb[5 + 2 * accrot])
                        accrot ^= 1
                    po, poB, pd, pdB = accs[g]
                    P.op("pe", lambda e, po=po, nn=nn, vs3=vs3, vblk=vblk, hh=hh, pt_=pt_, part=part: e.matmul(
                        po[:, 128 * nn:128 * (nn + 1)], vs3[:, vblk, 128 * hh:128 * (hh + 1)], pt_, start=(part == 0), stop=(part == 1)),
                         reads=[vsB, ptB_], writes=[poB], accum=not (nn == 0 and part == 0))
                    P.op("pe", lambda e, pd=pd, nn=nn, pt_=pt_, part=part: e.matmul(
                        pd[:, 128 * nn:128 * (nn + 1)], cx.ones[:], pt_, start=(part == 0), stop=(part == 1)),
                         reads=[cx.onesB, ptB_], writes=[pdB], accum=not (nn == 0 and part == 0))
                    if not (nn == 3 and part == 1):
                        continue
                    if d == 1:
                        oview = ov[:, 512 * g:512 * (g + 1)]
                        dview = dv[:, 512 * g:512 * (g + 1)]
                        pov, pdv = po[:, :], pd[:, :]
                    elif d == 4:
                        oview = ov[:, g, :]
                        dview = dv[:, g, :]
                        pov, pdv = po[:, :], pd[:, :]
                    else:
                        oview = ov[:, 4 * g:4 * g + 4, :]
                        dview = dv[:, 4 * g:4 * g + 4, :]
                        pov = po[:, :].rearrange("p (r m) -> p r m", r=4)
                        pdv = pd[:, :].rearrange("p (r m) -> p r m", r=4)
                    if bi == 0:
                        P.op("dve", lambda e, oview=oview, pov=pov: e.tensor_copy(out=oview, in_=pov), reads=[poB], writes=[oaccB[hh]], accum=(g > 0))
                        P.op("act", lambda e, dview=dview, pdv=pdv: e.activation(out=dview, in_=pdv, func=AF.Identity),
                             reads=[pdB], writes=[daccB[hh]], accum=(g > 0))
                    else:
                        P.op("dve", lambda e, oview=oview, pov=pov: e.tensor_tensor(out=oview, in0=pov, in1=oview, op=ALU.add),
                             reads=[poB, oaccB[hh]], writes=[oaccB[hh]])
                        dt3 = dtmp if d != 16 else dtmp.rearrange("p (r m) -> p r m", r=4)
                        P.op("act", lambda e, dt3=dt3, pdv=pdv: e.activation(out=dt3, in_=pdv, func=AF.Identity), reads=[pdB], writes=[dtmpB])
                        P.op("pool", lambda e, dview=dview, dt3=dt3: e.tensor_tensor(out=dview, in0=dt3, in1=dview, op=ALU.add),
                             reads=[dtmpB, daccB[hh]], writes=[daccB[hh]])
            if d != 1:
                drot ^= 1
        for hh in range(2):
            hd = 2 * hp + hh
            P.op("dve", lambda e, hh=hh: e.reciprocal(out=dacc[hh], in_=dacc[hh]), reads=[daccB[hh]], writes=[daccB[hh]])
            P.op("dve", lambda e, hh=hh: e.tensor_tensor(out=ost[hh], in0=oacc[hh], in1=dacc[hh], op=ALU.mult),
                 reads=[oaccB[hh], daccB[hh]], writes=[ostB[hh]])
            P.op("sp", lambda e, hh=hh, hd=hd: e.dma_start(out=oT[128 * hd:128 * (hd + 1), :], in_=ost[hh]),
                 reads=[ostB[hh]], writes=[oTdB], accum=True, dsem=f"BO{hh}")

    TB, NH = 1024, 2
    cx.nmain = 4
    cx.rot = 0
    off[0] = 0
    xf = carve(2 * NCH * TB).bitcast(F32).rearrange("p (c t) -> p c t", c=NCH)
    xb = carve(NCH * TB).rearrange("p (c t) -> p c t", c=NCH)
    hq = carve(16 * TB).rearrange("p (c t) -> p c t", c=16)
    sg = [carve(2 * T).bitcast(F32) for _ in range(2)]

    def ln_alloc(name, n, dt):
        return carve(n) if dt == BF16 else carve(2 * n).bitcast(F32)

    lns = [LN(cx, alloc=ln_alloc, banks=(4, 5)), LN(cx, alloc=ln_alloc, banks=(6, 7))]
    xfB = [[Buf(f"xf{c}_{hf}") for hf in range(NH)] for c in range(NCH)]
    xbB = [[Buf(f"xb{c}_{hf}") for hf in range(NH)] for c in range(NCH)]
    hqB = [[Buf(f"hq{c}_{hf}") for hf in range(NH)] for c in range(16)]
    sgB = [Buf("sg0"), Buf("sg1")]
    xfB_flat = [b_ for l_ in xfB for b_ in l_]
    xbB_flat = [b_ for l_ in xbB for b_ in l_]
    vecs = cx.vecs

    def hs(hf):
        return slice(T * hf, T * (hf + 1))

    def post_ln2(gcol, bcol):
        for hf in range(NH):
            ln = lns[hf]
            ln.finalize()
            for c in range(NCH):
                src = xf[:, c, hs(hf)]
                ln.norm_chunk(src, xfB[c][hf])
                P.op("act", lambda e, c=c, src=src: e.activation(out=src, in_=src, func=AF.Identity,
                                                                 bias=vecs[:, bcol + c:bcol + c + 1], scale=vecs[:, gcol + c:gcol + c + 1]),
                     reads=[xfB[c][hf]], writes=[xfB[c][hf]])
                P.op("act", lambda e, c=c, hf=hf, src=src: e.activation(out=xb[:, c, hs(hf)], in_=src, func=AF.Identity),
                     reads=[xfB[c][hf]], writes=[xbB[c][hf]])

    w2v = w2.rearrange("(q c p) n -> q p c n", p=128, c=16)
    for jj in range(TOK // TB):
        for g in range(4):
            cx.wadd(("wo", jj, g), wsrc_k2048(wo, 512 * g), 16)
        for q_ in range(4):
            for g in range(4):
                cx.wadd(("w1", jj, q_, g), wsrc_k2048(w1, 2048 * q_ + 512 * g), 16)
            for g in range(4):
                cx.wadd(("w2", jj, q_, g), w2v[q_][:, :, 512 * g:512 * (g + 1)], 16)

    for jj in range(TOK // TB):
        tsl = slice(TB * jj, TB * (jj + 1))
        P.op("sp", lambda e, tsl=tsl: e.dma_start(out=xf, in_=x2T.rearrange("(c p) t -> p c t", p=128)[:, :, tsl]),
             reads=[x2TB], writes=(xfB_flat + b1_bufs if jj == 0 else xfB_flat), dsem="B2X")
        P.op("sp", lambda e, tsl=tsl: e.dma_start(out=xb, in_=oT.rearrange("(c p) t -> p c t", p=128)[:, :, tsl]),
             reads=[oTdB], writes=xbB_flat, dsem="B2O")
        for g in range(4):
            wv, wB = cx.wget(("wo", jj, g))
            for i in range(4):
                dc = 4 * g + i
                for hf in range(NH):
                    pt, pB = cx.bank()
                    for k in range(NCH):
                        P.op("pe", lambda e, k=k, i=i, hf=hf, wv=wv, pt=pt: e.matmul(pt[:], wv[:, k, i * 128:(i + 1) * 128], xb[:, k, hs(hf)],
                                                                                     start=(k == 0), stop=(k == NCH - 1)),
                             reads=[wB, xbB[k][hf]], writes=[pB], accum=(k > 0))
                    P.op("dve", lambda e, dc=dc, hf=hf, pt=pt: e.scalar_tensor_tensor(out=xf[:, dc, hs(hf)], in0=xf[:, dc, hs(hf)], scalar=ALPHA,
                                                                                     in1=pt[:], op0=ALU.mult, op1=ALU.add),
                         reads=[pB, xfB[dc][hf]], writes=[xfB[dc][hf]])
                    lns[hf].stats_chunk(dc, xf[:, dc, hs(hf)], xfB[dc][hf])
        post_ln2(V_MIXG + 16, V_MIXB + 16)
        rr = 0
        for q_ in range(4):
            for g in range(4):
                wv, wB = cx.wget(("w1", jj, q_, g))
                for i in range(4):
                    hc = 4 * g + i
                    for hf in range(NH):
                        pt, pB = cx.bank()
                        for k in range(NCH):
                            P.op("pe", lambda e, k=k, i=i, hf=hf, wv=wv, pt=pt: e.matmul(pt[:], wv[:, k, i * 128:(i + 1) * 128], xb[:, k, hs(hf)],
                                                                                         start=(k == 0), stop=(k == NCH - 1)),
                                 reads=[wB, xbB[k][hf]], writes=[pB], accum=(k > 0))
                        r, rb = sg[rr], sgB[rr]
                        rr ^= 1
                        P.op("act", lambda e, pt=pt, r=r: e.activation(out=r, in_=pt[:], func=AF.Relu), reads=[pB], writes=[rb])
                        P.op("dve", lambda e, hc=hc, hf=hf, r=r: e.tensor_tensor(out=hq[:, hc, hs(hf)], in0=r, in1=r, op=ALU.mult),
                             reads=[rb], writes=[hqB[hc][hf]])
            for g in range(4):
                wv, wB = cx.wget(("w2", jj, q_, g))
                for i in range(4):
                    dc = 4 * g + i
                    for hf in range(NH):
                        pt, pB = cx.bank()
                        for k in range(16):
                            P.op("pe", lambda e, k=k, i=i, hf=hf, wv=wv, pt=pt: e.matmul(pt[:], wv[:, k, i * 128:(i + 1) * 128], hq[:, k, hs(hf)],
                                                                                         start=(k == 0), stop=(k == 15)),
                                 reads=[wB, hqB[k][hf]], writes=[pB], accum=(k > 0))
                        if q_ == 0:
                            P.op("dve", lambda e, dc=dc, hf=hf, pt=pt: e.scalar_tensor_tensor(out=xf[:, dc, hs(hf)], in0=xf[:, dc, hs(hf)], scalar=ALPHA,
                                                                                             in1=pt[:], op0=ALU.mult, op1=ALU.add),
                                 reads=[pB, xfB[dc][hf]], writes=[xfB[dc][hf]])
                        else:
                            P.op("dve", lambda e, dc=dc, hf=hf, pt=pt: e.tensor_tensor(out=xf[:, dc, hs(hf)], in0=pt[:], in1=xf[:, dc, hs(hf)], op=ALU.add),
                                 reads=[pB, xfB[dc][hf]], writes=[xfB[dc][hf]])
                        if q_ == 3:
                            lns[hf].stats_chunk(dc, xf[:, dc, hs(hf)], xfB[dc][hf])
        post_ln2(V_MLPG + 16, V_MLPB + 16)
        P.op("sp", lambda e, tsl=tsl: e.dma_start(out=out.rearrange("(c p) t -> p c t", p=128)[:, :, tsl], in_=xf),
             reads=xfB_flat, dsem="OUT")


def _fm(v):
    return np.ascontiguousarray(v.reshape(-1, 128).T)


def _t5_bucket(dist):
    max_exact = 16
    large = max_exact + (np.log(np.maximum(dist, 1) / max_exact) / math.log(2048 / max_exact) * (32 - max_exact)).astype(np.int32)
    large = np.minimum(large, 31)
    return np.where(dist < max_exact, dist, large).astype(np.int32)


def _bias_tiles(rel_bias, has_prev):
    i = np.arange(128)[:, None]
    j = np.arange(256)[None, :]
    delta = i - j + 128
    ok = (delta >= 0) & (delta <= 128)
    res = np.full((NHEAD, 128, 9, 128), NEG, np.float32)
    for bi, d in enumerate(BR):
        bucket = _t5_bucket(np.clip(delta, 0, None) * d)
        b = rel_bias[bucket]
        b = np.where(ok[:, :, None], b, np.float32(NEG))
        bt = np.transpose(b, (2, 1, 0))
        res[:, :, bi * 3 + 0, :] = bt[:, 0:128, :]
        res[:, :, bi * 3 + 1, :] = bt[:, 128:256, :]
        if has_prev:
            res[:, :, bi * 3 + 2, :] = bt[:, 0:128, :]
    return np.ascontiguousarray(res.reshape(NHEAD, 128, 9 * 128))


_NC_CACHE = {}


def build_fused():
    nc = bass.Bass("TRN2", target_bir_lowering=False)
    I32 = mybir.dt.int32
    xc = nc.dram_tensor("xc", [D, XH + TOK], F32, kind="ExternalInput").ap()
    hmask_d = nc.dram_tensor("hmask", [128, 1], F32, kind="ExternalInput").ap()
    prow_d = nc.dram_tensor("prevrow", [1, 1], I32, kind="ExternalInput").ap()
    vecs_d = nc.dram_tensor("vecs", [128, NV], F32, kind="ExternalInput").ap()
    ident_d = nc.dram_tensor("ident", [128, 128], F32, kind="ExternalInput").ap()
    bias_d = nc.dram_tensor("biasT", [NHEAD, 128, 9 * 128], F32, kind="ExternalInput").ap()
    pw1 = nc.dram_tensor("pw1", [D, 2 * D], F32, kind="ExternalInput").ap()
    pw2 = nc.dram_tensor("pw2", [D, D], F32, kind="ExternalInput").ap()
    w1a = nc.dram_tensor("w1a", [D, DFF], F32, kind="ExternalInput").ap()
    w2a = nc.dram_tensor("w2a", [DFF, D], F32, kind="ExternalInput").ap()
    w1b = nc.dram_tensor("w1b", [D, DFF], F32, kind="ExternalInput").ap()
    w2b = nc.dram_tensor("w2b", [DFF, D], F32, kind="ExternalInput").ap()
    wkv = nc.dram_tensor("wkv", [D, 2 * D], F32, kind="ExternalInput").ap()
    wq = nc.dram_tensor("wq", [D, D], F32, kind="ExternalInput").ap()
    wo = nc.dram_tensor("wo", [D, D], F32, kind="ExternalInput").ap()
    out = nc.dram_tensor("out", [D, TOK], F32, kind="ExternalOutput").ap()
    x2T = nc.dram_tensor("x2T", [D, TOK], F32).ap()
    qT = nc.dram_tensor("qT", [D, TOK], BF16).ap()
    kTs_t = nc.dram_tensor("kTs", [D, TOK], BF16)
    vts_t = nc.dram_tensor("vts", [TOK, D], BF16)
    kps = [[nc.dram_tensor(f"kps{j}{p}", [D // 2, T], BF16) for p in range(2)] for j in range(NT)]
    kpa = [[nc.dram_tensor(f"kpa{j}{p}", [4 * (D // 2), T], BF16) for p in range(2)] for j in range(NT)]
    vps = [[nc.dram_tensor(f"vps{j}{p}", [T // 2, D], BF16) for p in range(2)] for j in range(NT)]
    vpa = [[nc.dram_tensor(f"vpa{j}{p}", [4 * (T // 2), D], BF16) for p in range(2)] for j in range(NT)]
    kTp = nc.dram_tensor("kTp", [D, TOK], BF16).ap()
    vtp = nc.dram_tensor("vtp", [TOK, D], BF16).ap()
    oT = nc.dram_tensor("oT", [D, TOK], BF16).ap()
    kTs, vts = kTs_t.ap(), vts_t.ap()

    with contextlib.ExitStack() as st:
        cx = Ctx(nc, st, nslots=3)
        P = cx.P
        st.enter_context(nc.allow_low_precision("bf16 matmul operands, fp32 accumulation"))
        load_consts(cx, vecs_d, ident_d)
        identb = cx.sb("identb", [128, 128], BF16)
        identbB = Buf("identb")
        P.op("act", lambda e: e.activation(out=identb[:], in_=cx.ident[:], func=AF.Identity), reads=[cx.identB], writes=[identbB])
        cx.identb = identb
        P.op("dve", lambda e: e.tensor_copy(out=cx.dmy[:, 3:4], in_=identb[:, 0:1]), reads=[identbB], writes=[Buf("dmy_d2")])
        big = cx.sb("big", [128, BIGN], BF16)
        off = [0]

        def carve(n):
            n = (n + 15) // 16 * 16
            o = off[0]
            off[0] += n
            assert off[0] <= BIGN, off[0]
            return big[:, o:o + n]

        def alloc(name, n, dt):
            return carve(n) if dt == BF16 else carve(2 * n).bitcast(F32)[:, 0:n]

        x2TB, qTB, kTsB, vtsB, kTallB, vallB, kTpB, vtpB = (Buf(n_) for n_ in ("x2T", "qT", "kTs", "vts", "kTall", "vall", "kTp", "vtp"))
        groups = [[0, 1, 2, 3], [4, 5, 6, 7]]
        kpieces = [[(kps[j][p].ap(), Buf(f"kps{j}{p}")) for p in range(2)] for j in range(NT)]
        vpieces = [[(vps[j][p].ap(), Buf(f"vps{j}{p}")) for p in range(2)] for j in range(NT)]

        def after_tile(j):
            for p in range(2):
                cx.cc_pending.append((lambda e, j=j, p=p: e.collective_compute("AllGather", ALU.bypass, replica_groups=groups,
                                                                              ins=[kps[j][p].ap().opt()], outs=[kpa[j][p].ap().opt()]),
                                      [kpieces[j][p][1]], [kTallB]))
                cx.cc_pending.append((lambda e, j=j, p=p: e.collective_compute("AllGather", ALU.bypass, replica_groups=groups,
                                                                              ins=[vps[j][p].ap().opt()], outs=[vpa[j][p].ap().opt()]),
                                      [vpieces[j][p][1]], [vallB]))
            if j == NT - 1:
                cx.cc_flush()

        a_bufs = layer0_and_qkv(cx, alloc, xc, hmask_d, pw1, pw2, w1a, w2a, wkv, wq, x2T, qT, kTs, vts, x2TB, qTB, kTsB, vtsB,
                                kpieces, vpieces, after_tile)

        reg = st.enter_context(nc.sync.register("prevrow"))

        pval = []

        def copy_prev(dst, src_t, rows):
            def fn(e):
                if not pval:
                    e.reg_load(reg, prow_d[0:1, 0:1])
                    pval.append(e.snap(reg, min_val=0, max_val=3))
                return e.dma_start(out=dst, in_=src_t.ap()[bass.ts(pval[0], rows), :])
            return fn

        for j in range(NT):
            for p in range(2):
                P.op("sp", copy_prev(kTp[(D // 2) * p:(D // 2) * (p + 1), T * j:T * (j + 1)], kpa[j][p], D // 2),
                     reads=[kTallB], writes=[kTpB], accum=True, dsem="XK")
                P.op("sp", copy_prev(vtp[T * j + (T // 2) * p:T * j + (T // 2) * (p + 1), :], vpa[j][p], T // 2),
                     reads=[vallB], writes=[vtpB], accum=True, dsem="XV")

        attention_and_layer1(cx, carve, off, a_bufs, x2T, qT, kTs, kTp, vts, vtp, bias_d, wo, w1b, w2b, oT, out, identb, identbB,
                             x2TB, qTB, kTsB, kTpB, vtsB, vtpB)
        assert cx.wnext_get == len(cx.witems)
        import os
        if os.environ.get("KDEBUG"):
            for nm, ap_, shp, dt_, rb in (("dbg_kTp", kTp, [D, TOK], BF16, kTpB), ("dbg_kTs", kTs, [D, TOK], BF16, kTsB),
                                          ("dbg_vtp", vtp, [TOK, D], BF16, vtpB), ("dbg_vts", vts, [TOK, D], BF16, vtsB),
                                          ("dbg_oT", oT, [D, TOK], BF16, None), ("dbg_x2T", x2T, [D, TOK], F32, x2TB)):
                dd = nc.dram_tensor(nm, shp, dt_, kind="ExternalOutput").ap()
                P.op("sp", lambda e, dd=dd, ap_=ap_: e.dma_start(out=dd, in_=ap_), reads=([rb] if rb is not None else []), dsem="DBG")
        P.emit()
    return nc


def kernel(x, conv_pw1_w, conv_pw1_b, conv_dw_w, conv_dw_b, conv_ln_g, conv_ln_b, conv_pw2_w, conv_pw2_b,
           w_kv, attn_wq, attn_wo, rel_bias, mlp_w1, mlp_w2, ln_mix_g, ln_mix_b, ln_mlp_g, ln_mlp_b):
    f32 = np.float32
    x = np.asarray(x, f32)
    ncore = 8
    vecs = np.zeros((128, NV), f32)
    vecs[:, V_PW1B:V_PW1B + 32] = _fm(np.asarray(conv_pw1_b, f32)[0])
    dw = np.asarray(conv_dw_w, f32)[0]
    for tap in range(CONVW):
        vecs[:, V_DWW + 16 * tap:V_DWW + 16 * tap + 16] = _fm(dw[tap])
    vecs[:, V_DWB:V_DWB + 16] = _fm(np.asarray(conv_dw_b, f32)[0])
    vecs[:, V_CLNG:V_CLNG + 16] = _fm(np.asarray(conv_ln_g, f32)[0])
    vecs[:, V_CLNB:V_CLNB + 16] = _fm(np.asarray(conv_ln_b, f32)[0])
    vecs[:, V_PW2B:V_PW2B + 16] = _fm(np.asarray(conv_pw2_b, f32)[0])
    for l in range(2):
        vecs[:, V_MIXG + 16 * l:V_MIXG + 16 * l + 16] = _fm(np.asarray(ln_mix_g, f32)[l])
        vecs[:, V_MIXB + 16 * l:V_MIXB + 16 * l + 16] = _fm(np.asarray(ln_mix_b, f32)[l])
        vecs[:, V_MLPG + 16 * l:V_MLPG + 16 * l + 16] = _fm(np.asarray(ln_mlp_g, f32)[l])
        vecs[:, V_MLPB + 16 * l:V_MLPB + 16 * l + 16] = _fm(np.asarray(ln_mlp_b, f32)[l])
    ident = np.eye(128, dtype=f32)
    w1 = np.asarray(mlp_w1, f32)
    w2 = np.asarray(mlp_w2, f32)
    rel_bias = np.asarray(rel_bias, f32)
    shared = {
        "vecs": vecs, "ident": ident,
        "pw1": np.ascontiguousarray(np.asarray(conv_pw1_w, f32)[0]),
        "pw2": np.ascontiguousarray(np.asarray(conv_pw2_w, f32)[0]),
        "w1a": np.ascontiguousarray(w1[0]), "w2a": np.ascontiguousarray(w2[0]),
        "w1b": np.ascontiguousarray(w1[1]), "w2b": np.ascontiguousarray(w2[1]),
        "wkv": np.ascontiguousarray(np.asarray(w_kv, f32)),
        "wq": np.ascontiguousarray(np.asarray(attn_wq, f32)[0]),
        "wo": np.ascontiguousarray(np.asarray(attn_wo, f32)[0]),
    }
    bias_tiles = {True: _bias_tiles(rel_bias, True), False: _bias_tiles(rel_bias, False)}
    if "f" not in _NC_CACHE:
        _NC_CACHE["f"] = build_fused()
    ncf = _NC_CACHE["f"]
    in_maps = []
    for c in range(ncore):
        b, q = divmod(c, 4)
        xc = np.zeros((D, XH + TOK), f32)
        xc[:, XH:] = x[b, q * TOK:(q + 1) * TOK].T
        if q > 0:
            xc[:, :XH] = x[b, q * TOK - XH:q * TOK].T
        m = dict(shared)
        m.update({"xc": xc, "hmask": np.full((128, 1), 1.0 if q > 0 else 0.0, f32),
                  "prevrow": np.array([[max(q - 1, 0)]], np.int32), "biasT": bias_tiles[q > 0]})
        in_maps.append(m)
    res = run_bass_kernel_spmd(ncf, in_maps, core_ids=list(range(ncore))).results
    _NC_CACHE["last"] = res
    outp = np.zeros((2, 8192, D), f32)
    for c in range(ncore):
        b, q = divmod(c, 4)
        outp[b, q * TOK:(q + 1) * TOK] = np.asarray(res[c]["out"]).T
    return outp
```

```python
import contextlib
import math
import numpy as np
import ml_dtypes
import concourse.bass as bass
import concourse.mybir as mybir
from concourse.bass_utils import run_bass_kernel_spmd

F32 = mybir.dt.float32
BF16 = mybir.dt.bfloat16
ALU = mybir.AluOpType
AF = mybir.ActivationFunctionType

D = 2048
NCH = 16
DFF = 8192
T = 512
NT = 4
TOK = 2048
HALO = 32
XH = 128
NHEAD = 16
ALPHA = float(4 ** 0.25)
EPS = 1e-5
QSCALE = float(128 ** -0.5)
NEG = -30000.0
CONVW = 31

V_PW1B = 0
V_DWW = V_PW1B + 32
V_DWB = V_DWW + CONVW * 16
V_CLNG = V_DWB + 16
V_CLNB = V_CLNG + 16
V_PW2B = V_CLNB + 16
V_MIXG = V_PW2B + 16
V_MIXB = V_MIXG + 32
V_MLPG = V_MIXB + 32
V_MLPB = V_MLPG + 32
NV = V_MLPB + 32


class Buf:
    __slots__ = ("name", "writers", "readers")

    def __init__(self, name):
        self.name = name
        self.writers = []
        self.readers = []


class Op:
    __slots__ = ("eng", "fn", "deps", "dsem", "sig", "val", "waits", "inc")

    def __init__(self, eng, fn, dsem, inc=None):
        self.eng = eng
        self.fn = fn
        self.deps = []
        self.dsem = dsem
        self.inc = inc if inc is not None else (16 if dsem is not None else 1)
        self.sig = False
        self.val = 0
        self.waits = None


class Prog:
    ENGS = ("pe", "act", "dve", "pool", "sp")

    def __init__(self, nc):
        self.nc = nc
        self.ops = {e: [] for e in self.ENGS}

    def op(self, eng, fn, reads=(), writes=(), accum=False, dsem=None, inc=None):
        o = Op(eng, fn, dsem, inc)
        deps = o.deps
        for b in reads:
            deps.extend(b.writers)
        for b in writes:
            if not accum:
                deps.extend(b.writers)
            deps.extend(b.readers)
        for b in writes:
            if accum:
                b.writers.append(o)
            else:
                b.writers = [o]
            b.readers = []
        for b in reads:
            b.readers.append(o)
        self.ops[eng].append(o)
        return o

    def emit(self, final_wait_eng="sp"):
        nc = self.nc
        for e in self.ENGS:
            for o in self.ops[e]:
                if o.dsem is not None:
                    o.sig = True
                nd = []
                for d in o.deps:
                    if d.eng == "pe" and e == "pe" and d.dsem is None:
                        continue
                    d.sig = True
                    nd.append(d)
                o.deps = nd
        counts = {}
        sem_names = []
        for e in self.ENGS:
            for o in self.ops[e]:
                if not o.sig:
                    continue
                key = o.dsem if o.dsem is not None else "E_" + e
                counts[key] = counts.get(key, 0) + o.inc
                o.val = counts[key]
                if key not in sem_names:
                    sem_names.append(key)
        with contextlib.ExitStack() as st:
            sems = {k: st.enter_context(nc.semaphore(k)) for k in sem_names}
            for e in self.ENGS:
                seen = {}
                for o in self.ops[e]:
                    need = {}
                    for d in o.deps:
                        key = d.dsem if d.dsem is not None else "E_" + d.eng
                        if d.val > need.get(key, 0):
                            need[key] = d.val
                    w = []
                    for k, v in need.items():
                        if seen.get(k, 0) < v:
                            seen[k] = v
                            w.append((k, v))
                    o.waits = w
            block = st.enter_context(nc.Block())
            ops = self.ops
            final = [(k, v) for k, v in counts.items() if not k.startswith("E_")]

            def run(eh, ename):
                for o in ops[ename]:
                    for k, v in o.waits:
                        eh.wait_ge(sems[k], v)
                    ins = o.fn(eh)
                    if o.sig:
                        key = o.dsem if o.dsem is not None else "E_" + ename
                        ins.then_inc(sems[key], o.inc)
                if ename == final_wait_eng:
                    for k, v in final:
                        eh.wait_ge(sems[k], v)

            @block.tensor
            def _(eh):
                run(eh, "pe")

            @block.scalar
            def _(eh):
                run(eh, "act")

            @block.vector
            def _(eh):
                run(eh, "dve")

            @block.gpsimd
            def _(eh):
                run(eh, "pool")

            @block.sync
            def _(eh):
                run(eh, "sp")


class Ctx:
    def __init__(self, nc, st, nslots, keep_prev=False):
        self.keep_prev = keep_prev
        self.nc = nc
        self.st = st
        self.P = Prog(nc)
        self.ps = [st.enter_context(nc.psum_tensor(f"ps{i}", [128, 512], F32)) for i in range(8)]
        self.pb = [Buf(f"ps{i}") for i in range(8)]
        self.rot = 0
        self.nmain = 6
        self.nslots = nslots
        self.wslots = [st.enter_context(nc.sbuf_tensor(f"wsl{i}", [128, 8192], BF16)) for i in range(nslots)]
        self.wbufs = [Buf(f"wsl{i}") for i in range(nslots)]
        self.witems = []
        self.wnext_load = 0
        self.wnext_get = 0
        self.uid = 0
        self.cc_pending = []
        self.cc_since = 0
        self.cc_gap = 14
        self.ccB = Buf("cc_order")
        self.cc_n = 0

    def sb(self, name, shape, dt):
        return self.st.enter_context(self.nc.sbuf_tensor("sb_" + name, shape, dt))

    def bank(self):
        i = self.rot
        self.rot = (self.rot + 1) % self.nmain
        return self.ps[i], self.pb[i]

    def wadd(self, key, src, c):
        self.witems.append((key, src, c))

    def _wload(self, i):
        key, src, c = self.witems[i]
        s = i % self.nslots
        dst = self.wslots[s][:, :].rearrange("p (c n) -> p c n", c=c)
        if not isinstance(src, list):
            self.P.op("pool", lambda e: e.dma_start(out=dst, in_=src), writes=[self.wbufs[s]], dsem=f"W{s}")
            return
        for pi, (lo, hi, sv) in enumerate(src):
            self.P.op("pool", lambda e, lo=lo, hi=hi, sv=sv: e.dma_start(out=dst[:, :, lo:hi], in_=sv),
                      writes=[self.wbufs[s]], accum=(pi > 0), dsem=f"W{s}")

    def cc_issue(self):
        fn, reads, writes = self.cc_pending.pop(0)
        self.cc_n += 1
        self.P.op("pool", fn, reads=list(reads) + [self.ccB], writes=list(writes) + [self.ccB], accum=True, dsem=f"CC{self.cc_n % 4}", inc=1)
        self.cc_since = 0

    def cc_flush(self):
        while self.cc_pending:
            self.cc_issue()

    def wget(self, key):
        self.cc_since += 1
        if self.cc_pending and self.cc_since >= self.cc_gap:
            self.cc_issue()
        i = self.wnext_get
        self.wnext_get += 1
        assert self.witems[i][0] == key, (self.witems[i][0], key)
        lim = min(i + self.nslots - (1 if self.keep_prev else 0), len(self.witems))
        while self.wnext_load < lim:
            self._wload(self.wnext_load)
            self.wnext_load += 1
        s = i % self.nslots
        c = self.witems[i][2]
        return self.wslots[s][:, :].rearrange("p (c n) -> p c n", c=c), self.wbufs[s]


def wsrc_k2048(w, col0):
    return w.rearrange("(c p) n -> p c n", p=128)[:, :, col0:col0 + 512]


def wsrc_k8192(w, col0):
    return w.rearrange("(c p) n -> p c n", p=128)[:, :, col0:col0 + 128]


class LN:
    def __init__(self, cx, alloc=None, banks=(6, 7)):
        self.cx = cx
        self.b1, self.b2 = banks
        if alloc is None:
            alloc = lambda name, n, dt: cx.sb(name, [128, n], dt)[:, :]
        self.zb = [alloc(f"ln_zb{i}", T, BF16) for i in range(2)]
        self.z2b = [alloc(f"ln_z2b{i}", T, BF16) for i in range(2)]
        self.zbB = [Buf(f"ln_zb{i}") for i in range(2)]
        self.z2bB = [Buf(f"ln_z2b{i}") for i in range(2)]
        self.mean = alloc("ln_mean", T, F32)
        self.var = alloc("ln_var", T, F32)
        self.rstd = alloc("ln_rstd", T, F32)
        self.meanB, self.varB, self.rstdB = Buf("ln_mean"), Buf("ln_var"), Buf("ln_rstd")
        self.r = 0

    def stats_chunk(self, c, src, srcB):
        cx, P = self.cx, self.cx.P
        i = self.r
        self.r ^= 1
        zb, z2b = self.zb[i], self.z2b[i]
        P.op("act", lambda e: e.activation(out=zb, in_=src, func=AF.Identity), reads=[srcB], writes=[self.zbB[i]])
        P.op("act", lambda e: e.activation(out=z2b, in_=src, func=AF.Square), reads=[srcB], writes=[self.z2bB[i]])
        ones = cx.ones
        P.op("pe", lambda e: e.matmul(cx.ps[self.b1][:], ones[:], zb, start=(c == 0), stop=(c == NCH - 1)),
             reads=[self.zbB[i], cx.onesB], writes=[cx.pb[self.b1]], accum=(c > 0))
        P.op("pe", lambda e: e.matmul(cx.ps[self.b2][:], ones[:], z2b, start=(c == 0), stop=(c == NCH - 1)),
             reads=[self.z2bB[i], cx.onesB], writes=[cx.pb[self.b2]], accum=(c > 0))

    def finalize(self):
        cx, P = self.cx, self.cx.P
        mean, var, rstd = self.mean, self.var, self.rstd
        P.op("dve", lambda e: e.tensor_scalar(out=mean, in0=cx.ps[self.b1][:], scalar1=1.0 / D, scalar2=None, op0=ALU.mult),
             reads=[cx.pb[self.b1]], writes=[self.meanB])
        P.op("dve", lambda e: e.tensor_tensor(out=var, in0=mean, in1=mean, op=ALU.mult),
             reads=[self.meanB], writes=[self.varB])
        P.op("dve", lambda e: e.scalar_tensor_tensor(out=var, in0=cx.ps[self.b2][:], scalar=1.0 / D, in1=var,
                                                     op0=ALU.mult, op1=ALU.subtract),
             reads=[cx.pb[self.b2], self.varB], writes=[self.varB])
        P.op("act", lambda e: e.activation(out=var, in_=var, func=AF.Sqrt, bias=cx.epsc[:, 0:1]),
             reads=[self.varB], writes=[self.varB])
        P.op("dve", lambda e: e.reciprocal(out=rstd, in_=var), reads=[self.varB], writes=[self.rstdB])
        P.op("dve", lambda e: e.scalar_tensor_tensor(out=mean, in0=mean, scalar=-1.0, in1=rstd,
                                                     op0=ALU.mult, op1=ALU.mult),
             reads=[self.meanB, self.rstdB], writes=[self.meanB])

    def norm_chunk(self, src, srcB):
        P = self.cx.P
        rstd, mean = self.rstd, self.mean
        P.op("dve", lambda e: e.tensor_tensor(out=src, in0=src, in1=rstd, op=ALU.mult),
             reads=[srcB, self.rstdB], writes=[srcB])
        P.op("dve", lambda e: e.tensor_tensor(out=src, in0=src, in1=mean, op=ALU.add),
             reads=[srcB, self.meanB], writes=[srcB])


def post_ln(cx, ln, xf, xfB, xb, xbB, gcol, bcol):
    P = cx.P
    vecs = cx.vecs
    ln.finalize()
    for c in range(NCH):
        src = xf[:, c, :]
        ln.norm_chunk(src, xfB[c])
        P.op("act", lambda e, c=c, src=src: e.activation(out=src, in_=src, func=AF.Identity,
                                                         bias=vecs[:, bcol + c:bcol + c + 1],
                                                         scale=vecs[:, gcol + c:gcol + c + 1]),
             reads=[xfB[c]], writes=[xfB[c]])
        P.op("act", lambda e, c=c, src=src: e.activation(out=xb[:, c, :], in_=src, func=AF.Identity),
             reads=[xfB[c]], writes=[xbB[c]])


def mlp_sublayer(cx, ln, layer, xf, xfB, xb, xbB, h, hB, rbuf, rB):
    P = cx.P
    rr = 0
    for g in range(16):
        wv, wB = cx.wget(("w1", layer, g))
        for i in range(4):
            hc = 4 * g + i
            pt, pB = cx.bank()
            for k in range(NCH):
                P.op("pe", lambda e, k=k, i=i, wv=wv, pt=pt: e.matmul(pt[:], wv[:, k, i * 128:(i + 1) * 128], xb[:, k, :],
                                                                      start=(k == 0), stop=(k == NCH - 1)),
                     reads=[wB, xbB[k]], writes=[pB], accum=(k > 0))
            r, rb = rbuf[rr], rB[rr]
            rr ^= 1
            P.op("act", lambda e, pt=pt, r=r: e.activation(out=r[:], in_=pt[:], func=AF.Relu), reads=[pB], writes=[rb])
            P.op("dve", lambda e, hc=hc, r=r: e.tensor_tensor(out=h[:, hc, :], in0=r[:], in1=r[:], op=ALU.mult),
                 reads=[rb], writes=[hB[hc]])
    for dc in range(NCH):
        wv, wB = cx.wget(("w2", layer, dc))
        pt, pB = cx.bank()
        for k in range(64):
            P.op("pe", lambda e, k=k, wv=wv, pt=pt: e.matmul(pt[:], wv[:, k, :], h[:, k, :], start=(k == 0), stop=(k == 63)),
                 reads=[wB, hB[k]], writes=[pB], accum=(k > 0))
        P.op("dve", lambda e, dc=dc, pt=pt: e.scalar_tensor_tensor(out=xf[:, dc, :], in0=xf[:, dc, :], scalar=ALPHA, in1=pt[:],
                                                                   op0=ALU.mult, op1=ALU.add),
             reads=[pB, xfB[dc]], writes=[xfB[dc]])
        ln.stats_chunk(dc, xf[:, dc, :], xfB[dc])
    post_ln(cx, ln, xf, xfB, xb, xbB, V_MLPG + 16 * layer, V_MLPB + 16 * layer)


def mlp_sublayer_q(cx, ln, lkey, xf, xfB, xb, xbB, hq, hqB, rbuf, rB, gcol, bcol):
    P = cx.P
    rr = 0
    for q_ in range(4):
        for g in range(4):
            wv, wB = cx.wget(("w1", lkey, q_, g))
            for i in range(4):
                hc = 4 * g + i
                pt, pB = cx.bank()
                for k in range(NCH):
                    P.op("pe", lambda e, k=k, i=i, wv=wv, pt=pt: e.matmul(pt[:], wv[:, k, i * 128:(i + 1) * 128], xb[:, k, :],
                                                                          start=(k == 0), stop=(k == NCH - 1)),
                         reads=[wB, xbB[k]], writes=[pB], accum=(k > 0))
                r, rb = rbuf[rr], rB[rr]
                rr ^= 1
                P.op("act", lambda e, pt=pt, r=r: e.activation(out=r, in_=pt[:], func=AF.Relu), reads=[pB], writes=[rb])
                P.op("dve", lambda e, hc=hc, r=r: e.tensor_tensor(out=hq[:, hc, :], in0=r, in1=r, op=ALU.mult),
                     reads=[rb], writes=[hqB[hc]])
        for g in range(4):
            wv, wB = cx.wget(("w2", lkey, q_, g))
            for i in range(4):
                dc = 4 * g + i
                pt, pB = cx.bank()
                for k in range(16):
                    P.op("pe", lambda e, k=k, i=i, wv=wv, pt=pt: e.matmul(pt[:], wv[:, k, i * 128:(i + 1) * 128], hq[:, k, :],
                                                                          start=(k == 0), stop=(k == 15)),
                         reads=[wB, hqB[k]], writes=[pB], accum=(k > 0))
                if q_ == 0:
                    P.op("dve", lambda e, dc=dc, pt=pt: e.scalar_tensor_tensor(out=xf[:, dc, :], in0=xf[:, dc, :], scalar=ALPHA, in1=pt[:],
                                                                               op0=ALU.mult, op1=ALU.add),
                         reads=[pB, xfB[dc]], writes=[xfB[dc]])
                else:
                    P.op("dve", lambda e, dc=dc, pt=pt: e.tensor_tensor(out=xf[:, dc, :], in0=pt[:], in1=xf[:, dc, :], op=ALU.add),
                         reads=[pB, xfB[dc]], writes=[xfB[dc]])
                if q_ == 3:
                    ln.stats_chunk(dc, xf[:, dc, :], xfB[dc])
    post_ln(cx, ln, xf, xfB, xb, xbB, gcol, bcol)


def proj_residual_sublayer(cx, ln, wkey, inb, inbB, xf, xfB, xb, xbB, gcol, bcol, bias_col, tb=None, tbB=None):
    P = cx.P
    vecs = cx.vecs
    tr = 0
    for g in range(4):
        wv, wB = cx.wget((wkey, g))
        for i in range(4):
            dc = 4 * g + i
            pt, pB = cx.bank()
            for k in range(NCH):
                P.op("pe", lambda e, k=k, i=i, wv=wv, pt=pt: e.matmul(pt[:], wv[:, k, i * 128:(i + 1) * 128], inb[:, k, :],
                                                                      start=(k == 0), stop=(k == NCH - 1)),
                     reads=[wB, inbB[k]], writes=[pB], accum=(k > 0))
            if bias_col is not None:
                t_, tB_ = tb[tr], tbB[tr]
                tr ^= 1
                P.op("act", lambda e, dc=dc, pt=pt, t_=t_: e.activation(out=t_[:], in_=pt[:], func=AF.Identity,
                                                                        bias=vecs[:, bias_col + dc:bias_col + dc + 1]),
                     reads=[pB], writes=[tB_])
                P.op("dve", lambda e, dc=dc, t_=t_: e.scalar_tensor_tensor(out=xf[:, dc, :], in0=xf[:, dc, :], scalar=ALPHA,
                                                                          in1=t_[:], op0=ALU.mult, op1=ALU.add),
                     reads=[tB_, xfB[dc]], writes=[xfB[dc]])
            else:
                P.op("dve", lambda e, dc=dc, pt=pt: e.scalar_tensor_tensor(out=xf[:, dc, :], in0=xf[:, dc, :], scalar=ALPHA,
                                                                          in1=pt[:], op0=ALU.mult, op1=ALU.add),
                     reads=[pB, xfB[dc]], writes=[xfB[dc]])
            ln.stats_chunk(dc, xf[:, dc, :], xfB[dc])
    post_ln(cx, ln, xf, xfB, xb, xbB, gcol, bcol)


def load_consts(cx, vecs_d, ident_d):
    P = cx.P
    cx.vecs = cx.sb("vecs", [128, NV], F32)
    cx.ident = cx.sb("ident", [128, 128], F32)
    cx.ones = cx.sb("ones", [128, 128], BF16)
    cx.epsc = cx.sb("epsc", [128, 1], F32)
    cx.vecsB, cx.identB, cx.onesB = Buf("vecs"), Buf("ident"), Buf("ones")
    P.op("sp", lambda e: e.dma_start(out=cx.vecs[:], in_=vecs_d), writes=[cx.vecsB], dsem="CV")
    P.op("sp", lambda e: e.dma_start(out=cx.ident[:], in_=ident_d), writes=[cx.identB], dsem="CI")
    P.op("dve", lambda e: e.memset(cx.ones[:], 1.0), writes=[cx.onesB])
    P.op("dve", lambda e: e.memset(cx.epsc[:], EPS), writes=[cx.onesB], accum=True)
    cx.dmy = cx.sb("dmy", [128, 4], F32)
    consts = [cx.vecsB, cx.identB, cx.onesB]
    P.op("act", lambda e: e.activation(out=cx.dmy[:, 0:1], in_=cx.vecs[:, 0:1], func=AF.Identity), reads=consts, writes=[Buf("dmy_a")])
    P.op("dve", lambda e: e.tensor_copy(out=cx.dmy[:, 1:2], in_=cx.vecs[:, 0:1]), reads=consts, writes=[Buf("dmy_d")])
    P.op("pool", lambda e: e.tensor_copy(out=cx.dmy[:, 2:3], in_=cx.vecs[:, 0:1]), reads=consts, writes=[Buf("dmy_p")])


def layer0_and_qkv(cx, alloc, xc, hmask_d, pw1, pw2, w1, w2, wkv, wq, x2T, qT, kT, vtm, x2TB, qTB, kTB, vtB,
                   kpieces, vpieces, after_tile):
    P = cx.P
    vecs = cx.vecs
    bufs = []

    def NB(name):
        b_ = Buf(name)
        bufs.append(b_)
        return b_

    hmask = alloc("hmask", 2, F32)
    hmB = NB("hmask")
    P.op("sp", lambda e: e.dma_start(out=hmask[:, 0:1], in_=hmask_d), writes=[hmB], dsem="CH")
    xf = alloc("xf", NCH * T, F32).rearrange("p (c t) -> p c t", c=NCH)
    xb = alloc("xb", NCH * T, BF16).rearrange("p (c t) -> p c t", c=NCH)
    xfB = [NB(f"xf{c}") for c in range(NCH)]
    xbB = [NB(f"xb{c}") for c in range(NCH)]
    scratch = alloc("scratch", 64 * T, BF16)
    hB = [NB(f"h{c}") for c in range(64)]
    h = scratch.rearrange("p (c t) -> p c t", c=64)
    cv = scratch[:, 0:32 * T].bitcast(F32).rearrange("p (c t) -> p c t", c=NCH)
    u = scratch[:, 32 * T:48 * T].rearrange("p (c t) -> p c t", c=NCH)
    kst = scratch[:, 0:16 * T].rearrange("p (c t) -> p c t", c=NCH)
    qst = scratch[:, 16 * T:32 * T].rearrange("p (c t) -> p c t", c=NCH)
    vst = scratch[:, 32 * T:48 * T].rearrange("p (b f) -> p b f", b=4)

    def cvB(c):
        return [hB[2 * c], hB[2 * c + 1]]

    xh = alloc("xh", NCH * HALO, BF16).rearrange("p (c t) -> p c t", c=NCH)
    xhB = NB("xh")
    xhf = alloc("xhf", NCH * HALO, F32).rearrange("p (c t) -> p c t", c=NCH)
    xhfB = NB("xhf")
    ghalo = alloc("ghalo", NCH * HALO, F32).rearrange("p (c t) -> p c t", c=NCH)
    ghB = [NB(f"gh{c}") for c in range(NCH)]
    gbuf = [alloc(f"gbuf{i}", T + 32, BF16) for i in range(2)]
    gbB = [NB(f"gbuf{i}") for i in range(2)]
    NDG = 16
    dg = [alloc(f"dg{i}", 128, BF16) for i in range(NDG)]
    dgB = [NB(f"dg{i}") for i in range(NDG)]
    dgr = 0
    sg = [alloc(f"sg{i}", T, F32) for i in range(2)]
    sgB = [NB(f"sg{i}") for i in range(2)]
    sgh = alloc("sgh", HALO, F32)
    sghB = NB("sgh")
    ln = LN(cx, alloc=alloc)
    bufs += ln.zbB + ln.z2bB + [ln.meanB, ln.varB, ln.rstdB]

    pw1v = pw1.rearrange("(c p) n -> p c n", p=128)
    w2v = w2.rearrange("(q c p) n -> q p c n", p=128, c=16)
    for j in range(NT):
        for g in range(8):
            cx.wadd(("pw1", j, g), [(0, 256, pw1v[:, :, 256 * g:256 * g + 256]),
                                    (256, 512, pw1v[:, :, D + 256 * g:D + 256 * g + 256])], 16)
        for g in range(4):
            cx.wadd(("pw2", g), wsrc_k2048(pw2, 512 * g), 16)
        for q_ in range(4):
            for g in range(4):
                cx.wadd(("w1", ("A", j), q_, g), wsrc_k2048(w1, 2048 * q_ + 512 * g), 16)
            for g in range(4):
                cx.wadd(("w2", ("A", j), q_, g), w2v[q_][:, :, 512 * g:512 * (g + 1)], 16)
        for g in range(4):
            cx.wadd(("wk", g), wsrc_k2048(wkv, 512 * g), 16)
        for g in range(4):
            cx.wadd(("wq", g), wsrc_k2048(wq, 512 * g), 16)
        for g in range(4):
            cx.wadd(("wv", g), wsrc_k2048(wkv, D + 512 * g), 16)

    xcv = xc.rearrange("(c p) t -> p c t", p=128)
    P.op("sp", lambda e: e.dma_start(out=xhf, in_=xcv[:, :, XH - HALO:XH]), writes=[xhfB], dsem="X0")
    P.op("act", lambda e: e.activation(out=xh, in_=xhf, func=AF.Identity), reads=[xhfB], writes=[xhB])

    for j in range(NT):
        P.op("sp", lambda e, j=j: e.dma_start(out=xf, in_=xcv[:, :, XH + T * j:XH + T * (j + 1)]), writes=xfB, dsem="X1")
        for c in range(NCH):
            if c % 2 == 0:
                P.op("act", lambda e, c=c: e.activation(out=xb[:, c, :], in_=xf[:, c, :], func=AF.Identity), reads=[xfB[c]], writes=[xbB[c]])
            else:
                P.op("dve", lambda e, c=c: e.tensor_copy(out=xb[:, c, :], in_=xf[:, c, :]), reads=[xfB[c]], writes=[xbB[c]])
        gr = 0
        for g in range(8):
            wv, wB = cx.wget(("pw1", j, g))
            for i in range(2):
                c = 2 * g + i
                acol = slice(i * 128, (i + 1) * 128)
                gcol = slice(256 + i * 128, 256 + (i + 1) * 128)
                pa, paB = cx.bank()
                for k in range(NCH):
                    P.op("pe", lambda e, k=k, acol=acol, wv=wv, pa=pa: e.matmul(pa[:], wv[:, k, acol], xb[:, k, :],
                                                                              start=(k == 0), stop=(k == NCH - 1)),
                         reads=[wB, xbB[k]], writes=[paB], accum=(k > 0))
                pg, pgB = cx.bank()
                for k in range(NCH):
                    P.op("pe", lambda e, k=k, gcol=gcol, wv=wv, pg=pg: e.matmul(pg[:], wv[:, k, gcol], xb[:, k, :],
                                                                              start=(k == 0), stop=(k == NCH - 1)),
                         reads=[wB, xbB[k]], writes=[pgB], accum=(k > 0))
                gi = gr
                gr ^= 1
                gb_, gbB_ = gbuf[gi], gbB[gi]
                sg_, sgB_ = sg[gi], sgB[gi]
                ba = vecs[:, V_PW1B + c:V_PW1B + c + 1]
                bg = vecs[:, V_PW1B + 16 + c:V_PW1B + 16 + c + 1]
                if j == 0:
                    ph, phB = cx.bank()
                    for k in range(NCH):
                        P.op("pe", lambda e, k=k, acol=acol, wv=wv, ph=ph: e.matmul(ph[:, 0:HALO], wv[:, k, acol], xh[:, k, :],
                                                                                  start=(k == 0), stop=(k == NCH - 1)),
                             reads=[wB, xhB], writes=[phB], accum=(k > 0))
                    for k in range(NCH):
                        P.op("pe", lambda e, k=k, gcol=gcol, wv=wv, ph=ph: e.matmul(ph[:, HALO:2 * HALO], wv[:, k, gcol], xh[:, k, :],
                                                                                  start=(k == 0), stop=(k == NCH - 1)),
                             reads=[wB, xhB], writes=[phB], accum=True)
                    P.op("act", lambda e, ph=ph, bg=bg: e.activation(out=sgh, in_=ph[:, HALO:2 * HALO], func=AF.Sigmoid, bias=bg),
                         reads=[phB], writes=[sghB])
                    P.op("dve", lambda e, c=c, ph=ph, ba=ba: e.scalar_tensor_tensor(out=ghalo[:, c, :], in0=ph[:, 0:HALO], scalar=ba, in1=sgh,
                                                                                   op0=ALU.add, op1=ALU.mult),
                         reads=[phB, sghB], writes=[ghB[c]])
                    P.op("dve", lambda e, c=c, gb_=gb_: e.tensor_scalar(out=gb_[:, 0:30], in0=ghalo[:, c, 2:32], scalar1=hmask[:, 0:1], scalar2=None,
                                                                       op0=ALU.mult),
                         reads=[ghB[c], hmB], writes=[gbB_])
                else:
                    P.op("dve", lambda e, c=c, gb_=gb_: e.tensor_copy(out=gb_[:, 0:30], in_=ghalo[:, c, 2:32]),
                         reads=[ghB[c]], writes=[gbB_])
                P.op("act", lambda e, pg=pg, sg_=sg_, bg=bg: e.activation(out=sg_, in_=pg[:], func=AF.Sigmoid, bias=bg),
                     reads=[pgB], writes=[sgB_])
                P.op("dve", lambda e, pa=pa, sg_=sg_, gb_=gb_, ba=ba: e.scalar_tensor_tensor(out=gb_[:, 30:30 + T], in0=pa[:], scalar=ba, in1=sg_,
                                                                                           op0=ALU.add, op1=ALU.mult),
                     reads=[paB, sgB_], writes=[gbB_], accum=True)
                if j < NT - 1:
                    P.op("dve", lambda e, c=c, gb_=gb_: e.tensor_copy(out=ghalo[:, c, 2:32], in_=gb_[:, T:T + 30]),
                         reads=[gbB_], writes=[ghB[c]])
                cvc = cv[:, c, :]
                bdw = vecs[:, V_DWB + c:V_DWB + c + 1]
                pc, pcB = cx.bank()
                for tap in range(CONVW):
                    wj = vecs[:, V_DWW + 16 * tap + c:V_DWW + 16 * tap + c + 1]
                    dgi, dgiB = dg[dgr], dgB[dgr]
                    dgr = (dgr + 1) % NDG
                    if tap % 2 == 0:
                        P.op("dve", lambda e, dgi=dgi, wj=wj: e.tensor_scalar(out=dgi, in0=cx.identb[:], scalar1=wj, scalar2=None, op0=ALU.mult),
                             writes=[dgiB])
                    else:
                        P.op("act", lambda e, dgi=dgi, wj=wj: e.activation(out=dgi, in_=cx.identb[:], func=AF.Identity, scale=wj),
                             writes=[dgiB])
                    P.op("pe", lambda e, pc=pc, dgi=dgi, gb_=gb_, tap=tap: e.matmul(pc[:], dgi, gb_[:, tap:tap + T],
                                                                                  start=(tap == 0), stop=(tap == CONVW - 1)),
                         reads=[dgiB, gbB_], writes=[pcB], accum=(tap > 0))
                P.op("act", lambda e, cvc=cvc, pc=pc, bdw=bdw: e.activation(out=cvc, in_=pc[:], func=AF.Identity, bias=bdw),
                     reads=[pcB], writes=cvB(c))
                ln.stats_chunk(c, cvc, hB[2 * c])
        ln.finalize()
        for c in range(NCH):
            cvc = cv[:, c, :]
            P.op("dve", lambda e, cvc=cvc: e.tensor_tensor(out=cvc, in0=cvc, in1=ln.rstd, op=ALU.mult),
                 reads=cvB(c) + [ln.rstdB], writes=cvB(c))
            P.op("dve", lambda e, cvc=cvc: e.tensor_tensor(out=cvc, in0=cvc, in1=ln.mean, op=ALU.add),
                 reads=cvB(c) + [ln.meanB], writes=cvB(c))
            P.op("act", lambda e, c=c, cvc=cvc: e.activation(out=u[:, c, :], in_=cvc, func=AF.Silu,
                                                             bias=vecs[:, V_CLNB + c:V_CLNB + c + 1], scale=vecs[:, V_CLNG + c:V_CLNG + c + 1]),
                 reads=cvB(c), writes=[hB[32 + c]])
        uB = [hB[32 + c] for c in range(NCH)]
        proj_residual_sublayer(cx, ln, "pw2", u, uB, xf, xfB, xb, xbB, V_MIXG, V_MIXB, V_PW2B, tb=sg, tbB=sgB)
        mlp_sublayer_q(cx, ln, ("A", j), xf, xfB, xb, xbB, h[:, 0:16, :], hB[0:16], sg, sgB, V_MLPG, V_MLPB)
        P.op("sp", lambda e, j=j: e.dma_start(out=x2T.rearrange("(c p) t -> p c t", p=128)[:, :, T * j:T * (j + 1)], in_=xf),
             reads=xfB, writes=[x2TB], accum=True, dsem="OX")
        for (nm, stg, cell0, scale) in (("wk", kst, 0, 1.0), ("wq", qst, 16, QSCALE)):
            for g in range(4):
                wv_, wB_ = cx.wget((nm, g))
                for i in range(4):
                    hd = 4 * g + i
                    pt, pB = cx.bank()
                    for k in range(NCH):
                        P.op("pe", lambda e, k=k, i=i, wv_=wv_, pt=pt: e.matmul(pt[:], wv_[:, k, i * 128:(i + 1) * 128], xb[:, k, :],
                                                                                start=(k == 0), stop=(k == NCH - 1)),
                             reads=[wB_, xbB[k]], writes=[pB], accum=(k > 0))
                    P.op("act", lambda e, hd=hd, pt=pt, stg=stg, scale=scale: e.activation(out=stg[:, hd, :], in_=pt[:], func=AF.Identity, scale=scale),
                         reads=[pB], writes=[hB[cell0 + hd]])
            dst, dB = (kT, kTB) if nm == "wk" else (qT, qTB)
            P.op("sp", lambda e, j=j, dst=dst, stg=stg: e.dma_start(out=dst.rearrange("(c p) t -> p c t", p=128)[:, :, T * j:T * (j + 1)], in_=stg),
                 reads=[hB[cell0 + c] for c in range(NCH)], writes=[dB], accum=True, dsem="OK" if nm == "wk" else "OQ")
            if nm == "wk":
                for p_ in range(2):
                    pap, pB_ = kpieces[j][p_]
                    P.op("sp", lambda e, pap=pap, p_=p_, stg=stg: e.dma_start(out=pap.rearrange("(c p) t -> p c t", p=128), in_=stg[:, 8 * p_:8 * p_ + 8, :]),
                         reads=[hB[cell0 + c] for c in range(8 * p_, 8 * p_ + 8)], writes=[pB_], dsem=f"PK{p_}")
        for g in range(4):
            wv_, wB_ = cx.wget(("wv", g))
            for tb in range(4):
                pt, pB = cx.bank()
                for k in range(NCH):
                    P.op("pe", lambda e, k=k, tb=tb, wv_=wv_, pt=pt: e.matmul(pt[:], xb[:, k, tb * 128:(tb + 1) * 128], wv_[:, k, :],
                                                                              start=(k == 0), stop=(k == NCH - 1)),
                         reads=[wB_, xbB[k]], writes=[pB], accum=(k > 0))
                cells = [hB[32 + 4 * tb + q] for q in range(4)]
                if g % 2 == 0:
                    P.op("act", lambda e, g=g, tb=tb, pt=pt: e.activation(out=vst[:, tb, 512 * g:512 * (g + 1)], in_=pt[:], func=AF.Identity),
                         reads=[pB], writes=cells, accum=(g > 0))
                else:
                    P.op("dve", lambda e, g=g, tb=tb, pt=pt: e.tensor_copy(out=vst[:, tb, 512 * g:512 * (g + 1)], in_=pt[:]),
                         reads=[pB], writes=cells, accum=True)
        P.op("sp", lambda e, j=j: e.dma_start(out=vtm[T * j:T * (j + 1), :].rearrange("(b p) f -> p b f", p=128), in_=vst),
             reads=[hB[32 + q] for q in range(16)], writes=[vtB], accum=True, dsem="OV")
        for p_ in range(2):
            pap, pB_ = vpieces[j][p_]
            P.op("sp", lambda e, pap=pap, p_=p_: e.dma_start(out=pap.rearrange("(b p) f -> p b f", p=128), in_=vst[:, 2 * p_:2 * p_ + 2, :]),
                 reads=[hB[32 + q] for q in range(8 * p_, 8 * p_ + 8)], writes=[pB_], dsem=f"PV{p_}")
        after_tile(j)
    return bufs


BR = (1, 4, 16)


BIGN = 76 * 1024


def attention_and_layer1(cx, carve, off, a_bufs, x2T, qT, kTs, kTp, vts, vtp, bias_d, wo, w1, w2, oT, out, identb, identbB,
                         x2TB, qTB, kTsB, kTpB, vtsB, vtpB):
    P = cx.P
    off[0] = 0
    kwin = [carve(4096) for _ in range(2)]
    qn = [carve(2048) for _ in range(2)]
    kd = [[carve(4096) for _ in range(2)] for _ in range(2)]
    qd = [[carve(2048) for _ in range(2)] for _ in range(2)]
    vsl = [carve(32 * 256) for _ in range(2)]
    bia = [carve(9 * 128) for _ in range(2)]
    ptb = [carve(256) for _ in range(4)]
    oacc = [carve(4096).bitcast(F32) for _ in range(2)]
    dacc = [carve(4096).bitcast(F32) for _ in range(2)]
    ost = [carve(2048) for _ in range(2)]
    kwinB = [Buf("kwin0"), Buf("kwin1")]
    qnB = [Buf("qn0"), Buf("qn1")]
    kdB = [[Buf(f"kd{r}{h}") for h in range(2)] for r in range(2)]
    qdB = [[Buf(f"qd{r}{h}") for h in range(2)] for r in range(2)]
    vslB = [Buf(f"vsl{i}") for i in range(2)]
    biaB = [Buf("bia0"), Buf("bia1")]
    ptbB = [Buf(f"ptb{i}") for i in range(4)]
    oaccB = [Buf("oacc0"), Buf("oacc1")]
    daccB = [Buf("dacc0"), Buf("dacc1")]
    ostB = [Buf("ost0"), Buf("ost1")]
    oTdB = Buf("oT_dram")
    b1_bufs = kwinB + qnB + kdB[0] + kdB[1] + qdB[0] + qdB[1] + vslB + biaB + ptbB + oaccB + daccB + ostB
    P.op("dve", lambda e: e.memset(cx.dmy[:, 3:4], 0.0), reads=a_bufs, writes=a_bufs + b1_bufs)
    cx.nmain = 4
    cx.rot = 0
    accrot = 0
    vrot = 0
    drot = 0
    prot = 0
    LA = 2
    qTv = qT.rearrange("(h p) t -> h p t", p=128)
    kTsv = kTs.rearrange("(h p) t -> h p t", p=128)
    kTpv = kTp.rearrange("(h p) t -> h p t", p=128)
    for hp in range(8):
        for hh in range(2):
            hd = 2 * hp + hh
            P.op("sp", lambda e, hh=hh, hd=hd: e.dma_start(out=kwin[hh][:, 0:TOK], in_=kTpv[hd]), reads=[kTpB], writes=[kwinB[hh]], dsem=f"BK{hh}")
            P.op("sp", lambda e, hh=hh, hd=hd: e.dma_start(out=kwin[hh][:, TOK:2 * TOK], in_=kTsv[hd]), reads=[kTsB], writes=[kwinB[hh]],
                 accum=True, dsem=f"BK{hh}")
            P.op("sp", lambda e, hh=hh, hd=hd: e.dma_start(out=qn[hh], in_=qTv[hd]), reads=[qTB], writes=[qnB[hh]], dsem=f"BQ{hh}")
            P.op("pool", lambda e, hh=hh, hd=hd: e.dma_start(out=bia[hh], in_=bias_d[hd]), writes=[biaB[hh]], dsem=f"BB{hh}")
        for bi, d in enumerate(BR):
            Lw = 4096 // d
            Lq = 2048 // d
            nbr = 32 // d
            vs = vsl[vrot]
            vsB = vslB[vrot]
            vsem = f"BV{vrot}"
            vrot ^= 1
            vs3 = vs.rearrange("p (b f) -> p b f", f=256)
            cols = slice(256 * hp, 256 * hp + 256)
            lo = nbr // 2 - 1
            hb = nbr // 2
            for r in range(d):
                srcp = vtp.rearrange("(b i r) f -> r i b f", i=128, r=d)[r, :, hb - 1:hb, cols]
                P.op("sp", lambda e, vs3=vs3, srcp=srcp, r=r, lo=lo, nbr=nbr: e.dma_start(out=vs3[:, r * nbr + lo:r * nbr + lo + 1, :], in_=srcp),
                     reads=[vtpB], writes=[vsB], accum=(r > 0), dsem=vsem)
                srco = vts.rearrange("(b i r) f -> r i b f", i=128, r=d)[r, :, 0:hb, cols]
                P.op("sp", lambda e, vs3=vs3, srco=srco, r=r, hb=hb, nbr=nbr: e.dma_start(out=vs3[:, r * nbr + hb:r * nbr + nbr, :], in_=srco),
                     reads=[vtsB], writes=[vsB], accum=True, dsem=vsem)
            for hh in range(2):
                if d == 1:
                    kdv, kdvB = kwin[hh], kwinB[hh]
                    qdv, qdvB = qn[hh], qnB[hh]
                else:
                    kdv, kdvB = kd[drot][hh], kdB[drot][hh]
                    qdv, qdvB = qd[drot][hh], qdB[drot][hh]
                    P.op("pool", lambda e, hh=hh, kdv=kdv, d=d: e.tensor_copy(out=kdv.rearrange("p (r m) -> p r m", r=d),
                                                                            in_=kwin[hh].rearrange("p (m r) -> p r m", r=d)),
                         reads=[kwinB[hh]], writes=[kdvB])
                    P.op("pool", lambda e, hh=hh, qdv=qdv, d=d: e.tensor_copy(out=qdv.rearrange("p (r m) -> p r m", r=d),
                                                                            in_=qn[hh].rearrange("p (m r) -> p r m", r=d)),
                         reads=[qnB[hh]], writes=[qdvB])
                ov = oacc[hh] if d == 1 else oacc[hh].rearrange("p (m r) -> p r m", r=d)
                dv = dacc[hh] if d == 1 else dacc[hh].rearrange("p (m r) -> p r m", r=d)
                steps = [(g, nn) for g in range(4) for nn in range(4)]
                pts = {}
                accs = {}
                for s_ in range(len(steps) + LA):
                    if s_ < len(steps):
                        g, nn = steps[s_]
                        qpos = 512 * g + 128 * nn
                        r = qpos // Lq
                        n = (qpos % Lq) // 128
                        pst, pstB = cx.bank()
                        vblks = []
                        for part in range(2):
                            kpos = r * Lw + Lw // 2 + 128 * (n - 1 + part)
                            vblks.append(r * nbr + nbr // 2 + n - 1 + part)
                            bcol = (bi * 3 + (part if not (part == 0 and n == 0) else 2)) * 128
                            sq = pst[:, 128 * part:128 * (part + 1)]
                            P.op("pe", lambda e, sq=sq, kdv=kdv, qdv=qdv, kpos=kpos, qpos=qpos: e.matmul(sq, kdv[:, kpos:kpos + 128], qdv[:, qpos:qpos + 128],
                                                                                                       start=True, stop=False),
                                 reads=[kdvB, qdvB], writes=[pstB], accum=(part > 0))
                            P.op("pe", lambda e, sq=sq, hh=hh, bcol=bcol: e.matmul(sq, identb[:], bia[hh][:, bcol:bcol + 128], start=False, stop=True),
                                 reads=[identbB, biaB[hh]], writes=[pstB], accum=True)
                        pt_, ptB_ = ptb[prot], ptbB[prot]
                        prot = (prot + 1) % 4
                        P.op("act", lambda e, pst=pst, pt_=pt_: e.activation(out=pt_, in_=pst[:, 0:256], func=AF.Exp), reads=[pstB], writes=[ptB_])
                        pts[s_] = (pt_, ptB_, vblks)
                    t_ = s_ - LA
                    if t_ < 0:
                        continue
                    g, nn = steps[t_]
                    pt_, ptB_, vblks = pts.pop(t_)
                    if nn == 0:
                        accs[g] = (cx.ps[4 + 2 * accrot], cx.pb[4 + 2 * accrot], cx.ps[5 + 2 * accrot], cx.pb[5 + 2 * accrot])
                        accrot ^= 1
                    po, poB, pd, pdB = accs[g]
                    for part in range(2):
                        vblk = vblks[part]
                        ptp = pt_[:, 128 * part:128 * (part + 1)]
                        P.op("pe", lambda e, po=po, nn=nn, vs3=vs3, vblk=vblk, hh=hh, ptp=ptp, part=part: e.matmul(
                            po[:, 128 * nn:128 * (nn + 1)], vs3[:, vblk, 128 * hh:128 * (hh + 1)], ptp, start=(part == 0), stop=(part == 1)),
                             reads=[vsB, ptB_], writes=[poB], accum=not (nn == 0 and part == 0))
                        P.op("pe", lambda e, pd=pd, nn=nn, ptp=ptp, part=part: e.matmul(
                            pd[:, 128 * nn:128 * (nn + 1)], cx.ones[:], ptp, start=(part == 0), stop=(part == 1)),
                             reads=[cx.onesB, ptB_], writes=[pdB], accum=not (nn == 0 and part == 0))
                    if nn != 3:
                        continue
                    if d == 1:
                        oview = ov[:, 512 * g:512 * (g + 1)]
                        dview = dv[:, 512 * g:512 * (g + 1)]
                        pov, pdv = po[:, :], pd[:, :]
                    elif d == 4:
                        oview = ov[:, g, :]
                        dview = dv[:, g, :]
                        pov, pdv = po[:, :], pd[:, :]
                    else:
                        oview = ov[:, 4 * g:4 * g + 4, :]
                        dview = dv[:, 4 * g:4 * g + 4, :]
                        pov = po[:, :].rearrange("p (r m) -> p r m", r=4)
                        pdv = pd[:, :].rearrange("p (r m) -> p r m", r=4)
                    if bi == 0:
                        P.op("dve", lambda e, oview=oview, pov=pov: e.tensor_copy(out=oview, in_=pov), reads=[poB], writes=[oaccB[hh]], accum=(g > 0))
                        P.op("dve", lambda e, dview=dview, pdv=pdv: e.tensor_copy(out=dview, in_=pdv), reads=[pdB], writes=[daccB[hh]], accum=(g > 0))
                    else:
                        P.op("dve", lambda e, oview=oview, pov=pov: e.tensor_tensor(out=oview, in0=pov, in1=oview, op=ALU.add),
                             reads=[poB, oaccB[hh]], writes=[oaccB[hh]])
                        P.op("dve", lambda e, dview=dview, pdv=pdv: e.tensor_tensor(out=dview, in0=pdv, in1=dview, op=ALU.add),
                             reads=[pdB, daccB[hh]], writes=[daccB[hh]])
            if d != 1:
                drot ^= 1
        for hh in range(2):
            hd = 2 * hp + hh
            P.op("dve", lambda e, hh=hh: e.reciprocal(out=dacc[hh], in_=dacc[hh]), reads=[daccB[hh]], writes=[daccB[hh]])
            P.op("dve", lambda e, hh=hh: e.tensor_tensor(out=ost[hh], in0=oacc[hh], in1=dacc[hh], op=ALU.mult),
                 reads=[oaccB[hh], daccB[hh]], writes=[ostB[hh]])
            P.op("sp", lambda e, hh=hh, hd=hd: e.dma_start(out=oT[128 * hd:128 * (hd + 1), :], in_=ost[hh]),
                 reads=[ostB[hh]], writes=[oTdB], accum=True, dsem=f"BO{hh}")

    TB, NH = 1024, 2
    cx.nmain = 4
    cx.rot = 0
    off[0] = 0
    xf = carve(2 * NCH * TB).bitcast(F32).rearrange("p (c t) -> p c t", c=NCH)
    xb = carve(NCH * TB).rearrange("p (c t) -> p c t", c=NCH)
    hq = carve(16 * TB).rearrange("p (c t) -> p c t", c=16)
    sg = [carve(2 * T).bitcast(F32) for _ in range(2)]

    def ln_alloc(name, n, dt):
        return carve(n) if dt == BF16 else carve(2 * n).bitcast(F32)

    lns = [LN(cx, alloc=ln_alloc, banks=(4, 5)), LN(cx, alloc=ln_alloc, banks=(6, 7))]
    xfB = [[Buf(f"xf{c}_{hf}") for hf in range(NH)] for c in range(NCH)]
    xbB = [[Buf(f"xb{c}_{hf}") for hf in range(NH)] for c in range(NCH)]
    hqB = [[Buf(f"hq{c}_{hf}") for hf in range(NH)] for c in range(16)]
    sgB = [Buf("sg0"), Buf("sg1")]
    xfB_flat = [b_ for l_ in xfB for b_ in l_]
    xbB_flat = [b_ for l_ in xbB for b_ in l_]
    vecs = cx.vecs

    def hs(hf):
        return slice(T * hf, T * (hf + 1))

    def post_ln2(gcol, bcol):
        for hf in range(NH):
            ln = lns[hf]
            ln.finalize()
            for c in range(NCH):
                src = xf[:, c, hs(hf)]
                ln.norm_chunk(src, xfB[c][hf])
                P.op("act", lambda e, c=c, src=src: e.activation(out=src, in_=src, func=AF.Identity,
                                                                 bias=vecs[:, bcol + c:bcol + c + 1], scale=vecs[:, gcol + c:gcol + c + 1]),
                     reads=[xfB[c][hf]], writes=[xfB[c][hf]])
                P.op("act", lambda e, c=c, hf=hf, src=src: e.activation(out=xb[:, c, hs(hf)], in_=src, func=AF.Identity),
                     reads=[xfB[c][hf]], writes=[xbB[c][hf]])

    w2v = w2.rearrange("(q c p) n -> q p c n", p=128, c=16)
    for jj in range(TOK // TB):
        for g in range(4):
            cx.wadd(("wo", jj, g), wsrc_k2048(wo, 512 * g), 16)
        for q_ in range(4):
            for g in range(4):
                cx.wadd(("w1", jj, q_, g), wsrc_k2048(w1, 2048 * q_ + 512 * g), 16)
            for g in range(4):
                cx.wadd(("w2", jj, q_, g), w2v[q_][:, :, 512 * g:512 * (g + 1)], 16)

    for jj in range(TOK // TB):
        tsl = slice(TB * jj, TB * (jj + 1))
        P.op("sp", lambda e, tsl=tsl: e.dma_start(out=xf, in_=x2T.rearrange("(c p) t -> p c t", p=128)[:, :, tsl]),
             reads=[x2TB], writes=(xfB_flat + b1_bufs if jj == 0 else xfB_flat), dsem="B2X")
        P.op("sp", lambda e, tsl=tsl: e.dma_start(out=xb, in_=oT.rearrange("(c p) t -> p c t", p=128)[:, :, tsl]),
             reads=[oTdB], writes=xbB_flat, dsem="B2O")
        for g in range(4):
            wv, wB = cx.wget(("wo", jj, g))
            for i in range(4):
                dc = 4 * g + i
                for hf in range(NH):
                    pt, pB = cx.bank()
                    for k in range(NCH):
                        P.op("pe", lambda e, k=k, i=i, hf=hf, wv=wv, pt=pt: e.matmul(pt[:], wv[:, k, i * 128:(i + 1) * 128], xb[:, k, hs(hf)],
                                                                                     start=(k == 0), stop=(k == NCH - 1)),
                             reads=[wB, xbB[k][hf]], writes=[pB], accum=(k > 0))
                    P.op("dve", lambda e, dc=dc, hf=hf, pt=pt: e.scalar_tensor_tensor(out=xf[:, dc, hs(hf)], in0=xf[:, dc, hs(hf)], scalar=ALPHA,
                                                                                     in1=pt[:], op0=ALU.mult, op1=ALU.add),
                         reads=[pB, xfB[dc][hf]], writes=[xfB[dc][hf]])
                    lns[hf].stats_chunk(dc, xf[:, dc, hs(hf)], xfB[dc][hf])
        post_ln2(V_MIXG + 16, V_MIXB + 16)
        rr = 0
        for q_ in range(4):
            for g in range(4):
                wv, wB = cx.wget(("w1", jj, q_, g))
                for i in range(4):
                    hc = 4 * g + i
                    for hf in range(NH):
                        pt, pB = cx.bank()
                        for k in range(NCH):
                            P.op("pe", lambda e, k=k, i=i, hf=hf, wv=wv, pt=pt: e.matmul(pt[:], wv[:, k, i * 128:(i + 1) * 128], xb[:, k, hs(hf)],
                                                                                         start=(k == 0), stop=(k == NCH - 1)),
                                 reads=[wB, xbB[k][hf]], writes=[pB], accum=(k > 0))
                        r, rb = sg[rr], sgB[rr]
                        rr ^= 1
                        P.op("act", lambda e, pt=pt, r=r: e.activation(out=r, in_=pt[:], func=AF.Relu), reads=[pB], writes=[rb])
                        P.op("dve", lambda e, hc=hc, hf=hf, r=r: e.tensor_tensor(out=hq[:, hc, hs(hf)], in0=r, in1=r, op=ALU.mult),
                             reads=[rb], writes=[hqB[hc][hf]])
            for g in range(4):
                wv, wB = cx.wget(("w2", jj, q_, g))
                for i in range(4):
                    dc = 4 * g + i
                    for hf in range(NH):
                        pt, pB = cx.bank()
                        for k in range(16):
                            P.op("pe", lambda e, k=k, i=i, hf=hf, wv=wv, pt=pt: e.matmul(pt[:], wv[:, k, i * 128:(i + 1) * 128], hq[:, k, hs(hf)],
                                                                                         start=(k == 0), stop=(k == 15)),
                                 reads=[wB, hqB[k][hf]], writes=[pB], accum=(k > 0))
                        if q_ == 0:
                            P.op("dve", lambda e, dc=dc, hf=hf, pt=pt: e.scalar_tensor_tensor(out=xf[:, dc, hs(hf)], in0=xf[:, dc, hs(hf)], scalar=ALPHA,
                                                                                             in1=pt[:], op0=ALU.mult, op1=ALU.add),
                                 reads=[pB, xfB[dc][hf]], writes=[xfB[dc][hf]])
                        else:
                            P.op("dve", lambda e, dc=dc, hf=hf, pt=pt: e.tensor_tensor(out=xf[:, dc, hs(hf)], in0=pt[:], in1=xf[:, dc, hs(hf)], op=ALU.add),
                                 reads=[pB, xfB[dc][hf]], writes=[xfB[dc][hf]])
                        if q_ == 3:
                            lns[hf].stats_chunk(dc, xf[:, dc, hs(hf)], xfB[dc][hf])
        post_ln2(V_MLPG + 16, V_MLPB + 16)
        P.op("sp", lambda e, tsl=tsl: e.dma_start(out=out.rearrange("(c p) t -> p c t", p=128)[:, :, tsl], in_=xf),
             reads=xfB_flat, dsem="OUT")


def _fm(v):
    return np.ascontiguousarray(v.reshape(-1, 128).T)


def _t5_bucket(dist):
    max_exact = 16
    large = max_exact + (np.log(np.maximum(dist, 1) / max_exact) / math.log(2048 / max_exact) * (32 - max_exact)).astype(np.int32)
    large = np.minimum(large, 31)
    return np.where(dist < max_exact, dist, large).astype(np.int32)


def _bias_tiles(rel_bias, has_prev):
    i = np.arange(128)[:, None]
    j = np.arange(256)[None, :]
    delta = i - j + 128
    ok = (delta >= 0) & (delta <= 128)
    res = np.full((NHEAD, 128, 9, 128), NEG, np.float32)
    for bi, d in enumerate(BR):
        bucket = _t5_bucket(np.clip(delta, 0, None) * d)
        b = rel_bias[bucket]
        b = np.where(ok[:, :, None], b, np.float32(NEG))
        bt = np.transpose(b, (2, 1, 0))
        res[:, :, bi * 3 + 0, :] = bt[:, 0:128, :]
        res[:, :, bi * 3 + 1, :] = bt[:, 128:256, :]
        if has_prev:
            res[:, :, bi * 3 + 2, :] = bt[:, 0:128, :]
    return np.ascontiguousarray(res.reshape(NHEAD, 128, 9 * 128))


_NC_CACHE = {}


def build_fused():
    nc = bass.Bass("TRN2", target_bir_lowering=False)
    I32 = mybir.dt.int32
    xc = nc.dram_tensor("xc", [D, XH + TOK], F32, kind="ExternalInput").ap()
    hmask_d = nc.dram_tensor("hmask", [128, 1], F32, kind="ExternalInput").ap()
    prow_d = nc.dram_tensor("prevrow", [1, 1], I32, kind="ExternalInput").ap()
    vecs_d = nc.dram_tensor("vecs", [128, NV], F32, kind="ExternalInput").ap()
    ident_d = nc.dram_tensor("ident", [128, 128], F32, kind="ExternalInput").ap()
    bias_d = nc.dram_tensor("biasT", [NHEAD, 128, 9 * 128], F32, kind="ExternalInput").ap()
    pw1 = nc.dram_tensor("pw1", [D, 2 * D], F32, kind="ExternalInput").ap()
    pw2 = nc.dram_tensor("pw2", [D, D], F32, kind="ExternalInput").ap()
    w1a = nc.dram_tensor("w1a", [D, DFF], F32, kind="ExternalInput").ap()
    w2a = nc.dram_tensor("w2a", [DFF, D], F32, kind="ExternalInput").ap()
    w1b = nc.dram_tensor("w1b", [D, DFF], F32, kind="ExternalInput").ap()
    w2b = nc.dram_tensor("w2b", [DFF, D], F32, kind="ExternalInput").ap()
    wkv = nc.dram_tensor("wkv", [D, 2 * D], F32, kind="ExternalInput").ap()
    wq = nc.dram_tensor("wq", [D, D], F32, kind="ExternalInput").ap()
    wo = nc.dram_tensor("wo", [D, D], F32, kind="ExternalInput").ap()
    out = nc.dram_tensor("out", [D, TOK], F32, kind="ExternalOutput").ap()
    x2T = nc.dram_tensor("x2T", [D, TOK], F32).ap()
    qT = nc.dram_tensor("qT", [D, TOK], BF16).ap()
    kTs_t = nc.dram_tensor("kTs", [D, TOK], BF16)
    vts_t = nc.dram_tensor("vts", [TOK, D], BF16)
    kps = [[nc.dram_tensor(f"kps{j}{p}", [D // 2, T], BF16) for p in range(2)] for j in range(NT)]
    kpa = [[nc.dram_tensor(f"kpa{j}{p}", [4 * (D // 2), T], BF16) for p in range(2)] for j in range(NT)]
    vps = [[nc.dram_tensor(f"vps{j}{p}", [T // 2, D], BF16) for p in range(2)] for j in range(NT)]
    vpa = [[nc.dram_tensor(f"vpa{j}{p}", [4 * (T // 2), D], BF16) for p in range(2)] for j in range(NT)]
    kTp = nc.dram_tensor("kTp", [D, TOK], BF16).ap()
    vtp = nc.dram_tensor("vtp", [TOK, D], BF16).ap()
    oT = nc.dram_tensor("oT", [D, TOK], BF16).ap()
    kTs, vts = kTs_t.ap(), vts_t.ap()

    with contextlib.ExitStack() as st:
        cx = Ctx(nc, st, nslots=3)
        P = cx.P
        st.enter_context(nc.allow_low_precision("bf16 matmul operands, fp32 accumulation"))
        load_consts(cx, vecs_d, ident_d)
        identb = cx.sb("identb", [128, 128], BF16)
        identbB = Buf("identb")
        P.op("act", lambda e: e.activation(out=identb[:], in_=cx.ident[:], func=AF.Identity), reads=[cx.identB], writes=[identbB])
        cx.identb = identb
        P.op("dve", lambda e: e.tensor_copy(out=cx.dmy[:, 3:4], in_=identb[:, 0:1]), reads=[identbB], writes=[Buf("dmy_d2")])
        big = cx.sb("big", [128, BIGN], BF16)
        off = [0]

        def carve(n):
            n = (n + 15) // 16 * 16
            o = off[0]
            off[0] += n
            assert off[0] <= BIGN, off[0]
            return big[:, o:o + n]

        def alloc(name, n, dt):
            return carve(n) if dt == BF16 else carve(2 * n).bitcast(F32)[:, 0:n]

        x2TB, qTB, kTsB, vtsB, kTallB, vallB, kTpB, vtpB = (Buf(n_) for n_ in ("x2T", "qT", "kTs", "vts", "kTall", "vall", "kTp", "vtp"))
        groups = [[0, 1, 2, 3], [4, 5, 6, 7]]
        kpieces = [[(kps[j][p].ap(), Buf(f"kps{j}{p}")) for p in range(2)] for j in range(NT)]
        vpieces = [[(vps[j][p].ap(), Buf(f"vps{j}{p}")) for p in range(2)] for j in range(NT)]

        def after_tile(j):
            for p in range(2):
                cx.cc_pending.append((lambda e, j=j, p=p: e.collective_compute("AllGather", ALU.bypass, replica_groups=groups,
                                                                              ins=[kps[j][p].ap().opt()], outs=[kpa[j][p].ap().opt()]),
                                      [kpieces[j][p][1]], [kTallB]))
                cx.cc_pending.append((lambda e, j=j, p=p: e.collective_compute("AllGather", ALU.bypass, replica_groups=groups,
                                                                              ins=[vps[j][p].ap().opt()], outs=[vpa[j][p].ap().opt()]),
                                      [vpieces[j][p][1]], [vallB]))
            if j == NT - 1:
                cx.cc_flush()

        a_bufs = layer0_and_qkv(cx, alloc, xc, hmask_d, pw1, pw2, w1a, w2a, wkv, wq, x2T, qT, kTs, vts, x2TB, qTB, kTsB, vtsB,
                                kpieces, vpieces, after_tile)

        reg = st.enter_context(nc.sync.register("prevrow"))

        pval = []

        def copy_prev(dst, src_t, rows):
            def fn(e):
                if not pval:
                    e.reg_load(reg, prow_d[0:1, 0:1])
                    pval.append(e.snap(reg, min_val=0, max_val=3))
                return e.dma_start(out=dst, in_=src_t.ap()[bass.ts(pval[0], rows), :])
            return fn

        for j in range(NT):
            for p in range(2):
                P.op("sp", copy_prev(kTp[(D // 2) * p:(D // 2) * (p + 1), T * j:T * (j + 1)], kpa[j][p], D // 2),
                     reads=[kTallB], writes=[kTpB], accum=True, dsem="XK")
                P.op("sp", copy_prev(vtp[T * j + (T // 2) * p:T * j + (T // 2) * (p + 1), :], vpa[j][p], T // 2),
                     reads=[vallB], writes=[vtpB], accum=True, dsem="XV")

        attention_and_layer1(cx, carve, off, a_bufs, x2T, qT, kTs, kTp, vts, vtp, bias_d, wo, w1b, w2b, oT, out, identb, identbB,
                             x2TB, qTB, kTsB, kTpB, vtsB, vtpB)
        assert cx.wnext_get == len(cx.witems)
        import os
        if os.environ.get("KDEBUG"):
            for nm, ap_, shp, dt_, rb in (("dbg_kTp", kTp, [D, TOK], BF16, kTpB), ("dbg_kTs", kTs, [D, TOK], BF16, kTsB),
                                          ("dbg_vtp", vtp, [TOK, D], BF16, vtpB), ("dbg_vts", vts, [TOK, D], BF16, vtsB),
                                          ("dbg_oT", oT, [D, TOK], BF16, None), ("dbg_x2T", x2T, [D, TOK], F32, x2TB)):
                dd = nc.dram_tensor(nm, shp, dt_, kind="ExternalOutput").ap()
                P.op("sp", lambda e, dd=dd, ap_=ap_: e.dma_start(out=dd, in_=ap_), reads=([rb] if rb is not None else []), dsem="DBG")
        P.emit()
    return nc


def kernel(x, conv_pw1_w, conv_pw1_b, conv_dw_w, conv_dw_b, conv_ln_g, conv_ln_b, conv_pw2_w, conv_pw2_b,
           w_kv, attn_wq, attn_wo, rel_bias, mlp_w1, mlp_w2, ln_mix_g, ln_mix_b, ln_mlp_g, ln_mlp_b):
    f32 = np.float32
    x = np.asarray(x, f32)
    ncore = 8
    vecs = np.zeros((128, NV), f32)
    vecs[:, V_PW1B:V_PW1B + 32] = _fm(np.asarray(conv_pw1_b, f32)[0])
    dw = np.asarray(conv_dw_w, f32)[0]
    for tap in range(CONVW):
        vecs[:, V_DWW + 16 * tap:V_DWW + 16 * tap + 16] = _fm(dw[tap])
    vecs[:, V_DWB:V_DWB + 16] = _fm(np.asarray(conv_dw_b, f32)[0])
    vecs[:, V_CLNG:V_CLNG + 16] = _fm(np.asarray(conv_ln_g, f32)[0])
    vecs[:, V_CLNB:V_CLNB + 16] = _fm(np.asarray(conv_ln_b, f32)[0])
    vecs[:, V_PW2B:V_PW2B + 16] = _fm(np.asarray(conv_pw2_b, f32)[0])
    for l in range(2):
        vecs[:, V_MIXG + 16 * l:V_MIXG + 16 * l + 16] = _fm(np.asarray(ln_mix_g, f32)[l])
        vecs[:, V_MIXB + 16 * l:V_MIXB + 16 * l + 16] = _fm(np.asarray(ln_mix_b, f32)[l])
        vecs[:, V_MLPG + 16 * l:V_MLPG + 16 * l + 16] = _fm(np.asarray(ln_mlp_g, f32)[l])
        vecs[:, V_MLPB + 16 * l:V_MLPB + 16 * l + 16] = _fm(np.asarray(ln_mlp_b, f32)[l])
    ident = np.eye(128, dtype=f32)
    w1 = np.asarray(mlp_w1, f32)
    w2 = np.asarray(mlp_w2, f32)
    rel_bias = np.asarray(rel_bias, f32)
    shared = {
        "vecs": vecs, "ident": ident,
        "pw1": np.ascontiguousarray(np.asarray(conv_pw1_w, f32)[0]),
        "pw2": np.ascontiguousarray(np.asarray(conv_pw2_w, f32)[0]),
        "w1a": np.ascontiguousarray(w1[0]), "w2a": np.ascontiguousarray(w2[0]),
        "w1b": np.ascontiguousarray(w1[1]), "w2b": np.ascontiguousarray(w2[1]),
        "wkv": np.ascontiguousarray(np.asarray(w_kv, f32)),
        "wq": np.ascontiguousarray(np.asarray(attn_wq, f32)[0]),
        "wo": np.ascontiguousarray(np.asarray(attn_wo, f32)[0]),
    }
    bias_tiles = {True: _bias_tiles(rel_bias, True), False: _bias_tiles(rel_bias, False)}
    if "f" not in _NC_CACHE:
        _NC_CACHE["f"] = build_fused()
    ncf = _NC_CACHE["f"]
    in_maps = []
    for c in range(ncore):
        b, q = divmod(c, 4)
        xc = np.zeros((D, XH + TOK), f32)
        xc[:, XH:] = x[b, q * TOK:(q + 1) * TOK].T
        if q > 0:
            xc[:, :XH] = x[b, q * TOK - XH:q * TOK].T
        m = dict(shared)
        m.update({"xc": xc, "hmask": np.full((128, 1), 1.0 if q > 0 else 0.0, f32),
                  "prevrow": np.array([[max(q - 1, 0)]], np.int32), "biasT": bias_tiles[q > 0]})
        in_maps.append(m)
    res = run_bass_kernel_spmd(ncf, in_maps, core_ids=list(range(ncore))).results
    _NC_CACHE["last"] = res
    outp = np.zeros((2, 8192, D), f32)
    for c in range(ncore):
        b, q = divmod(c, 4)
        outp[b, q * TOK:(q + 1) * TOK] = np.asarray(res[c]["out"]).T
    return outp
```

```python
import contextlib
import math
import numpy as np
import ml_dtypes
import concourse.bass as bass
import concourse.mybir as mybir
from concourse.bass_utils import run_bass_kernel_spmd

F32 = mybir.dt.float32
BF16 = mybir.dt.bfloat16
ALU = mybir.AluOpType
AF = mybir.ActivationFunctionType

D = 2048
NCH = 16
DFF = 8192
T = 512
NT = 4
TOK = 2048
HALO = 32
XH = 128
NHEAD = 16
ALPHA = float(4 ** 0.25)
EPS = 1e-5
QSCALE = float(128 ** -0.5)
NEG = -30000.0
CONVW = 31

V_PW1B = 0
V_DWW = V_PW1B + 32
V_DWB = V_DWW + CONVW * 16
V_CLNG = V_DWB + 16
V_CLNB = V_CLNG + 16
V_PW2B = V_CLNB + 16
V_MIXG = V_PW2B + 16
V_MIXB = V_MIXG + 32
V_MLPG = V_MIXB + 32
V_MLPB = V_MLPG + 32
NV = V_MLPB + 32


class Buf:
    __slots__ = ("name", "writers", "readers")

    def __init__(self, name):
        self.name = name
        self.writers = []
        self.readers = []


class Op:
    __slots__ = ("eng", "fn", "deps", "dsem", "sig", "val", "waits", "inc")

    def __init__(self, eng, fn, dsem, inc=None):
        self.eng = eng
        self.fn = fn
        self.deps = []
        self.dsem = dsem
        self.inc = inc if inc is not None else (16 if dsem is not None else 1)
        self.sig = False
        self.val = 0
        self.waits = None


class Prog:
    ENGS = ("pe", "act", "dve", "pool", "sp")

    def __init__(self, nc):
        self.nc = nc
        self.ops = {e: [] for e in self.ENGS}

    def op(self, eng, fn, reads=(), writes=(), accum=False, dsem=None, inc=None):
        o = Op(eng, fn, dsem, inc)
        deps = o.deps
        for b in reads:
            deps.extend(b.writers)
        for b in writes:
            if not accum:
                deps.extend(b.writers)
            deps.extend(b.readers)
        for b in writes:
            if accum:
                b.writers.append(o)
            else:
                b.writers = [o]
            b.readers = []
        for b in reads:
            b.readers.append(o)
        self.ops[eng].append(o)
        return o

    def emit(self, final_wait_eng="sp"):
        nc = self.nc
        for e in self.ENGS:
            for o in self.ops[e]:
                if o.dsem is not None:
                    o.sig = True
                nd = []
                for d in o.deps:
                    if d.eng == "pe" and e == "pe" and d.dsem is None:
                        continue
                    d.sig = True
                    nd.append(d)
                o.deps = nd
        counts = {}
        sem_names = []
        for e in self.ENGS:
            for o in self.ops[e]:
                if not o.sig:
                    continue
                key = o.dsem if o.dsem is not None else "E_" + e
                counts[key] = counts.get(key, 0) + o.inc
                o.val = counts[key]
                if key not in sem_names:
                    sem_names.append(key)
        with contextlib.ExitStack() as st:
            sems = {k: st.enter_context(nc.semaphore(k)) for k in sem_names}
            for e in self.ENGS:
                seen = {}
                for o in self.ops[e]:
                    need = {}
                    for d in o.deps:
                        key = d.dsem if d.dsem is not None else "E_" + d.eng
                        if d.val > need.get(key, 0):
                            need[key] = d.val
                    w = []
                    for k, v in need.items():
                        if seen.get(k, 0) < v:
                            seen[k] = v
                            w.append((k, v))
                    o.waits = w
            block = st.enter_context(nc.Block())
            ops = self.ops
            final = [(k, v) for k, v in counts.items() if not k.startswith("E_")]

            def run(eh, ename):
                for o in ops[ename]:
                    for k, v in o.waits:
                        eh.wait_ge(sems[k], v)
                    ins = o.fn(eh)
                    if o.sig:
                        key = o.dsem if o.dsem is not None else "E_" + ename
                        ins.then_inc(sems[key], o.inc)
                if ename == final_wait_eng:
                    for k, v in final:
                        eh.wait_ge(sems[k], v)

            @block.tensor
            def _(eh):
                run(eh, "pe")

            @block.scalar
            def _(eh):
                run(eh, "act")

            @block.vector
            def _(eh):
                run(eh, "dve")

            @block.gpsimd
            def _(eh):
                run(eh, "pool")

            @block.sync
            def _(eh):
                run(eh, "sp")


class Ctx:
    def __init__(self, nc, st, nslots, keep_prev=False):
        self.keep_prev = keep_prev
        self.nc = nc
        self.st = st
        self.P = Prog(nc)
        self.ps = [st.enter_context(nc.psum_tensor(f"ps{i}", [128, 512], F32)) for i in range(8)]
        self.pb = [Buf(f"ps{i}") for i in range(8)]
        self.rot = 0
        self.nmain = 6
        self.nslots = nslots
        self.wslots = [st.enter_context(nc.sbuf_tensor(f"wsl{i}", [128, 8192], BF16)) for i in range(nslots)]
        self.wbufs = [Buf(f"wsl{i}") for i in range(nslots)]
        self.witems = []
        self.wnext_load = 0
        self.wnext_get = 0
        self.uid = 0
        self.wcache = None
        self.wcache_idx = {}
        self.wcacheB = {}
        self.cc_pending = []
        self.cc_since = 0
        self.cc_gap = 14
        self.ccB = Buf("cc_order")
        self.cc_n = 0

    def sb(self, name, shape, dt):
        return self.st.enter_context(self.nc.sbuf_tensor("sb_" + name, shape, dt))

    def bank(self):
        i = self.rot
        self.rot = (self.rot + 1) % self.nmain
        return self.ps[i], self.pb[i]

    def wadd(self, key, src, c, cache=None, store=True):
        self.witems.append((key, src, c, cache, store))

    def _wload(self, i):
        key, src, c, cache, store = self.witems[i]
        s = i % self.nslots
        if cache is not None and cache in self.wcacheB:
            ci = self.wcache_idx[cache]
            self.P.op("sp", lambda e: e.dma_start(out=self.wslots[s][:, :], in_=self.wcache[ci]),
                      reads=[self.wcacheB[cache]], writes=[self.wbufs[s]], dsem=f"WH{s}")
            return
        self._wload_src(i, s, src, c)
        if cache is not None and store:
            ci = self.wcache_idx.setdefault(cache, len(self.wcache_idx))
            self.wcacheB[cache] = Buf(f"wcache{ci}")
            self.P.op("sp", lambda e: e.dma_start(out=self.wcache[ci], in_=self.wslots[s][:, :]),
                      reads=[self.wbufs[s]], writes=[self.wcacheB[cache]], dsem=f"WC{s}")

    def _wload_src(self, i, s, src, c):
        dst = self.wslots[s][:, :].rearrange("p (c n) -> p c n", c=c)
        if not isinstance(src, list):
            self.P.op("pool", lambda e: e.dma_start(out=dst, in_=src), writes=[self.wbufs[s]], dsem=f"W{s}")
            return
        for pi, (lo, hi, sv) in enumerate(src):
            self.P.op("pool", lambda e, lo=lo, hi=hi, sv=sv: e.dma_start(out=dst[:, :, lo:hi], in_=sv),
                      writes=[self.wbufs[s]], accum=(pi > 0), dsem=f"W{s}")

    def cc_issue(self):
        fn, reads, writes = self.cc_pending.pop(0)
        self.cc_n += 1
        self.P.op("pool", fn, reads=list(reads) + [self.ccB], writes=list(writes) + [self.ccB], accum=True, dsem=f"CC{self.cc_n % 4}", inc=1)
        self.cc_since = 0

    def cc_flush(self):
        while self.cc_pending:
            self.cc_issue()

    def wget(self, key):
        self.cc_since += 1
        if self.cc_pending and self.cc_since >= self.cc_gap:
            self.cc_issue()
        i = self.wnext_get
        self.wnext_get += 1
        assert self.witems[i][0] == key, (self.witems[i][0], key)
        lim = min(i + self.nslots - (1 if self.keep_prev else 0), len(self.witems))
        while self.wnext_load < lim:
            self._wload(self.wnext_load)
            self.wnext_load += 1
        s = i % self.nslots
        c = self.witems[i][2]
        return self.wslots[s][:, :].rearrange("p (c n) -> p c n", c=c), self.wbufs[s]


def wsrc_k2048(w, col0):
    return w.rearrange("(c p) n -> p c n", p=128)[:, :, col0:col0 + 512]


def wsrc_k8192(w, col0):
    return w.rearrange("(c p) n -> p c n", p=128)[:, :, col0:col0 + 128]


class LN:
    def __init__(self, cx, alloc=None, banks=(6, 7)):
        self.cx = cx
        self.b1, self.b2 = banks
        if alloc is None:
            alloc = lambda name, n, dt: cx.sb(name, [128, n], dt)[:, :]
        self.zb = [alloc(f"ln_zb{i}", T, BF16) for i in range(2)]
        self.z2b = [alloc(f"ln_z2b{i}", T, BF16) for i in range(2)]
        self.zbB = [Buf(f"ln_zb{i}") for i in range(2)]
        self.z2bB = [Buf(f"ln_z2b{i}") for i in range(2)]
        self.mean = alloc("ln_mean", T, F32)
        self.var = alloc("ln_var", T, F32)
        self.rstd = alloc("ln_rstd", T, F32)
        self.meanB, self.varB, self.rstdB = Buf("ln_mean"), Buf("ln_var"), Buf("ln_rstd")
        self.r = 0

    def stats_chunk(self, c, src, srcB):
        cx, P = self.cx, self.cx.P
        i = self.r
        self.r ^= 1
        zb, z2b = self.zb[i], self.z2b[i]
        P.op("act", lambda e: e.activation(out=zb, in_=src, func=AF.Identity), reads=[srcB], writes=[self.zbB[i]])
        P.op("act", lambda e: e.activation(out=z2b, in_=src, func=AF.Square), reads=[srcB], writes=[self.z2bB[i]])
        ones = cx.ones
        P.op("pe", lambda e: e.matmul(cx.ps[self.b1][:], ones[:], zb, start=(c == 0), stop=(c == NCH - 1)),
             reads=[self.zbB[i], cx.onesB], writes=[cx.pb[self.b1]], accum=(c > 0))
        P.op("pe", lambda e: e.matmul(cx.ps[self.b2][:], ones[:], z2b, start=(c == 0), stop=(c == NCH - 1)),
             reads=[self.z2bB[i], cx.onesB], writes=[cx.pb[self.b2]], accum=(c > 0))

    def finalize(self):
        cx, P = self.cx, self.cx.P
        mean, var, rstd = self.mean, self.var, self.rstd
        P.op("dve", lambda e: e.tensor_scalar(out=mean, in0=cx.ps[self.b1][:], scalar1=1.0 / D, scalar2=None, op0=ALU.mult),
             reads=[cx.pb[self.b1]], writes=[self.meanB])
        P.op("dve", lambda e: e.tensor_tensor(out=var, in0=mean, in1=mean, op=ALU.mult),
             reads=[self.meanB], writes=[self.varB])
        P.op("dve", lambda e: e.scalar_tensor_tensor(out=var, in0=cx.ps[self.b2][:], scalar=1.0 / D, in1=var,
                                                     op0=ALU.mult, op1=ALU.subtract),
             reads=[cx.pb[self.b2], self.varB], writes=[self.varB])
        P.op("act", lambda e: e.activation(out=var, in_=var, func=AF.Sqrt, bias=cx.epsc[:, 0:1]),
             reads=[self.varB], writes=[self.varB])
        P.op("dve", lambda e: e.reciprocal(out=rstd, in_=var), reads=[self.varB], writes=[self.rstdB])
        P.op("dve", lambda e: e.scalar_tensor_tensor(out=mean, in0=mean, scalar=-1.0, in1=rstd,
                                                     op0=ALU.mult, op1=ALU.mult),
             reads=[self.meanB, self.rstdB], writes=[self.meanB])

    def norm_chunk(self, src, srcB):
        P = self.cx.P
        rstd, mean = self.rstd, self.mean
        P.op("dve", lambda e: e.tensor_tensor(out=src, in0=src, in1=rstd, op=ALU.mult),
             reads=[srcB, self.rstdB], writes=[srcB])
        P.op("dve", lambda e: e.tensor_tensor(out=src, in0=src, in1=mean, op=ALU.add),
             reads=[srcB, self.meanB], writes=[srcB])


def post_ln(cx, ln, xf, xfB, xb, xbB, gcol, bcol):
    P = cx.P
    vecs = cx.vecs
    ln.finalize()
    for c in range(NCH):
        src = xf[:, c, :]
        ln.norm_chunk(src, xfB[c])
        P.op("act", lambda e, c=c, src=src: e.activation(out=src, in_=src, func=AF.Identity,
                                                         bias=vecs[:, bcol + c:bcol + c + 1],
                                                         scale=vecs[:, gcol + c:gcol + c + 1]),
             reads=[xfB[c]], writes=[xfB[c]])
        P.op("act", lambda e, c=c, src=src: e.activation(out=xb[:, c, :], in_=src, func=AF.Identity),
             reads=[xfB[c]], writes=[xbB[c]])


def mlp_sublayer(cx, ln, layer, xf, xfB, xb, xbB, h, hB, rbuf, rB):
    P = cx.P
    rr = 0
    for g in range(16):
        wv, wB = cx.wget(("w1", layer, g))
        for i in range(4):
            hc = 4 * g + i
            pt, pB = cx.bank()
            for k in range(NCH):
                P.op("pe", lambda e, k=k, i=i, wv=wv, pt=pt: e.matmul(pt[:], wv[:, k, i * 128:(i + 1) * 128], xb[:, k, :],
                                                                      start=(k == 0), stop=(k == NCH - 1)),
                     reads=[wB, xbB[k]], writes=[pB], accum=(k > 0))
            r, rb = rbuf[rr], rB[rr]
            rr ^= 1
            P.op("act", lambda e, pt=pt, r=r: e.activation(out=r[:], in_=pt[:], func=AF.Relu), reads=[pB], writes=[rb])
            P.op("dve", lambda e, hc=hc, r=r: e.tensor_tensor(out=h[:, hc, :], in0=r[:], in1=r[:], op=ALU.mult),
                 reads=[rb], writes=[hB[hc]])
    for dc in range(NCH):
        wv, wB = cx.wget(("w2", layer, dc))
        pt, pB = cx.bank()
        for k in range(64):
            P.op("pe", lambda e, k=k, wv=wv, pt=pt: e.matmul(pt[:], wv[:, k, :], h[:, k, :], start=(k == 0), stop=(k == 63)),
                 reads=[wB, hB[k]], writes=[pB], accum=(k > 0))
        P.op("dve", lambda e, dc=dc, pt=pt: e.scalar_tensor_tensor(out=xf[:, dc, :], in0=xf[:, dc, :], scalar=ALPHA, in1=pt[:],
                                                                   op0=ALU.mult, op1=ALU.add),
             reads=[pB, xfB[dc]], writes=[xfB[dc]])
        ln.stats_chunk(dc, xf[:, dc, :], xfB[dc])
    post_ln(cx, ln, xf, xfB, xb, xbB, V_MLPG + 16 * layer, V_MLPB + 16 * layer)


def mlp_sublayer_q(cx, ln, lkey, xf, xfB, xb, xbB, hq, hqB, rbuf, rB, gcol, bcol):
    P = cx.P
    rr = 0
    for q_ in range(4):
        for g in range(4):
            wv, wB = cx.wget(("w1", lkey, q_, g))
            for i in range(4):
                hc = 4 * g + i
                pt, pB = cx.bank()
                for k in range(NCH):
                    P.op("pe", lambda e, k=k, i=i, wv=wv, pt=pt: e.matmul(pt[:], wv[:, k, i * 128:(i + 1) * 128], xb[:, k, :],
                                                                          start=(k == 0), stop=(k == NCH - 1)),
                         reads=[wB, xbB[k]], writes=[pB], accum=(k > 0))
                r, rb = rbuf[rr], rB[rr]
                rr ^= 1
                P.op("act", lambda e, pt=pt, r=r: e.activation(out=r, in_=pt[:], func=AF.Relu), reads=[pB], writes=[rb])
                P.op("dve", lambda e, hc=hc, r=r: e.tensor_tensor(out=hq[:, hc, :], in0=r, in1=r, op=ALU.mult),
                     reads=[rb], writes=[hqB[hc]])
        for g in range(4):
            wv, wB = cx.wget(("w2", lkey, q_, g))
            for i in range(4):
                dc = 4 * g + i
                pt, pB = cx.bank()
                for k in range(16):
                    P.op("pe", lambda e, k=k, i=i, wv=wv, pt=pt: e.matmul(pt[:], wv[:, k, i * 128:(i + 1) * 128], hq[:, k, :],
                                                                          start=(k == 0), stop=(k == 15)),
                         reads=[wB, hqB[k]], writes=[pB], accum=(k > 0))
                if q_ == 0:
                    P.op("dve", lambda e, dc=dc, pt=pt: e.scalar_tensor_tensor(out=xf[:, dc, :], in0=xf[:, dc, :], scalar=ALPHA, in1=pt[:],
                                                                               op0=ALU.mult, op1=ALU.add),
                         reads=[pB, xfB[dc]], writes=[xfB[dc]])
                else:
                    P.op("dve", lambda e, dc=dc, pt=pt: e.tensor_tensor(out=xf[:, dc, :], in0=pt[:], in1=xf[:, dc, :], op=ALU.add),
                         reads=[pB, xfB[dc]], writes=[xfB[dc]])
                if q_ == 3:
                    ln.stats_chunk(dc, xf[:, dc, :], xfB[dc])
    post_ln(cx, ln, xf, xfB, xb, xbB, gcol, bcol)


def proj_residual_sublayer(cx, ln, wkey, inb, inbB, xf, xfB, xb, xbB, gcol, bcol, bias_col, tb=None, tbB=None):
    P = cx.P
    vecs = cx.vecs
    tr = 0
    for g in range(4):
        wv, wB = cx.wget((wkey, g))
        for i in range(4):
            dc = 4 * g + i
            pt, pB = cx.bank()
            for k in range(NCH):
                P.op("pe", lambda e, k=k, i=i, wv=wv, pt=pt: e.matmul(pt[:], wv[:, k, i * 128:(i + 1) * 128], inb[:, k, :],
                                                                      start=(k == 0), stop=(k == NCH - 1)),
                     reads=[wB, inbB[k]], writes=[pB], accum=(k > 0))
            if bias_col is not None:
                t_, tB_ = tb[tr], tbB[tr]
                tr ^= 1
                P.op("act", lambda e, dc=dc, pt=pt, t_=t_: e.activation(out=t_[:], in_=pt[:], func=AF.Identity,
                                                                        bias=vecs[:, bias_col + dc:bias_col + dc + 1]),
                     reads=[pB], writes=[tB_])
                P.op("dve", lambda e, dc=dc, t_=t_: e.scalar_tensor_tensor(out=xf[:, dc, :], in0=xf[:, dc, :], scalar=ALPHA,
                                                                          in1=t_[:], op0=ALU.mult, op1=ALU.add),
                     reads=[tB_, xfB[dc]], writes=[xfB[dc]])
            else:
                P.op("dve", lambda e, dc=dc, pt=pt: e.scalar_tensor_tensor(out=xf[:, dc, :], in0=xf[:, dc, :], scalar=ALPHA,
                                                                          in1=pt[:], op0=ALU.mult, op1=ALU.add),
                     reads=[pB, xfB[dc]], writes=[xfB[dc]])
            ln.stats_chunk(dc, xf[:, dc, :], xfB[dc])
    post_ln(cx, ln, xf, xfB, xb, xbB, gcol, bcol)


def load_consts(cx, vecs_d, ident_d):
    P = cx.P
    cx.vecs = cx.sb("vecs", [128, NV], F32)
    cx.ident = cx.sb("ident", [128, 128], F32)
    cx.ones = cx.sb("ones", [128, 128], BF16)
    cx.epsc = cx.sb("epsc", [128, 1], F32)
    cx.vecsB, cx.identB, cx.onesB = Buf("vecs"), Buf("ident"), Buf("ones")
    P.op("sp", lambda e: e.dma_start(out=cx.vecs[:], in_=vecs_d), writes=[cx.vecsB], dsem="CV")
    P.op("sp", lambda e: e.dma_start(out=cx.ident[:], in_=ident_d), writes=[cx.identB], dsem="CI")
    P.op("dve", lambda e: e.memset(cx.ones[:], 1.0), writes=[cx.onesB])
    P.op("dve", lambda e: e.memset(cx.epsc[:], EPS), writes=[cx.onesB], accum=True)
    cx.dmy = cx.sb("dmy", [128, 4], F32)
    consts = [cx.vecsB, cx.identB, cx.onesB]
    P.op("act", lambda e: e.activation(out=cx.dmy[:, 0:1], in_=cx.vecs[:, 0:1], func=AF.Identity), reads=consts, writes=[Buf("dmy_a")])
    P.op("dve", lambda e: e.tensor_copy(out=cx.dmy[:, 1:2], in_=cx.vecs[:, 0:1]), reads=consts, writes=[Buf("dmy_d")])
    P.op("pool", lambda e: e.tensor_copy(out=cx.dmy[:, 2:3], in_=cx.vecs[:, 0:1]), reads=consts, writes=[Buf("dmy_p")])


def layer0_and_qkv(cx, alloc, xc, hmask_d, pw1, pw2, w1, w2, wkv, wq, x2T, qT, kT, vtm, x2TB, qTB, kTB, vtB,
                   kpieces, vpieces, after_tile):
    P = cx.P
    vecs = cx.vecs
    bufs = []

    def NB(name):
        b_ = Buf(name)
        bufs.append(b_)
        return b_

    hmask = alloc("hmask", 2, F32)
    hmB = NB("hmask")
    P.op("sp", lambda e: e.dma_start(out=hmask[:, 0:1], in_=hmask_d), writes=[hmB], dsem="CH")
    xf = alloc("xf", NCH * T, F32).rearrange("p (c t) -> p c t", c=NCH)
    xb = alloc("xb", NCH * T, BF16).rearrange("p (c t) -> p c t", c=NCH)
    xfB = [NB(f"xf{c}") for c in range(NCH)]
    xbB = [NB(f"xb{c}") for c in range(NCH)]
    scratch = alloc("scratch", 64 * T, BF16)
    hB = [NB(f"h{c}") for c in range(64)]
    h = scratch.rearrange("p (c t) -> p c t", c=64)
    cv = scratch[:, 0:32 * T].bitcast(F32).rearrange("p (c t) -> p c t", c=NCH)
    u = scratch[:, 32 * T:48 * T].rearrange("p (c t) -> p c t", c=NCH)
    kst = scratch[:, 0:16 * T].rearrange("p (c t) -> p c t", c=NCH)
    qst = scratch[:, 16 * T:32 * T].rearrange("p (c t) -> p c t", c=NCH)
    vst = scratch[:, 32 * T:48 * T].rearrange("p (b f) -> p b f", b=4)

    def cvB(c):
        return [hB[2 * c], hB[2 * c + 1]]

    xh = alloc("xh", NCH * HALO, BF16).rearrange("p (c t) -> p c t", c=NCH)
    xhB = NB("xh")
    xhf = alloc("xhf", NCH * HALO, F32).rearrange("p (c t) -> p c t", c=NCH)
    xhfB = NB("xhf")
    ghalo = alloc("ghalo", NCH * HALO, F32).rearrange("p (c t) -> p c t", c=NCH)
    ghB = [NB(f"gh{c}") for c in range(NCH)]
    gbuf = [alloc(f"gbuf{i}", T + 32, BF16) for i in range(2)]
    gbB = [NB(f"gbuf{i}") for i in range(2)]
    NDG = 16
    dg = [alloc(f"dg{i}", 128, BF16) for i in range(NDG)]
    dgB = [NB(f"dg{i}") for i in range(NDG)]
    dgr = 0
    sg = [alloc(f"sg{i}", T, F32) for i in range(2)]
    sgB = [NB(f"sg{i}") for i in range(2)]
    sgh = alloc("sgh", HALO, F32)
    sghB = NB("sgh")
    ln = LN(cx, alloc=alloc)
    bufs += ln.zbB + ln.z2bB + [ln.meanB, ln.varB, ln.rstdB]

    pw1v = pw1.rearrange("(c p) n -> p c n", p=128)
    w2v = w2.rearrange("(q c p) n -> q p c n", p=128, c=16)
    for j in range(NT):
        cnt = [0]

        def st_():
            cnt[0] += 1
            return (cnt[0] % 2) == (j % 2) or j >= 2

        for g in range(8):
            cx.wadd(("pw1", j, g), [(0, 256, pw1v[:, :, 256 * g:256 * g + 256]),
                                    (256, 512, pw1v[:, :, D + 256 * g:D + 256 * g + 256])], 16, cache=("pw1", g), store=st_())
        for g in range(4):
            cx.wadd(("pw2", g), wsrc_k2048(pw2, 512 * g), 16, cache=("pw2", g), store=st_())
        for q_ in range(4):
            for g in range(4):
                cx.wadd(("w1", ("A", j), q_, g), wsrc_k2048(w1, 2048 * q_ + 512 * g), 16, cache=("w1", q_, g), store=st_())
            for g in range(4):
                cx.wadd(("w2", ("A", j), q_, g), w2v[q_][:, :, 512 * g:512 * (g + 1)], 16, cache=("w2", q_, g), store=st_())
        for g in range(4):
            cx.wadd(("wk", g), wsrc_k2048(wkv, 512 * g), 16, cache=("wk", g), store=st_())
        for g in range(4):
            cx.wadd(("wq", g), wsrc_k2048(wq, 512 * g), 16, cache=("wq", g), store=st_())
        for g in range(4):
            cx.wadd(("wv", g), wsrc_k2048(wkv, D + 512 * g), 16, cache=("wv", g), store=st_())

    xcv = xc.rearrange("(c p) t -> p c t", p=128)
    P.op("sp", lambda e: e.dma_start(out=xhf, in_=xcv[:, :, XH - HALO:XH]), writes=[xhfB], dsem="X0")
    P.op("act", lambda e: e.activation(out=xh, in_=xhf, func=AF.Identity), reads=[xhfB], writes=[xhB])

    for j in range(NT):
        P.op("sp", lambda e, j=j: e.dma_start(out=xf, in_=xcv[:, :, XH + T * j:XH + T * (j + 1)]), writes=xfB, dsem="X1")
        for c in range(NCH):
            if c % 2 == 0:
                P.op("act", lambda e, c=c: e.activation(out=xb[:, c, :], in_=xf[:, c, :], func=AF.Identity), reads=[xfB[c]], writes=[xbB[c]])
            else:
                P.op("dve", lambda e, c=c: e.tensor_copy(out=xb[:, c, :], in_=xf[:, c, :]), reads=[xfB[c]], writes=[xbB[c]])
        gr = 0
        for g in range(8):
            wv, wB = cx.wget(("pw1", j, g))
            for i in range(2):
                c = 2 * g + i
                acol = slice(i * 128, (i + 1) * 128)
                gcol = slice(256 + i * 128, 256 + (i + 1) * 128)
                pa, paB = cx.bank()
                for k in range(NCH):
                    P.op("pe", lambda e, k=k, acol=acol, wv=wv, pa=pa: e.matmul(pa[:], wv[:, k, acol], xb[:, k, :],
                                                                              start=(k == 0), stop=(k == NCH - 1)),
                         reads=[wB, xbB[k]], writes=[paB], accum=(k > 0))
                pg, pgB = cx.bank()
                for k in range(NCH):
                    P.op("pe", lambda e, k=k, gcol=gcol, wv=wv, pg=pg: e.matmul(pg[:], wv[:, k, gcol], xb[:, k, :],
                                                                              start=(k == 0), stop=(k == NCH - 1)),
                         reads=[wB, xbB[k]], writes=[pgB], accum=(k > 0))
                gi = gr
                gr ^= 1
                gb_, gbB_ = gbuf[gi], gbB[gi]
                sg_, sgB_ = sg[gi], sgB[gi]
                ba = vecs[:, V_PW1B + c:V_PW1B + c + 1]
                bg = vecs[:, V_PW1B + 16 + c:V_PW1B + 16 + c + 1]
                if j == 0:
                    ph, phB = cx.bank()
                    for k in range(NCH):
                        P.op("pe", lambda e, k=k, acol=acol, wv=wv, ph=ph: e.matmul(ph[:, 0:HALO], wv[:, k, acol], xh[:, k, :],
                                                                                  start=(k == 0), stop=(k == NCH - 1)),
                             reads=[wB, xhB], writes=[phB], accum=(k > 0))
                    for k in range(NCH):
                        P.op("pe", lambda e, k=k, gcol=gcol, wv=wv, ph=ph: e.matmul(ph[:, HALO:2 * HALO], wv[:, k, gcol], xh[:, k, :],
                                                                                  start=(k == 0), stop=(k == NCH - 1)),
                             reads=[wB, xhB], writes=[phB], accum=True)
                    P.op("act", lambda e, ph=ph, bg=bg: e.activation(out=sgh, in_=ph[:, HALO:2 * HALO], func=AF.Sigmoid, bias=bg),
                         reads=[phB], writes=[sghB])
                    P.op("dve", lambda e, c=c, ph=ph, ba=ba: e.scalar_tensor_tensor(out=ghalo[:, c, :], in0=ph[:, 0:HALO], scalar=ba, in1=sgh,
                                                                                   op0=ALU.add, op1=ALU.mult),
                         reads=[phB, sghB], writes=[ghB[c]])
                    P.op("dve", lambda e, c=c, gb_=gb_: e.tensor_scalar(out=gb_[:, 0:30], in0=ghalo[:, c, 2:32], scalar1=hmask[:, 0:1], scalar2=None,
                                                                       op0=ALU.mult),
                         reads=[ghB[c], hmB], writes=[gbB_])
                else:
                    P.op("dve", lambda e, c=c, gb_=gb_: e.tensor_copy(out=gb_[:, 0:30], in_=ghalo[:, c, 2:32]),
                         reads=[ghB[c]], writes=[gbB_])
                P.op("act", lambda e, pg=pg, sg_=sg_, bg=bg: e.activation(out=sg_, in_=pg[:], func=AF.Sigmoid, bias=bg),
                     reads=[pgB], writes=[sgB_])
                P.op("dve", lambda e, pa=pa, sg_=sg_, gb_=gb_, ba=ba: e.scalar_tensor_tensor(out=gb_[:, 30:30 + T], in0=pa[:], scalar=ba, in1=sg_,
                                                                                           op0=ALU.add, op1=ALU.mult),
                     reads=[paB, sgB_], writes=[gbB_], accum=True)
                if j < NT - 1:
                    P.op("dve", lambda e, c=c, gb_=gb_: e.tensor_copy(out=ghalo[:, c, 2:32], in_=gb_[:, T:T + 30]),
                         reads=[gbB_], writes=[ghB[c]])
                cvc = cv[:, c, :]
                bdw = vecs[:, V_DWB + c:V_DWB + c + 1]
                pc, pcB = cx.bank()
                for tap in range(CONVW):
                    wj = vecs[:, V_DWW + 16 * tap + c:V_DWW + 16 * tap + c + 1]
                    dgi, dgiB = dg[dgr], dgB[dgr]
                    dgr = (dgr + 1) % NDG
                    if tap % 2 == 0:
                        P.op("dve", lambda e, dgi=dgi, wj=wj: e.tensor_scalar(out=dgi, in0=cx.identb[:], scalar1=wj, scalar2=None, op0=ALU.mult),
                             writes=[dgiB])
                    else:
                        P.op("act", lambda e, dgi=dgi, wj=wj: e.activation(out=dgi, in_=cx.identb[:], func=AF.Identity, scale=wj),
                             writes=[dgiB])
                    P.op("pe", lambda e, pc=pc, dgi=dgi, gb_=gb_, tap=tap: e.matmul(pc[:], dgi, gb_[:, tap:tap + T],
                                                                                  start=(tap == 0), stop=(tap == CONVW - 1)),
                         reads=[dgiB, gbB_], writes=[pcB], accum=(tap > 0))
                P.op("act", lambda e, cvc=cvc, pc=pc, bdw=bdw: e.activation(out=cvc, in_=pc[:], func=AF.Identity, bias=bdw),
                     reads=[pcB], writes=cvB(c))
                ln.stats_chunk(c, cvc, hB[2 * c])
        ln.finalize()
        for c in range(NCH):
            cvc = cv[:, c, :]
            P.op("dve", lambda e, cvc=cvc: e.tensor_tensor(out=cvc, in0=cvc, in1=ln.rstd, op=ALU.mult),
                 reads=cvB(c) + [ln.rstdB], writes=cvB(c))
            P.op("dve", lambda e, cvc=cvc: e.tensor_tensor(out=cvc, in0=cvc, in1=ln.mean, op=ALU.add),
                 reads=cvB(c) + [ln.meanB], writes=cvB(c))
            P.op("act", lambda e, c=c, cvc=cvc: e.activation(out=u[:, c, :], in_=cvc, func=AF.Silu,
                                                             bias=vecs[:, V_CLNB + c:V_CLNB + c + 1], scale=vecs[:, V_CLNG + c:V_CLNG + c + 1]),
                 reads=cvB(c), writes=[hB[32 + c]])
        uB = [hB[32 + c] for c in range(NCH)]
        proj_residual_sublayer(cx, ln, "pw2", u, uB, xf, xfB, xb, xbB, V_MIXG, V_MIXB, V_PW2B, tb=sg, tbB=sgB)
        mlp_sublayer_q(cx, ln, ("A", j), xf, xfB, xb, xbB, h[:, 0:16, :], hB[0:16], sg, sgB, V_MLPG, V_MLPB)
        P.op("sp", lambda e, j=j: e.dma_start(out=x2T.rearrange("(c p) t -> p c t", p=128)[:, :, T * j:T * (j + 1)], in_=xf),
             reads=xfB, writes=[x2TB], accum=True, dsem="OX")
        for (nm, stg, cell0, scale) in (("wk", kst, 0, 1.0), ("wq", qst, 16, QSCALE)):
            for g in range(4):
                wv_, wB_ = cx.wget((nm, g))
                for i in range(4):
                    hd = 4 * g + i
                    pt, pB = cx.bank()
                    for k in range(NCH):
                        P.op("pe", lambda e, k=k, i=i, wv_=wv_, pt=pt: e.matmul(pt[:], wv_[:, k, i * 128:(i + 1) * 128], xb[:, k, :],
                                                                                start=(k == 0), stop=(k == NCH - 1)),
                             reads=[wB_, xbB[k]], writes=[pB], accum=(k > 0))
                    P.op("act", lambda e, hd=hd, pt=pt, stg=stg, scale=scale: e.activation(out=stg[:, hd, :], in_=pt[:], func=AF.Identity, scale=scale),
                         reads=[pB], writes=[hB[cell0 + hd]])
            dst, dB = (kT, kTB) if nm == "wk" else (qT, qTB)
            P.op("sp", lambda e, j=j, dst=dst, stg=stg: e.dma_start(out=dst.rearrange("(c p) t -> p c t", p=128)[:, :, T * j:T * (j + 1)], in_=stg),
                 reads=[hB[cell0 + c] for c in range(NCH)], writes=[dB], accum=True, dsem="OK" if nm == "wk" else "OQ")
            if nm == "wk":
                for p_ in range(2):
                    pap, pB_ = kpieces[j][p_]
                    P.op("sp", lambda e, pap=pap, p_=p_, stg=stg: e.dma_start(out=pap.rearrange("(c p) t -> p c t", p=128), in_=stg[:, 8 * p_:8 * p_ + 8, :]),
                         reads=[hB[cell0 + c] for c in range(8 * p_, 8 * p_ + 8)], writes=[pB_], dsem=f"PK{p_}")
        for g in range(4):
            wv_, wB_ = cx.wget(("wv", g))
            for tb in range(4):
                pt, pB = cx.bank()
                for k in range(NCH):
                    P.op("pe", lambda e, k=k, tb=tb, wv_=wv_, pt=pt: e.matmul(pt[:], xb[:, k, tb * 128:(tb + 1) * 128], wv_[:, k, :],
                                                                              start=(k == 0), stop=(k == NCH - 1)),
                         reads=[wB_, xbB[k]], writes=[pB], accum=(k > 0))
                cells = [hB[32 + 4 * tb + q] for q in range(4)]
                if g % 2 == 0:
                    P.op("act", lambda e, g=g, tb=tb, pt=pt: e.activation(out=vst[:, tb, 512 * g:512 * (g + 1)], in_=pt[:], func=AF.Identity),
                         reads=[pB], writes=cells, accum=(g > 0))
                else:
                    P.op("dve", lambda e, g=g, tb=tb, pt=pt: e.tensor_copy(out=vst[:, tb, 512 * g:512 * (g + 1)], in_=pt[:]),
                         reads=[pB], writes=cells, accum=True)
        P.op("sp", lambda e, j=j: e.dma_start(out=vtm[T * j:T * (j + 1), :].rearrange("(b p) f -> p b f", p=128), in_=vst),
             reads=[hB[32 + q] for q in range(16)], writes=[vtB], accum=True, dsem="OV")
        for p_ in range(2):
            pap, pB_ = vpieces[j][p_]
            P.op("sp", lambda e, pap=pap, p_=p_: e.dma_start(out=pap.rearrange("(b p) f -> p b f", p=128), in_=vst[:, 2 * p_:2 * p_ + 2, :]),
                 reads=[hB[32 + q] for q in range(8 * p_, 8 * p_ + 8)], writes=[pB_], dsem=f"PV{p_}")
        after_tile(j)
    return bufs


BR = (1, 4, 16)


BIGN = 76 * 1024


def attention_and_layer1(cx, carve, off, a_bufs, x2T, qT, kTs, kTp, vts, vtp, bias_d, wo, w1, w2, oT, out, identb, identbB,
                         x2TB, qTB, kTsB, kTpB, vtsB, vtpB):
    P = cx.P
    off[0] = 0
    kwin = [carve(4096) for _ in range(2)]
    qn = [carve(2048) for _ in range(2)]
    kd = [[carve(4096) for _ in range(2)] for _ in range(2)]
    qd = [[carve(2048) for _ in range(2)] for _ in range(2)]
    vsl = [carve(32 * 256) for _ in range(2)]
    bia = [carve(9 * 128) for _ in range(2)]
    ptb = [carve(256) for _ in range(4)]
    oacc = [carve(4096).bitcast(F32) for _ in range(2)]
    dacc = [carve(4096).bitcast(F32) for _ in range(2)]
    ost = [carve(2048) for _ in range(2)]
    kwinB = [Buf("kwin0"), Buf("kwin1")]
    qnB = [Buf("qn0"), Buf("qn1")]
    kdB = [[Buf(f"kd{r}{h}") for h in range(2)] for r in range(2)]
    qdB = [[Buf(f"qd{r}{h}") for h in range(2)] for r in range(2)]
    vslB = [Buf(f"vsl{i}") for i in range(2)]
    biaB = [Buf("bia0"), Buf("bia1")]
    ptbB = [Buf(f"ptb{i}") for i in range(4)]
    oaccB = [Buf("oacc0"), Buf("oacc1")]
    daccB = [Buf("dacc0"), Buf("dacc1")]
    ostB = [Buf("ost0"), Buf("ost1")]
    oTdB = Buf("oT_dram")
    b1_bufs = kwinB + qnB + kdB[0] + kdB[1] + qdB[0] + qdB[1] + vslB + biaB + ptbB + oaccB + daccB + ostB
    P.op("dve", lambda e: e.memset(cx.dmy[:, 3:4], 0.0), reads=a_bufs, writes=a_bufs + b1_bufs)
    cx.nmain = 4
    cx.rot = 0
    accrot = 0
    vrot = 0
    drot = 0
    prot = 0
    LA = 2
    qTv = qT.rearrange("(h p) t -> h p t", p=128)
    kTsv = kTs.rearrange("(h p) t -> h p t", p=128)
    kTpv = kTp.rearrange("(h p) t -> h p t", p=128)
    for hp in range(8):
        for hh in range(2):
            hd = 2 * hp + hh
            P.op("sp", lambda e, hh=hh, hd=hd: e.dma_start(out=kwin[hh][:, 0:TOK], in_=kTpv[hd]), reads=[kTpB], writes=[kwinB[hh]], dsem=f"BK{hh}")
            P.op("sp", lambda e, hh=hh, hd=hd: e.dma_start(out=kwin[hh][:, TOK:2 * TOK], in_=kTsv[hd]), reads=[kTsB], writes=[kwinB[hh]],
                 accum=True, dsem=f"BK{hh}")
            P.op("sp", lambda e, hh=hh, hd=hd: e.dma_start(out=qn[hh], in_=qTv[hd]), reads=[qTB], writes=[qnB[hh]], dsem=f"BQ{hh}")
            P.op("pool", lambda e, hh=hh, hd=hd: e.dma_start(out=bia[hh], in_=bias_d[hd]), writes=[biaB[hh]], dsem=f"BB{hh}")
        for bi, d in enumerate(BR):
            Lw = 4096 // d
            Lq = 2048 // d
            nbr = 32 // d
            vs = vsl[vrot]
            vsB = vslB[vrot]
            vsem = f"BV{vrot}"
            vrot ^= 1
            vs3 = vs.rearrange("p (b f) -> p b f", f=256)
            cols = slice(256 * hp, 256 * hp + 256)
            lo = nbr // 2 - 1
            hb = nbr // 2
            for r in range(d):
                srcp = vtp.rearrange("(b i r) f -> r i b f", i=128, r=d)[r, :, hb - 1:hb, cols]
                P.op("sp", lambda e, vs3=vs3, srcp=srcp, r=r, lo=lo, nbr=nbr: e.dma_start(out=vs3[:, r * nbr + lo:r * nbr + lo + 1, :], in_=srcp),
                     reads=[vtpB], writes=[vsB], accum=(r > 0), dsem=vsem)
                srco = vts.rearrange("(b i r) f -> r i b f", i=128, r=d)[r, :, 0:hb, cols]
                P.op("sp", lambda e, vs3=vs3, srco=srco, r=r, hb=hb, nbr=nbr: e.dma_start(out=vs3[:, r * nbr + hb:r * nbr + nbr, :], in_=srco),
                     reads=[vtsB], writes=[vsB], accum=True, dsem=vsem)
            for hh in range(2):
                if d == 1:
                    kdv, kdvB = kwin[hh], kwinB[hh]
                    qdv, qdvB = qn[hh], qnB[hh]
                else:
                    kdv, kdvB = kd[drot][hh], kdB[drot][hh]
                    qdv, qdvB = qd[drot][hh], qdB[drot][hh]
                    P.op("pool", lambda e, hh=hh, kdv=kdv, d=d: e.tensor_copy(out=kdv.rearrange("p (r m) -> p r m", r=d),
                                                                            in_=kwin[hh].rearrange("p (m r) -> p r m", r=d)),
                         reads=[kwinB[hh]], writes=[kdvB])
                    P.op("pool", lambda e, hh=hh, qdv=qdv, d=d: e.tensor_copy(out=qdv.rearrange("p (r m) -> p r m", r=d),
                                                                            in_=qn[hh].rearrange("p (m r) -> p r m", r=d)),
                         reads=[qnB[hh]], writes=[qdvB])
                ov = oacc[hh] if d == 1 else oacc[hh].rearrange("p (m r) -> p r m", r=d)
                dv = dacc[hh] if d == 1 else dacc[hh].rearrange("p (m r) -> p r m", r=d)
                steps = [(g, nn) for g in range(4) for nn in range(4)]
                pts = {}
                accs = {}
                for s_ in range(len(steps) + LA):
                    if s_ < len(steps):
                        g, nn = steps[s_]
                        qpos = 512 * g + 128 * nn
                        r = qpos // Lq
                        n = (qpos % Lq) // 128
                        pst, pstB = cx.bank()
                        vblks = []
                        for part in range(2):
                            kpos = r * Lw + Lw // 2 + 128 * (n - 1 + part)
                            vblks.append(r * nbr + nbr // 2 + n - 1 + part)
                            bcol = (bi * 3 + (part if not (part == 0 and n == 0) else 2)) * 128
                            sq = pst[:, 128 * part:128 * (part + 1)]
                            P.op("pe", lambda e, sq=sq, kdv=kdv, qdv=qdv, kpos=kpos, qpos=qpos: e.matmul(sq, kdv[:, kpos:kpos + 128], qdv[:, qpos:qpos + 128],
                                                                                                       start=True, stop=False),
                                 reads=[kdvB, qdvB], writes=[pstB], accum=(part > 0))
                            P.op("pe", lambda e, sq=sq, hh=hh, bcol=bcol: e.matmul(sq, identb[:], bia[hh][:, bcol:bcol + 128], start=False, stop=True),
                                 reads=[identbB, biaB[hh]], writes=[pstB], accum=True)
                        pt_, ptB_ = ptb[prot], ptbB[prot]
                        prot = (prot + 1) % 4
                        P.op("act", lambda e, pst=pst, pt_=pt_: e.activation(out=pt_, in_=pst[:, 0:256], func=AF.Exp), reads=[pstB], writes=[ptB_])
                        pts[s_] = (pt_, ptB_, vblks)
                    t_ = s_ - LA
                    if t_ < 0:
                        continue
                    g, nn = steps[t_]
                    pt_, ptB_, vblks = pts.pop(t_)
                    if nn == 0:
                        accs[g] = (cx.ps[4 + 2 * accrot], cx.pb[4 + 2 * accrot], cx.ps[5 + 2 * accrot], cx.pb[5 + 2 * accrot])
                        accrot ^= 1
                    po, poB, pd, pdB = accs[g]
                    for part in range(2):
                        vblk = vblks[part]
                        ptp = pt_[:, 128 * part:128 * (part + 1)]
                        P.op("pe", lambda e, po=po, nn=nn, vs3=vs3, vblk=vblk, hh=hh, ptp=ptp, part=part: e.matmul(
                            po[:, 128 * nn:128 * (nn + 1)], vs3[:, vblk, 128 * hh:128 * (hh + 1)], ptp, start=(part == 0), stop=(part == 1)),
                             reads=[vsB, ptB_], writes=[poB], accum=not (nn == 0 and part == 0))
                        P.op("pe", lambda e, pd=pd, nn=nn, ptp=ptp, part=part: e.matmul(
                            pd[:, 128 * nn:128 * (nn + 1)], cx.ones[:], ptp, start=(part == 0), stop=(part == 1)),
                             reads=[cx.onesB, ptB_], writes=[pdB], accum=not (nn == 0 and part == 0))
                    if nn != 3:
                        continue
                    if d == 1:
                        oview = ov[:, 512 * g:512 * (g + 1)]
                        dview = dv[:, 512 * g:512 * (g + 1)]
                        pov, pdv = po[:, :], pd[:, :]
                    elif d == 4:
                        oview = ov[:, g, :]
                        dview = dv[:, g, :]
                        pov, pdv = po[:, :], pd[:, :]
                    else:
                        oview = ov[:, 4 * g:4 * g + 4, :]
                        dview = dv[:, 4 * g:4 * g + 4, :]
                        pov = po[:, :].rearrange("p (r m) -> p r m", r=4)
                        pdv = pd[:, :].rearrange("p (r m) -> p r m", r=4)
                    if bi == 0:
                        P.op("dve", lambda e, oview=oview, pov=pov: e.tensor_copy(out=oview, in_=pov), reads=[poB], writes=[oaccB[hh]], accum=(g > 0))
                        P.op("dve", lambda e, dview=dview, pdv=pdv: e.tensor_copy(out=dview, in_=pdv), reads=[pdB], writes=[daccB[hh]], accum=(g > 0))
                    else:
                        P.op("dve", lambda e, oview=oview, pov=pov: e.tensor_tensor(out=oview, in0=pov, in1=oview, op=ALU.add),
                             reads=[poB, oaccB[hh]], writes=[oaccB[hh]])
                        P.op("dve", lambda e, dview=dview, pdv=pdv: e.tensor_tensor(out=dview, in0=pdv, in1=dview, op=ALU.add),
                             reads=[pdB, daccB[hh]], writes=[daccB[hh]])
            if d != 1:
                drot ^= 1
        for hh in range(2):
            hd = 2 * hp + hh
            P.op("dve", lambda e, hh=hh: e.reciprocal(out=dacc[hh], in_=dacc[hh]), reads=[daccB[hh]], writes=[daccB[hh]])
            P.op("dve", lambda e, hh=hh: e.tensor_tensor(out=ost[hh], in0=oacc[hh], in1=dacc[hh], op=ALU.mult),
                 reads=[oaccB[hh], daccB[hh]], writes=[ostB[hh]])
            P.op("sp", lambda e, hh=hh, hd=hd: e.dma_start(out=oT[128 * hd:128 * (hd + 1), :], in_=ost[hh]),
                 reads=[ostB[hh]], writes=[oTdB], accum=True, dsem=f"BO{hh}")

    TB, NH = 1024, 2
    cx.nmain = 4
    cx.rot = 0
    off[0] = 0
    xf = carve(2 * NCH * TB).bitcast(F32).rearrange("p (c t) -> p c t", c=NCH)
    xb = carve(NCH * TB).rearrange("p (c t) -> p c t", c=NCH)
    hq = carve(16 * TB).rearrange("p (c t) -> p c t", c=16)
    sg = [carve(2 * T).bitcast(F32) for _ in range(2)]

    def ln_alloc(name, n, dt):
        return carve(n) if dt == BF16 else carve(2 * n).bitcast(F32)

    lns = [LN(cx, alloc=ln_alloc, banks=(4, 5)), LN(cx, alloc=ln_alloc, banks=(6, 7))]
    xfB = [[Buf(f"xf{c}_{hf}") for hf in range(NH)] for c in range(NCH)]
    xbB = [[Buf(f"xb{c}_{hf}") for hf in range(NH)] for c in range(NCH)]
    hqB = [[Buf(f"hq{c}_{hf}") for hf in range(NH)] for c in range(16)]
    sgB = [Buf("sg0"), Buf("sg1")]
    xfB_flat = [b_ for l_ in xfB for b_ in l_]
    xbB_flat = [b_ for l_ in xbB for b_ in l_]
    vecs = cx.vecs

    def hs(hf):
        return slice(T * hf, T * (hf + 1))

    def post_ln2(gcol, bcol):
        for hf in range(NH):
            ln = lns[hf]
            ln.finalize()
            for c in range(NCH):
                src = xf[:, c, hs(hf)]
                ln.norm_chunk(src, xfB[c][hf])
                P.op("act", lambda e, c=c, src=src: e.activation(out=src, in_=src, func=AF.Identity,
                                                                 bias=vecs[:, bcol + c:bcol + c + 1], scale=vecs[:, gcol + c:gcol + c + 1]),
                     reads=[xfB[c][hf]], writes=[xfB[c][hf]])
                P.op("act", lambda e, c=c, hf=hf, src=src: e.activation(out=xb[:, c, hs(hf)], in_=src, func=AF.Identity),
                     reads=[xfB[c][hf]], writes=[xbB[c][hf]])

    w2v = w2.rearrange("(q c p) n -> q p c n", p=128, c=16)
    for jj in range(TOK // TB):
        for g in range(4):
            cx.wadd(("wo", jj, g), wsrc_k2048(wo, 512 * g), 16)
        for q_ in range(4):
            for g in range(4):
                cx.wadd(("w1", jj, q_, g), wsrc_k2048(w1, 2048 * q_ + 512 * g), 16)
            for g in range(4):
                cx.wadd(("w2", jj, q_, g), w2v[q_][:, :, 512 * g:512 * (g + 1)], 16)

    for jj in range(TOK // TB):
        tsl = slice(TB * jj, TB * (jj + 1))
        P.op("sp", lambda e, tsl=tsl: e.dma_start(out=xf, in_=x2T.rearrange("(c p) t -> p c t", p=128)[:, :, tsl]),
             reads=[x2TB], writes=(xfB_flat + b1_bufs if jj == 0 else xfB_flat), dsem="B2X")
        P.op("sp", lambda e, tsl=tsl: e.dma_start(out=xb, in_=oT.rearrange("(c p) t -> p c t", p=128)[:, :, tsl]),
             reads=[oTdB], writes=xbB_flat, dsem="B2O")
        for g in range(4):
            wv, wB = cx.wget(("wo", jj, g))
            for i in range(4):
                dc = 4 * g + i
                for hf in range(NH):
                    pt, pB = cx.bank()
                    for k in range(NCH):
                        P.op("pe", lambda e, k=k, i=i, hf=hf, wv=wv, pt=pt: e.matmul(pt[:], wv[:, k, i * 128:(i + 1) * 128], xb[:, k, hs(hf)],
                                                                                     start=(k == 0), stop=(k == NCH - 1)),
                             reads=[wB, xbB[k][hf]], writes=[pB], accum=(k > 0))
                    P.op("dve", lambda e, dc=dc, hf=hf, pt=pt: e.scalar_tensor_tensor(out=xf[:, dc, hs(hf)], in0=xf[:, dc, hs(hf)], scalar=ALPHA,
                                                                                     in1=pt[:], op0=ALU.mult, op1=ALU.add),
                         reads=[pB, xfB[dc][hf]], writes=[xfB[dc][hf]])
                    lns[hf].stats_chunk(dc, xf[:, dc, hs(hf)], xfB[dc][hf])
        post_ln2(V_MIXG + 16, V_MIXB + 16)
        rr = 0
        for q_ in range(4):
            for g in range(4):
                wv, wB = cx.wget(("w1", jj, q_, g))
                for i in range(4):
                    hc = 4 * g + i
                    for hf in range(NH):
                        pt, pB = cx.bank()
                        for k in range(NCH):
                            P.op("pe", lambda e, k=k, i=i, hf=hf, wv=wv, pt=pt: e.matmul(pt[:], wv[:, k, i * 128:(i + 1) * 128], xb[:, k, hs(hf)],
                                                                                         start=(k == 0), stop=(k == NCH - 1)),
                                 reads=[wB, xbB[k][hf]], writes=[pB], accum=(k > 0))
                        r, rb = sg[rr], sgB[rr]
                        rr ^= 1
                        P.op("act", lambda e, pt=pt, r=r: e.activation(out=r, in_=pt[:], func=AF.Relu), reads=[pB], writes=[rb])
                        P.op("dve", lambda e, hc=hc, hf=hf, r=r: e.tensor_tensor(out=hq[:, hc, hs(hf)], in0=r, in1=r, op=ALU.mult),
                             reads=[rb], writes=[hqB[hc][hf]])
            for g in range(4):
                wv, wB = cx.wget(("w2", jj, q_, g))
                for i in range(4):
                    dc = 4 * g + i
                    for hf in range(NH):
                        pt, pB = cx.bank()
                        for k in range(16):
                            P.op("pe", lambda e, k=k, i=i, hf=hf, wv=wv, pt=pt: e.matmul(pt[:], wv[:, k, i * 128:(i + 1) * 128], hq[:, k, hs(hf)],
                                                                                         start=(k == 0), stop=(k == 15)),
                                 reads=[wB, hqB[k][hf]], writes=[pB], accum=(k > 0))
                        if q_ == 0:
                            P.op("dve", lambda e, dc=dc, hf=hf, pt=pt: e.scalar_tensor_tensor(out=xf[:, dc, hs(hf)], in0=xf[:, dc, hs(hf)], scalar=ALPHA,
                                                                                             in1=pt[:], op0=ALU.mult, op1=ALU.add),
                                 reads=[pB, xfB[dc][hf]], writes=[xfB[dc][hf]])
                        else:
                            P.op("dve", lambda e, dc=dc, hf=hf, pt=pt: e.tensor_tensor(out=xf[:, dc, hs(hf)], in0=pt[:], in1=xf[:, dc, hs(hf)], op=ALU.add),
                                 reads=[pB, xfB[dc][hf]], writes=[xfB[dc][hf]])
                        if q_ == 3:
                            lns[hf].stats_chunk(dc, xf[:, dc, hs(hf)], xfB[dc][hf])
        post_ln2(V_MLPG + 16, V_MLPB + 16)
        P.op("sp", lambda e, tsl=tsl: e.dma_start(out=out.rearrange("(c p) t -> p c t", p=128)[:, :, tsl], in_=xf),
             reads=xfB_flat, dsem="OUT")


def _fm(v):
    return np.ascontiguousarray(v.reshape(-1, 128).T)


def _t5_bucket(dist):
    max_exact = 16
    large = max_exact + (np.log(np.maximum(dist, 1) / max_exact) / math.log(2048 / max_exact) * (32 - max_exact)).astype(np.int32)
    large = np.minimum(large, 31)
    return np.where(dist < max_exact, dist, large).astype(np.int32)


def _bias_tiles(rel_bias, has_prev):
    i = np.arange(128)[:, None]
    j = np.arange(256)[None, :]
    delta = i - j + 128
    ok = (delta >= 0) & (delta <= 128)
    res = np.full((NHEAD, 128, 9, 128), NEG, np.float32)
    for bi, d in enumerate(BR):
        bucket = _t5_bucket(np.clip(delta, 0, None) * d)
        b = rel_bias[bucket]
        b = np.where(ok[:, :, None], b, np.float32(NEG))
        bt = np.transpose(b, (2, 1, 0))
        res[:, :, bi * 3 + 0, :] = bt[:, 0:128, :]
        res[:, :, bi * 3 + 1, :] = bt[:, 128:256, :]
        if has_prev:
            res[:, :, bi * 3 + 2, :] = bt[:, 0:128, :]
    return np.ascontiguousarray(res.reshape(NHEAD, 128, 9 * 128))


_NC_CACHE = {}


def build_fused():
    nc = bass.Bass("TRN2", target_bir_lowering=False)
    I32 = mybir.dt.int32
    xc = nc.dram_tensor("xc", [D, XH + TOK], F32, kind="ExternalInput").ap()
    hmask_d = nc.dram_tensor("hmask", [128, 1], F32, kind="ExternalInput").ap()
    prow_d = nc.dram_tensor("prevrow", [1, 1], I32, kind="ExternalInput").ap()
    vecs_d = nc.dram_tensor("vecs", [128, NV], F32, kind="ExternalInput").ap()
    ident_d = nc.dram_tensor("ident", [128, 128], F32, kind="ExternalInput").ap()
    bias_d = nc.dram_tensor("biasT", [NHEAD, 128, 9 * 128], F32, kind="ExternalInput").ap()
    pw1 = nc.dram_tensor("pw1", [D, 2 * D], F32, kind="ExternalInput").ap()
    pw2 = nc.dram_tensor("pw2", [D, D], F32, kind="ExternalInput").ap()
    w1a = nc.dram_tensor("w1a", [D, DFF], F32, kind="ExternalInput").ap()
    w2a = nc.dram_tensor("w2a", [DFF, D], F32, kind="ExternalInput").ap()
    w1b = nc.dram_tensor("w1b", [D, DFF], F32, kind="ExternalInput").ap()
    w2b = nc.dram_tensor("w2b", [DFF, D], F32, kind="ExternalInput").ap()
    wkv = nc.dram_tensor("wkv", [D, 2 * D], F32, kind="ExternalInput").ap()
    wq = nc.dram_tensor("wq", [D, D], F32, kind="ExternalInput").ap()
    wo = nc.dram_tensor("wo", [D, D], F32, kind="ExternalInput").ap()
    out = nc.dram_tensor("out", [D, TOK], F32, kind="ExternalOutput").ap()
    x2T = nc.dram_tensor("x2T", [D, TOK], F32).ap()
    qT = nc.dram_tensor("qT", [D, TOK], BF16).ap()
    kTs_t = nc.dram_tensor("kTs", [D, TOK], BF16)
    vts_t = nc.dram_tensor("vts", [TOK, D], BF16)
    kps = [[nc.dram_tensor(f"kps{j}{p}", [D // 2, T], BF16) for p in range(2)] for j in range(NT)]
    kpa = [[nc.dram_tensor(f"kpa{j}{p}", [4 * (D // 2), T], BF16) for p in range(2)] for j in range(NT)]
    vps = [[nc.dram_tensor(f"vps{j}{p}", [T // 2, D], BF16) for p in range(2)] for j in range(NT)]
    vpa = [[nc.dram_tensor(f"vpa{j}{p}", [4 * (T // 2), D], BF16) for p in range(2)] for j in range(NT)]
    kTp = nc.dram_tensor("kTp", [D, TOK], BF16).ap()
    vtp = nc.dram_tensor("vtp", [TOK, D], BF16).ap()
    oT = nc.dram_tensor("oT", [D, TOK], BF16).ap()
    kTs, vts = kTs_t.ap(), vts_t.ap()

    with contextlib.ExitStack() as st:
        cx = Ctx(nc, st, nslots=3)
        cx.wcache = nc.dram_tensor("wcache", [56, 128, 8192], BF16).ap()
        P = cx.P
        st.enter_context(nc.allow_low_precision("bf16 matmul operands, fp32 accumulation"))
        load_consts(cx, vecs_d, ident_d)
        identb = cx.sb("identb", [128, 128], BF16)
        identbB = Buf("identb")
        P.op("act", lambda e: e.activation(out=identb[:], in_=cx.ident[:], func=AF.Identity), reads=[cx.identB], writes=[identbB])
        cx.identb = identb
        P.op("dve", lambda e: e.tensor_copy(out=cx.dmy[:, 3:4], in_=identb[:, 0:1]), reads=[identbB], writes=[Buf("dmy_d2")])
        big = cx.sb("big", [128, BIGN], BF16)
        off = [0]

        def carve(n):
            n = (n + 15) // 16 * 16
            o = off[0]
            off[0] += n
            assert off[0] <= BIGN, off[0]
            return big[:, o:o + n]

        def alloc(name, n, dt):
            return carve(n) if dt == BF16 else carve(2 * n).bitcast(F32)[:, 0:n]

        x2TB, qTB, kTsB, vtsB, kTallB, vallB, kTpB, vtpB = (Buf(n_) for n_ in ("x2T", "qT", "kTs", "vts", "kTall", "vall", "kTp", "vtp"))
        groups = [[0, 1, 2, 3], [4, 5, 6, 7]]
        kpieces = [[(kps[j][p].ap(), Buf(f"kps{j}{p}")) for p in range(2)] for j in range(NT)]
        vpieces = [[(vps[j][p].ap(), Buf(f"vps{j}{p}")) for p in range(2)] for j in range(NT)]

        def after_tile(j):
            for p in range(2):
                cx.cc_pending.append((lambda e, j=j, p=p: e.collective_compute("AllGather", ALU.bypass, replica_groups=groups,
                                                                              ins=[kps[j][p].ap().opt()], outs=[kpa[j][p].ap().opt()]),
                                      [kpieces[j][p][1]], [kTallB]))
                cx.cc_pending.append((lambda e, j=j, p=p: e.collective_compute("AllGather", ALU.bypass, replica_groups=groups,
                                                                              ins=[vps[j][p].ap().opt()], outs=[vpa[j][p].ap().opt()]),
                                      [vpieces[j][p][1]], [vallB]))
            if j == NT - 1:
                cx.cc_flush()

        a_bufs = layer0_and_qkv(cx, alloc, xc, hmask_d, pw1, pw2, w1a, w2a, wkv, wq, x2T, qT, kTs, vts, x2TB, qTB, kTsB, vtsB,
                                kpieces, vpieces, after_tile)

        reg = st.enter_context(nc.sync.register("prevrow"))

        pval = []

        def copy_prev(dst, src_t, rows):
            def fn(e):
                if not pval:
                    e.reg_load(reg, prow_d[0:1, 0:1])
                    pval.append(e.snap(reg, min_val=0, max_val=3))
                return e.dma_start(out=dst, in_=src_t.ap()[bass.ts(pval[0], rows), :])
            return fn

        for j in range(NT):
            for p in range(2):
                P.op("sp", copy_prev(kTp[(D // 2) * p:(D // 2) * (p + 1), T * j:T * (j + 1)], kpa[j][p], D // 2),
                     reads=[kTallB], writes=[kTpB], accum=True, dsem="XK")
                P.op("sp", copy_prev(vtp[T * j + (T // 2) * p:T * j + (T // 2) * (p + 1), :], vpa[j][p], T // 2),
                     reads=[vallB], writes=[vtpB], accum=True, dsem="XV")

        attention_and_layer1(cx, carve, off, a_bufs, x2T, qT, kTs, kTp, vts, vtp, bias_d, wo, w1b, w2b, oT, out, identb, identbB,
                             x2TB, qTB, kTsB, kTpB, vtsB, vtpB)
        assert cx.wnext_get == len(cx.witems)
        import os
        if os.environ.get("KDEBUG"):
            for nm, ap_, shp, dt_, rb in (("dbg_kTp", kTp, [D, TOK], BF16, kTpB), ("dbg_kTs", kTs, [D, TOK], BF16, kTsB),
                                          ("dbg_vtp", vtp, [TOK, D], BF16, vtpB), ("dbg_vts", vts, [TOK, D], BF16, vtsB),
                                          ("dbg_oT", oT, [D, TOK], BF16, None), ("dbg_x2T", x2T, [D, TOK], F32, x2TB)):
                dd = nc.dram_tensor(nm, shp, dt_, kind="ExternalOutput").ap()
                P.op("sp", lambda e, dd=dd, ap_=ap_: e.dma_start(out=dd, in_=ap_), reads=([rb] if rb is not None else []), dsem="DBG")
        P.emit()
    return nc


def kernel(x, conv_pw1_w, conv_pw1_b, conv_dw_w, conv_dw_b, conv_ln_g, conv_ln_b, conv_pw2_w, conv_pw2_b,
           w_kv, attn_wq, attn_wo, rel_bias, mlp_w1, mlp_w2, ln_mix_g, ln_mix_b, ln_mlp_g, ln_mlp_b):
    f32 = np.float32
    x = np.asarray(x, f32)
    ncore = 8
    vecs = np.zeros((128, NV), f32)
    vecs[:, V_PW1B:V_PW1B + 32] = _fm(np.asarray(conv_pw1_b, f32)[0])
    dw = np.asarray(conv_dw_w, f32)[0]
    for tap in range(CONVW):
        vecs[:, V_DWW + 16 * tap:V_DWW + 16 * tap + 16] = _fm(dw[tap])
    vecs[:, V_DWB:V_DWB + 16] = _fm(np.asarray(conv_dw_b, f32)[0])
    vecs[:, V_CLNG:V_CLNG + 16] = _fm(np.asarray(conv_ln_g, f32)[0])
    vecs[:, V_CLNB:V_CLNB + 16] = _fm(np.asarray(conv_ln_b, f32)[0])
    vecs[:, V_PW2B:V_PW2B + 16] = _fm(np.asarray(conv_pw2_b, f32)[0])
    for l in range(2):
        vecs[:, V_MIXG + 16 * l:V_MIXG + 16 * l + 16] = _fm(np.asarray(ln_mix_g, f32)[l])
        vecs[:, V_MIXB + 16 * l:V_MIXB + 16 * l + 16] = _fm(np.asarray(ln_mix_b, f32)[l])
        vecs[:, V_MLPG + 16 * l:V_MLPG + 16 * l + 16] = _fm(np.asarray(ln_mlp_g, f32)[l])
        vecs[:, V_MLPB + 16 * l:V_MLPB + 16 * l + 16] = _fm(np.asarray(ln_mlp_b, f32)[l])
    ident = np.eye(128, dtype=f32)
    w1 = np.asarray(mlp_w1, f32)
    w2 = np.asarray(mlp_w2, f32)
    rel_bias = np.asarray(rel_bias, f32)
    shared = {
        "vecs": vecs, "ident": ident,
        "pw1": np.ascontiguousarray(np.asarray(conv_pw1_w, f32)[0]),
        "pw2": np.ascontiguousarray(np.asarray(conv_pw2_w, f32)[0]),
        "w1a": np.ascontiguousarray(w1[0]), "w2a": np.ascontiguousarray(w2[0]),
        "w1b": np.ascontiguousarray(w1[1]), "w2b": np.ascontiguousarray(w2[1]),
        "wkv": np.ascontiguousarray(np.asarray(w_kv, f32)),
        "wq": np.ascontiguousarray(np.asarray(attn_wq, f32)[0]),
        "wo": np.ascontiguousarray(np.asarray(attn_wo, f32)[0]),
    }
    bias_tiles = {True: _bias_tiles(rel_bias, True), False: _bias_tiles(rel_bias, False)}
    if "f" not in _NC_CACHE:
        _NC_CACHE["f"] = build_fused()
    ncf = _NC_CACHE["f"]
    in_maps = []
    for c in range(ncore):
        b, q = divmod(c, 4)
        xc = np.zeros((D, XH + TOK), f32)
        xc[:, XH:] = x[b, q * TOK:(q + 1) * TOK].T
        if q > 0:
            xc[:, :XH] = x[b, q * TOK - XH:q * TOK].T
        m = dict(shared)
        m.update({"xc": xc, "hmask": np.full((128, 1), 1.0 if q > 0 else 0.0, f32),
                  "prevrow": np.array([[max(q - 1, 0)]], np.int32), "biasT": bias_tiles[q > 0]})
        in_maps.append(m)
    res = run_bass_kernel_spmd(ncf, in_maps, core_ids=list(range(ncore))).results
    _NC_CACHE["last"] = res
    outp = np.zeros((2, 8192, D), f32)
    for c in range(ncore):
        b, q = divmod(c, 4)
        outp[b, q * TOK:(q + 1) * TOK] = np.asarray(res[c]["out"]).T
    return outp
```

```python
import contextlib
import math
import numpy as np
import ml_dtypes
import concourse.bass as bass
import concourse.mybir as mybir
from concourse.bass_utils import run_bass_kernel_spmd

F32 = mybir.dt.float32
BF16 = mybir.dt.bfloat16
ALU = mybir.AluOpType
AF = mybir.ActivationFunctionType

D = 2048
NCH = 16
DFF = 8192
T = 512
NT = 4
TOK = 2048
HALO = 32
XH = 128
NHEAD = 16
ALPHA = float(4 ** 0.25)
EPS = 1e-5
QSCALE = float(128 ** -0.5)
NEG = -30000.0
CONVW = 31

V_PW1B = 0
V_DWW = V_PW1B + 32
V_DWB = V_DWW + CONVW * 16
V_CLNG = V_DWB + 16
V_CLNB = V_CLNG + 16
V_PW2B = V_CLNB + 16
V_MIXG = V_PW2B + 16
V_MIXB = V_MIXG + 32
V_MLPG = V_MIXB + 32
V_MLPB = V_MLPG + 32
NV = V_MLPB + 32


class Buf:
    __slots__ = ("name", "writers", "readers")

    def __init__(self, name):
        self.name = name
        self.writers = []
        self.readers = []


class Op:
    __slots__ = ("eng", "fn", "deps", "dsem", "sig", "val", "waits", "inc")

    def __init__(self, eng, fn, dsem, inc=None):
        self.eng = eng
        self.fn = fn
        self.deps = []
        self.dsem = dsem
        self.inc = inc if inc is not None else (16 if dsem is not None else 1)
        self.sig = False
        self.val = 0
        self.waits = None


class Prog:
    ENGS = ("pe", "act", "dve", "pool", "sp")

    def __init__(self, nc):
        self.nc = nc
        self.ops = {e: [] for e in self.ENGS}

    def op(self, eng, fn, reads=(), writes=(), accum=False, dsem=None, inc=None):
        o = Op(eng, fn, dsem, inc)
        deps = o.deps
        for b in reads:
            deps.extend(b.writers)
        for b in writes:
            if not accum:
                deps.extend(b.writers)
            deps.extend(b.readers)
        for b in writes:
            if accum:
                b.writers.append(o)
            else:
                b.writers = [o]
            b.readers = []
        for b in reads:
            b.readers.append(o)
        self.ops[eng].append(o)
        return o

    def emit(self, final_wait_eng="sp"):
        nc = self.nc
        for e in self.ENGS:
            for o in self.ops[e]:
                if o.dsem is not None:
                    o.sig = True
                nd = []
                for d in o.deps:
                    if d.eng == "pe" and e == "pe" and d.dsem is None:
                        continue
                    d.sig = True
                    nd.append(d)
                o.deps = nd
        counts = {}
        sem_names = []
        for e in self.ENGS:
            for o in self.ops[e]:
                if not o.sig:
                    continue
                key = o.dsem if o.dsem is not None else "E_" + e
                counts[key] = counts.get(key, 0) + o.inc
                o.val = counts[key]
                if key not in sem_names:
                    sem_names.append(key)
        with contextlib.ExitStack() as st:
            sems = {k: st.enter_context(nc.semaphore(k)) for k in sem_names}
            for e in self.ENGS:
                seen = {}
                for o in self.ops[e]:
                    need = {}
                    for d in o.deps:
                        key = d.dsem if d.dsem is not None else "E_" + d.eng
                        if d.val > need.get(key, 0):
                            need[key] = d.val
                    w = []
                    for k, v in need.items():
                        if seen.get(k, 0) < v:
                            seen[k] = v
                            w.append((k, v))
                    o.waits = w
            block = st.enter_context(nc.Block())
            ops = self.ops
            final = [(k, v) for k, v in counts.items() if not k.startswith("E_")]

            def run(eh, ename):
                for o in ops[ename]:
                    for k, v in o.waits:
                        eh.wait_ge(sems[k], v)
                    ins = o.fn(eh)
                    if o.sig:
                        key = o.dsem if o.dsem is not None else "E_" + ename
                        ins.then_inc(sems[key], o.inc)
                if ename == final_wait_eng:
                    for k, v in final:
                        eh.wait_ge(sems[k], v)

            @block.tensor
            def _(eh):
                run(eh, "pe")

            @block.scalar
            def _(eh):
                run(eh, "act")

            @block.vector
            def _(eh):
                run(eh, "dve")

            @block.gpsimd
            def _(eh):
                run(eh, "pool")

            @block.sync
            def _(eh):
                run(eh, "sp")


class Ctx:
    def __init__(self, nc, st, nslots, keep_prev=False):
        self.keep_prev = keep_prev
        self.nc = nc
        self.st = st
        self.P = Prog(nc)
        self.ps = [st.enter_context(nc.psum_tensor(f"ps{i}", [128, 512], F32)) for i in range(8)]
        self.pb = [Buf(f"ps{i}") for i in range(8)]
        self.rot = 0
        self.nmain = 6
        self.nslots = nslots
        self.wslots = [st.enter_context(nc.sbuf_tensor(f"wsl{i}", [128, 8192], BF16)) for i in range(nslots)]
        self.wbufs = [Buf(f"wsl{i}") for i in range(nslots)]
        self.witems = []
        self.wnext_load = 0
        self.wnext_get = 0
        self.uid = 0
        self.wcache = None
        self.wcache_idx = {}
        self.wcacheB = {}
        self.cc_pending = []
        self.cc_since = 0
        self.cc_gap = 14
        self.ccB = Buf("cc_order")
        self.cc_n = 0

    def sb(self, name, shape, dt):
        return self.st.enter_context(self.nc.sbuf_tensor("sb_" + name, shape, dt))

    def bank(self):
        i = self.rot
        self.rot = (self.rot + 1) % self.nmain
        return self.ps[i], self.pb[i]

    def wadd(self, key, src, c, cache=None, store=True):
        self.witems.append((key, src, c, cache, store))

    def _wload(self, i):
        key, src, c, cache, store = self.witems[i]
        s = i % self.nslots
        if cache is not None and cache in self.wcacheB:
            ci = self.wcache_idx[cache]
            self.P.op("sp", lambda e: e.dma_start(out=self.wslots[s][:, :], in_=self.wcache[ci]),
                      reads=[self.wcacheB[cache]], writes=[self.wbufs[s]], dsem=f"WH{s}")
            return
        self._wload_src(i, s, src, c)
        if cache is not None and store:
            ci = self.wcache_idx.setdefault(cache, len(self.wcache_idx))
            self.wcacheB[cache] = Buf(f"wcache{ci}")
            self.P.op("sp", lambda e: e.dma_start(out=self.wcache[ci], in_=self.wslots[s][:, :]),
                      reads=[self.wbufs[s]], writes=[self.wcacheB[cache]], dsem=f"WC{s}")

    def _wload_src(self, i, s, src, c):
        dst = self.wslots[s][:, :].rearrange("p (c n) -> p c n", c=c)
        if not isinstance(src, list):
            self.P.op("pool", lambda e: e.dma_start(out=dst, in_=src), writes=[self.wbufs[s]], dsem=f"W{s}")
            return
        for pi, (lo, hi, sv) in enumerate(src):
            self.P.op("pool", lambda e, lo=lo, hi=hi, sv=sv: e.dma_start(out=dst[:, :, lo:hi], in_=sv),
                      writes=[self.wbufs[s]], accum=(pi > 0), dsem=f"W{s}")

    def cc_issue(self):
        fn, reads, writes = self.cc_pending.pop(0)
        self.cc_n += 1
        self.P.op("pool", fn, reads=list(reads) + [self.ccB], writes=list(writes) + [self.ccB], accum=True, dsem=f"CC{self.cc_n % 4}", inc=1)
        self.cc_since = 0

    def cc_flush(self):
        while self.cc_pending:
            self.cc_issue()

    def wget(self, key):
        self.cc_since += 1
        if self.cc_pending and self.cc_since >= self.cc_gap:
            self.cc_issue()
        i = self.wnext_get
        self.wnext_get += 1
        assert self.witems[i][0] == key, (self.witems[i][0], key)
        lim = min(i + self.nslots - (1 if self.keep_prev else 0), len(self.witems))
        while self.wnext_load < lim:
            self._wload(self.wnext_load)
            self.wnext_load += 1
        s = i % self.nslots
        c = self.witems[i][2]
        return self.wslots[s][:, :].rearrange("p (c n) -> p c n", c=c), self.wbufs[s]


def wsrc_k2048(w, col0):
    return w.rearrange("(c p) n -> p c n", p=128)[:, :, col0:col0 + 512]


def wsrc_k8192(w, col0):
    return w.rearrange("(c p) n -> p c n", p=128)[:, :, col0:col0 + 128]


class LN:
    def __init__(self, cx, alloc=None, banks=(6, 7)):
        self.cx = cx
        self.b1, self.b2 = banks
        if alloc is None:
            alloc = lambda name, n, dt: cx.sb(name, [128, n], dt)[:, :]
        self.zb = [alloc(f"ln_zb{i}", T, BF16) for i in range(2)]
        self.z2b = [alloc(f"ln_z2b{i}", T, BF16) for i in range(2)]
        self.zbB = [Buf(f"ln_zb{i}") for i in range(2)]
        self.z2bB = [Buf(f"ln_z2b{i}") for i in range(2)]
        self.mean = alloc("ln_mean", T, F32)
        self.var = alloc("ln_var", T, F32)
        self.rstd = alloc("ln_rstd", T, F32)
        self.meanB, self.varB, self.rstdB = Buf("ln_mean"), Buf("ln_var"), Buf("ln_rstd")
        self.r = 0

    def stats_chunk(self, c, src, srcB):
        cx, P = self.cx, self.cx.P
        i = self.r
        self.r ^= 1
        zb, z2b = self.zb[i], self.z2b[i]
        P.op("act", lambda e: e.activation(out=zb, in_=src, func=AF.Identity), reads=[srcB], writes=[self.zbB[i]])
        P.op("act", lambda e: e.activation(out=z2b, in_=src, func=AF.Square), reads=[srcB], writes=[self.z2bB[i]])
        ones = cx.ones
        P.op("pe", lambda e: e.matmul(cx.ps[self.b1][:], ones[:], zb, start=(c == 0), stop=(c == NCH - 1)),
             reads=[self.zbB[i], cx.onesB], writes=[cx.pb[self.b1]], accum=(c > 0))
        P.op("pe", lambda e: e.matmul(cx.ps[self.b2][:], ones[:], z2b, start=(c == 0), stop=(c == NCH - 1)),
             reads=[self.z2bB[i], cx.onesB], writes=[cx.pb[self.b2]], accum=(c > 0))

    def finalize(self):
        cx, P = self.cx, self.cx.P
        mean, var, rstd = self.mean, self.var, self.rstd
        P.op("dve", lambda e: e.tensor_scalar(out=mean, in0=cx.ps[self.b1][:], scalar1=1.0 / D, scalar2=None, op0=ALU.mult),
             reads=[cx.pb[self.b1]], writes=[self.meanB])
        P.op("dve", lambda e: e.tensor_tensor(out=var, in0=mean, in1=mean, op=ALU.mult),
             reads=[self.meanB], writes=[self.varB])
        P.op("dve", lambda e: e.scalar_tensor_tensor(out=var, in0=cx.ps[self.b2][:], scalar=1.0 / D, in1=var,
                                                     op0=ALU.mult, op1=ALU.subtract),
             reads=[cx.pb[self.b2], self.varB], writes=[self.varB])
        P.op("act", lambda e: e.activation(out=var, in_=var, func=AF.Sqrt, bias=cx.epsc[:, 0:1]),
             reads=[self.varB], writes=[self.varB])
        P.op("dve", lambda e: e.reciprocal(out=rstd, in_=var), reads=[self.varB], writes=[self.rstdB])
        P.op("dve", lambda e: e.scalar_tensor_tensor(out=mean, in0=mean, scalar=-1.0, in1=rstd,
                                                     op0=ALU.mult, op1=ALU.mult),
             reads=[self.meanB, self.rstdB], writes=[self.meanB])

    def norm_chunk(self, src, srcB):
        P = self.cx.P
        rstd, mean = self.rstd, self.mean
        P.op("dve", lambda e: e.tensor_tensor(out=src, in0=src, in1=rstd, op=ALU.mult),
             reads=[srcB, self.rstdB], writes=[srcB])
        P.op("dve", lambda e: e.tensor_tensor(out=src, in0=src, in1=mean, op=ALU.add),
             reads=[srcB, self.meanB], writes=[srcB])


def post_ln(cx, ln, xf, xfB, xb, xbB, gcol, bcol):
    P = cx.P
    vecs = cx.vecs
    ln.finalize()
    for c in range(NCH):
        src = xf[:, c, :]
        ln.norm_chunk(src, xfB[c])
        P.op("act", lambda e, c=c, src=src: e.activation(out=src, in_=src, func=AF.Identity,
                                                         bias=vecs[:, bcol + c:bcol + c + 1],
                                                         scale=vecs[:, gcol + c:gcol + c + 1]),
             reads=[xfB[c]], writes=[xfB[c]])
        P.op("act", lambda e, c=c, src=src: e.activation(out=xb[:, c, :], in_=src, func=AF.Identity),
             reads=[xfB[c]], writes=[xbB[c]])


def mlp_sublayer(cx, ln, layer, xf, xfB, xb, xbB, h, hB, rbuf, rB):
    P = cx.P
    rr = 0
    for g in range(16):
        wv, wB = cx.wget(("w1", layer, g))
        for i in range(4):
            hc = 4 * g + i
            pt, pB = cx.bank()
            for k in range(NCH):
                P.op("pe", lambda e, k=k, i=i, wv=wv, pt=pt: e.matmul(pt[:], wv[:, k, i * 128:(i + 1) * 128], xb[:, k, :],
                                                                      start=(k == 0), stop=(k == NCH - 1)),
                     reads=[wB, xbB[k]], writes=[pB], accum=(k > 0))
            r, rb = rbuf[rr], rB[rr]
            rr ^= 1
            P.op("act", lambda e, pt=pt, r=r: e.activation(out=r[:], in_=pt[:], func=AF.Relu), reads=[pB], writes=[rb])
            P.op("dve", lambda e, hc=hc, r=r: e.tensor_tensor(out=h[:, hc, :], in0=r[:], in1=r[:], op=ALU.mult),
                 reads=[rb], writes=[hB[hc]])
    for dc in range(NCH):
        wv, wB = cx.wget(("w2", layer, dc))
        pt, pB = cx.bank()
        for k in range(64):
            P.op("pe", lambda e, k=k, wv=wv, pt=pt: e.matmul(pt[:], wv[:, k, :], h[:, k, :], start=(k == 0), stop=(k == 63)),
                 reads=[wB, hB[k]], writes=[pB], accum=(k > 0))
        P.op("dve", lambda e, dc=dc, pt=pt: e.scalar_tensor_tensor(out=xf[:, dc, :], in0=xf[:, dc, :], scalar=ALPHA, in1=pt[:],
                                                                   op0=ALU.mult, op1=ALU.add),
             reads=[pB, xfB[dc]], writes=[xfB[dc]])
        ln.stats_chunk(dc, xf[:, dc, :], xfB[dc])
    post_ln(cx, ln, xf, xfB, xb, xbB, V_MLPG + 16 * layer, V_MLPB + 16 * layer)


def mlp_sublayer_q(cx, ln, lkey, xf, xfB, xb, xbB, hq, hqB, rbuf, rB, gcol, bcol):
    P = cx.P
    rr = 0
    for q_ in range(4):
        for g in range(4):
            wv, wB = cx.wget(("w1", lkey, q_, g))
            for i in range(4):
                hc = 4 * g + i
                pt, pB = cx.bank()
                for k in range(NCH):
                    P.op("pe", lambda e, k=k, i=i, wv=wv, pt=pt: e.matmul(pt[:], wv[:, k, i * 128:(i + 1) * 128], xb[:, k, :],
                                                                          start=(k == 0), stop=(k == NCH - 1)),
                         reads=[wB, xbB[k]], writes=[pB], accum=(k > 0))
                r, rb = rbuf[rr], rB[rr]
                rr ^= 1
                P.op("act", lambda e, pt=pt, r=r: e.activation(out=r, in_=pt[:], func=AF.Relu), reads=[pB], writes=[rb])
                P.op("dve", lambda e, hc=hc, r=r: e.tensor_tensor(out=hq[:, hc, :], in0=r, in1=r, op=ALU.mult),
                     reads=[rb], writes=[hqB[hc]])
        for g in range(4):
            wv, wB = cx.wget(("w2", lkey, q_, g))
            for i in range(4):
                dc = 4 * g + i
                pt, pB = cx.bank()
                for k in range(16):
                    P.op("pe", lambda e, k=k, i=i, wv=wv, pt=pt: e.matmul(pt[:], wv[:, k, i * 128:(i + 1) * 128], hq[:, k, :],
                                                                          start=(k == 0), stop=(k == 15)),
                         reads=[wB, hqB[k]], writes=[pB], accum=(k > 0))
                if q_ == 0:
                    P.op("dve", lambda e, dc=dc, pt=pt: e.scalar_tensor_tensor(out=xf[:, dc, :], in0=xf[:, dc, :], scalar=ALPHA, in1=pt[:],
                                                                               op0=ALU.mult, op1=ALU.add),
                         reads=[pB, xfB[dc]], writes=[xfB[dc]])
                else:
                    P.op("dve", lambda e, dc=dc, pt=pt: e.tensor_tensor(out=xf[:, dc, :], in0=pt[:], in1=xf[:, dc, :], op=ALU.add),
                         reads=[pB, xfB[dc]], writes=[xfB[dc]])
                if q_ == 3:
                    ln.stats_chunk(dc, xf[:, dc, :], xfB[dc])
    post_ln(cx, ln, xf, xfB, xb, xbB, gcol, bcol)


def proj_residual_sublayer(cx, ln, wkey, inb, inbB, xf, xfB, xb, xbB, gcol, bcol, bias_col, tb=None, tbB=None):
    P = cx.P
    vecs = cx.vecs
    tr = 0
    for g in range(4):
        wv, wB = cx.wget((wkey, g))
        for i in range(4):
            dc = 4 * g + i
            pt, pB = cx.bank()
            for k in range(NCH):
                P.op("pe", lambda e, k=k, i=i, wv=wv, pt=pt: e.matmul(pt[:], wv[:, k, i * 128:(i + 1) * 128], inb[:, k, :],
                                                                      start=(k == 0), stop=(k == NCH - 1)),
                     reads=[wB, inbB[k]], writes=[pB], accum=(k > 0))
            if bias_col is not None:
                t_, tB_ = tb[tr], tbB[tr]
                tr ^= 1
                P.op("act", lambda e, dc=dc, pt=pt, t_=t_: e.activation(out=t_[:], in_=pt[:], func=AF.Identity,
                                                                        bias=vecs[:, bias_col + dc:bias_col + dc + 1]),
                     reads=[pB], writes=[tB_])
                P.op("dve", lambda e, dc=dc, t_=t_: e.scalar_tensor_tensor(out=xf[:, dc, :], in0=xf[:, dc, :], scalar=ALPHA,
                                                                          in1=t_[:], op0=ALU.mult, op1=ALU.add),
                     reads=[tB_, xfB[dc]], writes=[xfB[dc]])
            else:
                P.op("dve", lambda e, dc=dc, pt=pt: e.scalar_tensor_tensor(out=xf[:, dc, :], in0=xf[:, dc, :], scalar=ALPHA,
                                                                          in1=pt[:], op0=ALU.mult, op1=ALU.add),
                     reads=[pB, xfB[dc]], writes=[xfB[dc]])
            ln.stats_chunk(dc, xf[:, dc, :], xfB[dc])
    post_ln(cx, ln, xf, xfB, xb, xbB, gcol, bcol)


def load_consts(cx, vecs_d, ident_d):
    P = cx.P
    cx.vecs = cx.sb("vecs", [128, NV], F32)
    cx.ident = cx.sb("ident", [128, 128], F32)
    cx.ones = cx.sb("ones", [128, 128], BF16)
    cx.epsc = cx.sb("epsc", [128, 1], F32)
    cx.vecsB, cx.identB, cx.onesB = Buf("vecs"), Buf("ident"), Buf("ones")
    P.op("sp", lambda e: e.dma_start(out=cx.vecs[:], in_=vecs_d), writes=[cx.vecsB], dsem="CV")
    P.op("sp", lambda e: e.dma_start(out=cx.ident[:], in_=ident_d), writes=[cx.identB], dsem="CI")
    P.op("dve", lambda e: e.memset(cx.ones[:], 1.0), writes=[cx.onesB])
    P.op("dve", lambda e: e.memset(cx.epsc[:], EPS), writes=[cx.onesB], accum=True)
    cx.dmy = cx.sb("dmy", [128, 4], F32)
    consts = [cx.vecsB, cx.identB, cx.onesB]
    P.op("act", lambda e: e.activation(out=cx.dmy[:, 0:1], in_=cx.vecs[:, 0:1], func=AF.Identity), reads=consts, writes=[Buf("dmy_a")])
    P.op("dve", lambda e: e.tensor_copy(out=cx.dmy[:, 1:2], in_=cx.vecs[:, 0:1]), reads=consts, writes=[Buf("dmy_d")])
    P.op("pool", lambda e: e.tensor_copy(out=cx.dmy[:, 2:3], in_=cx.vecs[:, 0:1]), reads=consts, writes=[Buf("dmy_p")])


def layer0_and_qkv(cx, alloc, xc, hmask_d, pw1, pw2, w1, w2, wkv, wq, x2T, qT, kT, vtm, x2TB, qTB, kTB, vtB,
                   kpieces, vpieces, after_k, after_v):
    P = cx.P
    vecs = cx.vecs
    bufs = []

    def NB(name):
        b_ = Buf(name)
        bufs.append(b_)
        return b_

    hmask = alloc("hmask", 2, F32)
    hmB = NB("hmask")
    P.op("sp", lambda e: e.dma_start(out=hmask[:, 0:1], in_=hmask_d), writes=[hmB], dsem="CH")
    xf = alloc("xf", NCH * T, F32).rearrange("p (c t) -> p c t", c=NCH)
    xb = alloc("xb", NCH * T, BF16).rearrange("p (c t) -> p c t", c=NCH)
    xfB = [NB(f"xf{c}") for c in range(NCH)]
    xbB = [NB(f"xb{c}") for c in range(NCH)]
    scratch = alloc("scratch", 64 * T, BF16)
    hB = [NB(f"h{c}") for c in range(64)]
    h = scratch.rearrange("p (c t) -> p c t", c=64)
    cv = scratch[:, 0:32 * T].bitcast(F32).rearrange("p (c t) -> p c t", c=NCH)
    u = scratch[:, 32 * T:48 * T].rearrange("p (c t) -> p c t", c=NCH)
    kst = scratch[:, 0:16 * T].rearrange("p (c t) -> p c t", c=NCH)
    qst = scratch[:, 16 * T:32 * T].rearrange("p (c t) -> p c t", c=NCH)
    vst = scratch[:, 32 * T:48 * T].rearrange("p (b f) -> p b f", b=4)

    def cvB(c):
        return [hB[2 * c], hB[2 * c + 1]]

    xh = alloc("xh", NCH * HALO, BF16).rearrange("p (c t) -> p c t", c=NCH)
    xhB = NB("xh")
    xhf = alloc("xhf", NCH * HALO, F32).rearrange("p (c t) -> p c t", c=NCH)
    xhfB = NB("xhf")
    ghalo = alloc("ghalo", NCH * HALO, F32).rearrange("p (c t) -> p c t", c=NCH)
    ghB = [NB(f"gh{c}") for c in range(NCH)]
    gbuf = [alloc(f"gbuf{i}", T + 32, BF16) for i in range(2)]
    gbB = [NB(f"gbuf{i}") for i in range(2)]
    NDG = 16
    dg = [alloc(f"dg{i}", 128, BF16) for i in range(NDG)]
    dgB = [NB(f"dg{i}") for i in range(NDG)]
    dgr = 0
    sg = [alloc(f"sg{i}", T, F32) for i in range(2)]
    sgB = [NB(f"sg{i}") for i in range(2)]
    sgh = alloc("sgh", HALO, F32)
    sghB = NB("sgh")
    ln = LN(cx, alloc=alloc)
    bufs += ln.zbB + ln.z2bB + [ln.meanB, ln.varB, ln.rstdB]

    pw1v = pw1.rearrange("(c p) n -> p c n", p=128)
    w2v = w2.rearrange("(q c p) n -> q p c n", p=128, c=16)
    for j in range(NT):
        cnt = [0]

        def st_():
            cnt[0] += 1
            return (cnt[0] % 2) == (j % 2) or j >= 2

        for g in range(8):
            cx.wadd(("pw1", j, g), [(0, 256, pw1v[:, :, 256 * g:256 * g + 256]),
                                    (256, 512, pw1v[:, :, D + 256 * g:D + 256 * g + 256])], 16, cache=("pw1", g), store=st_())
        for g in range(4):
            cx.wadd(("pw2", g), wsrc_k2048(pw2, 512 * g), 16, cache=("pw2", g), store=st_())
        for q_ in range(4):
            for g in range(4):
                cx.wadd(("w1", ("A", j), q_, g), wsrc_k2048(w1, 2048 * q_ + 512 * g), 16, cache=("w1", q_, g), store=st_())
            for g in range(4):
                cx.wadd(("w2", ("A", j), q_, g), w2v[q_][:, :, 512 * g:512 * (g + 1)], 16, cache=("w2", q_, g), store=st_())
        for g in range(4):
            cx.wadd(("wk", g), wsrc_k2048(wkv, 512 * g), 16, cache=("wk", g), store=st_())
        for g in range(4):
            cx.wadd(("wv", g), wsrc_k2048(wkv, D + 512 * g), 16, cache=("wv", g), store=st_())
        for g in range(4):
            cx.wadd(("wq", g), wsrc_k2048(wq, 512 * g), 16, cache=("wq", g), store=st_())

    xcv = xc.rearrange("(c p) t -> p c t", p=128)
    P.op("sp", lambda e: e.dma_start(out=xhf, in_=xcv[:, :, XH - HALO:XH]), writes=[xhfB], dsem="X0")
    P.op("act", lambda e: e.activation(out=xh, in_=xhf, func=AF.Identity), reads=[xhfB], writes=[xhB])

    for j in range(NT):
        P.op("sp", lambda e, j=j: e.dma_start(out=xf, in_=xcv[:, :, XH + T * j:XH + T * (j + 1)]), writes=xfB, dsem="X1")
        for c in range(NCH):
            if c % 2 == 0:
                P.op("act", lambda e, c=c: e.activation(out=xb[:, c, :], in_=xf[:, c, :], func=AF.Identity), reads=[xfB[c]], writes=[xbB[c]])
            else:
                P.op("dve", lambda e, c=c: e.tensor_copy(out=xb[:, c, :], in_=xf[:, c, :]), reads=[xfB[c]], writes=[xbB[c]])
        gr = 0
        for g in range(8):
            wv, wB = cx.wget(("pw1", j, g))
            for i in range(2):
                c = 2 * g + i
                acol = slice(i * 128, (i + 1) * 128)
                gcol = slice(256 + i * 128, 256 + (i + 1) * 128)
                pa, paB = cx.bank()
                for k in range(NCH):
                    P.op("pe", lambda e, k=k, acol=acol, wv=wv, pa=pa: e.matmul(pa[:], wv[:, k, acol], xb[:, k, :],
                                                                              start=(k == 0), stop=(k == NCH - 1)),
                         reads=[wB, xbB[k]], writes=[paB], accum=(k > 0))
                pg, pgB = cx.bank()
                for k in range(NCH):
                    P.op("pe", lambda e, k=k, gcol=gcol, wv=wv, pg=pg: e.matmul(pg[:], wv[:, k, gcol], xb[:, k, :],
                                                                              start=(k == 0), stop=(k == NCH - 1)),
                         reads=[wB, xbB[k]], writes=[pgB], accum=(k > 0))
                gi = gr
                gr ^= 1
                gb_, gbB_ = gbuf[gi], gbB[gi]
                sg_, sgB_ = sg[gi], sgB[gi]
                ba = vecs[:, V_PW1B + c:V_PW1B + c + 1]
                bg = vecs[:, V_PW1B + 16 + c:V_PW1B + 16 + c + 1]
                if j == 0:
                    ph, phB = cx.bank()
                    for k in range(NCH):
                        P.op("pe", lambda e, k=k, acol=acol, wv=wv, ph=ph: e.matmul(ph[:, 0:HALO], wv[:, k, acol], xh[:, k, :],
                                                                                  start=(k == 0), stop=(k == NCH - 1)),
                             reads=[wB, xhB], writes=[phB], accum=(k > 0))
                    for k in range(NCH):
                        P.op("pe", lambda e, k=k, gcol=gcol, wv=wv, ph=ph: e.matmul(ph[:, HALO:2 * HALO], wv[:, k, gcol], xh[:, k, :],
                                                                                  start=(k == 0), stop=(k == NCH - 1)),
                             reads=[wB, xhB], writes=[phB], accum=True)
                    P.op("act", lambda e, ph=ph, bg=bg: e.activation(out=sgh, in_=ph[:, HALO:2 * HALO], func=AF.Sigmoid, bias=bg),
                         reads=[phB], writes=[sghB])
                    P.op("dve", lambda e, c=c, ph=ph, ba=ba: e.scalar_tensor_tensor(out=ghalo[:, c, :], in0=ph[:, 0:HALO], scalar=ba, in1=sgh,
                                                                                   op0=ALU.add, op1=ALU.mult),
                         reads=[phB, sghB], writes=[ghB[c]])
                    P.op("dve", lambda e, c=c, gb_=gb_: e.tensor_scalar(out=gb_[:, 0:30], in0=ghalo[:, c, 2:32], scalar1=hmask[:, 0:1], scalar2=None,
                                                                       op0=ALU.mult),
                         reads=[ghB[c], hmB], writes=[gbB_])
                else:
                    P.op("dve", lambda e, c=c, gb_=gb_: e.tensor_copy(out=gb_[:, 0:30], in_=ghalo[:, c, 2:32]),
                         reads=[ghB[c]], writes=[gbB_])
                P.op("act", lambda e, pg=pg, sg_=sg_, bg=bg: e.activation(out=sg_, in_=pg[:], func=AF.Sigmoid, bias=bg),
                     reads=[pgB], writes=[sgB_])
                P.op("dve", lambda e, pa=pa, sg_=sg_, gb_=gb_, ba=ba: e.scalar_tensor_tensor(out=gb_[:, 30:30 + T], in0=pa[:], scalar=ba, in1=sg_,
                                                                                           op0=ALU.add, op1=ALU.mult),
                     reads=[paB, sgB_], writes=[gbB_], accum=True)
                if j < NT - 1:
                    P.op("dve", lambda e, c=c, gb_=gb_: e.tensor_copy(out=ghalo[:, c, 2:32], in_=gb_[:, T:T + 30]),
                         reads=[gbB_], writes=[ghB[c]])
                cvc = cv[:, c, :]
                bdw = vecs[:, V_DWB + c:V_DWB + c + 1]
                pc, pcB = cx.bank()
                for tap in range(CONVW):
                    wj = vecs[:, V_DWW + 16 * tap + c:V_DWW + 16 * tap + c + 1]
                    dgi, dgiB = dg[dgr], dgB[dgr]
                    dgr = (dgr + 1) % NDG
                    if tap % 2 == 0:
                        P.op("dve", lambda e, dgi=dgi, wj=wj: e.tensor_scalar(out=dgi, in0=cx.identb[:], scalar1=wj, scalar2=None, op0=ALU.mult),
                             writes=[dgiB])
                    else:
                        P.op("act", lambda e, dgi=dgi, wj=wj: e.activation(out=dgi, in_=cx.identb[:], func=AF.Identity, scale=wj),
                             writes=[dgiB])
                    P.op("pe", lambda e, pc=pc, dgi=dgi, gb_=gb_, tap=tap: e.matmul(pc[:], dgi, gb_[:, tap:tap + T],
                                                                                  start=(tap == 0), stop=(tap == CONVW - 1)),
                         reads=[dgiB, gbB_], writes=[pcB], accum=(tap > 0))
                P.op("act", lambda e, cvc=cvc, pc=pc, bdw=bdw: e.activation(out=cvc, in_=pc[:], func=AF.Identity, bias=bdw),
                     reads=[pcB], writes=cvB(c))
                ln.stats_chunk(c, cvc, hB[2 * c])
        ln.finalize()
        for c in range(NCH):
            cvc = cv[:, c, :]
            P.op("dve", lambda e, cvc=cvc: e.tensor_tensor(out=cvc, in0=cvc, in1=ln.rstd, op=ALU.mult),
                 reads=cvB(c) + [ln.rstdB], writes=cvB(c))
            P.op("dve", lambda e, cvc=cvc: e.tensor_tensor(out=cvc, in0=cvc, in1=ln.mean, op=ALU.add),
                 reads=cvB(c) + [ln.meanB], writes=cvB(c))
            P.op("act", lambda e, c=c, cvc=cvc: e.activation(out=u[:, c, :], in_=cvc, func=AF.Silu,
                                                             bias=vecs[:, V_CLNB + c:V_CLNB + c + 1], scale=vecs[:, V_CLNG + c:V_CLNG + c + 1]),
                 reads=cvB(c), writes=[hB[32 + c]])
        uB = [hB[32 + c] for c in range(NCH)]
        proj_residual_sublayer(cx, ln, "pw2", u, uB, xf, xfB, xb, xbB, V_MIXG, V_MIXB, V_PW2B, tb=sg, tbB=sgB)
        mlp_sublayer_q(cx, ln, ("A", j), xf, xfB, xb, xbB, h[:, 0:16, :], hB[0:16], sg, sgB, V_MLPG, V_MLPB)
        P.op("sp", lambda e, j=j: e.dma_start(out=x2T.rearrange("(c p) t -> p c t", p=128)[:, :, T * j:T * (j + 1)], in_=xf),
             reads=xfB, writes=[x2TB], accum=True, dsem="OX")
        def proj_fm(nm, stg, cell0, scale):
            for g in range(4):
                wv_, wB_ = cx.wget((nm, g))
                for i in range(4):
                    hd = 4 * g + i
                    pt, pB = cx.bank()
                    for k in range(NCH):
                        P.op("pe", lambda e, k=k, i=i, wv_=wv_, pt=pt: e.matmul(pt[:], wv_[:, k, i * 128:(i + 1) * 128], xb[:, k, :],
                                                                                start=(k == 0), stop=(k == NCH - 1)),
                             reads=[wB_, xbB[k]], writes=[pB], accum=(k > 0))
                    P.op("act", lambda e, hd=hd, pt=pt, stg=stg, scale=scale: e.activation(out=stg[:, hd, :], in_=pt[:], func=AF.Identity, scale=scale),
                         reads=[pB], writes=[hB[cell0 + hd]])
            dst, dB = (kT, kTB) if nm == "wk" else (qT, qTB)
            P.op("sp", lambda e, j=j, dst=dst, stg=stg: e.dma_start(out=dst.rearrange("(c p) t -> p c t", p=128)[:, :, T * j:T * (j + 1)], in_=stg),
                 reads=[hB[cell0 + c] for c in range(NCH)], writes=[dB], accum=True, dsem="OK" if nm == "wk" else "OQ")
            if nm == "wk":
                for p_ in range(2):
                    pap, pB_ = kpieces[j][p_]
                    P.op("sp", lambda e, pap=pap, p_=p_, stg=stg: e.dma_start(out=pap.rearrange("(c p) t -> p c t", p=128), in_=stg[:, 8 * p_:8 * p_ + 8, :]),
                         reads=[hB[cell0 + c] for c in range(8 * p_, 8 * p_ + 8)], writes=[pB_], dsem=f"PK{p_}")
        proj_fm("wk", kst, 0, 1.0)
        after_k(j)
        for g in range(4):
            wv_, wB_ = cx.wget(("wv", g))
            for tb in range(4):
                pt, pB = cx.bank()
                for k in range(NCH):
                    P.op("pe", lambda e, k=k, tb=tb, wv_=wv_, pt=pt: e.matmul(pt[:], xb[:, k, tb * 128:(tb + 1) * 128], wv_[:, k, :],
                                                                              start=(k == 0), stop=(k == NCH - 1)),
                         reads=[wB_, xbB[k]], writes=[pB], accum=(k > 0))
                cells = [hB[32 + 4 * tb + q] for q in range(4)]
                if g % 2 == 0:
                    P.op("act", lambda e, g=g, tb=tb, pt=pt: e.activation(out=vst[:, tb, 512 * g:512 * (g + 1)], in_=pt[:], func=AF.Identity),
                         reads=[pB], writes=cells, accum=(g > 0))
                else:
                    P.op("dve", lambda e, g=g, tb=tb, pt=pt: e.tensor_copy(out=vst[:, tb, 512 * g:512 * (g + 1)], in_=pt[:]),
                         reads=[pB], writes=cells, accum=True)
        P.op("sp", lambda e, j=j: e.dma_start(out=vtm[T * j:T * (j + 1), :].rearrange("(b p) f -> p b f", p=128), in_=vst),
             reads=[hB[32 + q] for q in range(16)], writes=[vtB], accum=True, dsem="OV")
        for p_ in range(2):
            pap, pB_ = vpieces[j][p_]
            P.op("sp", lambda e, pap=pap, p_=p_: e.dma_start(out=pap.rearrange("(b p) f -> p b f", p=128), in_=vst[:, 2 * p_:2 * p_ + 2, :]),
                 reads=[hB[32 + q] for q in range(8 * p_, 8 * p_ + 8)], writes=[pB_], dsem=f"PV{p_}")
        after_v(j)
        proj_fm("wq", qst, 16, QSCALE)
    return bufs


BR = (1, 4, 16)


BIGN = 76 * 1024


def attention_and_layer1(cx, carve, off, a_bufs, x2T, qT, kTs, kTp, vts, vtp, bias_d, wo, w1, w2, oT, out, identb, identbB,
                         x2TB, qTB, kTsB, kTpB, vtsB, vtpB):
    P = cx.P
    off[0] = 0
    kwin = [carve(4096) for _ in range(2)]
    qn = [carve(2048) for _ in range(2)]
    kd = [[carve(4096) for _ in range(2)] for _ in range(2)]
    qd = [[carve(2048) for _ in range(2)] for _ in range(2)]
    vsl = [carve(32 * 256) for _ in range(2)]
    bia = [carve(9 * 128) for _ in range(2)]
    ptb = [carve(256) for _ in range(4)]
    oacc = [carve(4096).bitcast(F32) for _ in range(2)]
    dacc = [carve(4096).bitcast(F32) for _ in range(2)]
    ost = [carve(2048) for _ in range(2)]
    kwinB = [Buf("kwin0"), Buf("kwin1")]
    qnB = [Buf("qn0"), Buf("qn1")]
    kdB = [[Buf(f"kd{r}{h}") for h in range(2)] for r in range(2)]
    qdB = [[Buf(f"qd{r}{h}") for h in range(2)] for r in range(2)]
    vslB = [Buf(f"vsl{i}") for i in range(2)]
    biaB = [Buf("bia0"), Buf("bia1")]
    ptbB = [Buf(f"ptb{i}") for i in range(4)]
    oaccB = [Buf("oacc0"), Buf("oacc1")]
    daccB = [Buf("dacc0"), Buf("dacc1")]
    ostB = [Buf("ost0"), Buf("ost1")]
    oTdB = Buf("oT_dram")
    b1_bufs = kwinB + qnB + kdB[0] + kdB[1] + qdB[0] + qdB[1] + vslB + biaB + ptbB + oaccB + daccB + ostB
    P.op("dve", lambda e: e.memset(cx.dmy[:, 3:4], 0.0), reads=a_bufs, writes=a_bufs + b1_bufs)
    cx.nmain = 4
    cx.rot = 0
    accrot = 0
    vrot = 0
    drot = 0
    prot = 0
    LA = 2
    qTv = qT.rearrange("(h p) t -> h p t", p=128)
    kTsv = kTs.rearrange("(h p) t -> h p t", p=128)
    kTpv = kTp.rearrange("(h p) t -> h p t", p=128)
    for hp in range(8):
        for hh in range(2):
            hd = 2 * hp + hh
            P.op("sp", lambda e, hh=hh, hd=hd: e.dma_start(out=kwin[hh][:, 0:TOK], in_=kTpv[hd]), reads=[kTpB], writes=[kwinB[hh]], dsem=f"BK{hh}")
            P.op("sp", lambda e, hh=hh, hd=hd: e.dma_start(out=kwin[hh][:, TOK:2 * TOK], in_=kTsv[hd]), reads=[kTsB], writes=[kwinB[hh]],
                 accum=True, dsem=f"BK{hh}")
            P.op("sp", lambda e, hh=hh, hd=hd: e.dma_start(out=qn[hh], in_=qTv[hd]), reads=[qTB], writes=[qnB[hh]], dsem=f"BQ{hh}")
            P.op("pool", lambda e, hh=hh, hd=hd: e.dma_start(out=bia[hh], in_=bias_d[hd]), writes=[biaB[hh]], dsem=f"BB{hh}")
        for bi, d in enumerate(BR):
            Lw = 4096 // d
            Lq = 2048 // d
            nbr = 32 // d
            vs = vsl[vrot]
            vsB = vslB[vrot]
            vsem = f"BV{vrot}"
            vrot ^= 1
            vs3 = vs.rearrange("p (b f) -> p b f", f=256)
            cols = slice(256 * hp, 256 * hp + 256)
            lo = nbr // 2 - 1
            hb = nbr // 2
            for r in range(d):
                srcp = vtp.rearrange("(b i r) f -> r i b f", i=128, r=d)[r, :, hb - 1:hb, cols]
                P.op("sp", lambda e, vs3=vs3, srcp=srcp, r=r, lo=lo, nbr=nbr: e.dma_start(out=vs3[:, r * nbr + lo:r * nbr + lo + 1, :], in_=srcp),
                     reads=[vtpB], writes=[vsB], accum=(r > 0), dsem=vsem)
                srco = vts.rearrange("(b i r) f -> r i b f", i=128, r=d)[r, :, 0:hb, cols]
                P.op("sp", lambda e, vs3=vs3, srco=srco, r=r, hb=hb, nbr=nbr: e.dma_start(out=vs3[:, r * nbr + hb:r * nbr + nbr, :], in_=srco),
                     reads=[vtsB], writes=[vsB], accum=True, dsem=vsem)
            for hh in range(2):
                if d == 1:
                    kdv, kdvB = kwin[hh], kwinB[hh]
                    qdv, qdvB = qn[hh], qnB[hh]
                else:
                    kdv, kdvB = kd[drot][hh], kdB[drot][hh]
                    qdv, qdvB = qd[drot][hh], qdB[drot][hh]
                    P.op("pool", lambda e, hh=hh, kdv=kdv, d=d: e.tensor_copy(out=kdv.rearrange("p (r m) -> p r m", r=d),
                                                                            in_=kwin[hh].rearrange("p (m r) -> p r m", r=d)),
                         reads=[kwinB[hh]], writes=[kdvB])
                    P.op("pool", lambda e, hh=hh, qdv=qdv, d=d: e.tensor_copy(out=qdv.rearrange("p (r m) -> p r m", r=d),
                                                                            in_=qn[hh].rearrange("p (m r) -> p r m", r=d)),
                         reads=[qnB[hh]], writes=[qdvB])
                ov = oacc[hh] if d == 1 else oacc[hh].rearrange("p (m r) -> p r m", r=d)
                dv = dacc[hh] if d == 1 else dacc[hh].rearrange("p (m r) -> p r m", r=d)
                steps = [(g, nn) for g in range(4) for nn in range(4)]
                pts = {}
                accs = {}
                for s_ in range(len(steps) + LA):
                    if s_ < len(steps):
                        g, nn = steps[s_]
                        qpos = 512 * g + 128 * nn
                        r = qpos // Lq
                        n = (qpos % Lq) // 128
                        pst, pstB = cx.bank()
                        vblks = []
                        for part in range(2):
                            kpos = r * Lw + Lw // 2 + 128 * (n - 1 + part)
                            vblks.append(r * nbr + nbr // 2 + n - 1 + part)
                            bcol = (bi * 3 + (part if not (part == 0 and n == 0) else 2)) * 128
                            sq = pst[:, 128 * part:128 * (part + 1)]
                            P.op("pe", lambda e, sq=sq, kdv=kdv, qdv=qdv, kpos=kpos, qpos=qpos: e.matmul(sq, kdv[:, kpos:kpos + 128], qdv[:, qpos:qpos + 128],
                                                                                                       start=True, stop=False),
                                 reads=[kdvB, qdvB], writes=[pstB], accum=(part > 0))
                            P.op("pe", lambda e, sq=sq, hh=hh, bcol=bcol: e.matmul(sq, identb[:], bia[hh][:, bcol:bcol + 128], start=False, stop=True),
                                 reads=[identbB, biaB[hh]], writes=[pstB], accum=True)
                        pt_, ptB_ = ptb[prot], ptbB[prot]
                        prot = (prot + 1) % 4
                        P.op("act", lambda e, pst=pst, pt_=pt_: e.activation(out=pt_, in_=pst[:, 0:256], func=AF.Exp), reads=[pstB], writes=[ptB_])
                        pts[s_] = (pt_, ptB_, vblks)
                    t_ = s_ - LA
                    if t_ < 0:
                        continue
                    g, nn = steps[t_]
                    pt_, ptB_, vblks = pts.pop(t_)
                    if nn == 0:
                        accs[g] = (cx.ps[4 + 2 * accrot], cx.pb[4 + 2 * accrot], cx.ps[5 + 2 * accrot], cx.pb[5 + 2 * accrot])
                        accrot ^= 1
                    po, poB, pd, pdB = accs[g]
                    for part in range(2):
                        vblk = vblks[part]
                        ptp = pt_[:, 128 * part:128 * (part + 1)]
                        P.op("pe", lambda e, po=po, nn=nn, vs3=vs3, vblk=vblk, hh=hh, ptp=ptp, part=part: e.matmul(
                            po[:, 128 * nn:128 * (nn + 1)], vs3[:, vblk, 128 * hh:128 * (hh + 1)], ptp, start=(part == 0), stop=(part == 1)),
                             reads=[vsB, ptB_], writes=[poB], accum=not (nn == 0 and part == 0))
                        P.op("pe", lambda e, pd=pd, nn=nn, ptp=ptp, part=part: e.matmul(
                            pd[:, 128 * nn:128 * (nn + 1)], cx.ones[:], ptp, start=(part == 0), stop=(part == 1)),
                             reads=[cx.onesB, ptB_], writes=[pdB], accum=not (nn == 0 and part == 0))
                    if nn != 3:
                        continue
                    if d == 1:
                        oview = ov[:, 512 * g:512 * (g + 1)]
                        dview = dv[:, 512 * g:512 * (g + 1)]
                        pov, pdv = po[:, :], pd[:, :]
                    elif d == 4:
                        oview = ov[:, g, :]
                        dview = dv[:, g, :]
                        pov, pdv = po[:, :], pd[:, :]
                    else:
                        oview = ov[:, 4 * g:4 * g + 4, :]
                        dview = dv[:, 4 * g:4 * g + 4, :]
                        pov = po[:, :].rearrange("p (r m) -> p r m", r=4)
                        pdv = pd[:, :].rearrange("p (r m) -> p r m", r=4)
                    if bi == 0:
                        P.op("dve", lambda e, oview=oview, pov=pov: e.tensor_copy(out=oview, in_=pov), reads=[poB], writes=[oaccB[hh]], accum=(g > 0))
                        P.op("dve", lambda e, dview=dview, pdv=pdv: e.tensor_copy(out=dview, in_=pdv), reads=[pdB], writes=[daccB[hh]], accum=(g > 0))
                    else:
                        P.op("dve", lambda e, oview=oview, pov=pov: e.tensor_tensor(out=oview, in0=pov, in1=oview, op=ALU.add),
                             reads=[poB, oaccB[hh]], writes=[oaccB[hh]])
                        P.op("dve", lambda e, dview=dview, pdv=pdv: e.tensor_tensor(out=dview, in0=pdv, in1=dview, op=ALU.add),
                             reads=[pdB, daccB[hh]], writes=[daccB[hh]])
            if d != 1:
                drot ^= 1
        for hh in range(2):
            hd = 2 * hp + hh
            P.op("dve", lambda e, hh=hh: e.reciprocal(out=dacc[hh], in_=dacc[hh]), reads=[daccB[hh]], writes=[daccB[hh]])
            P.op("dve", lambda e, hh=hh: e.tensor_tensor(out=ost[hh], in0=oacc[hh], in1=dacc[hh], op=ALU.mult),
                 reads=[oaccB[hh], daccB[hh]], writes=[ostB[hh]])
            P.op("sp", lambda e, hh=hh, hd=hd: e.dma_start(out=oT[128 * hd:128 * (hd + 1), :], in_=ost[hh]),
                 reads=[ostB[hh]], writes=[oTdB], accum=True, dsem=f"BO{hh}")

    TB, NH = 1024, 2
    cx.nmain = 4
    cx.rot = 0
    off[0] = 0
    xf = carve(2 * NCH * TB).bitcast(F32).rearrange("p (c t) -> p c t", c=NCH)
    xb = carve(NCH * TB).rearrange("p (c t) -> p c t", c=NCH)
    hq = carve(16 * TB).rearrange("p (c t) -> p c t", c=16)
    sg = [carve(2 * T).bitcast(F32) for _ in range(2)]

    def ln_alloc(name, n, dt):
        return carve(n) if dt == BF16 else carve(2 * n).bitcast(F32)

    lns = [LN(cx, alloc=ln_alloc, banks=(4, 5)), LN(cx, alloc=ln_alloc, banks=(6, 7))]
    xfB = [[Buf(f"xf{c}_{hf}") for hf in range(NH)] for c in range(NCH)]
    xbB = [[Buf(f"xb{c}_{hf}") for hf in range(NH)] for c in range(NCH)]
    hqB = [[Buf(f"hq{c}_{hf}") for hf in range(NH)] for c in range(16)]
    sgB = [Buf("sg0"), Buf("sg1")]
    xfB_flat = [b_ for l_ in xfB for b_ in l_]
    xbB_flat = [b_ for l_ in xbB for b_ in l_]
    vecs = cx.vecs

    def hs(hf):
        return slice(T * hf, T * (hf + 1))

    def post_ln2(gcol, bcol):
        for hf in range(NH):
            ln = lns[hf]
            ln.finalize()
            for c in range(NCH):
                src = xf[:, c, hs(hf)]
                ln.norm_chunk(src, xfB[c][hf])
                P.op("act", lambda e, c=c, src=src: e.activation(out=src, in_=src, func=AF.Identity,
                                                                 bias=vecs[:, bcol + c:bcol + c + 1], scale=vecs[:, gcol + c:gcol + c + 1]),
                     reads=[xfB[c][hf]], writes=[xfB[c][hf]])
                P.op("act", lambda e, c=c, hf=hf, src=src: e.activation(out=xb[:, c, hs(hf)], in_=src, func=AF.Identity),
                     reads=[xfB[c][hf]], writes=[xbB[c][hf]])

    w2v = w2.rearrange("(q c p) n -> q p c n", p=128, c=16)
    for jj in range(TOK // TB):
        for g in range(4):
            cx.wadd(("wo", jj, g), wsrc_k2048(wo, 512 * g), 16)
        for q_ in range(4):
            for g in range(4):
                cx.wadd(("w1", jj, q_, g), wsrc_k2048(w1, 2048 * q_ + 512 * g), 16)
            for g in range(4):
                cx.wadd(("w2", jj, q_, g), w2v[q_][:, :, 512 * g:512 * (g + 1)], 16)

    for jj in range(TOK // TB):
        tsl = slice(TB * jj, TB * (jj + 1))
        P.op("sp", lambda e, tsl=tsl: e.dma_start(out=xf, in_=x2T.rearrange("(c p) t -> p c t", p=128)[:, :, tsl]),
             reads=[x2TB], writes=(xfB_flat + b1_bufs if jj == 0 else xfB_flat), dsem="B2X")
        P.op("sp", lambda e, tsl=tsl: e.dma_start(out=xb, in_=oT.rearrange("(c p) t -> p c t", p=128)[:, :, tsl]),
             reads=[oTdB], writes=xbB_flat, dsem="B2O")
        for g in range(4):
            wv, wB = cx.wget(("wo", jj, g))
            for i in range(4):
                dc = 4 * g + i
                for hf in range(NH):
                    pt, pB = cx.bank()
                    for k in range(NCH):
                        P.op("pe", lambda e, k=k, i=i, hf=hf, wv=wv, pt=pt: e.matmul(pt[:], wv[:, k, i * 128:(i + 1) * 128], xb[:, k, hs(hf)],
                                                                                     start=(k == 0), stop=(k == NCH - 1)),
                             reads=[wB, xbB[k][hf]], writes=[pB], accum=(k > 0))
                    P.op("dve", lambda e, dc=dc, hf=hf, pt=pt: e.scalar_tensor_tensor(out=xf[:, dc, hs(hf)], in0=xf[:, dc, hs(hf)], scalar=ALPHA,
                                                                                     in1=pt[:], op0=ALU.mult, op1=ALU.add),
                         reads=[pB, xfB[dc][hf]], writes=[xfB[dc][hf]])
                    lns[hf].stats_chunk(dc, xf[:, dc, hs(hf)], xfB[dc][hf])
        post_ln2(V_MIXG + 16, V_MIXB + 16)
        rr = 0
        for q_ in range(4):
            for g in range(4):
                wv, wB = cx.wget(("w1", jj, q_, g))
                for i in range(4):
                    hc = 4 * g + i
                    for hf in range(NH):
                        pt, pB = cx.bank()
                        for k in range(NCH):
                            P.op("pe", lambda e, k=k, i=i, hf=hf, wv=wv, pt=pt: e.matmul(pt[:], wv[:, k, i * 128:(i + 1) * 128], xb[:, k, hs(hf)],
                                                                                         start=(k == 0), stop=(k == NCH - 1)),
                                 reads=[wB, xbB[k][hf]], writes=[pB], accum=(k > 0))
                        r, rb = sg[rr], sgB[rr]
                        rr ^= 1
                        P.op("act", lambda e, pt=pt, r=r: e.activation(out=r, in_=pt[:], func=AF.Relu), reads=[pB], writes=[rb])
                        P.op("dve", lambda e, hc=hc, hf=hf, r=r: e.tensor_tensor(out=hq[:, hc, hs(hf)], in0=r, in1=r, op=ALU.mult),
                             reads=[rb], writes=[hqB[hc][hf]])
            for g in range(4):
                wv, wB = cx.wget(("w2", jj, q_, g))
                for i in range(4):
                    dc = 4 * g + i
                    for hf in range(NH):
                        pt, pB = cx.bank()
                        for k in range(16):
                            P.op("pe", lambda e, k=k, i=i, hf=hf, wv=wv, pt=pt: e.matmul(pt[:], wv[:, k, i * 128:(i + 1) * 128], hq[:, k, hs(hf)],
                                                                                         start=(k == 0), stop=(k == 15)),
                                 reads=[wB, hqB[k][hf]], writes=[pB], accum=(k > 0))
                        if q_ == 0:
                            P.op("dve", lambda e, dc=dc, hf=hf, pt=pt: e.scalar_tensor_tensor(out=xf[:, dc, hs(hf)], in0=xf[:, dc, hs(hf)], scalar=ALPHA,
                                                                                             in1=pt[:], op0=ALU.mult, op1=ALU.add),
                                 reads=[pB, xfB[dc][hf]], writes=[xfB[dc][hf]])
                        else:
                            P.op("dve", lambda e, dc=dc, hf=hf, pt=pt: e.tensor_tensor(out=xf[:, dc, hs(hf)], in0=pt[:], in1=xf[:, dc, hs(hf)], op=ALU.add),
                                 reads=[pB, xfB[dc][hf]], writes=[xfB[dc][hf]])
                        if q_ == 3:
                            lns[hf].stats_chunk(dc, xf[:, dc, hs(hf)], xfB[dc][hf])
        post_ln2(V_MLPG + 16, V_MLPB + 16)
        P.op("sp", lambda e, tsl=tsl: e.dma_start(out=out.rearrange("(c p) t -> p c t", p=128)[:, :, tsl], in_=xf),
             reads=xfB_flat, dsem="OUT")


def _fm(v):
    return np.ascontiguousarray(v.reshape(-1, 128).T)


def _t5_bucket(dist):
    max_exact = 16
    large = max_exact + (np.log(np.maximum(dist, 1) / max_exact) / math.log(2048 / max_exact) * (32 - max_exact)).astype(np.int32)
    large = np.minimum(large, 31)
    return np.where(dist < max_exact, dist, large).astype(np.int32)


def _bias_tiles(rel_bias, has_prev):
    i = np.arange(128)[:, None]
    j = np.arange(256)[None, :]
    delta = i - j + 128
    ok = (delta >= 0) & (delta <= 128)
    res = np.full((NHEAD, 128, 9, 128), NEG, np.float32)
    for bi, d in enumerate(BR):
        bucket = _t5_bucket(np.clip(delta, 0, None) * d)
        b = rel_bias[bucket]
        b = np.where(ok[:, :, None], b, np.float32(NEG))
        bt = np.transpose(b, (2, 1, 0))
        res[:, :, bi * 3 + 0, :] = bt[:, 0:128, :]
        res[:, :, bi * 3 + 1, :] = bt[:, 128:256, :]
        if has_prev:
            res[:, :, bi * 3 + 2, :] = bt[:, 0:128, :]
    return np.ascontiguousarray(res.reshape(NHEAD, 128, 9 * 128))


_NC_CACHE = {}


def build_fused():
    nc = bass.Bass("TRN2", target_bir_lowering=False)
    I32 = mybir.dt.int32
    xc = nc.dram_tensor("xc", [D, XH + TOK], F32, kind="ExternalInput").ap()
    hmask_d = nc.dram_tensor("hmask", [128, 1], F32, kind="ExternalInput").ap()
    prow_d = nc.dram_tensor("prevrow", [1, 1], I32, kind="ExternalInput").ap()
    vecs_d = nc.dram_tensor("vecs", [128, NV], F32, kind="ExternalInput").ap()
    ident_d = nc.dram_tensor("ident", [128, 128], F32, kind="ExternalInput").ap()
    bias_d = nc.dram_tensor("biasT", [NHEAD, 128, 9 * 128], F32, kind="ExternalInput").ap()
    pw1 = nc.dram_tensor("pw1", [D, 2 * D], F32, kind="ExternalInput").ap()
    pw2 = nc.dram_tensor("pw2", [D, D], F32, kind="ExternalInput").ap()
    w1a = nc.dram_tensor("w1a", [D, DFF], F32, kind="ExternalInput").ap()
    w2a = nc.dram_tensor("w2a", [DFF, D], F32, kind="ExternalInput").ap()
    w1b = nc.dram_tensor("w1b", [D, DFF], F32, kind="ExternalInput").ap()
    w2b = nc.dram_tensor("w2b", [DFF, D], F32, kind="ExternalInput").ap()
    wkv = nc.dram_tensor("wkv", [D, 2 * D], F32, kind="ExternalInput").ap()
    wq = nc.dram_tensor("wq", [D, D], F32, kind="ExternalInput").ap()
    wo = nc.dram_tensor("wo", [D, D], F32, kind="ExternalInput").ap()
    out = nc.dram_tensor("out", [D, TOK], F32, kind="ExternalOutput").ap()
    x2T = nc.dram_tensor("x2T", [D, TOK], F32).ap()
    qT = nc.dram_tensor("qT", [D, TOK], BF16).ap()
    kTs_t = nc.dram_tensor("kTs", [D, TOK], BF16)
    vts_t = nc.dram_tensor("vts", [TOK, D], BF16)
    kps = [[nc.dram_tensor(f"kps{j}{p}", [D // 2, T], BF16) for p in range(2)] for j in range(NT)]
    kpa = [[nc.dram_tensor(f"kpa{j}{p}", [4 * (D // 2), T], BF16) for p in range(2)] for j in range(NT)]
    vps = [[nc.dram_tensor(f"vps{j}{p}", [T // 2, D], BF16) for p in range(2)] for j in range(NT)]
    vpa = [[nc.dram_tensor(f"vpa{j}{p}", [4 * (T // 2), D], BF16) for p in range(2)] for j in range(NT)]
    kTp = nc.dram_tensor("kTp", [D, TOK], BF16).ap()
    vtp = nc.dram_tensor("vtp", [TOK, D], BF16).ap()
    oT = nc.dram_tensor("oT", [D, TOK], BF16).ap()
    kTs, vts = kTs_t.ap(), vts_t.ap()

    with contextlib.ExitStack() as st:
        cx = Ctx(nc, st, nslots=3)
        cx.wcache = nc.dram_tensor("wcache", [56, 128, 8192], BF16).ap()
        P = cx.P
        st.enter_context(nc.allow_low_precision("bf16 matmul operands, fp32 accumulation"))
        load_consts(cx, vecs_d, ident_d)
        identb = cx.sb("identb", [128, 128], BF16)
        identbB = Buf("identb")
        P.op("act", lambda e: e.activation(out=identb[:], in_=cx.ident[:], func=AF.Identity), reads=[cx.identB], writes=[identbB])
        cx.identb = identb
        P.op("dve", lambda e: e.tensor_copy(out=cx.dmy[:, 3:4], in_=identb[:, 0:1]), reads=[identbB], writes=[Buf("dmy_d2")])
        big = cx.sb("big", [128, BIGN], BF16)
        off = [0]

        def carve(n):
            n = (n + 15) // 16 * 16
            o = off[0]
            off[0] += n
            assert off[0] <= BIGN, off[0]
            return big[:, o:o + n]

        def alloc(name, n, dt):
            return carve(n) if dt == BF16 else carve(2 * n).bitcast(F32)[:, 0:n]

        x2TB, qTB, kTsB, vtsB, kTallB, vallB, kTpB, vtpB = (Buf(n_) for n_ in ("x2T", "qT", "kTs", "vts", "kTall", "vall", "kTp", "vtp"))
        groups = [[0, 1, 2, 3], [4, 5, 6, 7]]
        kpieces = [[(kps[j][p].ap(), Buf(f"kps{j}{p}")) for p in range(2)] for j in range(NT)]
        vpieces = [[(vps[j][p].ap(), Buf(f"vps{j}{p}")) for p in range(2)] for j in range(NT)]

        def after_k(j):
            for p in range(2):
                cx.cc_pending.append((lambda e, j=j, p=p: e.collective_compute("AllGather", ALU.bypass, replica_groups=groups,
                                                                              ins=[kps[j][p].ap().opt()], outs=[kpa[j][p].ap().opt()]),
                                      [kpieces[j][p][1]], [kTallB]))

        def after_v(j):
            for p in range(2):
                cx.cc_pending.append((lambda e, j=j, p=p: e.collective_compute("AllGather", ALU.bypass, replica_groups=groups,
                                                                              ins=[vps[j][p].ap().opt()], outs=[vpa[j][p].ap().opt()]),
                                      [vpieces[j][p][1]], [vallB]))
            if j == NT - 1:
                cx.cc_flush()

        a_bufs = layer0_and_qkv(cx, alloc, xc, hmask_d, pw1, pw2, w1a, w2a, wkv, wq, x2T, qT, kTs, vts, x2TB, qTB, kTsB, vtsB,
                                kpieces, vpieces, after_k, after_v)

        reg = st.enter_context(nc.sync.register("prevrow"))

        pval = []

        def copy_prev(dst, src_t, rows):
            def fn(e):
                if not pval:
                    e.reg_load(reg, prow_d[0:1, 0:1])
                    pval.append(e.snap(reg, min_val=0, max_val=3))
                return e.dma_start(out=dst, in_=src_t.ap()[bass.ts(pval[0], rows), :])
            return fn

        for j in range(NT):
            for p in range(2):
                P.op("sp", copy_prev(kTp[(D // 2) * p:(D // 2) * (p + 1), T * j:T * (j + 1)], kpa[j][p], D // 2),
                     reads=[kTallB], writes=[kTpB], accum=True, dsem="XK")
                P.op("sp", copy_prev(vtp[T * j + (T // 2) * p:T * j + (T // 2) * (p + 1), :], vpa[j][p], T // 2),
                     reads=[vallB], writes=[vtpB], accum=True, dsem="XV")

        attention_and_layer1(cx, carve, off, a_bufs, x2T, qT, kTs, kTp, vts, vtp, bias_d, wo, w1b, w2b, oT, out, identb, identbB,
                             x2TB, qTB, kTsB, kTpB, vtsB, vtpB)
        assert cx.wnext_get == len(cx.witems)
        import os
        if os.environ.get("KDEBUG"):
            for nm, ap_, shp, dt_, rb in (("dbg_kTp", kTp, [D, TOK], BF16, kTpB), ("dbg_kTs", kTs, [D, TOK], BF16, kTsB),
                                          ("dbg_vtp", vtp, [TOK, D], BF16, vtpB), ("dbg_vts", vts, [TOK, D], BF16, vtsB),
                                          ("dbg_oT", oT, [D, TOK], BF16, None), ("dbg_x2T", x2T, [D, TOK], F32, x2TB)):
                dd = nc.dram_tensor(nm, shp, dt_, kind="ExternalOutput").ap()
                P.op("sp", lambda e, dd=dd, ap_=ap_: e.dma_start(out=dd, in_=ap_), reads=([rb] if rb is not None else []), dsem="DBG")
        P.emit()
    return nc


def kernel(x, conv_pw1_w, conv_pw1_b, conv_dw_w, conv_dw_b, conv_ln_g, conv_ln_b, conv_pw2_w, conv_pw2_b,
           w_kv, attn_wq, attn_wo, rel_bias, mlp_w1, mlp_w2, ln_mix_g, ln_mix_b, ln_mlp_g, ln_mlp_b):
    f32 = np.float32
    x = np.asarray(x, f32)
    ncore = 8
    vecs = np.zeros((128, NV), f32)
    vecs[:, V_PW1B:V_PW1B + 32] = _fm(np.asarray(conv_pw1_b, f32)[0])
    dw = np.asarray(conv_dw_w, f32)[0]
    for tap in range(CONVW):
        vecs[:, V_DWW + 16 * tap:V_DWW + 16 * tap + 16] = _fm(dw[tap])
    vecs[:, V_DWB:V_DWB + 16] = _fm(np.asarray(conv_dw_b, f32)[0])
    vecs[:, V_CLNG:V_CLNG + 16] = _fm(np.asarray(conv_ln_g, f32)[0])
    vecs[:, V_CLNB:V_CLNB + 16] = _fm(np.asarray(conv_ln_b, f32)[0])
    vecs[:, V_PW2B:V_PW2B + 16] = _fm(np.asarray(conv_pw2_b, f32)[0])
    for l in range(2):
        vecs[:, V_MIXG + 16 * l:V_MIXG + 16 * l + 16] = _fm(np.asarray(ln_mix_g, f32)[l])
        vecs[:, V_MIXB + 16 * l:V_MIXB + 16 * l + 16] = _fm(np.asarray(ln_mix_b, f32)[l])
        vecs[:, V_MLPG + 16 * l:V_MLPG + 16 * l + 16] = _fm(np.asarray(ln_mlp_g, f32)[l])
        vecs[:, V_MLPB + 16 * l:V_MLPB + 16 * l + 16] = _fm(np.asarray(ln_mlp_b, f32)[l])
    ident = np.eye(128, dtype=f32)
    w1 = np.asarray(mlp_w1, f32)
    w2 = np.asarray(mlp_w2, f32)
    rel_bias = np.asarray(rel_bias, f32)
    shared = {
        "vecs": vecs, "ident": ident,
        "pw1": np.ascontiguousarray(np.asarray(conv_pw1_w, f32)[0]),
        "pw2": np.ascontiguousarray(np.asarray(conv_pw2_w, f32)[0]),
        "w1a": np.ascontiguousarray(w1[0]), "w2a": np.ascontiguousarray(w2[0]),
        "w1b": np.ascontiguousarray(w1[1]), "w2b": np.ascontiguousarray(w2[1]),
        "wkv": np.ascontiguousarray(np.asarray(w_kv, f32)),
        "wq": np.ascontiguousarray(np.asarray(attn_wq, f32)[0]),
        "wo": np.ascontiguousarray(np.asarray(attn_wo, f32)[0]),
    }
    bias_tiles = {True: _bias_tiles(rel_bias, True), False: _bias_tiles(rel_bias, False)}
    if "f" not in _NC_CACHE:
        _NC_CACHE["f"] = build_fused()
    ncf = _NC_CACHE["f"]
    in_maps = []
    for c in range(ncore):
        b, q = divmod(c, 4)
        xc = np.zeros((D, XH + TOK), f32)
        xc[:, XH:] = x[b, q * TOK:(q + 1) * TOK].T
        if q > 0:
            xc[:, :XH] = x[b, q * TOK - XH:q * TOK].T
        m = dict(shared)
        m.update({"xc": xc, "hmask": np.full((128, 1), 1.0 if q > 0 else 0.0, f32),
                  "prevrow": np.array([[max(q - 1, 0)]], np.int32), "biasT": bias_tiles[q > 0]})
        in_maps.append(m)
    res = run_bass_kernel_spmd(ncf, in_maps, core_ids=list(range(ncore))).results
    _NC_CACHE["last"] = res
    outp = np.zeros((2, 8192, D), f32)
    for c in range(ncore):
        b, q = divmod(c, 4)
        outp[b, q * TOK:(q + 1) * TOK] = np.asarray(res[c]["out"]).T
    return outp
```

```python
import contextlib
import math
import numpy as np
import ml_dtypes
import concourse.bass as bass
import concourse.mybir as mybir
from concourse.bass_utils import run_bass_kernel_spmd

F32 = mybir.dt.float32
BF16 = mybir.dt.bfloat16
ALU = mybir.AluOpType
AF = mybir.ActivationFunctionType

D = 2048
NCH = 16
DFF = 8192
T = 512
NT = 4
TOK = 2048
HALO = 32
XH = 128
NHEAD = 16
ALPHA = float(4 ** 0.25)
EPS = 1e-5
QSCALE = float(128 ** -0.5)
NEG = -30000.0
CONVW = 31

V_PW1B = 0
V_DWW = V_PW1B + 32
V_DWB = V_DWW + CONVW * 16
V_CLNG = V_DWB + 16
V_CLNB = V_CLNG + 16
V_PW2B = V_CLNB + 16
V_MIXG = V_PW2B + 16
V_MIXB = V_MIXG + 32
V_MLPG = V_MIXB + 32
V_MLPB = V_MLPG + 32
NV = V_MLPB + 32


class Buf:
    __slots__ = ("name", "writers", "readers")

    def __init__(self, name):
        self.name = name
        self.writers = []
        self.readers = []


class Op:
    __slots__ = ("eng", "fn", "deps", "dsem", "sig", "val", "waits", "inc")

    def __init__(self, eng, fn, dsem, inc=None):
        self.eng = eng
        self.fn = fn
        self.deps = []
        self.dsem = dsem
        self.inc = inc if inc is not None else (16 if dsem is not None else 1)
        self.sig = False
        self.val = 0
        self.waits = None


class Prog:
    ENGS = ("pe", "act", "dve", "pool", "sp")

    def __init__(self, nc):
        self.nc = nc
        self.ops = {e: [] for e in self.ENGS}

    def op(self, eng, fn, reads=(), writes=(), accum=False, dsem=None, inc=None):
        o = Op(eng, fn, dsem, inc)
        deps = o.deps
        for b in reads:
            deps.extend(b.writers)
        for b in writes:
            if not accum:
                deps.extend(b.writers)
            deps.extend(b.readers)
        for b in writes:
            if accum:
                b.writers.append(o)
            else:
                b.writers = [o]
            b.readers = []
        for b in reads:
            b.readers.append(o)
        self.ops[eng].append(o)
        return o

    def emit(self, final_wait_eng="sp"):
        nc = self.nc
        for e in self.ENGS:
            for o in self.ops[e]:
                if o.dsem is not None:
                    o.sig = True
                nd = []
                for d in o.deps:
                    if d.eng == "pe" and e == "pe" and d.dsem is None:
                        continue
                    d.sig = True
                    nd.append(d)
                o.deps = nd
        counts = {}
        sem_names = []
        for e in self.ENGS:
            for o in self.ops[e]:
                if not o.sig:
                    continue
                key = o.dsem if o.dsem is not None else "E_" + e
                counts[key] = counts.get(key, 0) + o.inc
                o.val = counts[key]
                if key not in sem_names:
                    sem_names.append(key)
        with contextlib.ExitStack() as st:
            sems = {k: st.enter_context(nc.semaphore(k)) for k in sem_names}
            for e in self.ENGS:
                seen = {}
                for o in self.ops[e]:
                    need = {}
                    for d in o.deps:
                        key = d.dsem if d.dsem is not None else "E_" + d.eng
                        if d.val > need.get(key, 0):
                            need[key] = d.val
                    w = []
                    for k, v in need.items():
                        if seen.get(k, 0) < v:
                            seen[k] = v
                            w.append((k, v))
                    o.waits = w
            block = st.enter_context(nc.Block())
            ops = self.ops
            final = [(k, v) for k, v in counts.items() if not k.startswith("E_")]

            def run(eh, ename):
                for o in ops[ename]:
                    for k, v in o.waits:
                        eh.wait_ge(sems[k], v)
                    ins = o.fn(eh)
                    if o.sig:
                        key = o.dsem if o.dsem is not None else "E_" + ename
                        ins.then_inc(sems[key], o.inc)
                if ename == final_wait_eng:
                    for k, v in final:
                        eh.wait_ge(sems[k], v)

            @block.tensor
            def _(eh):
                run(eh, "pe")

            @block.scalar
            def _(eh):
                run(eh, "act")

            @block.vector
            def _(eh):
                run(eh, "dve")

            @block.gpsimd
            def _(eh):
                run(eh, "pool")

            @block.sync
            def _(eh):
                run(eh, "sp")


class Ctx:
    def __init__(self, nc, st, nslots, keep_prev=False):
        self.keep_prev = keep_prev
        self.nc = nc
        self.st = st
        self.P = Prog(nc)
        self.ps = [st.enter_context(nc.psum_tensor(f"ps{i}", [128, 512], F32)) for i in range(8)]
        self.pb = [Buf(f"ps{i}") for i in range(8)]
        self.rot = 0
        self.nmain = 6
        self.nslots = nslots
        self.wslots = [st.enter_context(nc.sbuf_tensor(f"wsl{i}", [128, 8192], BF16)) for i in range(nslots)]
        self.wbufs = [Buf(f"wsl{i}") for i in range(nslots)]
        self.witems = []
        self.wnext_load = 0
        self.wnext_get = 0
        self.uid = 0
        self.wcache = None
        self.wcache_idx = {}
        self.wcacheB = {}
        self.cc_pending = []
        self.cc_since = 0
        self.cc_gap = 14
        self.ccB = Buf("cc_order")
        self.cc_n = 0

    def sb(self, name, shape, dt):
        return self.st.enter_context(self.nc.sbuf_tensor("sb_" + name, shape, dt))

    def bank(self):
        i = self.rot
        self.rot = (self.rot + 1) % self.nmain
        return self.ps[i], self.pb[i]

    def wadd(self, key, src, c, cache=None, store=True):
        self.witems.append((key, src, c, cache, store))

    def _wload(self, i):
        key, src, c, cache, store = self.witems[i]
        s = i % self.nslots
        if cache is not None and cache in self.wcacheB:
            ci = self.wcache_idx[cache]
            self.P.op("sp", lambda e: e.dma_start(out=self.wslots[s][:, :], in_=self.wcache[ci]),
                      reads=[self.wcacheB[cache]], writes=[self.wbufs[s]], dsem=f"WH{s}")
            return
        self._wload_src(i, s, src, c)
        if cache is not None and store:
            ci = self.wcache_idx.setdefault(cache, len(self.wcache_idx))
            self.wcacheB[cache] = Buf(f"wcache{ci}")
            self.P.op("sp", lambda e: e.dma_start(out=self.wcache[ci], in_=self.wslots[s][:, :]),
                      reads=[self.wbufs[s]], writes=[self.wcacheB[cache]], dsem=f"WC{s}")

    def _wload_src(self, i, s, src, c):
        dst = self.wslots[s][:, :].rearrange("p (c n) -> p c n", c=c)
        if not isinstance(src, list):
            self.P.op("pool", lambda e: e.dma_start(out=dst, in_=src), writes=[self.wbufs[s]], dsem=f"W{s}")
            return
        for pi, (lo, hi, sv) in enumerate(src):
            self.P.op("pool", lambda e, lo=lo, hi=hi, sv=sv: e.dma_start(out=dst[:, :, lo:hi], in_=sv),
                      writes=[self.wbufs[s]], accum=(pi > 0), dsem=f"W{s}")

    def cc_issue(self):
        fn, reads, writes = self.cc_pending.pop(0)
        self.cc_n += 1
        self.P.op("pool", fn, reads=list(reads) + [self.ccB], writes=list(writes) + [self.ccB], accum=True, dsem=f"CC{self.cc_n % 4}", inc=1)
        self.cc_since = 0

    def cc_flush(self):
        while self.cc_pending:
            self.cc_issue()

    def wget(self, key):
        self.cc_since += 1
        if self.cc_pending and self.cc_since >= self.cc_gap:
            self.cc_issue()
        i = self.wnext_get
        self.wnext_get += 1
        assert self.witems[i][0] == key, (self.witems[i][0], key)
        lim = min(i + self.nslots - (1 if self.keep_prev else 0), len(self.witems))
        while self.wnext_load < lim:
            self._wload(self.wnext_load)
            self.wnext_load += 1
        s = i % self.nslots
        c = self.witems[i][2]
        return self.wslots[s][:, :].rearrange("p (c n) -> p c n", c=c), self.wbufs[s]


def wsrc_k2048(w, col0):
    return w.rearrange("(c p) n -> p c n", p=128)[:, :, col0:col0 + 512]


def wsrc_k8192(w, col0):
    return w.rearrange("(c p) n -> p c n", p=128)[:, :, col0:col0 + 128]


class LN:
    def __init__(self, cx, alloc=None, banks=(6, 7)):
        self.cx = cx
        self.b1, self.b2 = banks
        if alloc is None:
            alloc = lambda name, n, dt: cx.sb(name, [128, n], dt)[:, :]
        self.zb = [alloc(f"ln_zb{i}", T, BF16) for i in range(2)]
        self.z2b = [alloc(f"ln_z2b{i}", T, BF16) for i in range(2)]
        self.zbB = [Buf(f"ln_zb{i}") for i in range(2)]
        self.z2bB = [Buf(f"ln_z2b{i}") for i in range(2)]
        self.mean = alloc("ln_mean", T, F32)
        self.var = alloc("ln_var", T, F32)
        self.rstd = alloc("ln_rstd", T, F32)
        self.meanB, self.varB, self.rstdB = Buf("ln_mean"), Buf("ln_var"), Buf("ln_rstd")
        self.r = 0

    def stats_chunk(self, c, src, srcB):
        cx, P = self.cx, self.cx.P
        i = self.r
        self.r ^= 1
        zb, z2b = self.zb[i], self.z2b[i]
        P.op("act", lambda e: e.activation(out=zb, in_=src, func=AF.Identity), reads=[srcB], writes=[self.zbB[i]])
        P.op("act", lambda e: e.activation(out=z2b, in_=src, func=AF.Square), reads=[srcB], writes=[self.z2bB[i]])
        ones = cx.ones
        P.op("pe", lambda e: e.matmul(cx.ps[self.b1][:], ones[:], zb, start=(c == 0), stop=(c == NCH - 1)),
             reads=[self.zbB[i], cx.onesB], writes=[cx.pb[self.b1]], accum=(c > 0))
        P.op("pe", lambda e: e.matmul(cx.ps[self.b2][:], ones[:], z2b, start=(c == 0), stop=(c == NCH - 1)),
             reads=[self.z2bB[i], cx.onesB], writes=[cx.pb[self.b2]], accum=(c > 0))

    def finalize(self):
        cx, P = self.cx, self.cx.P
        mean, var, rstd = self.mean, self.var, self.rstd
        P.op("dve", lambda e: e.tensor_scalar(out=mean, in0=cx.ps[self.b1][:], scalar1=1.0 / D, scalar2=None, op0=ALU.mult),
             reads=[cx.pb[self.b1]], writes=[self.meanB])
        P.op("dve", lambda e: e.tensor_tensor(out=var, in0=mean, in1=mean, op=ALU.mult),
             reads=[self.meanB], writes=[self.varB])
        P.op("dve", lambda e: e.scalar_tensor_tensor(out=var, in0=cx.ps[self.b2][:], scalar=1.0 / D, in1=var,
                                                     op0=ALU.mult, op1=ALU.subtract),
             reads=[cx.pb[self.b2], self.varB], writes=[self.varB])
        P.op("act", lambda e: e.activation(out=var, in_=var, func=AF.Sqrt, bias=cx.epsc[:, 0:1]),
             reads=[self.varB], writes=[self.varB])
        P.op("dve", lambda e: e.reciprocal(out=rstd, in_=var), reads=[self.varB], writes=[self.rstdB])
        P.op("dve", lambda e: e.scalar_tensor_tensor(out=mean, in0=mean, scalar=-1.0, in1=rstd,
                                                     op0=ALU.mult, op1=ALU.mult),
             reads=[self.meanB, self.rstdB], writes=[self.meanB])

    def norm_chunk(self, src, srcB):
        P = self.cx.P
        rstd, mean = self.rstd, self.mean
        P.op("dve", lambda e: e.tensor_tensor(out=src, in0=src, in1=rstd, op=ALU.mult),
             reads=[srcB, self.rstdB], writes=[srcB])
        P.op("dve", lambda e: e.tensor_tensor(out=src, in0=src, in1=mean, op=ALU.add),
             reads=[srcB, self.meanB], writes=[srcB])


def post_ln(cx, ln, xf, xfB, xb, xbB, gcol, bcol):
    P = cx.P
    vecs = cx.vecs
    ln.finalize()
    for c in range(NCH):
        src = xf[:, c, :]
        ln.norm_chunk(src, xfB[c])
        P.op("act", lambda e, c=c, src=src: e.activation(out=src, in_=src, func=AF.Identity,
                                                         bias=vecs[:, bcol + c:bcol + c + 1],
                                                         scale=vecs[:, gcol + c:gcol + c + 1]),
             reads=[xfB[c]], writes=[xfB[c]])
        P.op("act", lambda e, c=c, src=src: e.activation(out=xb[:, c, :], in_=src, func=AF.Identity),
             reads=[xfB[c]], writes=[xbB[c]])


def mlp_sublayer(cx, ln, layer, xf, xfB, xb, xbB, h, hB, rbuf, rB):
    P = cx.P
    rr = 0
    for g in range(16):
        wv, wB = cx.wget(("w1", layer, g))
        for i in range(4):
            hc = 4 * g + i
            pt, pB = cx.bank()
            for k in range(NCH):
                P.op("pe", lambda e, k=k, i=i, wv=wv, pt=pt: e.matmul(pt[:], wv[:, k, i * 128:(i + 1) * 128], xb[:, k, :],
                                                                      start=(k == 0), stop=(k == NCH - 1)),
                     reads=[wB, xbB[k]], writes=[pB], accum=(k > 0))
            r, rb = rbuf[rr], rB[rr]
            rr ^= 1
            P.op("act", lambda e, pt=pt, r=r: e.activation(out=r[:], in_=pt[:], func=AF.Relu), reads=[pB], writes=[rb])
            P.op("dve", lambda e, hc=hc, r=r: e.tensor_tensor(out=h[:, hc, :], in0=r[:], in1=r[:], op=ALU.mult),
                 reads=[rb], writes=[hB[hc]])
    for dc in range(NCH):
        wv, wB = cx.wget(("w2", layer, dc))
        pt, pB = cx.bank()
        for k in range(64):
            P.op("pe", lambda e, k=k, wv=wv, pt=pt: e.matmul(pt[:], wv[:, k, :], h[:, k, :], start=(k == 0), stop=(k == 63)),
                 reads=[wB, hB[k]], writes=[pB], accum=(k > 0))
        P.op("dve", lambda e, dc=dc, pt=pt: e.scalar_tensor_tensor(out=xf[:, dc, :], in0=xf[:, dc, :], scalar=ALPHA, in1=pt[:],
                                                                   op0=ALU.mult, op1=ALU.add),
             reads=[pB, xfB[dc]], writes=[xfB[dc]])
        ln.stats_chunk(dc, xf[:, dc, :], xfB[dc])
    post_ln(cx, ln, xf, xfB, xb, xbB, V_MLPG + 16 * layer, V_MLPB + 16 * layer)


def mlp_sublayer_q(cx, ln, lkey, xf, xfB, xb, xbB, hq, hqB, rbuf, rB, gcol, bcol):
    P = cx.P
    rr = 0
    for q_ in range(4):
        for g in range(4):
            wv, wB = cx.wget(("w1", lkey, q_, g))
            for i in range(4):
                hc = 4 * g + i
                pt, pB = cx.bank()
                for k in range(NCH):
                    P.op("pe", lambda e, k=k, i=i, wv=wv, pt=pt: e.matmul(pt[:], wv[:, k, i * 128:(i + 1) * 128], xb[:, k, :],
                                                                          start=(k == 0), stop=(k == NCH - 1)),
                         reads=[wB, xbB[k]], writes=[pB], accum=(k > 0))
                r, rb = rbuf[rr], rB[rr]
                rr ^= 1
                P.op("act", lambda e, pt=pt, r=r: e.activation(out=r, in_=pt[:], func=AF.Relu), reads=[pB], writes=[rb])
                P.op("dve", lambda e, hc=hc, r=r: e.tensor_tensor(out=hq[:, hc, :], in0=r, in1=r, op=ALU.mult),
                     reads=[rb], writes=[hqB[hc]])
        for g in range(4):
            wv, wB = cx.wget(("w2", lkey, q_, g))
            for i in range(4):
                dc = 4 * g + i
                pt, pB = cx.bank()
                for k in range(16):
                    P.op("pe", lambda e, k=k, i=i, wv=wv, pt=pt: e.matmul(pt[:], wv[:, k, i * 128:(i + 1) * 128], hq[:, k, :],
                                                                          start=(k == 0), stop=(k == 15)),
                         reads=[wB, hqB[k]], writes=[pB], accum=(k > 0))
                if q_ == 0:
                    P.op("dve", lambda e, dc=dc, pt=pt: e.scalar_tensor_tensor(out=xf[:, dc, :], in0=xf[:, dc, :], scalar=ALPHA, in1=pt[:],
                                                                               op0=ALU.mult, op1=ALU.add),
                         reads=[pB, xfB[dc]], writes=[xfB[dc]])
                else:
                    P.op("dve", lambda e, dc=dc, pt=pt: e.tensor_tensor(out=xf[:, dc, :], in0=pt[:], in1=xf[:, dc, :], op=ALU.add),
                         reads=[pB, xfB[dc]], writes=[xfB[dc]])
                if q_ == 3:
                    ln.stats_chunk(dc, xf[:, dc, :], xfB[dc])
    post_ln(cx, ln, xf, xfB, xb, xbB, gcol, bcol)


def proj_residual_sublayer(cx, ln, wkey, inb, inbB, xf, xfB, xb, xbB, gcol, bcol, bias_col, tb=None, tbB=None):
    P = cx.P
    vecs = cx.vecs
    tr = 0
    for g in range(4):
        wv, wB = cx.wget((wkey, g))
        for i in range(4):
            dc = 4 * g + i
            pt, pB = cx.bank()
            for k in range(NCH):
                P.op("pe", lambda e, k=k, i=i, wv=wv, pt=pt: e.matmul(pt[:], wv[:, k, i * 128:(i + 1) * 128], inb[:, k, :],
                                                                      start=(k == 0), stop=(k == NCH - 1)),
                     reads=[wB, inbB[k]], writes=[pB], accum=(k > 0))
            if bias_col is not None:
                t_, tB_ = tb[tr], tbB[tr]
                tr ^= 1
                P.op("act", lambda e, dc=dc, pt=pt, t_=t_: e.activation(out=t_[:], in_=pt[:], func=AF.Identity,
                                                                        bias=vecs[:, bias_col + dc:bias_col + dc + 1]),
                     reads=[pB], writes=[tB_])
                P.op("dve", lambda e, dc=dc, t_=t_: e.scalar_tensor_tensor(out=xf[:, dc, :], in0=xf[:, dc, :], scalar=ALPHA,
                                                                          in1=t_[:], op0=ALU.mult, op1=ALU.add),
                     reads=[tB_, xfB[dc]], writes=[xfB[dc]])
            else:
                P.op("dve", lambda e, dc=dc, pt=pt: e.scalar_tensor_tensor(out=xf[:, dc, :], in0=xf[:, dc, :], scalar=ALPHA,
                                                                          in1=pt[:], op0=ALU.mult, op1=ALU.add),
                     reads=[pB, xfB[dc]], writes=[xfB[dc]])
            ln.stats_chunk(dc, xf[:, dc, :], xfB[dc])
    post_ln(cx, ln, xf, xfB, xb, xbB, gcol, bcol)


def load_consts(cx, vecs_d, ident_d):
    P = cx.P
    cx.vecs = cx.sb("vecs", [128, NV], F32)
    cx.ident = cx.sb("ident", [128, 128], F32)
    cx.ones = cx.sb("ones", [128, 128], BF16)
    cx.epsc = cx.sb("epsc", [128, 1], F32)
    cx.vecsB, cx.identB, cx.onesB = Buf("vecs"), Buf("ident"), Buf("ones")
    P.op("sp", lambda e: e.dma_start(out=cx.vecs[:], in_=vecs_d), writes=[cx.vecsB], dsem="CV")
    P.op("sp", lambda e: e.dma_start(out=cx.ident[:], in_=ident_d), writes=[cx.identB], dsem="CI")
    P.op("dve", lambda e: e.memset(cx.ones[:], 1.0), writes=[cx.onesB])
    P.op("dve", lambda e: e.memset(cx.epsc[:], EPS), writes=[cx.onesB], accum=True)
    cx.dmy = cx.sb("dmy", [128, 4], F32)
    consts = [cx.vecsB, cx.identB, cx.onesB]
    P.op("act", lambda e: e.activation(out=cx.dmy[:, 0:1], in_=cx.vecs[:, 0:1], func=AF.Identity), reads=consts, writes=[Buf("dmy_a")])
    P.op("dve", lambda e: e.tensor_copy(out=cx.dmy[:, 1:2], in_=cx.vecs[:, 0:1]), reads=consts, writes=[Buf("dmy_d")])
    P.op("pool", lambda e: e.tensor_copy(out=cx.dmy[:, 2:3], in_=cx.vecs[:, 0:1]), reads=consts, writes=[Buf("dmy_p")])


def layer0_and_qkv(cx, alloc, xc, hmask_d, pw1, pw2, w1, w2, wkv, wq, x2T, qT, kT, vtm, x2TB, qTB, kTB, vtB,
                   kpieces, vpieces, after_k, after_v):
    P = cx.P
    vecs = cx.vecs
    bufs = []

    def NB(name):
        b_ = Buf(name)
        bufs.append(b_)
        return b_

    hmask = alloc("hmask", 2, F32)
    hmB = NB("hmask")
    P.op("sp", lambda e: e.dma_start(out=hmask[:, 0:1], in_=hmask_d), writes=[hmB], dsem="CH")
    xf = alloc("xf", NCH * T, F32).rearrange("p (c t) -> p c t", c=NCH)
    xb = alloc("xb", NCH * T, BF16).rearrange("p (c t) -> p c t", c=NCH)
    xfB = [NB(f"xf{c}") for c in range(NCH)]
    xbB = [NB(f"xb{c}") for c in range(NCH)]
    scratch = alloc("scratch", 64 * T, BF16)
    hB = [NB(f"h{c}") for c in range(64)]
    h = scratch.rearrange("p (c t) -> p c t", c=64)
    cv = scratch[:, 0:32 * T].bitcast(F32).rearrange("p (c t) -> p c t", c=NCH)
    u = scratch[:, 32 * T:48 * T].rearrange("p (c t) -> p c t", c=NCH)
    kst = scratch[:, 0:16 * T].rearrange("p (c t) -> p c t", c=NCH)
    qst = scratch[:, 16 * T:32 * T].rearrange("p (c t) -> p c t", c=NCH)
    vst = scratch[:, 32 * T:48 * T].rearrange("p (b f) -> p b f", b=4)

    def cvB(c):
        return [hB[2 * c], hB[2 * c + 1]]

    xh = alloc("xh", NCH * HALO, BF16).rearrange("p (c t) -> p c t", c=NCH)
    xhB = NB("xh")
    xhf = alloc("xhf", NCH * HALO, F32).rearrange("p (c t) -> p c t", c=NCH)
    xhfB = NB("xhf")
    ghalo = alloc("ghalo", NCH * HALO, F32).rearrange("p (c t) -> p c t", c=NCH)
    ghB = [NB(f"gh{c}") for c in range(NCH)]
    gbuf = [alloc(f"gbuf{i}", T + 32, BF16) for i in range(2)]
    gbB = [NB(f"gbuf{i}") for i in range(2)]
    NDG = 16
    dg = [alloc(f"dg{i}", 128, BF16) for i in range(NDG)]
    dgB = [NB(f"dg{i}") for i in range(NDG)]
    dgr = 0
    sg = [alloc(f"sg{i}", T, F32) for i in range(2)]
    sgB = [NB(f"sg{i}") for i in range(2)]
    sgh = alloc("sgh", HALO, F32)
    sghB = NB("sgh")
    ln = LN(cx, alloc=alloc)
    bufs += ln.zbB + ln.z2bB + [ln.meanB, ln.varB, ln.rstdB]

    pw1v = pw1.rearrange("(c p) n -> p c n", p=128)
    w2v = w2.rearrange("(q c p) n -> q p c n", p=128, c=16)
    for j in range(NT):
        cnt = [0]

        def st_():
            cnt[0] += 1
            return (cnt[0] % 2) == (j % 2) or j >= 2

        for g in range(8):
            cx.wadd(("pw1", j, g), [(0, 256, pw1v[:, :, 256 * g:256 * g + 256]),
                                    (256, 512, pw1v[:, :, D + 256 * g:D + 256 * g + 256])], 16, cache=("pw1", g), store=st_())
        for g in range(4):
            cx.wadd(("pw2", g), wsrc_k2048(pw2, 512 * g), 16, cache=("pw2", g), store=st_())
        for q_ in range(4):
            for g in range(4):
                cx.wadd(("w1", ("A", j), q_, g), wsrc_k2048(w1, 2048 * q_ + 512 * g), 16, cache=("w1", q_, g), store=st_())
            for g in range(4):
                cx.wadd(("w2", ("A", j), q_, g), w2v[q_][:, :, 512 * g:512 * (g + 1)], 16, cache=("w2", q_, g), store=st_())
        for g in range(4):
            cx.wadd(("wk", g), wsrc_k2048(wkv, 512 * g), 16, cache=("wk", g), store=st_())
        for g in range(4):
            cx.wadd(("wv", g), wsrc_k2048(wkv, D + 512 * g), 16, cache=("wv", g), store=st_())
        for g in range(4):
            cx.wadd(("wq", g), wsrc_k2048(wq, 512 * g), 16, cache=("wq", g), store=st_())

    xcv = xc.rearrange("(c p) t -> p c t", p=128)
    P.op("sp", lambda e: e.dma_start(out=xhf, in_=xcv[:, :, XH - HALO:XH]), writes=[xhfB], dsem="X0")
    P.op("act", lambda e: e.activation(out=xh, in_=xhf, func=AF.Identity), reads=[xhfB], writes=[xhB])

    for j in range(NT):
        P.op("sp", lambda e, j=j: e.dma_start(out=xf, in_=xcv[:, :, XH + T * j:XH + T * (j + 1)]), writes=xfB, dsem="X1")
        for c in range(NCH):
            if c % 2 == 0:
                P.op("act", lambda e, c=c: e.activation(out=xb[:, c, :], in_=xf[:, c, :], func=AF.Identity), reads=[xfB[c]], writes=[xbB[c]])
            else:
                P.op("dve", lambda e, c=c: e.tensor_copy(out=xb[:, c, :], in_=xf[:, c, :]), reads=[xfB[c]], writes=[xbB[c]])
        gr = 0
        for g in range(8):
            wv, wB = cx.wget(("pw1", j, g))
            for i in range(2):
                c = 2 * g + i
                acol = slice(i * 128, (i + 1) * 128)
                gcol = slice(256 + i * 128, 256 + (i + 1) * 128)
                pa, paB = cx.bank()
                for k in range(NCH):
                    P.op("pe", lambda e, k=k, acol=acol, wv=wv, pa=pa: e.matmul(pa[:], wv[:, k, acol], xb[:, k, :],
                                                                              start=(k == 0), stop=(k == NCH - 1)),
                         reads=[wB, xbB[k]], writes=[paB], accum=(k > 0))
                pg, pgB = cx.bank()
                for k in range(NCH):
                    P.op("pe", lambda e, k=k, gcol=gcol, wv=wv, pg=pg: e.matmul(pg[:], wv[:, k, gcol], xb[:, k, :],
                                                                              start=(k == 0), stop=(k == NCH - 1)),
                         reads=[wB, xbB[k]], writes=[pgB], accum=(k > 0))
                gi = gr
                gr ^= 1
                gb_, gbB_ = gbuf[gi], gbB[gi]
                sg_, sgB_ = sg[gi], sgB[gi]
                ba = vecs[:, V_PW1B + c:V_PW1B + c + 1]
                bg = vecs[:, V_PW1B + 16 + c:V_PW1B + 16 + c + 1]
                if j == 0:
                    ph, phB = cx.bank()
                    for k in range(NCH):
                        P.op("pe", lambda e, k=k, acol=acol, wv=wv, ph=ph: e.matmul(ph[:, 0:HALO], wv[:, k, acol], xh[:, k, :],
                                                                                  start=(k == 0), stop=(k == NCH - 1)),
                             reads=[wB, xhB], writes=[phB], accum=(k > 0))
                    for k in range(NCH):
                        P.op("pe", lambda e, k=k, gcol=gcol, wv=wv, ph=ph: e.matmul(ph[:, HALO:2 * HALO], wv[:, k, gcol], xh[:, k, :],
                                                                                  start=(k == 0), stop=(k == NCH - 1)),
                             reads=[wB, xhB], writes=[phB], accum=True)
                    P.op("act", lambda e, ph=ph, bg=bg: e.activation(out=sgh, in_=ph[:, HALO:2 * HALO], func=AF.Sigmoid, bias=bg),
                         reads=[phB], writes=[sghB])
                    P.op("dve", lambda e, c=c, ph=ph, ba=ba: e.scalar_tensor_tensor(out=ghalo[:, c, :], in0=ph[:, 0:HALO], scalar=ba, in1=sgh,
                                                                                   op0=ALU.add, op1=ALU.mult),
                         reads=[phB, sghB], writes=[ghB[c]])
                    P.op("dve", lambda e, c=c, gb_=gb_: e.tensor_scalar(out=gb_[:, 0:30], in0=ghalo[:, c, 2:32], scalar1=hmask[:, 0:1], scalar2=None,
                                                                       op0=ALU.mult),
                         reads=[ghB[c], hmB], writes=[gbB_])
                else:
                    P.op("dve", lambda e, c=c, gb_=gb_: e.tensor_copy(out=gb_[:, 0:30], in_=ghalo[:, c, 2:32]),
                         reads=[ghB[c]], writes=[gbB_])
                P.op("act", lambda e, pg=pg, sg_=sg_, bg=bg: e.activation(out=sg_, in_=pg[:], func=AF.Sigmoid, bias=bg),
                     reads=[pgB], writes=[sgB_])
                P.op("dve", lambda e, pa=pa, sg_=sg_, gb_=gb_, ba=ba: e.scalar_tensor_tensor(out=gb_[:, 30:30 + T], in0=pa[:], scalar=ba, in1=sg_,
                                                                                           op0=ALU.add, op1=ALU.mult),
                     reads=[paB, sgB_], writes=[gbB_], accum=True)
                if j < NT - 1:
                    P.op("dve", lambda e, c=c, gb_=gb_: e.tensor_copy(out=ghalo[:, c, 2:32], in_=gb_[:, T:T + 30]),
                         reads=[gbB_], writes=[ghB[c]])
                cvc = cv[:, c, :]
                bdw = vecs[:, V_DWB + c:V_DWB + c + 1]
                pc, pcB = cx.bank()
                for tap in range(CONVW):
                    wj = vecs[:, V_DWW + 16 * tap + c:V_DWW + 16 * tap + c + 1]
                    dgi, dgiB = dg[dgr], dgB[dgr]
                    dgr = (dgr + 1) % NDG
                    if tap % 2 == 0:
                        P.op("dve", lambda e, dgi=dgi, wj=wj: e.tensor_scalar(out=dgi, in0=cx.identb[:], scalar1=wj, scalar2=None, op0=ALU.mult),
                             writes=[dgiB])
                    else:
                        P.op("act", lambda e, dgi=dgi, wj=wj: e.activation(out=dgi, in_=cx.identb[:], func=AF.Identity, scale=wj),
                             writes=[dgiB])
                    P.op("pe", lambda e, pc=pc, dgi=dgi, gb_=gb_, tap=tap: e.matmul(pc[:], dgi, gb_[:, tap:tap + T],
                                                                                  start=(tap == 0), stop=(tap == CONVW - 1)),
                         reads=[dgiB, gbB_], writes=[pcB], accum=(tap > 0))
                P.op("act", lambda e, cvc=cvc, pc=pc, bdw=bdw: e.activation(out=cvc, in_=pc[:], func=AF.Identity, bias=bdw),
                     reads=[pcB], writes=cvB(c))
                ln.stats_chunk(c, cvc, hB[2 * c])
        ln.finalize()
        for c in range(NCH):
            cvc = cv[:, c, :]
            P.op("dve", lambda e, cvc=cvc: e.tensor_tensor(out=cvc, in0=cvc, in1=ln.rstd, op=ALU.mult),
                 reads=cvB(c) + [ln.rstdB], writes=cvB(c))
            P.op("dve", lambda e, cvc=cvc: e.tensor_tensor(out=cvc, in0=cvc, in1=ln.mean, op=ALU.add),
                 reads=cvB(c) + [ln.meanB], writes=cvB(c))
            P.op("act", lambda e, c=c, cvc=cvc: e.activation(out=u[:, c, :], in_=cvc, func=AF.Silu,
                                                             bias=vecs[:, V_CLNB + c:V_CLNB + c + 1], scale=vecs[:, V_CLNG + c:V_CLNG + c + 1]),
                 reads=cvB(c), writes=[hB[32 + c]])
        uB = [hB[32 + c] for c in range(NCH)]
        proj_residual_sublayer(cx, ln, "pw2", u, uB, xf, xfB, xb, xbB, V_MIXG, V_MIXB, V_PW2B, tb=sg, tbB=sgB)
        mlp_sublayer_q(cx, ln, ("A", j), xf, xfB, xb, xbB, h[:, 0:16, :], hB[0:16], sg, sgB, V_MLPG, V_MLPB)
        P.op("sp", lambda e, j=j: e.dma_start(out=x2T.rearrange("(c p) t -> p c t", p=128)[:, :, T * j:T * (j + 1)], in_=xf),
             reads=xfB, writes=[x2TB], accum=True, dsem="OX")
        def proj_fm(nm, stg, cell0, scale):
            for g in range(4):
                wv_, wB_ = cx.wget((nm, g))
                for i in range(4):
                    hd = 4 * g + i
                    pt, pB = cx.bank()
                    for k in range(NCH):
                        P.op("pe", lambda e, k=k, i=i, wv_=wv_, pt=pt: e.matmul(pt[:], wv_[:, k, i * 128:(i + 1) * 128], xb[:, k, :],
                                                                                start=(k == 0), stop=(k == NCH - 1)),
                             reads=[wB_, xbB[k]], writes=[pB], accum=(k > 0))
                    P.op("act", lambda e, hd=hd, pt=pt, stg=stg, scale=scale: e.activation(out=stg[:, hd, :], in_=pt[:], func=AF.Identity, scale=scale),
                         reads=[pB], writes=[hB[cell0 + hd]])
            dst, dB = (kT, kTB) if nm == "wk" else (qT, qTB)
            P.op("sp", lambda e, j=j, dst=dst, stg=stg: e.dma_start(out=dst.rearrange("(c p) t -> p c t", p=128)[:, :, T * j:T * (j + 1)], in_=stg),
                 reads=[hB[cell0 + c] for c in range(NCH)], writes=[dB], accum=True, dsem="OK" if nm == "wk" else "OQ")
            if nm == "wk":
                for p_ in range(2):
                    pap, pB_ = kpieces[j][p_]
                    P.op("sp", lambda e, pap=pap, p_=p_, stg=stg: e.dma_start(out=pap.rearrange("(c p) t -> p c t", p=128), in_=stg[:, 8 * p_:8 * p_ + 8, :]),
                         reads=[hB[cell0 + c] for c in range(8 * p_, 8 * p_ + 8)], writes=[pB_], dsem=f"PK{p_}")
        proj_fm("wk", kst, 0, 1.0)
        after_k(j)
        for g in range(4):
            wv_, wB_ = cx.wget(("wv", g))
            for tb in range(4):
                pt, pB = cx.bank()
                for k in range(NCH):
                    P.op("pe", lambda e, k=k, tb=tb, wv_=wv_, pt=pt: e.matmul(pt[:], xb[:, k, tb * 128:(tb + 1) * 128], wv_[:, k, :],
                                                                              start=(k == 0), stop=(k == NCH - 1)),
                         reads=[wB_, xbB[k]], writes=[pB], accum=(k > 0))
                cells = [hB[32 + 4 * tb + q] for q in range(4)]
                if g % 2 == 0:
                    P.op("act", lambda e, g=g, tb=tb, pt=pt: e.activation(out=vst[:, tb, 512 * g:512 * (g + 1)], in_=pt[:], func=AF.Identity),
                         reads=[pB], writes=cells, accum=(g > 0))
                else:
                    P.op("dve", lambda e, g=g, tb=tb, pt=pt: e.tensor_copy(out=vst[:, tb, 512 * g:512 * (g + 1)], in_=pt[:]),
                         reads=[pB], writes=cells, accum=True)
        P.op("sp", lambda e, j=j: e.dma_start(out=vtm[T * j:T * (j + 1), :].rearrange("(b p) f -> p b f", p=128), in_=vst),
             reads=[hB[32 + q] for q in range(16)], writes=[vtB], accum=True, dsem="OV")
        for p_ in range(2):
            pap, pB_ = vpieces[j][p_]
            P.op("sp", lambda e, pap=pap, p_=p_: e.dma_start(out=pap.rearrange("(b p) f -> p b f", p=128), in_=vst[:, 2 * p_:2 * p_ + 2, :]),
                 reads=[hB[32 + q] for q in range(8 * p_, 8 * p_ + 8)], writes=[pB_], dsem=f"PV{p_}")
        after_v(j)
        proj_fm("wq", qst, 16, QSCALE)
    return bufs


BR = (1, 4, 16)


BIGN = 76 * 1024


def attention_and_layer1(cx, carve, off, a_bufs, x2T, qT, kTs, kTp, vts, vtp, bias_d, wo, w1, w2, oT, out, identb, identbB,
                         x2TB, qTB, kTsB, kTpB, vtsB, vtpB):
    P = cx.P
    off[0] = 0
    kwin = [carve(4096) for _ in range(2)]
    qn = [carve(2048) for _ in range(2)]
    kd = [[carve(4096) for _ in range(2)] for _ in range(2)]
    qd = [[carve(2048) for _ in range(2)] for _ in range(2)]
    vsl = [carve(32 * 256) for _ in range(2)]
    bia = [carve(9 * 128) for _ in range(2)]
    ptb = [carve(256) for _ in range(4)]
    oacc = [carve(4096).bitcast(F32) for _ in range(2)]
    dacc = [carve(4096).bitcast(F32) for _ in range(2)]
    ost = [carve(2048) for _ in range(2)]
    kwinB = [Buf("kwin0"), Buf("kwin1")]
    qnB = [Buf("qn0"), Buf("qn1")]
    kdB = [[Buf(f"kd{r}{h}") for h in range(2)] for r in range(2)]
    qdB = [[Buf(f"qd{r}{h}") for h in range(2)] for r in range(2)]
    vslB = [Buf(f"vsl{i}") for i in range(2)]
    biaB = [Buf("bia0"), Buf("bia1")]
    ptbB = [Buf(f"ptb{i}") for i in range(4)]
    oaccB = [Buf("oacc0"), Buf("oacc1")]
    daccB = [Buf("dacc0"), Buf("dacc1")]
    ostB = [Buf("ost0"), Buf("ost1")]
    oTdB = Buf("oT_dram")
    b1_bufs = kwinB + qnB + kdB[0] + kdB[1] + qdB[0] + qdB[1] + vslB + biaB + ptbB + oaccB + daccB + ostB
    P.op("dve", lambda e: e.memset(cx.dmy[:, 3:4], 0.0), reads=a_bufs, writes=a_bufs + b1_bufs)
    cx.nmain = 4
    cx.rot = 0
    accrot = 0
    vrot = 0
    drot = 0
    prot = 0
    LA = 3
    qTv = qT.rearrange("(h p) t -> h p t", p=128)
    kTsv = kTs.rearrange("(h p) t -> h p t", p=128)
    kTpv = kTp.rearrange("(h p) t -> h p t", p=128)
    for hp in range(8):
        for hh in range(2):
            hd = 2 * hp + hh
            P.op("sp", lambda e, hh=hh, hd=hd: e.dma_start(out=kwin[hh][:, 0:TOK], in_=kTpv[hd]), reads=[kTpB], writes=[kwinB[hh]], dsem=f"BK{hh}")
            P.op("sp", lambda e, hh=hh, hd=hd: e.dma_start(out=kwin[hh][:, TOK:2 * TOK], in_=kTsv[hd]), reads=[kTsB], writes=[kwinB[hh]],
                 accum=True, dsem=f"BK{hh}")
            P.op("sp", lambda e, hh=hh, hd=hd: e.dma_start(out=qn[hh], in_=qTv[hd]), reads=[qTB], writes=[qnB[hh]], dsem=f"BQ{hh}")
            P.op("pool", lambda e, hh=hh, hd=hd: e.dma_start(out=bia[hh], in_=bias_d[hd]), writes=[biaB[hh]], dsem=f"BB{hh}")
        for bi, d in enumerate(BR):
            Lw = 4096 // d
            Lq = 2048 // d
            nbr = 32 // d
            vs = vsl[vrot]
            vsB = vslB[vrot]
            vsem = f"BV{vrot}"
            vrot ^= 1
            vs3 = vs.rearrange("p (b f) -> p b f", f=256)
            cols = slice(256 * hp, 256 * hp + 256)
            lo = nbr // 2 - 1
            hb = nbr // 2
            for r in range(d):
                srcp = vtp.rearrange("(b i r) f -> r i b f", i=128, r=d)[r, :, hb - 1:hb, cols]
                P.op("sp", lambda e, vs3=vs3, srcp=srcp, r=r, lo=lo, nbr=nbr: e.dma_start(out=vs3[:, r * nbr + lo:r * nbr + lo + 1, :], in_=srcp),
                     reads=[vtpB], writes=[vsB], accum=(r > 0), dsem=vsem)
                srco = vts.rearrange("(b i r) f -> r i b f", i=128, r=d)[r, :, 0:hb, cols]
                P.op("sp", lambda e, vs3=vs3, srco=srco, r=r, hb=hb, nbr=nbr: e.dma_start(out=vs3[:, r * nbr + hb:r * nbr + nbr, :], in_=srco),
                     reads=[vtsB], writes=[vsB], accum=True, dsem=vsem)
            for hh in range(2):
                if d == 1:
                    kdv, kdvB = kwin[hh], kwinB[hh]
                    qdv, qdvB = qn[hh], qnB[hh]
                else:
                    kdv, kdvB = kd[drot][hh], kdB[drot][hh]
                    qdv, qdvB = qd[drot][hh], qdB[drot][hh]
                    P.op("pool", lambda e, hh=hh, kdv=kdv, d=d: e.tensor_copy(out=kdv.rearrange("p (r m) -> p r m", r=d),
                                                                            in_=kwin[hh].rearrange("p (m r) -> p r m", r=d)),
                         reads=[kwinB[hh]], writes=[kdvB])
                    P.op("pool", lambda e, hh=hh, qdv=qdv, d=d: e.tensor_copy(out=qdv.rearrange("p (r m) -> p r m", r=d),
                                                                            in_=qn[hh].rearrange("p (m r) -> p r m", r=d)),
                         reads=[qnB[hh]], writes=[qdvB])
                ov = oacc[hh] if d == 1 else oacc[hh].rearrange("p (m r) -> p r m", r=d)
                dv = dacc[hh] if d == 1 else dacc[hh].rearrange("p (m r) -> p r m", r=d)
                steps = [(g, nn) for g in range(4) for nn in range(4)]
                pts = {}
                accs = {}
                for s_ in range(len(steps) + LA):
                    if s_ < len(steps):
                        g, nn = steps[s_]
                        qpos = 512 * g + 128 * nn
                        r = qpos // Lq
                        n = (qpos % Lq) // 128
                        pst, pstB = cx.bank()
                        vblks = []
                        for part in range(2):
                            kpos = r * Lw + Lw // 2 + 128 * (n - 1 + part)
                            vblks.append(r * nbr + nbr // 2 + n - 1 + part)
                            bcol = (bi * 3 + (part if not (part == 0 and n == 0) else 2)) * 128
                            sq = pst[:, 128 * part:128 * (part + 1)]
                            P.op("pe", lambda e, sq=sq, kdv=kdv, qdv=qdv, kpos=kpos, qpos=qpos: e.matmul(sq, kdv[:, kpos:kpos + 128], qdv[:, qpos:qpos + 128],
                                                                                                       start=True, stop=False),
                                 reads=[kdvB, qdvB], writes=[pstB], accum=(part > 0))
                            P.op("pe", lambda e, sq=sq, hh=hh, bcol=bcol: e.matmul(sq, identb[:], bia[hh][:, bcol:bcol + 128], start=False, stop=True),
                                 reads=[identbB, biaB[hh]], writes=[pstB], accum=True)
                        pt_, ptB_ = ptb[prot], ptbB[prot]
                        prot = (prot + 1) % 4
                        P.op("act", lambda e, pst=pst, pt_=pt_: e.activation(out=pt_, in_=pst[:, 0:256], func=AF.Exp), reads=[pstB], writes=[ptB_])
                        pts[s_] = (pt_, ptB_, vblks)
                    t_ = s_ - LA
                    if t_ < 0:
                        continue
                    g, nn = steps[t_]
                    pt_, ptB_, vblks = pts.pop(t_)
                    if nn == 0:
                        accs[g] = (cx.ps[4 + 2 * accrot], cx.pb[4 + 2 * accrot], cx.ps[5 + 2 * accrot], cx.pb[5 + 2 * accrot])
                        accrot ^= 1
                    po, poB, pd, pdB = accs[g]
                    for part in range(2):
                        vblk = vblks[part]
                        ptp = pt_[:, 128 * part:128 * (part + 1)]
                        P.op("pe", lambda e, po=po, nn=nn, vs3=vs3, vblk=vblk, hh=hh, ptp=ptp, part=part: e.matmul(
                            po[:, 128 * nn:128 * (nn + 1)], vs3[:, vblk, 128 * hh:128 * (hh + 1)], ptp, start=(part == 0), stop=(part == 1)),
                             reads=[vsB, ptB_], writes=[poB], accum=not (nn == 0 and part == 0))
                        P.op("pe", lambda e, pd=pd, nn=nn, ptp=ptp, part=part: e.matmul(
                            pd[:, 128 * nn:128 * (nn + 1)], cx.ones[:], ptp, start=(part == 0), stop=(part == 1)),
                             reads=[cx.onesB, ptB_], writes=[pdB], accum=not (nn == 0 and part == 0))
                    if nn != 3:
                        continue
                    if d == 1:
                        oview = ov[:, 512 * g:512 * (g + 1)]
                        dview = dv[:, 512 * g:512 * (g + 1)]
                        pov, pdv = po[:, :], pd[:, :]
                    elif d == 4:
                        oview = ov[:, g, :]
                        dview = dv[:, g, :]
                        pov, pdv = po[:, :], pd[:, :]
                    else:
                        oview = ov[:, 4 * g:4 * g + 4, :]
                        dview = dv[:, 4 * g:4 * g + 4, :]
                        pov = po[:, :].rearrange("p (r m) -> p r m", r=4)
                        pdv = pd[:, :].rearrange("p (r m) -> p r m", r=4)
                    if bi == 0:
                        P.op("dve", lambda e, oview=oview, pov=pov: e.tensor_copy(out=oview, in_=pov), reads=[poB], writes=[oaccB[hh]], accum=(g > 0))
                        P.op("dve", lambda e, dview=dview, pdv=pdv: e.tensor_copy(out=dview, in_=pdv), reads=[pdB], writes=[daccB[hh]], accum=(g > 0))
                    else:
                        P.op("dve", lambda e, oview=oview, pov=pov: e.tensor_tensor(out=oview, in0=pov, in1=oview, op=ALU.add),
                             reads=[poB, oaccB[hh]], writes=[oaccB[hh]])
                        P.op("dve", lambda e, dview=dview, pdv=pdv: e.tensor_tensor(out=dview, in0=pdv, in1=dview, op=ALU.add),
                             reads=[pdB, daccB[hh]], writes=[daccB[hh]])
            if d != 1:
                drot ^= 1
        for hh in range(2):
            hd = 2 * hp + hh
            P.op("dve", lambda e, hh=hh: e.reciprocal(out=dacc[hh], in_=dacc[hh]), reads=[daccB[hh]], writes=[daccB[hh]])
            P.op("dve", lambda e, hh=hh: e.tensor_tensor(out=ost[hh], in0=oacc[hh], in1=dacc[hh], op=ALU.mult),
                 reads=[oaccB[hh], daccB[hh]], writes=[ostB[hh]])
            P.op("sp", lambda e, hh=hh, hd=hd: e.dma_start(out=oT[128 * hd:128 * (hd + 1), :], in_=ost[hh]),
                 reads=[ostB[hh]], writes=[oTdB], accum=True, dsem=f"BO{hh}")

    TB, NH = 1024, 2
    cx.nmain = 4
    cx.rot = 0
    off[0] = 0
    xf = carve(2 * NCH * TB).bitcast(F32).rearrange("p (c t) -> p c t", c=NCH)
    xb = carve(NCH * TB).rearrange("p (c t) -> p c t", c=NCH)
    hq = carve(16 * TB).rearrange("p (c t) -> p c t", c=16)
    sg = [carve(2 * T).bitcast(F32) for _ in range(2)]

    def ln_alloc(name, n, dt):
        return carve(n) if dt == BF16 else carve(2 * n).bitcast(F32)

    lns = [LN(cx, alloc=ln_alloc, banks=(4, 5)), LN(cx, alloc=ln_alloc, banks=(6, 7))]
    xfB = [[Buf(f"xf{c}_{hf}") for hf in range(NH)] for c in range(NCH)]
    xbB = [[Buf(f"xb{c}_{hf}") for hf in range(NH)] for c in range(NCH)]
    hqB = [[Buf(f"hq{c}_{hf}") for hf in range(NH)] for c in range(16)]
    sgB = [Buf("sg0"), Buf("sg1")]
    xfB_flat = [b_ for l_ in xfB for b_ in l_]
    xbB_flat = [b_ for l_ in xbB for b_ in l_]
    vecs = cx.vecs

    def hs(hf):
        return slice(T * hf, T * (hf + 1))

    def post_ln2(gcol, bcol):
        for hf in range(NH):
            ln = lns[hf]
            ln.finalize()
            for c in range(NCH):
                src = xf[:, c, hs(hf)]
                ln.norm_chunk(src, xfB[c][hf])
                P.op("act", lambda e, c=c, src=src: e.activation(out=src, in_=src, func=AF.Identity,
                                                                 bias=vecs[:, bcol + c:bcol + c + 1], scale=vecs[:, gcol + c:gcol + c + 1]),
                     reads=[xfB[c][hf]], writes=[xfB[c][hf]])
                P.op("act", lambda e, c=c, hf=hf, src=src: e.activation(out=xb[:, c, hs(hf)], in_=src, func=AF.Identity),
                     reads=[xfB[c][hf]], writes=[xbB[c][hf]])

    w2v = w2.rearrange("(q c p) n -> q p c n", p=128, c=16)
    for jj in range(TOK // TB):
        for g in range(4):
            cx.wadd(("wo", jj, g), wsrc_k2048(wo, 512 * g), 16)
        for q_ in range(4):
            for g in range(4):
                cx.wadd(("w1", jj, q_, g), wsrc_k2048(w1, 2048 * q_ + 512 * g), 16)
            for g in range(4):
                cx.wadd(("w2", jj, q_, g), w2v[q_][:, :, 512 * g:512 * (g + 1)], 16)

    for jj in range(TOK // TB):
        tsl = slice(TB * jj, TB * (jj + 1))
        P.op("sp", lambda e, tsl=tsl: e.dma_start(out=xf, in_=x2T.rearrange("(c p) t -> p c t", p=128)[:, :, tsl]),
             reads=[x2TB], writes=(xfB_flat + b1_bufs if jj == 0 else xfB_flat), dsem="B2X")
        P.op("sp", lambda e, tsl=tsl: e.dma_start(out=xb, in_=oT.rearrange("(c p) t -> p c t", p=128)[:, :, tsl]),
             reads=[oTdB], writes=xbB_flat, dsem="B2O")
        for g in range(4):
            wv, wB = cx.wget(("wo", jj, g))
            for i in range(4):
                dc = 4 * g + i
                for hf in range(NH):
                    pt, pB = cx.bank()
                    for k in range(NCH):
                        P.op("pe", lambda e, k=k, i=i, hf=hf, wv=wv, pt=pt: e.matmul(pt[:], wv[:, k, i * 128:(i + 1) * 128], xb[:, k, hs(hf)],
                                                                                     start=(k == 0), stop=(k == NCH - 1)),
                             reads=[wB, xbB[k][hf]], writes=[pB], accum=(k > 0))
                    P.op("dve", lambda e, dc=dc, hf=hf, pt=pt: e.scalar_tensor_tensor(out=xf[:, dc, hs(hf)], in0=xf[:, dc, hs(hf)], scalar=ALPHA,
                                                                                     in1=pt[:], op0=ALU.mult, op1=ALU.add),
                         reads=[pB, xfB[dc][hf]], writes=[xfB[dc][hf]])
                    lns[hf].stats_chunk(dc, xf[:, dc, hs(hf)], xfB[dc][hf])
        post_ln2(V_MIXG + 16, V_MIXB + 16)
        rr = 0
        for q_ in range(4):
            for g in range(4):
                wv, wB = cx.wget(("w1", jj, q_, g))
                for i in range(4):
                    hc = 4 * g + i
                    for hf in range(NH):
                        pt, pB = cx.bank()
                        for k in range(NCH):
                            P.op("pe", lambda e, k=k, i=i, hf=hf, wv=wv, pt=pt: e.matmul(pt[:], wv[:, k, i * 128:(i + 1) * 128], xb[:, k, hs(hf)],
                                                                                         start=(k == 0), stop=(k == NCH - 1)),
                                 reads=[wB, xbB[k][hf]], writes=[pB], accum=(k > 0))
                        r, rb = sg[rr], sgB[rr]
                        rr ^= 1
                        P.op("act", lambda e, pt=pt, r=r: e.activation(out=r, in_=pt[:], func=AF.Relu), reads=[pB], writes=[rb])
                        P.op("dve", lambda e, hc=hc, hf=hf, r=r: e.tensor_tensor(out=hq[:, hc, hs(hf)], in0=r, in1=r, op=ALU.mult),
                             reads=[rb], writes=[hqB[hc][hf]])
            for g in range(4):
                wv, wB = cx.wget(("w2", jj, q_, g))
                for i in range(4):
                    dc = 4 * g + i
                    for hf in range(NH):
                        pt, pB = cx.bank()
                        for k in range(16):
                            P.op("pe", lambda e, k=k, i=i, hf=hf, wv=wv, pt=pt: e.matmul(pt[:], wv[:, k, i * 128:(i + 1) * 128], hq[:, k, hs(hf)],
                                                                                         start=(k == 0), stop=(k == 15)),
                                 reads=[wB, hqB[k][hf]], writes=[pB], accum=(k > 0))
                        if q_ == 0:
                            P.op("dve", lambda e, dc=dc, hf=hf, pt=pt: e.scalar_tensor_tensor(out=xf[:, dc, hs(hf)], in0=xf[:, dc, hs(hf)], scalar=ALPHA,
                                                                                             in1=pt[:], op0=ALU.mult, op1=ALU.add),
                                 reads=[pB, xfB[dc][hf]], writes=[xfB[dc][hf]])
                        else:
                            P.op("dve", lambda e, dc=dc, hf=hf, pt=pt: e.tensor_tensor(out=xf[:, dc, hs(hf)], in0=pt[:], in1=xf[:, dc, hs(hf)], op=ALU.add),
                                 reads=[pB, xfB[dc][hf]], writes=[xfB[dc][hf]])
                        if q_ == 3:
                            lns[hf].stats_chunk(dc, xf[:, dc, hs(hf)], xfB[dc][hf])
        post_ln2(V_MLPG + 16, V_MLPB + 16)
        P.op("sp", lambda e, tsl=tsl: e.dma_start(out=out.rearrange("(c p) t -> p c t", p=128)[:, :, tsl], in_=xf),
             reads=xfB_flat, dsem="OUT")


def _fm(v):
    return np.ascontiguousarray(v.reshape(-1, 128).T)


def _t5_bucket(dist):
    max_exact = 16
    large = max_exact + (np.log(np.maximum(dist, 1) / max_exact) / math.log(2048 / max_exact) * (32 - max_exact)).astype(np.int32)
    large = np.minimum(large, 31)
    return np.where(dist < max_exact, dist, large).astype(np.int32)


def _bias_tiles(rel_bias, has_prev):
    i = np.arange(128)[:, None]
    j = np.arange(256)[None, :]
    delta = i - j + 128
    ok = (delta >= 0) & (delta <= 128)
    res = np.full((NHEAD, 128, 9, 128), NEG, np.float32)
    for bi, d in enumerate(BR):
        bucket = _t5_bucket(np.clip(delta, 0, None) * d)
        b = rel_bias[bucket]
        b = np.where(ok[:, :, None], b, np.float32(NEG))
        bt = np.transpose(b, (2, 1, 0))
        res[:, :, bi * 3 + 0, :] = bt[:, 0:128, :]
        res[:, :, bi * 3 + 1, :] = bt[:, 128:256, :]
        if has_prev:
            res[:, :, bi * 3 + 2, :] = bt[:, 0:128, :]
    return np.ascontiguousarray(res.reshape(NHEAD, 128, 9 * 128))


_NC_CACHE = {}


def build_fused():
    nc = bass.Bass("TRN2", target_bir_lowering=False)
    I32 = mybir.dt.int32
    xc = nc.dram_tensor("xc", [D, XH + TOK], F32, kind="ExternalInput").ap()
    hmask_d = nc.dram_tensor("hmask", [128, 1], F32, kind="ExternalInput").ap()
    prow_d = nc.dram_tensor("prevrow", [1, 1], I32, kind="ExternalInput").ap()
    vecs_d = nc.dram_tensor("vecs", [128, NV], F32, kind="ExternalInput").ap()
    ident_d = nc.dram_tensor("ident", [128, 128], F32, kind="ExternalInput").ap()
    bias_d = nc.dram_tensor("biasT", [NHEAD, 128, 9 * 128], F32, kind="ExternalInput").ap()
    pw1 = nc.dram_tensor("pw1", [D, 2 * D], F32, kind="ExternalInput").ap()
    pw2 = nc.dram_tensor("pw2", [D, D], F32, kind="ExternalInput").ap()
    w1a = nc.dram_tensor("w1a", [D, DFF], F32, kind="ExternalInput").ap()
    w2a = nc.dram_tensor("w2a", [DFF, D], F32, kind="ExternalInput").ap()
    w1b = nc.dram_tensor("w1b", [D, DFF], F32, kind="ExternalInput").ap()
    w2b = nc.dram_tensor("w2b", [DFF, D], F32, kind="ExternalInput").ap()
    wkv = nc.dram_tensor("wkv", [D, 2 * D], F32, kind="ExternalInput").ap()
    wq = nc.dram_tensor("wq", [D, D], F32, kind="ExternalInput").ap()
    wo = nc.dram_tensor("wo", [D, D], F32, kind="ExternalInput").ap()
    out = nc.dram_tensor("out", [D, TOK], F32, kind="ExternalOutput").ap()
    x2T = nc.dram_tensor("x2T", [D, TOK], F32).ap()
    qT = nc.dram_tensor("qT", [D, TOK], BF16).ap()
    kTs_t = nc.dram_tensor("kTs", [D, TOK], BF16)
    vts_t = nc.dram_tensor("vts", [TOK, D], BF16)
    kps = [[nc.dram_tensor(f"kps{j}{p}", [D // 2, T], BF16) for p in range(2)] for j in range(NT)]
    kpa = [[nc.dram_tensor(f"kpa{j}{p}", [4 * (D // 2), T], BF16) for p in range(2)] for j in range(NT)]
    vps = [[nc.dram_tensor(f"vps{j}{p}", [T // 2, D], BF16) for p in range(2)] for j in range(NT)]
    vpa = [[nc.dram_tensor(f"vpa{j}{p}", [4 * (T // 2), D], BF16) for p in range(2)] for j in range(NT)]
    kTp = nc.dram_tensor("kTp", [D, TOK], BF16).ap()
    vtp = nc.dram_tensor("vtp", [TOK, D], BF16).ap()
    oT = nc.dram_tensor("oT", [D, TOK], BF16).ap()
    kTs, vts = kTs_t.ap(), vts_t.ap()

    with contextlib.ExitStack() as st:
        cx = Ctx(nc, st, nslots=3)
        cx.wcache = nc.dram_tensor("wcache", [56, 128, 8192], BF16).ap()
        P = cx.P
        st.enter_context(nc.allow_low_precision("bf16 matmul operands, fp32 accumulation"))
        load_consts(cx, vecs_d, ident_d)
        identb = cx.sb("identb", [128, 128], BF16)
        identbB = Buf("identb")
        P.op("act", lambda e: e.activation(out=identb[:], in_=cx.ident[:], func=AF.Identity), reads=[cx.identB], writes=[identbB])
        cx.identb = identb
        P.op("dve", lambda e: e.tensor_copy(out=cx.dmy[:, 3:4], in_=identb[:, 0:1]), reads=[identbB], writes=[Buf("dmy_d2")])
        big = cx.sb("big", [128, BIGN], BF16)
        off = [0]

        def carve(n):
            n = (n + 15) // 16 * 16
            o = off[0]
            off[0] += n
            assert off[0] <= BIGN, off[0]
            return big[:, o:o + n]

        def alloc(name, n, dt):
            return carve(n) if dt == BF16 else carve(2 * n).bitcast(F32)[:, 0:n]

        x2TB, qTB, kTsB, vtsB, kTallB, vallB, kTpB, vtpB = (Buf(n_) for n_ in ("x2T", "qT", "kTs", "vts", "kTall", "vall", "kTp", "vtp"))
        groups = [[0, 1, 2, 3], [4, 5, 6, 7]]
        kpieces = [[(kps[j][p].ap(), Buf(f"kps{j}{p}")) for p in range(2)] for j in range(NT)]
        vpieces = [[(vps[j][p].ap(), Buf(f"vps{j}{p}")) for p in range(2)] for j in range(NT)]

        def after_k(j):
            for p in range(2):
                cx.cc_pending.append((lambda e, j=j, p=p: e.collective_compute("AllGather", ALU.bypass, replica_groups=groups,
                                                                              ins=[kps[j][p].ap().opt()], outs=[kpa[j][p].ap().opt()]),
                                      [kpieces[j][p][1]], [kTallB]))

        def after_v(j):
            for p in range(2):
                cx.cc_pending.append((lambda e, j=j, p=p: e.collective_compute("AllGather", ALU.bypass, replica_groups=groups,
                                                                              ins=[vps[j][p].ap().opt()], outs=[vpa[j][p].ap().opt()]),
                                      [vpieces[j][p][1]], [vallB]))
            if j == NT - 1:
                cx.cc_flush()

        a_bufs = layer0_and_qkv(cx, alloc, xc, hmask_d, pw1, pw2, w1a, w2a, wkv, wq, x2T, qT, kTs, vts, x2TB, qTB, kTsB, vtsB,
                                kpieces, vpieces, after_k, after_v)

        reg = st.enter_context(nc.sync.register("prevrow"))

        pval = []

        def copy_prev(dst, src_t, rows):
            def fn(e):
                if not pval:
                    e.reg_load(reg, prow_d[0:1, 0:1])
                    pval.append(e.snap(reg, min_val=0, max_val=3))
                return e.dma_start(out=dst, in_=src_t.ap()[bass.ts(pval[0], rows), :])
            return fn

        for j in range(NT):
            for p in range(2):
                P.op("sp", copy_prev(kTp[(D // 2) * p:(D // 2) * (p + 1), T * j:T * (j + 1)], kpa[j][p], D // 2),
                     reads=[kTallB], writes=[kTpB], accum=True, dsem="XK")
                P.op("sp", copy_prev(vtp[T * j + (T // 2) * p:T * j + (T // 2) * (p + 1), :], vpa[j][p], T // 2),
                     reads=[vallB], writes=[vtpB], accum=True, dsem="XV")

        attention_and_layer1(cx, carve, off, a_bufs, x2T, qT, kTs, kTp, vts, vtp, bias_d, wo, w1b, w2b, oT, out, identb, identbB,
                             x2TB, qTB, kTsB, kTpB, vtsB, vtpB)
        assert cx.wnext_get == len(cx.witems)
        import os
        if os.environ.get("KDEBUG"):
            for nm, ap_, shp, dt_, rb in (("dbg_kTp", kTp, [D, TOK], BF16, kTpB), ("dbg_kTs", kTs, [D, TOK], BF16, kTsB),
                                          ("dbg_vtp", vtp, [TOK, D], BF16, vtpB), ("dbg_vts", vts, [TOK, D], BF16, vtsB),
                                          ("dbg_oT", oT, [D, TOK], BF16, None), ("dbg_x2T", x2T, [D, TOK], F32, x2TB)):
                dd = nc.dram_tensor(nm, shp, dt_, kind="ExternalOutput").ap()
                P.op("sp", lambda e, dd=dd, ap_=ap_: e.dma_start(out=dd, in_=ap_), reads=([rb] if rb is not None else []), dsem="DBG")
        P.emit()
    return nc


def kernel(x, conv_pw1_w, conv_pw1_b, conv_dw_w, conv_dw_b, conv_ln_g, conv_ln_b, conv_pw2_w, conv_pw2_b,
           w_kv, attn_wq, attn_wo, rel_bias, mlp_w1, mlp_w2, ln_mix_g, ln_mix_b, ln_mlp_g, ln_mlp_b):
    f32 = np.float32
    x = np.asarray(x, f32)
    ncore = 8
    vecs = np.zeros((128, NV), f32)
    vecs[:, V_PW1B:V_PW1B + 32] = _fm(np.asarray(conv_pw1_b, f32)[0])
    dw = np.asarray(conv_dw_w, f32)[0]
    for tap in range(CONVW):
        vecs[:, V_DWW + 16 * tap:V_DWW + 16 * tap + 16] = _fm(dw[tap])
    vecs[:, V_DWB:V_DWB + 16] = _fm(np.asarray(conv_dw_b, f32)[0])
    vecs[:, V_CLNG:V_CLNG + 16] = _fm(np.asarray(conv_ln_g, f32)[0])
    vecs[:, V_CLNB:V_CLNB + 16] = _fm(np.asarray(conv_ln_b, f32)[0])
    vecs[:, V_PW2B:V_PW2B + 16] = _fm(np.asarray(conv_pw2_b, f32)[0])
    for l in range(2):
        vecs[:, V_MIXG + 16 * l:V_MIXG + 16 * l + 16] = _fm(np.asarray(ln_mix_g, f32)[l])
        vecs[:, V_MIXB + 16 * l:V_MIXB + 16 * l + 16] = _fm(np.asarray(ln_mix_b, f32)[l])
        vecs[:, V_MLPG + 16 * l:V_MLPG + 16 * l + 16] = _fm(np.asarray(ln_mlp_g, f32)[l])
        vecs[:, V_MLPB + 16 * l:V_MLPB + 16 * l + 16] = _fm(np.asarray(ln_mlp_b, f32)[l])
    ident = np.eye(128, dtype=f32)
    w1 = np.asarray(mlp_w1, f32)
    w2 = np.asarray(mlp_w2, f32)
    rel_bias = np.asarray(rel_bias, f32)
    shared = {
        "vecs": vecs, "ident": ident,
        "pw1": np.ascontiguousarray(np.asarray(conv_pw1_w, f32)[0]),
        "pw2": np.ascontiguousarray(np.asarray(conv_pw2_w, f32)[0]),
        "w1a": np.ascontiguousarray(w1[0]), "w2a": np.ascontiguousarray(w2[0]),
        "w1b": np.ascontiguousarray(w1[1]), "w2b": np.ascontiguousarray(w2[1]),
        "wkv": np.ascontiguousarray(np.asarray(w_kv, f32)),
        "wq": np.ascontiguousarray(np.asarray(attn_wq, f32)[0]),
        "wo": np.ascontiguousarray(np.asarray(attn_wo, f32)[0]),
    }
    bias_tiles = {True: _bias_tiles(rel_bias, True), False: _bias_tiles(rel_bias, False)}
    if "f" not in _NC_CACHE:
        _NC_CACHE["f"] = build_fused()
    ncf = _NC_CACHE["f"]
    in_maps = []
    for c in range(ncore):
        b, q = divmod(c, 4)
        xc = np.zeros((D, XH + TOK), f32)
        xc[:, XH:] = x[b, q * TOK:(q + 1) * TOK].T
        if q > 0:
            xc[:, :XH] = x[b, q * TOK - XH:q * TOK].T
        m = dict(shared)
        m.update({"xc": xc, "hmask": np.full((128, 1), 1.0 if q > 0 else 0.0, f32),
                  "prevrow": np.array([[max(q - 1, 0)]], np.int32), "biasT": bias_tiles[q > 0]})
        in_maps.append(m)
    res = run_bass_kernel_spmd(ncf, in_maps, core_ids=list(range(ncore))).results
    _NC_CACHE["last"] = res
    outp = np.zeros((2, 8192, D), f32)
    for c in range(ncore):
        b, q = divmod(c, 4)
        outp[b, q * TOK:(q + 1) * TOK] = np.asarray(res[c]["out"]).T
    return outp
```
